# Optimizing a Trainium2 kernel written in Bass

```python
import jax, jax.numpy as jnp
from jax import lax
import numpy as np

D_MODEL = 1024
BATCH = 8
SEQ = 2048
DEPTH = 2

GRID_W = 64
CTX_LEN = 256
MIX_W = D_MODEL
N_MIXERS = 4
GROUP_W = MIX_W // N_MIXERS
HEAD_DIM = 64
A_HEADS = GROUP_W // HEAD_DIM
A_KV_HEADS = A_HEADS // 2
A_WINDOW = 128
A_BLOCK = 128
B_CHUNK = 128
B_GROUPS = 4
B_GROUP_W = GROUP_W // B_GROUPS
C_POOLS = (2, 4, 8, 16)
C_GROUP_W = GROUP_W // len(C_POOLS)
D_HEADS = GROUP_W // HEAD_DIM
D_WIN_ROWS = 8
D_WIN_COLS = 16
FF_DIM = ((8 * D_MODEL + 3 * 256 - 1) // (3 * 256)) * 256
ROPE_BASE = 10000.0
NORM_EPS = 1e-6
NEG_INF = -1e30
IN_WIDTHS = (A_HEADS * HEAD_DIM, A_KV_HEADS * HEAD_DIM, A_KV_HEADS * HEAD_DIM,
             GROUP_W, GROUP_W,
             GROUP_W,
             D_HEADS * HEAD_DIM, D_HEADS * HEAD_DIM, D_HEADS * HEAD_DIM)
IN_TOTAL = sum(IN_WIDTHS)

kernel_name = "hymba_style_hybrid_dit_block"


def rms_norm(x, g):
    xf = x.astype(jnp.float32)
    y = xf * lax.rsqrt(jnp.mean(xf * xf, axis=-1, keepdims=True) + NORM_EPS)
    return (y * g.astype(jnp.float32)).astype(x.dtype)


def layer_norm(x, g):
    xf = x.astype(jnp.float32)
    mu = jnp.mean(xf, axis=-1, keepdims=True)
    var = jnp.mean(jnp.square(xf - mu), axis=-1, keepdims=True)
    return ((xf - mu) * lax.rsqrt(var + NORM_EPS) * g.astype(jnp.float32)).astype(x.dtype)


def modulate(h, shift, scale):
    return h * (1 + scale) + shift


def heads(t, n):
    return t.reshape(t.shape[:-1] + (n, HEAD_DIM))


def split_cols(p):
    offs = np.cumsum(IN_WIDTHS)[:-1].tolist()
    return jnp.split(p, offs, axis=-1)


def col_block(w, i):
    start = sum(IN_WIDTHS[:i])
    return w[:, start:start + IN_WIDTHS[i]]


def rope_1d(x, pos):
    half = x.shape[-1] // 2
    inv = jnp.power(ROPE_BASE, -jnp.arange(half, dtype=jnp.float32) / half)
    ang = pos.astype(jnp.float32)[:, None] * inv[None, :]
    cos = jnp.cos(ang)[None, :, None, :]
    sin = jnp.sin(ang)[None, :, None, :]
    xf = x.astype(jnp.float32)
    x1, x2 = xf[..., :half], xf[..., half:]
    return jnp.concatenate([x1 * cos - x2 * sin, x1 * sin + x2 * cos], axis=-1).astype(x.dtype)


def axial_rope(x, rows, cols):
    h = x.shape[-1] // 2
    return jnp.concatenate([rope_1d(x[..., :h], rows), rope_1d(x[..., h:], cols)], axis=-1)


def window_attention(q, k, v, kc, vc, sink):
    b, s, nh, dh = q.shape
    nkv = k.shape[2]
    g = nh // nkv
    nb = s // A_BLOCK
    scale = dh ** -0.5
    qb = q.reshape(b, nb, A_BLOCK, nkv, g, dh)
    pad = ((0, 0), (A_BLOCK, A_BLOCK), (0, 0), (0, 0))
    kp = jnp.pad(k, pad).reshape(b, nb + 2, A_BLOCK, nkv, dh)
    vp = jnp.pad(v, pad).reshape(b, nb + 2, A_BLOCK, nkv, dh)
    kb = jnp.concatenate([kp[:, :-2], kp[:, 1:-1], kp[:, 2:]], axis=2)
    vb = jnp.concatenate([vp[:, :-2], vp[:, 1:-1], vp[:, 2:]], axis=2)
    s_loc = jnp.einsum('bnqhgd,bnkhd->bnhgqk', qb, kb).astype(jnp.float32) * scale
    blk = jnp.arange(nb)[:, None, None]
    qpos = blk * A_BLOCK + jnp.arange(A_BLOCK)[None, :, None]
    kpos = (blk - 1) * A_BLOCK + jnp.arange(3 * A_BLOCK)[None, None, :]
    valid = (jnp.abs(kpos - qpos) <= A_WINDOW) & (kpos >= 0) & (kpos < s)
    s_loc = jnp.where(valid[None, :, None, None], s_loc, NEG_INF)
    s_ctx = jnp.einsum('bnqhgd,blhd->bnhgql', qb, kc).astype(jnp.float32) * scale
    s_sink = jnp.broadcast_to(sink.astype(jnp.float32).reshape(nkv, g)[None, None, :, :, None, None],
                              s_loc.shape[:-1] + (1,))
    probs = jax.nn.softmax(jnp.concatenate([s_loc, s_ctx, s_sink], axis=-1), axis=-1)
    nloc = 3 * A_BLOCK
    nctx = kc.shape[1]
    p_loc = probs[..., :nloc].astype(v.dtype)
    p_ctx = probs[..., nloc:nloc + nctx].astype(v.dtype)
    out = (jnp.einsum('bnhgqk,bnkhd->bnqhgd', p_loc, vb)
           + jnp.einsum('bnhgql,blhd->bnqhgd', p_ctx, vc))
    return out.reshape(b, s, nh * dh)


def context_attention(qc, kc, vc, sink):
    b, l, nh, dh = qc.shape
    nkv = kc.shape[2]
    g = nh // nkv
    qg = qc.reshape(b, l, nkv, g, dh)
    s = jnp.einsum('blhgd,bmhd->bhglm', qg, kc).astype(jnp.float32) * dh ** -0.5
    if sink is not None:
        s_sink = jnp.broadcast_to(sink.astype(jnp.float32).reshape(nkv, g)[None, :, :, None, None],
                                  s.shape[:-1] + (1,))
        s = jnp.concatenate([s, s_sink], axis=-1)
    probs = jax.nn.softmax(s, axis=-1)[..., :l].astype(vc.dtype)
    out = jnp.einsum('bhglm,bmhd->blhgd', probs, vc)
    return out.reshape(b, l, nh * dh)


def neighbourhood_attention(q, k, v, kc, vc, rpb):
    b, s, nh, dh = q.shape
    rows = s // GRID_W
    kh = min(D_WIN_ROWS, rows)
    kw = D_WIN_COLS
    scale = dh ** -0.5
    qg = q.reshape(b, rows, GRID_W, nh, dh)
    kg = k.reshape(b, rows, GRID_W, nh, dh)
    vg = v.reshape(b, rows, GRID_W, nh, dh)
    r = jnp.arange(rows)
    cidx = jnp.arange(GRID_W)
    row_start = jnp.clip(r - kh // 2, 0, rows - kh)
    row_idx = row_start[:, None] + jnp.arange(kh)[None, :]
    k_rows = kg[:, row_idx]
    v_rows = vg[:, row_idx]
    s_loc = jnp.einsum('brchd,brkwhd->bhrckw', qg, k_rows).astype(jnp.float32) * scale
    col_start = jnp.clip(cidx - kw // 2, 0, GRID_W - kw)
    col_ok = (cidx[None, :] >= col_start[:, None]) & (cidx[None, :] < col_start[:, None] + kw)
    roff = row_idx - r[:, None] + (D_WIN_ROWS - 1)
    coff = jnp.clip(cidx[None, :] - cidx[:, None], -(kw - 1), kw - 1) + (D_WIN_COLS - 1)
    bias = rpb[:, roff[:, None, :, None], coff[None, :, None, :]]
    s_loc = jnp.where(col_ok[None, None, None, :, None, :], s_loc + bias.astype(jnp.float32)[None], NEG_INF)
    nloc = kh * GRID_W
    s_loc = s_loc.reshape(b, nh, rows, GRID_W, nloc)
    s_ctx = jnp.einsum('brchd,blhd->bhrcl', qg, kc).astype(jnp.float32) * scale
    probs = jax.nn.softmax(jnp.concatenate([s_loc, s_ctx], axis=-1), axis=-1)
    p_loc = probs[..., :nloc].reshape(b, nh, rows, GRID_W, kh, GRID_W).astype(v.dtype)
    p_ctx = probs[..., nloc:].astype(v.dtype)
    out = (jnp.einsum('bhrckw,brkwhd->brchd', p_loc, v_rows)
           + jnp.einsum('bhrcl,blhd->brchd', p_ctx, vc))
    return out.reshape(b, s, nh * dh)


def spatial_gating(u, v, v_gain, w_s, b_s):
    b, n, _ = u.shape
    nc = n // B_CHUNK
    u = jax.nn.gelu(u)
    v = layer_norm(jax.nn.gelu(v), v_gain)
    vc = v.reshape(b, nc, B_CHUNK, B_GROUPS, B_GROUP_W)
    z = jnp.einsum('gpq,bnqgc->bnpgc', w_s, vc) + b_s.T[None, None, :, :, None]
    return u * z.reshape(b, n, GROUP_W)


def multiscale_pool(y, w_pool, scale):
    b, n, ch = y.shape
    yf = y.astype(jnp.float32)
    cs = jnp.concatenate([jnp.zeros((b, 1, ch), jnp.float32), jnp.cumsum(yf, axis=1)], axis=1)
    t = jnp.arange(n)
    outs = []
    for gi, w in enumerate(C_POOLS):
        sl = slice(gi * C_GROUP_W, (gi + 1) * C_GROUP_W)
        lo = jnp.clip(t - w // 2, 0, n)
        hi = jnp.clip(t - w // 2 + w, 0, n)
        csg = cs[..., sl]
        cnt = (hi - lo).astype(jnp.float32)[None, :, None]
        pooled = (csg[:, hi] - csg[:, lo]) / cnt - yf[..., sl]
        outs.append(jnp.einsum('bnc,cd->bnd', pooled.astype(y.dtype), w_pool[gi]))
    return jnp.concatenate(outs, axis=-1) * scale


def swiglu(h, wg, wu, wd):
    return (jax.nn.silu(h @ wg) * (h @ wu)) @ wd


def setup_inputs(seed: int = 0) -> dict:
    key = jax.random.key(seed)
    ks = jax.random.split(key, 24)
    f32 = jnp.float32

    def nrm(k, shape, s):
        return jax.random.normal(k, shape, f32) * s

    D, L = D_MODEL, DEPTH
    return {
        "x": nrm(ks[0], (BATCH, SEQ, D), 1.0),
        "c": nrm(ks[1], (BATCH, D), 1.0),
        "ctx": nrm(ks[2], (BATCH, CTX_LEN, D), 1.0),
        "c_ctx": nrm(ks[3], (D,), 1.0),
        "w_mod": nrm(ks[4], (L, D, 6 * D), 0.5 * D ** -0.5),
        "b_mod": nrm(ks[5], (L, 6 * D), 0.02),
        "g_mix": 1.0 + nrm(ks[6], (L, D), 0.05),
        "g_ffn": 1.0 + nrm(ks[7], (L, D), 0.05),
        "w_in": nrm(ks[8], (L, D, IN_TOTAL), D ** -0.5),
        "w_out": nrm(ks[9], (L, MIX_W, D), MIX_W ** -0.5),
        "a_q_gain": 1.0 + nrm(ks[10], (L, HEAD_DIM), 0.05),
        "a_k_gain": 1.0 + nrm(ks[11], (L, HEAD_DIM), 0.05),
        "a_sink": nrm(ks[12], (L, A_HEADS), 0.5),
        "b_v_gain": 1.0 + nrm(ks[13], (L, GROUP_W), 0.05),
        "b_w_s": nrm(ks[14], (L, B_GROUPS, B_CHUNK, B_CHUNK), B_CHUNK ** -0.5),
        "b_b_s": 1.0 + nrm(ks[15], (L, B_GROUPS, B_CHUNK), 0.02),
        "c_w_pool": nrm(ks[16], (L, len(C_POOLS), C_GROUP_W, C_GROUP_W), C_GROUP_W ** -0.5),
        "c_scale": 1.0 + nrm(ks[17], (L, GROUP_W), 0.1),
        "d_q_gain": 1.0 + nrm(ks[18], (L, HEAD_DIM), 0.05),
        "d_k_gain": 1.0 + nrm(ks[19], (L, HEAD_DIM), 0.05),
        "d_rpb": nrm(ks[20], (L, D_HEADS, 2 * D_WIN_ROWS - 1, 2 * D_WIN_COLS - 1), 0.1),
        "w_gate": nrm(ks[21], (L, D, FF_DIM), D ** -0.5),
        "w_up": nrm(ks[22], (L, D, FF_DIM), D ** -0.5),
        "w_down": nrm(ks[23], (L, FF_DIM, D), FF_DIM ** -0.5),
    }


def reference(x, c, ctx, c_ctx, w_mod, b_mod, g_mix, g_ffn, w_in, w_out, a_q_gain, a_k_gain, a_sink,
              b_v_gain, b_w_s, b_b_s, c_w_pool, c_scale, d_q_gain, d_k_gain, d_rpb, w_gate, w_up, w_down):
    s = x.shape[1]
    t = jnp.arange(s)
    row_pos = t // GRID_W
    col_pos = t % GRID_W
    xc = ctx
    c_act = jax.nn.silu(c)
    cc_act = jax.nn.silu(c_ctx)
    for l in range(DEPTH):
        last = l == DEPTH - 1
        mod = (c_act @ w_mod[l] + b_mod[l])[:, None, :]
        sh1, sc1, g1, sh2, sc2, g2 = jnp.split(mod, 6, axis=-1)
        modc = cc_act @ w_mod[l] + b_mod[l]
        csh1, csc1, cg1, csh2, csc2, cg2 = jnp.split(modc, 6, axis=-1)

        h = modulate(rms_norm(x, g_mix[l]), sh1, sc1)
        hc = modulate(rms_norm(xc, g_mix[l]), csh1, csc1)
        aq, ak, av, bu, bv, cin, dq, dk, dv = split_cols(h @ w_in[l])
        if last:
            cak, cav, cdk, cdv = [hc @ col_block(w_in[l], i) for i in (1, 2, 7, 8)]
        else:
            caq, cak, cav, cbu, cbv, ccin, cdq, cdk, cdv = split_cols(hc @ w_in[l])

        kc_a = rms_norm(heads(cak, A_KV_HEADS), a_k_gain[l])
        vc_a = heads(cav, A_KV_HEADS)
        kc_d = rms_norm(heads(cdk, D_HEADS), d_k_gain[l])
        vc_d = heads(cdv, D_HEADS)

        q_a = axial_rope(rms_norm(heads(aq, A_HEADS), a_q_gain[l]), row_pos, col_pos)
        k_a = axial_rope(rms_norm(heads(ak, A_KV_HEADS), a_k_gain[l]), row_pos, col_pos)
        out_a = window_attention(q_a, k_a, heads(av, A_KV_HEADS), kc_a, vc_a, a_sink[l])
        out_b = spatial_gating(bu, bv, b_v_gain[l], b_w_s[l], b_b_s[l])
        out_c = multiscale_pool(cin, c_w_pool[l], c_scale[l])
        out_d = neighbourhood_attention(rms_norm(heads(dq, D_HEADS), d_q_gain[l]),
                                        rms_norm(heads(dk, D_HEADS), d_k_gain[l]),
                                        heads(dv, D_HEADS), kc_d, vc_d, d_rpb[l])
        mix = jnp.concatenate([out_a, out_b, out_c, out_d], axis=-1) @ w_out[l]
        x = x + g1 * mix
        x = x + g2 * swiglu(modulate(rms_norm(x, g_ffn[l]), sh2, sc2), w_gate[l], w_up[l], w_down[l])

        if not last:
            oc_a = context_attention(rms_norm(heads(caq, A_HEADS), a_q_gain[l]), kc_a, vc_a, a_sink[l])
            oc_b = spatial_gating(cbu, cbv, b_v_gain[l], b_w_s[l], b_b_s[l])
            oc_c = multiscale_pool(ccin, c_w_pool[l], c_scale[l])
            oc_d = context_attention(rms_norm(heads(cdq, D_HEADS), d_q_gain[l]), kc_d, vc_d, None)
            mixc = jnp.concatenate([oc_a, oc_b, oc_c, oc_d], axis=-1) @ w_out[l]
            xc = xc + cg1 * mixc
            xc = xc + cg2 * swiglu(modulate(rms_norm(xc, g_ffn[l]), csh2, csc2), w_gate[l], w_up[l], w_down[l])
    return x
```

```python
import contextlib
import numpy as np
import concourse.bass as bass
import concourse.mybir as mybir
from concourse.bass_utils import run_bass_kernel_spmd

F32 = mybir.dt.float32
BF16 = mybir.dt.bfloat16
AF = mybir.ActivationFunctionType
ALU = mybir.AluOpType

D = 1024
SEQ = 2048
CTX = 256
NTOK = SEQ + CTX
FF = 2816
NFF = 22
NEG = -30000.0
EPS = 1e-6
GROUPS = [(0, 512), (512, 512), (1024, 512), (1536, 512), (2048, 256)]
HALVES = [[0, 1], [2, 3, 4]]
HSTART = [0, 1024]
FFBLOCKS = [list(range(0, 8)), list(range(8, 15)), list(range(15, 22))]
PIECE = 2048
import os as _os
_DBG_DELAY = int(_os.environ.get('DBG_DELAY', '0'))
_DBG_SKIP = set(int(v) for v in _os.environ.get('DBG_SKIP_A', '').split(',') if v)


class _Op:
    __slots__ = ("eng", "fn", "dma", "deps", "has_dep", "sig")

    def __init__(self, eng, fn, dma):
        self.eng = eng
        self.fn = fn
        self.dma = dma
        self.deps = []
        self.has_dep = False
        self.sig = None


class Sched:
    COMPUTE = ("pe", "act", "dve")

    def __init__(self, nc, n_dma_sems=8):
        self.nc = nc
        self.ops = []
        self.res = {}
        self.n_dma_sems = n_dma_sems
        self.last = {}
        self.pending = {}
        self.dma_since = []

    def op(self, eng, fn, reads=(), writes=(), dma=False):
        o = _Op(eng, fn, dma)
        deps = {}
        for r in reads:
            st = self.res.get(r)
            if st is not None and st[0] is not None:
                deps[id(st[0])] = (st[0], True)
        for w in writes:
            st = self.res.get(w)
            if st is not None:
                if st[0] is not None and id(st[0]) not in deps:
                    deps[id(st[0])] = (st[0], False)
                for rd in st[1]:
                    if id(rd) not in deps:
                        deps[id(rd)] = (rd, False)
        for d in self.pending.pop(eng, ()):
            if id(d) not in deps:
                deps[id(d)] = (d, True)
        for d, raw in deps.values():
            if d is o:
                continue
            if not d.dma and not o.dma and d.eng == eng:
                if eng == "pe" or not raw:
                    continue
            o.deps.append(d)
            d.has_dep = True
        for r in reads:
            st = self.res.get(r)
            if st is None:
                self.res[r] = [None, [o]]
            else:
                st[1].append(o)
        for w in writes:
            self.res[w] = [o, []]
        self.ops.append(o)
        self.last[eng] = o
        if dma:
            self.dma_since.append(o)
        return o

    def barrier(self):
        lasts = [o for e, o in self.last.items() if e in self.COMPUTE]
        lasts += self.dma_since
        self.dma_since = []
        for e in self.COMPUTE:
            self.pending[e] = self.pending.get(e, []) + lasts

    def emit(self, stack):
        nc = self.nc
        eobj = {"pe": nc.tensor, "act": nc.scalar, "dve": nc.vector, "pool": nc.gpsimd, "sp": nc.sync}
        semh = {}
        cnt = {}
        waited = {e: {} for e in eobj}
        dma_rr = {}
        dcnt = {}

        def get_sem(key):
            if key not in semh:
                semh[key] = stack.enter_context(nc.semaphore("s_" + "_".join(str(k) for k in key)))
            return semh[key]

        for o in self.ops:
            E = eobj[o.eng]
            need = {}
            for d in o.deps:
                key, val = d.sig
                if need.get(key, 0) < val:
                    need[key] = val
            if o.dma:
                i = dma_rr.get(o.eng, 0)
                dma_rr[o.eng] = (i + 1) % self.n_dma_sems
                k = ("dma", o.eng, i)
                n = dcnt.get(k, 0) + 1
                dcnt[k] = n
                if n > 1 and need.get(k, 0) < 16 * (n - 1):
                    need[k] = 16 * (n - 1)
            for key, val in need.items():
                if waited[o.eng].get(key, 0) < val:
                    E.wait_ge(get_sem(key), val)
                    waited[o.eng][key] = val
            ins = o.fn()
            if o.dma:
                ins.then_inc(get_sem(k), 16)
                o.sig = (k, 16 * n)
            elif o.has_dep:
                key = ("e", o.eng)
                cnt[key] = cnt.get(key, 0) + 1
                ins.then_inc(get_sem(key), 1)
                o.sig = (key, cnt[key])
        for key, n in dcnt.items():
            nc.sync.wait_ge(get_sem(key), 16 * n)
        for key, n in cnt.items():
            nc.sync.wait_ge(get_sem(key), n)


def _kmajor(w):
    K, N = w.shape
    return np.ascontiguousarray(w.reshape(K // 128, 128, N).transpose(1, 0, 2))


def _pad_piece(a):
    a = np.ascontiguousarray(a, dtype=np.float32).reshape(128, -1)
    out = np.zeros((128, PIECE), np.float32)
    out[:, : a.shape[1]] = a
    return out


def _dslot(interior, s):
    if 3 <= s <= 9:
        return 8 + (9 - s)
    if s >= 10:
        return (0 if interior else 4) + (13 - s)
    return (18 if interior else 15) + (2 - s)


def _consts():
    c = np.zeros((128, 6, 128), np.float32)
    c[:, 0, :] = np.eye(128)
    c[:, 1, :] = 1.0
    c[0:64, 2, 0:64] = 1.0
    c[64:128, 2, 64:128] = 1.0
    for m in range(128):
        if (m % 32) < 16:
            c[m + 16, 3, m] = -1.0
        else:
            c[m - 16, 3, m] = 1.0
    j = np.arange(128)[:, None]
    i = np.arange(128)[None, :]
    c[:, 4, :] = np.where(j >= i, 0.0, NEG)
    c[:, 5, :] = np.where(j <= i, 0.0, NEG)
    p = np.arange(128)
    idx = p % 64
    f = (idx % 16).astype(np.float64)
    inv = np.power(10000.0, -f / 16.0)
    t = np.arange(SEQ)
    pos = np.where((idx < 32)[:, None], (t // 64)[None, :], (t % 64)[None, :]).astype(np.float64)
    ang = pos * inv[:, None]
    rope = np.stack([np.cos(ang), np.sin(ang)], axis=1).astype(np.float32)
    pm = np.zeros((128, 20, 128), np.float32)
    jj = np.arange(128)[:, None]
    tt = np.arange(128)[None, :]
    for g, w in enumerate((2, 4, 8, 16)):
        h = w // 2
        eye = (jj == tt).astype(np.float32)
        pm[:, g * 5 + 0, :] = ((jj >= tt - h) & (jj < tt + h)) / w - eye
        lo = np.maximum(tt - h, 0)
        cntf = (tt + h) - lo
        pm[:, g * 5 + 1, :] = ((jj >= lo) & (jj < tt + h)) / cntf - eye
        hi = np.minimum(tt + h, 128)
        cntl = hi - (tt - h)
        pm[:, g * 5 + 2, :] = ((jj >= tt - h) & (jj < hi)) / cntl - eye
        pm[:, g * 5 + 3, :] = (jj >= 128 + tt - h) / w
        pm[:, g * 5 + 4, :] = (jj < tt + h - 128) / w
    return c, rope, pm


def _dtab(rpb):
    kc = np.arange(64)[:, None]
    qc = np.arange(64)[None, :]
    cs = np.clip(qc - 8, 0, 48)
    col_ok = (kc >= cs) & (kc < cs + 16)
    coff = np.clip(kc - qc, -15, 15) + 15
    out = np.full((128, 4, 21, 64), NEG, np.float32)
    for h in range(4):
        for interior in (True, False):
            for s in range(14):
                sl = _dslot(interior, s)
                for kr in range(2):
                    rho = s + kr
                    ok = 0 <= rho <= 14 and ((3 <= rho <= 10) or not interior)
                    if not ok:
                        continue
                    T = np.where(col_ok, rpb[h, rho][coff], NEG)
                    out[kr * 64:(kr + 1) * 64, h, sl, :] = T
    return out


def _layer_arrays(inp, l):
    f32 = np.float32
    w_in = inp["w_in"][l]
    pieces = []
    wm = _kmajor(inp["w_mod"][l])
    for jj in range(24):
        pieces.append(_pad_piece(wm[:, :, jj * 256:(jj + 1) * 256]))
    aq = w_in[:, 0:256]
    fm_cols = [np.concatenate([aq[:, 0:64], aq[:, 128:192]], 1), np.concatenate([aq[:, 64:128], aq[:, 192:256]], 1),
               w_in[:, 256:384], w_in[:, 1280:1408], w_in[:, 1408:1536], w_in[:, 1536:1664], w_in[:, 1664:1792]]
    fm = _kmajor(np.concatenate(fm_cols, 1))
    for a in range(4):
        pieces.append(_pad_piece(fm[:, :, a * 256:min((a + 1) * 256, 896)]))
    tm = [w_in[:, 512:768], w_in[:, 768:1024], w_in[:, 1024:1280], w_in[:, 1792:2048], w_in[:, 384:512]]
    for a in tm:
        pieces.append(_pad_piece(_kmajor(a)))
    wo = _kmajor(inp["w_out"][l])
    for a in range(4):
        pieces.append(_pad_piece(wo[:, :, a * 256:(a + 1) * 256]))
    wg = _kmajor(inp["w_gate"][l])
    wu = _kmajor(inp["w_up"][l])
    for f in range(NFF):
        pieces.append(_pad_piece(np.concatenate([wg[:, :, f * 128:(f + 1) * 128].reshape(128, -1),
                                                 wu[:, :, f * 128:(f + 1) * 128].reshape(128, -1)], 1)))
    wd = _kmajor(inp["w_down"][l])
    for blk in FFBLOCKS:
        for dch in range(8):
            pieces.append(_pad_piece(wd[:, blk[0]:blk[-1] + 1, dch * 128:(dch + 1) * 128]))
    W = np.stack(pieces, 0)
    vec = np.zeros((128, 26), f32)
    vec[:, 0:8] = inp["g_mix"][l].reshape(8, 128).T
    vec[:, 8:16] = inp["g_ffn"][l].reshape(8, 128).T
    vec[:, 16] = np.tile(inp["a_q_gain"][l], 2)
    vec[:, 17] = np.tile(inp["a_k_gain"][l], 2)
    vec[:, 18] = np.tile(inp["d_q_gain"][l], 2)
    vec[:, 19] = np.tile(inp["d_k_gain"][l], 2)
    vec[:, 20:22] = inp["c_scale"][l].reshape(2, 128).T
    vec[:, 22:26] = inp["b_b_s"][l].T
    bmod = np.ascontiguousarray(inp["b_mod"][l].reshape(48, 128).T)
    bc = np.zeros((128, 260), f32)
    bc[:, 0:4] = inp["a_sink"][l][None, :]
    bc[:, 4:260] = inp["b_v_gain"][l][None, :]
    sm = np.zeros((128, 6, 128), f32)
    sm[:, 0:4, :] = inp["b_w_s"][l].transpose(2, 0, 1)
    wp = inp["c_w_pool"][l]
    for g in range(4):
        o = (g % 2) * 64
        sm[o:o + 64, 4 + g // 2, o:o + 64] = wp[g]
    dt = _dtab(inp["d_rpb"][l])
    return {"W": W, "vec": vec, "bmod": bmod, "bc": bc, "sm": np.ascontiguousarray(sm.reshape(128, -1)), "dtab": np.ascontiguousarray(dt.reshape(128, -1))}


NP_MOD, NP_FM, NP_TM, NP_WO, NP_GU, NP_WD = 0, 24, 28, 33, 37, 59
NPIECES = 83


class _Stop(Exception):
    pass


def build(n_layers=2, taps=(), stop=None):
    nc = bass.Bass("TRN2", target_bir_lowering=False)
    dr = {}

    def din(name, shape):
        dr[name] = nc.dram_tensor(name, list(shape), F32, kind="ExternalInput").ap()
        return dr[name]

    x_d = din("x", [SEQ, D])
    ctx_d = din("ctx", [CTX, D])
    cvec_d = din("cvec", [128, 16])
    cbf_d = din("cbf", [128, 6 * 128])
    idf_d = din("idf", [128, 128])
    rope_d = din("rope", [128, 2 * SEQ])
    pm_d = din("pm", [128, 20 * 128])
    L = []
    for l in range(n_layers):
        L.append({
            "W": din(f"W{l}", [NPIECES, 128, PIECE]), "vec": din(f"vec{l}", [128, 26]), "bmod": din(f"bmod{l}", [128, 48]),
            "bc": din(f"bc{l}", [128, 260]), "sm": din(f"sm{l}", [128, 6 * 128]), "dtab": din(f"dtab{l}", [128, 4 * 21 * 64]),
        })
    out_d = nc.dram_tensor("out", [SEQ, D], F32, kind="ExternalOutput").ap()
    tap_d = {}

    st = contextlib.ExitStack()
    with st:
        S = Sched(nc)
        sb = lambda n, s, d: st.enter_context(nc.sbuf_tensor(n, list(s), d))
        ps = lambda n, s, d: st.enter_context(nc.psum_tensor(n, list(s), d))
        X = sb("X", [128, 8, NTOK], F32)
        HB = sb("HB", [128, 8, 1280], BF16)
        RR = sb("RR", [128, 32472], BF16)
        QT = RR[:, 0:9216].rearrange("p (k t) -> p k t", k=4)
        KT = RR[:, 9216:16128].rearrange("p (k t) -> p k t", k=3)
        VA = RR[:, 16128:18504].rearrange("p (t h e) -> p t h e", t=18, h=2)
        VD = RR[:, 18504:23256].rearrange("p (t h e) -> p t h e", t=18, h=4)
        OB = RR[:, 23256:27864].rearrange("p (t c) -> p t c", t=18)
        CY = RR[:, 27864:32472].rearrange("p (t c) -> p t c", t=18)
        ACTB = RR[:, 0:18432].rearrange("p (f t) -> p f t", f=8)
        H2B = RR[:, 18432:28672].rearrange("p (k t) -> p k t", k=8)
        RING = [sb(f"ring{i}", [128, PIECE], BF16) for i in range(4)]
        TAB = sb("TAB", [128, 5376], BF16)
        SCR = sb("SCR", [128, 2048], F32)
        Fs = [SCR[:, i * 512:(i + 1) * 512] for i in range(4)]
        SQ = [sb(f"SQ{i}", [128, 512], BF16) for i in range(2)]
        QG = sb("QG", [128, 512], BF16)
        RQ = sb("RQ", [128, 512], BF16)
        PT = [sb(f"PT{i}", [128, 896], BF16) for i in range(2)]
        OT = [sb(f"OT{i}", [128, 256], BF16) for i in range(2)]
        PC = sb("PC", [128, 2, 512], BF16)
        VN = [sb(f"VN{i}", [128, 256], BF16) for i in range(2)]
        CBF = sb("CBF", [128, 6, 128], BF16)
        IDF = sb("IDF", [128, 128], F32)
        CV = sb("CV", [128, 8, 2], F32)
        CA = sb("CA", [128, 8, 2], BF16)
        SMALL = sb("SMALL", [128, 64], F32)
        IDB = CBF[:, 0, :]
        ONES = CBF[:, 1, :]
        BDONES = CBF[:, 2, :]
        ROTT = CBF[:, 3, :]
        MASKP = CBF[:, 4, :]
        MASKN = CBF[:, 5, :]
        P0 = ps("P0", [128, 1024], F32)
        P1 = ps("P1", [128, 1024], F32)
        P2 = ps("P2", [128, 1024], F32)
        P3 = ps("P3", [128, 512], F32)
        PST = ps("PST", [128, 1024], BF16)
        PSV = {"P0a": P0[:, 0:512], "P0b": P0[:, 512:1024], "P1a": P1[:, 0:512], "P1b": P1[:, 512:1024],
               "P2a": P2[:, 0:512], "P2b": P2[:, 512:1024], "P3": P3[:, :]}

        def mm(out, lhsT, rhs, start, stop, reads, writes):
            S.op("pe", lambda: nc.tensor.matmul(out, lhsT=lhsT, rhs=rhs, start=start, stop=stop), reads, writes)

        def tr(out, in_, ident, reads, writes):
            S.op("pe", lambda: nc.tensor.transpose(out, in_, ident), reads, writes)

        def act(out, in_, func, reads, writes, scale=None, bias=None):
            kw = {}
            if scale is not None:
                kw["scale"] = scale
            if bias is not None:
                kw["bias"] = bias
            S.op("act", lambda: nc.scalar.activation(out=out, in_=in_, func=func, **kw), reads, writes)

        def tt(out, in0, in1, op, reads, writes):
            S.op("dve", lambda: nc.vector.tensor_tensor(out=out, in0=in0, in1=in1, op=op), reads, writes)

        def stt(out, in0, scalar, in1, op0, op1, reads, writes):
            S.op("dve", lambda: nc.vector.scalar_tensor_tensor(out=out, in0=in0, scalar=scalar, in1=in1, op0=op0, op1=op1),
                 reads, writes)

        def ts(out, in0, s1, s2, op0, op1, reads, writes):
            S.op("dve", lambda: nc.vector.tensor_scalar(out=out, in0=in0, scalar1=s1, scalar2=s2, op0=op0, op1=op1),
                 reads, writes)

        def cpv(out, in_, reads, writes):
            S.op("dve", lambda: nc.vector.tensor_copy(out=out, in_=in_), reads, writes)

        def dma_sp(out, in_, reads, writes):
            S.op("sp", lambda: nc.sync.dma_start(out=out, in_=in_), reads, writes, dma=True)

        def dma_pool(out, in_, reads, writes):
            S.op("pool", lambda: nc.gpsimd.dma_start(out=out, in_=in_, max_dma_last_dim=4096), reads, writes, dma=True)

        def stage(name):
            if stop == name:
                S.barrier()
                raise _Stop()

        def tap(name, ap, shape, dtype, reads):
            if name not in taps:
                return
            S.barrier()
            t = nc.dram_tensor("tap_" + name, list(shape), dtype, kind="ExternalOutput").ap()
            tap_d[name] = t
            dma_sp(t, ap, list(reads) + ["tapsrc"], ["tap_" + name])

        ring_i = [0]

        def load_piece(l, idx, nelem=PIECE):
            i = ring_i[0]
            ring_i[0] = (i + 1) % 4
            dma_pool(RING[i][:, 0:nelem], L[l]["W"][idx, :, 0:nelem], [], [("slot", i)])
            return RING[i], ("slot", i)

        dma_sp(IDF[:], idf_d, [], ["IDF"])
        dma_sp(CV[:].rearrange("p k s -> p (k s)"), cvec_d, [], ["CV"])
        dma_pool(CBF[:].rearrange("p a b -> p (a b)"), cbf_d, [], ["CBF"])
        act(CA[:], CV[:], AF.Silu, ["CV"], ["CA"])
        for t in range(18):
            stg = SCR[:, (t % 2) * 1024:(t % 2 + 1) * 1024]
            stg_r = ["F0", "F1"] if t % 2 == 0 else ["F2", "F3"]
            src = x_d[t * 128:(t + 1) * 128, :] if t < 16 else ctx_d[(t - 16) * 128:(t - 15) * 128, :]
            dma_sp(stg, src, [], stg_r)
            Pt = P0 if t % 2 == 0 else P1
            pr = ["P0a", "P0b"] if t % 2 == 0 else ["P1a", "P1b"]
            for k in range(8):
                tr(Pt[:, k * 128:(k + 1) * 128], stg[:, k * 128:(k + 1) * 128], IDF[:], stg_r + ["IDF"], pr)
            dst = X[:, :, t * 128:(t + 1) * 128]
            src_ps = Pt[:].rearrange("p (k t) -> p k t", k=8)
            xr = [("X", t // 4 if t < 16 else 4)]
            if t % 2 == 0:
                act(dst, src_ps, AF.Copy, pr, xr)
            else:
                cpv(dst, src_ps, pr, xr)
        S.barrier()

        def xres(c):
            return ("X", c)

        def sidx(c):
            return 1 if c == 4 else 0

        VEC = sb("VEC", [128, 26], F32)
        BMOD = sb("BMOD", [128, 48], F32)
        BC = sb("BC", [128, 260], F32)
        SM = sb("SM", [128, 6, 128], BF16)
        MOD = sb("MOD", [128, 48, 2], F32)
        A1 = sb("A1", [128, 8, 2], F32)
        A2 = sb("A2", [128, 8, 2], F32)
        GS = sb("GS", [128, 8], F32)

        def run_layers():
          for l in range(n_layers):
              last = l == n_layers - 1
              Ld = L[l]
              dma_sp(VEC[:], Ld["vec"], [], ["VEC"])
              dma_sp(BMOD[:], Ld["bmod"], [], ["BMOD"])
              dma_sp(BC[:], Ld["bc"], [], ["BC"])
              dma_pool(SM[:].rearrange("p a b -> p (a b)"), Ld["sm"], [], ["SM"])
              WST = SM[:, 0:4, :]
              WPBD = SM[:, 4:6, :]

              for jj in range(24):
                  slot, sr = load_piece(l, NP_MOD + jj)
                  sv = slot[:].rearrange("p (k n) -> p k n", k=8)
                  for cc in range(2):
                      j = 2 * jj + cc
                      for k in range(8):
                          mm(P3[:, 2 * j:2 * j + 2], sv[:, k, cc * 128:(cc + 1) * 128], CA[:, k, :], k == 0, k == 7,
                             [sr, "CA"], ["P3"])
              tt(MOD[:], P3[:, 0:96].rearrange("p (j s) -> p j s", s=2), BMOD[:].unsqueeze(2).to_broadcast([128, 48, 2]),
                 ALU.add, ["P3", "BMOD"], ["MOD"])
              stt(A1[:], MOD[:, 8:16, :], 1.0, VEC[:, 0:8].unsqueeze(2).to_broadcast([128, 8, 2]), ALU.add, ALU.mult,
                  ["MOD", "VEC"], ["A1"])
              stt(A2[:], MOD[:, 32:40, :], 1.0, VEC[:, 8:16].unsqueeze(2).to_broadcast([128, 8, 2]), ALU.add, ALU.mult,
                  ["MOD", "VEC"], ["A2"])
              ts(GS[:, 0:1], VEC[:, 16:17], 0.125, None, ALU.mult, ALU.bypass, ["VEC"], ["GS"])
              ts(GS[:, 1:2], VEC[:, 17:18], 1.0, None, ALU.mult, ALU.bypass, ["VEC"], ["GS"])
              ts(GS[:, 2:3], VEC[:, 18:19], 0.125, None, ALU.mult, ALU.bypass, ["VEC"], ["GS"])
              ts(GS[:, 3:4], VEC[:, 19:20], 1.0, None, ALU.mult, ALU.bypass, ["VEC"], ["GS"])
              act(GS[:, 4:8], BC[:, 0:4], AF.Exp, ["BC"], ["GS"])
              SH1, G1, SH2, G2 = MOD[:, 0:8, :], MOD[:, 16:24, :], MOD[:, 24:32, :], MOD[:, 40:48, :]
              stage("modload")
              tap(f"mod{l}", MOD[:], [128, 48, 2], F32, ["MOD"])

              stage("mod")
              def do_norm(c, Amul, SHv, dst_fn, tag):
                  s0, Lc = GROUPS[c]
                  s = sidx(c)
                  psn, psr = (PSV["P2a"], "P2a") if c % 2 == 0 else (PSV["P2b"], "P2b")
                  for k in range(8):
                      sq = SQ[k % 2]
                      act(sq[:, 0:Lc], X[:, k, s0:s0 + Lc], AF.Square, [xres(c)], [f"SQ{k % 2}"])
                      mm(psn[:, 0:Lc], ONES, sq[:, 0:Lc], k == 0, k == 7, [f"SQ{k % 2}", "CBF"], [psr])
                  act(Fs[0][:, 0:Lc], psn[:, 0:Lc], AF.Ln, [psr], ["F0"], scale=1.0 / D, bias=EPS)
                  act(Fs[1][:, 0:Lc], Fs[0][:, 0:Lc], AF.Exp, ["F0"], ["F1"], scale=-0.5)
                  for k in range(8):
                      tmp, tr_ = (Fs[2], "F2") if k % 2 == 0 else (Fs[3], "F3")
                      tt(tmp[:, 0:Lc], X[:, k, s0:s0 + Lc], Fs[1][:, 0:Lc], ALU.mult, [xres(c), "F1"], [tr_])
                      dst, dres = dst_fn(k, c)
                      act(dst, tmp[:, 0:Lc], AF.Identity, [tr_, "MOD", tag], [dres], scale=Amul[:, k, s:s + 1],
                          bias=SHv[:, k, s:s + 1])

              groups_all = [0, 1, 2, 3] if last else [0, 1, 2, 3, 4]

              dma_pool(TAB[:, 0:4096], rope_d, [], ["TAB"])
              COS = TAB[:, 0:2048]
              SIN = TAB[:, 2048:4096]
              S.op("dve", lambda: nc.vector.memset(VA[:, :, :, 64:65], 1.0), [], ["VAones"])
              S.op("dve", lambda: nc.vector.memset(VD[:, :, :, 64:65], 1.0), [], ["VDones"])

              def hcols(c, half):
                  s0, Lc = GROUPS[c]
                  return s0 - HSTART[half], Lc

              for half in range(2):
                  hgroups = HALVES[half]
                  for c in hgroups:
                      h0, Lc = hcols(c, half)
                      do_norm(c, A1, SH1, lambda k, c, h0=h0, Lc=Lc: (HB[:, k, h0:h0 + Lc], ("HB", c)), "A1")
                  if half == 0:
                      tap(f"hT{l}", HB[:, :, 0:1024], [128, 8, 1024], BF16, [("HB", 0), ("HB", 1)])
                  stage(f"norm1_{half}")
                  fcount = 0
                  for a in range(4):
                      js = [2 * a, 2 * a + 1] if a < 3 else [6]
                      slot, sr = load_piece(l, NP_FM + a, 8 * 128 * len(js))
                      sv = slot[:, 0:8 * 128 * len(js)].rearrange("p (k n) -> p k n", k=8)
                      for ji, j in enumerate(js):
                          for c in hgroups:
                              if last and c == 4 and j in (0, 1, 3, 4):
                                  continue
                              s0, Lc = GROUPS[c]
                              h0, _ = hcols(c, half)
                              pfn = ["P0a", "P0b", "P1a"][fcount % 3]
                              phn = ["P1b", "P2a"][fcount % 2]
                              prn = ["P2b", "P3"][fcount % 2]
                              fcount += 1
                              pf, ph, pr_ = PSV[pfn][:, 0:Lc], PSV[phn][:, 0:Lc], PSV[prn][:, 0:Lc]
                              for k in range(8):
                                  mm(pf, sv[:, k, ji * 128:(ji + 1) * 128], HB[:, k, h0:h0 + Lc], k == 0, k == 7,
                                     [sr, ("HB", c)], [pfn])
                              sq = SQ[fcount % 2]
                              sqn = f"SQ{fcount % 2}"
                              act(sq[:, 0:Lc], pf, AF.Square, [pfn], [sqn])
                              mm(ph, BDONES, sq[:, 0:Lc], True, True, [sqn, "CBF"], [phn])
                              act(Fs[0][:, 0:Lc], ph, AF.Ln, [phn], ["F0"], scale=1.0 / 64, bias=EPS)
                              act(Fs[1][:, 0:Lc], Fs[0][:, 0:Lc], AF.Exp, ["F0"], ["F1"], scale=-0.5)
                              gcol = {0: 0, 1: 0, 2: 1, 3: 2, 4: 2, 5: 3, 6: 3}[j]
                              gain = GS[:, gcol:gcol + 1]
                              if j < 2:
                                  dst, dres = QT[:, j, s0:s0 + Lc], ("QT", j, c)
                              elif j == 2:
                                  dst, dres = KT[:, 0, s0:s0 + Lc], ("KT", 0, c)
                              elif j < 5:
                                  dst, dres = QT[:, j - 1, s0:s0 + Lc], ("QT", j - 1, c)
                              else:
                                  dst, dres = KT[:, j - 4, s0:s0 + Lc], ("KT", j - 4, c)
                              if j < 3 and c != 4:
                                  act(QG[:, 0:Lc], pf, AF.Identity, [pfn, "GS"], ["QG"], scale=gain)
                                  mm(pr_, ROTT, QG[:, 0:Lc], True, True, ["QG", "CBF"], [prn])
                                  act(RQ[:, 0:Lc], pr_, AF.Copy, [prn], ["RQ"])
                                  tt(Fs[2][:, 0:Lc], QG[:, 0:Lc], COS[:, s0:s0 + Lc], ALU.mult, ["QG", "TAB"], ["F2"])
                                  tt(Fs[3][:, 0:Lc], RQ[:, 0:Lc], SIN[:, s0:s0 + Lc], ALU.mult, ["RQ", "TAB"], ["F3"])
                                  tt(Fs[2][:, 0:Lc], Fs[2][:, 0:Lc], Fs[3][:, 0:Lc], ALU.add, ["F2", "F3"], ["F2"])
                                  tt(dst, Fs[2][:, 0:Lc], Fs[1][:, 0:Lc], ALU.mult, ["F2", "F1"], [dres])
                              else:
                                  stt(dst, pf, gain, Fs[1][:, 0:Lc], ALU.mult, ALU.mult, [pfn, "GS", "F1"], [dres])
                  stage(f"fm_{half}")
                  tiles_h = []
                  for c in hgroups:
                      s0, Lc = GROUPS[c]
                      tiles_h += [(t, c) for t in range(s0 // 128, (s0 + Lc) // 128)]
                  tcount = 0
                  for pi in range(5):
                      ncol = 128 if pi == 4 else 256
                      slot, sr = load_piece(l, NP_TM + pi, 8 * ncol)
                      sv = slot[:, 0:8 * ncol].rearrange("p (k n) -> p k n", k=8)
                      for (t, c) in tiles_h:
                          if last and c == 4 and pi < 3:
                              continue
                          h0 = t * 128 - HSTART[half]
                          pn = ["P0a", "P0b", "P1a", "P1b"][tcount % 4]
                          tcount += 1
                          pp = PSV[pn][:, 0:ncol]
                          for k in range(8):
                              mm(pp, HB[:, k, h0:h0 + 128], sv[:, k, :], k == 0, k == 7, [sr, ("HB", c)], [pn])
                          if pi == 0:
                              act(OB[:, t, :], pp, AF.Gelu_apprx_tanh, [pn], [("OB", t)])
                          elif pi == 1:
                              gv, gvn = (Fs[0], "F0") if t % 2 == 0 else (Fs[1], "F1")
                              gv = gv[:, 0:256]
                              act(gv, pp, AF.Gelu_apprx_tanh, [pn], [gvn])
                              o = (t % 2) * 16
                              stn = f"SMALL{t % 2}"
                              S.op("dve", lambda gv=gv, o=o: nc.vector.bn_stats(out=SMALL[:, o:o + 6], in_=gv), [gvn], [stn])
                              S.op("dve", lambda o=o: nc.vector.bn_aggr(out=SMALL[:, o + 8:o + 10], in_=SMALL[:, o:o + 6]),
                                   [stn], [stn + "b"])
                              act(SMALL[:, o + 10:o + 11], SMALL[:, o + 9:o + 10], AF.Ln, [stn + "b"], [stn + "c"], bias=EPS)
                              act(SMALL[:, o + 11:o + 12], SMALL[:, o + 10:o + 11], AF.Exp, [stn + "c"], [stn + "d"], scale=-0.5)
                              ts(gv, gv, SMALL[:, o + 8:o + 9], SMALL[:, o + 11:o + 12], ALU.subtract, ALU.mult,
                                 [gvn, stn + "b", stn + "d"], [gvn])
                              vn = VN[t % 2]
                              tt(vn[:], gv, BC[:, 4:260], ALU.mult, [gvn, "BC"], [f"VN{t % 2}"])
                              pzn = "P2a" if t % 2 == 0 else "P2b"
                              pz = PSV[pzn]
                              for g in range(4):
                                  mm(pz[:, g * 64:(g + 1) * 64], WST[:, g, :], vn[:, g * 64:(g + 1) * 64], True, True,
                                     [f"VN{t % 2}", "SM"], [pzn])
                              guf, gufn = (Fs[2], "F2") if t % 2 == 0 else (Fs[3], "F3")
                              act(guf[:, 0:256], OB[:, t, :], AF.Copy, [("OB", t)], [gufn])
                              for g in range(4):
                                  stt(OB[:, t, g * 64:(g + 1) * 64], pz[:, g * 64:(g + 1) * 64], VEC[:, 22 + g:23 + g],
                                      guf[:, g * 64:(g + 1) * 64], ALU.add, ALU.mult, [pzn, "VEC", gufn], [("OB", t)])
                          elif pi == 2:
                              act(CY[:, t, :], pp, AF.Copy, [pn], [("CY", t)])
                          elif pi == 3:
                              cpv(VD[:, t, :, 0:64], pp.rearrange("p (h e) -> p h e", h=4), [pn, "VDones"], [("VD", t)])
                          else:
                              cpv(VA[:, t, :, 0:64], pp.rearrange("p (h e) -> p h e", h=2), [pn, "VAones"], [("VA", t)])
                  S.barrier()
              stage("p1")
              tap(f"qt{l}", QT[:, :, :], [128, 4, NTOK], BF16, [])
              tap(f"kt{l}", KT[:, :, :], [128, 3, NTOK], BF16, [])
              tap(f"ob{l}", OB[:, :, :], [128, 18, 256], BF16, [])
              tap(f"cy{l}", CY[:, :, :], [128, 18, 256], BF16, [])
              tap(f"vd{l}", VD[:, :, :, :], [128, 18, 4, 66], BF16, [])

              for half in range(2):
                  hgroups = [c for c in HALVES[half] if c in groups_all]
                  tiles_h = []
                  for c in hgroups:
                      s0, Lc = GROUPS[c]
                      tiles_h += [(t, c) for t in range(s0 // 128, (s0 + Lc) // 128)]
                  dma_pool(TAB[:, 0:2560], pm_d, [], ["TAB"])
                  PMv = TAB[:, 0:2560].rearrange("p (a t) -> p a t", a=20)
                  for (t, c) in tiles_h:
                      h0 = t * 128 - HSTART[half]
                      pv = PST[:, (t % 2) * 512:(t % 2) * 512 + 256]
                      pvn = "PST"
                      for cc in range(2):
                          tr(pv[:, cc * 128:(cc + 1) * 128], OB[:, t, cc * 128:(cc + 1) * 128], IDB, [("OB", t), "CBF"], [pvn])
                      cpv(HB[:, 2:4, h0:h0 + 128], pv.rearrange("p (c t) -> p c t", c=2), [pvn], [("HB", c)])
                  for c in hgroups:
                      s0, Lc = GROUPS[c]
                      h0, _ = hcols(c, half)
                      first_t = 16 if c == 4 else 0
                      last_t = 17 if c == 4 else 15
                      for t in range(s0 // 128, (s0 + Lc) // 128):
                          lo = (t * 128 - s0)
                          for g in range(4):
                              srcs = []
                              if t > first_t:
                                  srcs.append((t - 1, 3))
                              srcs.append((t, 1 if t == first_t else (2 if t == last_t else 0)))
                              if t < last_t:
                                  srcs.append((t + 1, 4))
                              o = (g % 2) * 64
                              pcn = "P0a" if g < 2 else "P0b"
                              outp = P0[o:o + 64, (g // 2) * 512 + lo:(g // 2) * 512 + lo + 128]
                              for i, (tj, v) in enumerate(srcs):
                                  mm(outp, CY[:, tj, g * 64:(g + 1) * 64], PMv[:, g * 5 + v, :], i == 0, i == len(srcs) - 1,
                                     [("CY", tj), "TAB"], [pcn])
                      for cc in range(2):
                          pcn = "P0a" if cc == 0 else "P0b"
                          pln = "P1a" if cc == 0 else "P1b"
                          act(PC[:, cc, 0:Lc], P0[:, cc * 512:cc * 512 + Lc], AF.Copy, [pcn], [("PC", cc)])
                          mm(P1[:, cc * 512:cc * 512 + Lc], WPBD[:, cc, :], PC[:, cc, 0:Lc], True, True, [("PC", cc), "SM"], [pln])
                          act(HB[:, 4 + cc, h0:h0 + Lc], P1[:, cc * 512:cc * 512 + Lc], AF.Identity, [pln, "VEC"], [("HB", c)],
                              scale=VEC[:, 20 + cc:21 + cc])
                  stage(f"p2a_{half}")
                  acount = [0]

                  def attention(t, c, h, Kap_fn, Qap, Vap_fn, chunks, Otile, Oname):
                      i = acount[0] % 2
                      acount[0] += 1
                      Sp = P0 if i == 0 else P1
                      Sn = ["P0a", "P0b"] if i == 0 else ["P1a", "P1b"]
                      n = len(chunks)
                      for ci, (kt, bias, kres) in enumerate(chunks):
                          o = Sp[:, ci * 128:(ci + 1) * 128]
                          mm(o, Kap_fn(kt), Qap, True, bias is None, kres + ["QTr"], Sn)
                          if bias is not None:
                              mm(o, IDB, bias, False, True, ["CBF", "TAB"], Sn)
                      pt = PT[i]
                      stage("attMM")
                      act(pt[:, 0:n * 128], Sp[:, 0:n * 128], AF.Exp, Sn, [f"PT{i}"])
                      stage("attS")
                      for ci, (kt, bias, kres) in enumerate(chunks):
                          mm(Otile[:, h * 66:h * 66 + 65], pt[:, ci * 128:(ci + 1) * 128], Vap_fn(kt), ci == 0, ci == n - 1,
                             [f"PT{i}", "Vr"], [Oname])
                      stage("attPV")

                  def finish_tile(t, c, Otile, Oname, sink, cat0):
                      h0 = t * 128 - HSTART[half]
                      Ov = Otile[:, 0:264].rearrange("p (h e) -> p h e", h=4)
                      o = 32 + (t % 2) * 8
                      dn = f"DEN{t % 2}"
                      if sink:
                          tt(SMALL[:, o:o + 4], Ov[:, :, 64], GS[:, 4:8], ALU.add, [Oname, "GS"], [dn])
                      else:
                          cpv(SMALL[:, o:o + 4], Ov[:, :, 64], [Oname], [dn])
                      stage("finA")
                      S.op("dve", lambda o=o: nc.vector.reciprocal(out=SMALL[:, o + 4:o + 8], in_=SMALL[:, o:o + 4]), [dn], [dn + "r"])
                      stage("finB")
                      ot = OT[t % 2]
                      otn = f"OT{t % 2}"
                      tt(ot[:].rearrange("p (h e) -> p h e", h=4), Ov[:, :, 0:64],
                         SMALL[:, o + 4:o + 8].unsqueeze(2).to_broadcast([128, 4, 64]), ALU.mult, [Oname, dn + "r"], [otn])
                      stage("finC")
                      pv = PST[:, (t % 2) * 512:(t % 2) * 512 + 256]
                      pvn = "PST"
                      for cc in range(2):
                          tr(pv[:, cc * 128:(cc + 1) * 128], ot[:, cc * 128:(cc + 1) * 128], IDB, [otn, "CBF"], [pvn])
                      cpv(HB[:, cat0:cat0 + 2, h0:h0 + 128], pv.rearrange("p (c t) -> p c t", c=2), [pvn], [("HB", c)])
                      stage("finD")

                  def tcols(t):
                      return slice(t * 128, (t + 1) * 128)

                  if _DBG_DELAY and half == 1:
                      for _i in range(_DBG_DELAY):
                          mm(P3[:, 0:64], IDB, CBF[:, 1, 0:64], True, True, ["CBF"], ["P3"])
                  for (t, c) in tiles_h:
                      if _DBG_SKIP and t in _DBG_SKIP:
                          continue
                      Ot, On = (PSV["P2a"], "P2a") if t % 2 == 0 else (PSV["P2b"], "P2b")
                      for h in range(4):
                          kv, g = h // 2, h % 2
                          po = kv * 64
                          if c == 4:
                              chunks = [(16, None, []), (17, None, [])]
                          else:
                              chunks = []
                              if t > 0:
                                  chunks.append((t - 1, MASKP, []))
                              chunks.append((t, None, []))
                              if t < 15:
                                  chunks.append((t + 1, MASKN, []))
                              chunks += [(16, None, []), (17, None, [])]
                          attention(t, c, h, lambda kt, po=po: KT[po:po + 64, 0, tcols(kt)], QT[po:po + 64, g, tcols(t)],
                                    lambda kt, kv=kv: VA[:, kt, kv, 0:65], chunks, Ot, On)
                      finish_tile(t, c, Ot, On, True, 0)
                      stage(f"attA_t{t}")
                  stage(f"attA_{half}")
                  dma_pool(TAB[:, 0:5376], Ld["dtab"], [], ["TAB"])
                  DT = TAB[:, 0:5376].rearrange("p (h s q) -> p h s q", h=4, s=21)
                  for (t, c) in tiles_h:
                      Ot, On = (PSV["P2a"], "P2a") if t % 2 == 0 else (PSV["P2b"], "P2b")
                      if c == 4:
                          deltas, interior = [], False
                      elif 2 <= t <= 13:
                          deltas, interior = [-2, -1, 0, 1, 2], True
                      elif t == 0:
                          deltas, interior = [0, 1, 2, 3], False
                      elif t == 1:
                          deltas, interior = [-1, 0, 1, 2], False
                      elif t == 14:
                          deltas, interior = [-2, -1, 0, 1], False
                      else:
                          deltas, interior = [-3, -2, -1, 0], False
                      for h in range(4):
                          po = (h % 2) * 64
                          ch = h // 2
                          chunks = []
                          for dl in deltas:
                              i0 = _dslot(interior, 2 * dl + 7)
                              i1 = _dslot(interior, 2 * dl + 6)
                              chunks.append((t + dl, DT[:, h, i0:i1 + 1:(i1 - i0), :], []))
                          chunks += [(16, None, []), (17, None, [])]
                          attention(t, c, h, lambda kt, po=po, ch=ch: KT[po:po + 64, 1 + ch, tcols(kt)],
                                    QT[po:po + 64, 2 + ch, tcols(t)], lambda kt, h=h: VD[:, kt, h, 0:65], chunks, Ot, On)
                      finish_tile(t, c, Ot, On, False, 6)
                      stage(f"attD_t{t}")
                  if half == 0:
                      tap(f"cat{l}", HB[:, :, 0:1024], [128, 8, 1024], BF16, [])
                  stage(f"attD_{half}")
                  wcount = 0
                  for a in range(4):
                      slot, sr = load_piece(l, NP_WO + a)
                      sv = slot[:].rearrange("p (k n) -> p k n", k=8)
                      for di in range(2):
                          dch = 2 * a + di
                          for c in hgroups:
                              s0, Lc = GROUPS[c]
                              h0, _ = hcols(c, half)
                              s = sidx(c)
                              pn = ["P0a", "P0b", "P1a", "P1b"][wcount % 4]
                              wcount += 1
                              pw = PSV[pn][:, 0:Lc]
                              for k in range(8):
                                  mm(pw, sv[:, k, di * 128:(di + 1) * 128], HB[:, k, h0:h0 + Lc], k == 0, k == 7,
                                     [sr, ("HB", c)], [pn])
                              stt(X[:, dch, s0:s0 + Lc], pw, G1[:, dch, s:s + 1], X[:, dch, s0:s0 + Lc], ALU.mult, ALU.add,
                                  [pn, "MOD", xres(c)], [xres(c)])
                  stage(f"wout_{half}")
                  S.barrier()
              stage("wout")
              tap(f"x1_{l}", X[:], [128, 8, NTOK], F32, [])

              def h2dst(k, c):
                  s0, Lc = GROUPS[c]
                  if c < 2:
                      return HB[:, k, s0:s0 + Lc], ("HB", c)
                  return H2B[:, k, s0 - 1024:s0 - 1024 + Lc], ("H2B", c)

              for c in groups_all:
                  do_norm(c, A2, SH2, h2dst, "A2")
              S.barrier()
              stage("norm2")
              gcount = 0
              dcount = 0
              for bi, blk in enumerate(FFBLOCKS):
                  for fl, f in enumerate(blk):
                      slot, sr = load_piece(l, NP_GU + f)
                      sv = slot[:].rearrange("p (u k n) -> p u k n", u=2, k=8)
                      for c in groups_all:
                          s0, Lc = GROUPS[c]
                          i = gcount % 2
                          gcount += 1
                          pgn, pun = ("P0a", "P0b") if i == 0 else ("P1a", "P1b")
                          pg, pu = PSV[pgn][:, 0:Lc], PSV[pun][:, 0:Lc]
                          for k in range(8):
                              hsrc, hres = h2dst(k, c)
                              mm(pg, sv[:, 0, k, :], hsrc, k == 0, k == 7, [sr, hres], [pgn])
                          for k in range(8):
                              hsrc, hres = h2dst(k, c)
                              mm(pu, sv[:, 1, k, :], hsrc, k == 0, k == 7, [sr, hres], [pun])
                          sg, sgn = (Fs[0], "F0") if i == 0 else (Fs[1], "F1")
                          act(sg[:, 0:Lc], pg, AF.Silu, [pgn], [sgn])
                          tt(ACTB[:, fl, s0:s0 + Lc], pu, sg[:, 0:Lc], ALU.mult, [sgn, pun], [("ACTB", fl, c)])
                  nf = len(blk)
                  for dch in range(8):
                      slot, sr = load_piece(l, NP_WD + bi * 8 + dch, nf * 128)
                      sv = slot[:, 0:nf * 128].rearrange("p (f n) -> p f n", f=nf)
                      for c in groups_all:
                          s0, Lc = GROUPS[c]
                          s = sidx(c)
                          pn = ["P2a", "P2b", "P3"][dcount % 3]
                          dcount += 1
                          pd = PSV[pn][:, 0:Lc]
                          for fl in range(nf):
                              mm(pd, sv[:, fl, :], ACTB[:, fl, s0:s0 + Lc], fl == 0, fl == nf - 1, [sr, ("ACTB", fl, c)], [pn])
                          stt(X[:, dch, s0:s0 + Lc], pd, G2[:, dch, s:s + 1], X[:, dch, s0:s0 + Lc], ALU.mult, ALU.add,
                              [pn, "MOD", xres(c)], [xres(c)])
              S.barrier()
              tap(f"x2_{l}", X[:], [128, 8, NTOK], F32, [])


        try:
            run_layers()
        except _Stop:
            pass
        for t in range(16):
            Pt = P0 if t % 2 == 0 else P1
            pr = ["P0a", "P0b"] if t % 2 == 0 else ["P1a", "P1b"]
            for k in range(8):
                tr(Pt[:, k * 128:(k + 1) * 128], X[:, k, t * 128:(t + 1) * 128], IDF[:], [xres(t // 4), "IDF"], pr)
            stg = SCR[:, (t % 2) * 1024:(t % 2 + 1) * 1024]
            stg_r = ["F0", "F1"] if t % 2 == 0 else ["F2", "F3"]
            if t % 2 == 0:
                act(stg, Pt[:], AF.Copy, pr, stg_r)
            else:
                cpv(stg, Pt[:], pr, stg_r)
            dma_sp(out_d[t * 128:(t + 1) * 128, :], stg, stg_r, [("out", t)])
        S.emit(st)
    return nc, tap_d


_CACHE = {}


def prep_inputs(inp, n_layers=2):
    inp = {k: np.asarray(v, dtype=np.float32) for k, v in inp.items()}
    cbf, rope, pm = _consts()
    shared = {"cbf": np.ascontiguousarray(cbf.reshape(128, -1)), "idf": np.eye(128, dtype=np.float32),
              "rope": np.ascontiguousarray(rope.reshape(128, -1)), "pm": np.ascontiguousarray(pm.reshape(128, -1))}
    for l in range(n_layers):
        la = _layer_arrays(inp, l)
        for k, v in la.items():
            shared[f"{k}{l}"] = v
    maps = []
    for b in range(8):
        m = dict(shared)
        m["x"] = np.ascontiguousarray(inp["x"][b])
        m["ctx"] = np.ascontiguousarray(inp["ctx"][b])
        cv = np.zeros((128, 8, 2), np.float32)
        cv[:, :, 0] = inp["c"][b].reshape(8, 128).T
        cv[:, :, 1] = inp["c_ctx"].reshape(8, 128).T
        m["cvec"] = np.ascontiguousarray(cv.reshape(128, 16))
        maps.append(m)
    return maps


def kernel(**inputs):
    if "nc" not in _CACHE:
        _CACHE["nc"] = build(2)[0]
    nc = _CACHE["nc"]
    maps = prep_inputs(inputs, 2)
    res = run_bass_kernel_spmd(nc, maps, core_ids=list(range(8)))
    return np.stack([np.asarray(r["out"], dtype=np.float32) for r in res.results], 0)
```

```python
import contextlib
import numpy as np
import concourse.bass as bass
import concourse.mybir as mybir
from concourse.bass_utils import run_bass_kernel_spmd

F32 = mybir.dt.float32
BF16 = mybir.dt.bfloat16
AF = mybir.ActivationFunctionType
ALU = mybir.AluOpType

D = 1024
SEQ = 2048
CTX = 256
NTOK = SEQ + CTX
FF = 2816
NFF = 22
NEG = -30000.0
EPS = 1e-6
GROUPS = [(0, 512), (512, 512), (1024, 512), (1536, 512), (2048, 256)]
HALVES = [[0, 1], [2, 3, 4]]
HSTART = [0, 1024]
FFBLOCKS = [list(range(0, 8)), list(range(8, 15)), list(range(15, 22))]
PIECE = 2048
import os as _os
_DBG_DELAY = int(_os.environ.get('DBG_DELAY', '0'))
_DBG_SKIP = set(int(v) for v in _os.environ.get('DBG_SKIP_A', '').split(',') if v)


class _Op:
    __slots__ = ("eng", "fn", "dma", "deps", "has_dep", "sig")

    def __init__(self, eng, fn, dma):
        self.eng = eng
        self.fn = fn
        self.dma = dma
        self.deps = []
        self.has_dep = False
        self.sig = None


class Sched:
    COMPUTE = ("pe", "act", "dve")

    def __init__(self, nc, n_dma_sems=8):
        self.nc = nc
        self.ops = []
        self.res = {}
        self.n_dma_sems = n_dma_sems
        self.last = {}
        self.pending = {}
        self.dma_since = []

    def op(self, eng, fn, reads=(), writes=(), dma=False):
        o = _Op(eng, fn, dma)
        deps = {}
        for r in reads:
            st = self.res.get(r)
            if st is not None and st[0] is not None:
                deps[id(st[0])] = (st[0], True)
        for w in writes:
            st = self.res.get(w)
            if st is not None:
                if st[0] is not None and id(st[0]) not in deps:
                    deps[id(st[0])] = (st[0], False)
                for rd in st[1]:
                    if id(rd) not in deps:
                        deps[id(rd)] = (rd, False)
        for d in self.pending.pop(eng, ()):
            if id(d) not in deps:
                deps[id(d)] = (d, True)
        for d, raw in deps.values():
            if d is o:
                continue
            if not d.dma and not o.dma and d.eng == eng:
                if eng == "pe" or not raw:
                    continue
            o.deps.append(d)
            d.has_dep = True
        for r in reads:
            st = self.res.get(r)
            if st is None:
                self.res[r] = [None, [o]]
            else:
                st[1].append(o)
        for w in writes:
            self.res[w] = [o, []]
        self.ops.append(o)
        self.last[eng] = o
        if dma:
            self.dma_since.append(o)
        return o

    def barrier(self):
        lasts = [o for e, o in self.last.items() if e in self.COMPUTE]
        lasts += self.dma_since
        self.dma_since = []
        for e in self.COMPUTE:
            self.pending[e] = self.pending.get(e, []) + lasts

    def emit(self, stack):
        nc = self.nc
        eobj = {"pe": nc.tensor, "act": nc.scalar, "dve": nc.vector, "pool": nc.gpsimd, "sp": nc.sync}
        semh = {}
        cnt = {}
        waited = {e: {} for e in eobj}
        dma_rr = {}
        dcnt = {}

        def get_sem(key):
            if key not in semh:
                semh[key] = stack.enter_context(nc.semaphore("s_" + "_".join(str(k) for k in key)))
            return semh[key]

        for o in self.ops:
            E = eobj[o.eng]
            need = {}
            for d in o.deps:
                key, val = d.sig
                if need.get(key, 0) < val:
                    need[key] = val
            if o.dma:
                i = dma_rr.get(o.eng, 0)
                dma_rr[o.eng] = (i + 1) % self.n_dma_sems
                k = ("dma", o.eng, i)
                n = dcnt.get(k, 0) + 1
                dcnt[k] = n
                if n > 1 and need.get(k, 0) < 16 * (n - 1):
                    need[k] = 16 * (n - 1)
            for key, val in need.items():
                if waited[o.eng].get(key, 0) < val:
                    E.wait_ge(get_sem(key), val)
                    waited[o.eng][key] = val
            ins = o.fn()
            if o.dma:
                ins.then_inc(get_sem(k), 16)
                o.sig = (k, 16 * n)
            elif o.has_dep:
                key = ("e", o.eng)
                cnt[key] = cnt.get(key, 0) + 1
                ins.then_inc(get_sem(key), 1)
                o.sig = (key, cnt[key])
        for key, n in dcnt.items():
            nc.sync.wait_ge(get_sem(key), 16 * n)
        for key, n in cnt.items():
            nc.sync.wait_ge(get_sem(key), n)


def _kmajor(w):
    K, N = w.shape
    return np.ascontiguousarray(w.reshape(K // 128, 128, N).transpose(1, 0, 2))


def _pad_piece(a):
    a = np.ascontiguousarray(a, dtype=np.float32).reshape(128, -1)
    out = np.zeros((128, PIECE), np.float32)
    out[:, : a.shape[1]] = a
    return out


def _dslot(interior, s):
    if 3 <= s <= 9:
        return 8 + (9 - s)
    if s >= 10:
        return (0 if interior else 4) + (13 - s)
    return (18 if interior else 15) + (2 - s)


def _consts():
    c = np.zeros((128, 6, 128), np.float32)
    c[:, 0, :] = np.eye(128)
    c[:, 1, :] = 1.0
    c[0:64, 2, 0:64] = 1.0
    c[64:128, 2, 64:128] = 1.0
    for m in range(128):
        if (m % 32) < 16:
            c[m + 16, 3, m] = -1.0
        else:
            c[m - 16, 3, m] = 1.0
    j = np.arange(128)[:, None]
    i = np.arange(128)[None, :]
    c[:, 4, :] = np.where(j >= i, 0.0, NEG)
    c[:, 5, :] = np.where(j <= i, 0.0, NEG)
    p = np.arange(128)
    idx = p % 64
    f = (idx % 16).astype(np.float64)
    inv = np.power(10000.0, -f / 16.0)
    t = np.arange(SEQ)
    pos = np.where((idx < 32)[:, None], (t // 64)[None, :], (t % 64)[None, :]).astype(np.float64)
    ang = pos * inv[:, None]
    rope = np.stack([np.cos(ang), np.sin(ang)], axis=1).astype(np.float32)
    pm = np.zeros((128, 20, 128), np.float32)
    jj = np.arange(128)[:, None]
    tt = np.arange(128)[None, :]
    for g, w in enumerate((2, 4, 8, 16)):
        h = w // 2
        eye = (jj == tt).astype(np.float32)
        pm[:, g * 5 + 0, :] = ((jj >= tt - h) & (jj < tt + h)) / w - eye
        lo = np.maximum(tt - h, 0)
        cntf = (tt + h) - lo
        pm[:, g * 5 + 1, :] = ((jj >= lo) & (jj < tt + h)) / cntf - eye
        hi = np.minimum(tt + h, 128)
        cntl = hi - (tt - h)
        pm[:, g * 5 + 2, :] = ((jj >= tt - h) & (jj < hi)) / cntl - eye
        pm[:, g * 5 + 3, :] = (jj >= 128 + tt - h) / w
        pm[:, g * 5 + 4, :] = (jj < tt + h - 128) / w
    return c, rope, pm


def _dtab(rpb):
    kc = np.arange(64)[:, None]
    qc = np.arange(64)[None, :]
    cs = np.clip(qc - 8, 0, 48)
    col_ok = (kc >= cs) & (kc < cs + 16)
    coff = np.clip(kc - qc, -15, 15) + 15
    out = np.full((128, 4, 21, 64), NEG, np.float32)
    for h in range(4):
        for interior in (True, False):
            for s in range(14):
                sl = _dslot(interior, s)
                for kr in range(2):
                    rho = s + kr
                    ok = 0 <= rho <= 14 and ((3 <= rho <= 10) or not interior)
                    if not ok:
                        continue
                    T = np.where(col_ok, rpb[h, rho][coff], NEG)
                    out[kr * 64:(kr + 1) * 64, h, sl, :] = T
    return out


def _layer_arrays(inp, l):
    f32 = np.float32
    w_in = inp["w_in"][l]
    pieces = []
    wm = _kmajor(inp["w_mod"][l])
    for jj in range(24):
        pieces.append(_pad_piece(wm[:, :, jj * 256:(jj + 1) * 256]))
    aq = w_in[:, 0:256]
    fm_cols = [np.concatenate([aq[:, 0:64], aq[:, 128:192]], 1), np.concatenate([aq[:, 64:128], aq[:, 192:256]], 1),
               w_in[:, 256:384], w_in[:, 1280:1408], w_in[:, 1408:1536], w_in[:, 1536:1664], w_in[:, 1664:1792]]
    fm = _kmajor(np.concatenate(fm_cols, 1))
    for a in range(4):
        pieces.append(_pad_piece(fm[:, :, a * 256:min((a + 1) * 256, 896)]))
    tm = [w_in[:, 512:768], w_in[:, 768:1024], w_in[:, 1024:1280], w_in[:, 1792:2048], w_in[:, 384:512]]
    for a in tm:
        pieces.append(_pad_piece(_kmajor(a)))
    wo = _kmajor(inp["w_out"][l])
    for a in range(4):
        pieces.append(_pad_piece(wo[:, :, a * 256:(a + 1) * 256]))
    wg = _kmajor(inp["w_gate"][l])
    wu = _kmajor(inp["w_up"][l])
    for f in range(NFF):
        pieces.append(_pad_piece(np.concatenate([wg[:, :, f * 128:(f + 1) * 128].reshape(128, -1),
                                                 wu[:, :, f * 128:(f + 1) * 128].reshape(128, -1)], 1)))
    wd = _kmajor(inp["w_down"][l])
    for blk in FFBLOCKS:
        for dch in range(8):
            pieces.append(_pad_piece(wd[:, blk[0]:blk[-1] + 1, dch * 128:(dch + 1) * 128]))
    W = np.stack(pieces, 0)
    vec = np.zeros((128, 26), f32)
    vec[:, 0:8] = inp["g_mix"][l].reshape(8, 128).T
    vec[:, 8:16] = inp["g_ffn"][l].reshape(8, 128).T
    vec[:, 16] = np.tile(inp["a_q_gain"][l], 2)
    vec[:, 17] = np.tile(inp["a_k_gain"][l], 2)
    vec[:, 18] = np.tile(inp["d_q_gain"][l], 2)
    vec[:, 19] = np.tile(inp["d_k_gain"][l], 2)
    vec[:, 20:22] = inp["c_scale"][l].reshape(2, 128).T
    vec[:, 22:26] = inp["b_b_s"][l].T
    bmod = np.ascontiguousarray(inp["b_mod"][l].reshape(48, 128).T)
    bc = np.zeros((128, 260), f32)
    bc[:, 0:4] = inp["a_sink"][l][None, :]
    bc[:, 4:260] = inp["b_v_gain"][l][None, :]
    sm = np.zeros((128, 6, 128), f32)
    sm[:, 0:4, :] = inp["b_w_s"][l].transpose(2, 0, 1)
    wp = inp["c_w_pool"][l]
    for g in range(4):
        o = (g % 2) * 64
        sm[o:o + 64, 4 + g // 2, o:o + 64] = wp[g]
    dt = _dtab(inp["d_rpb"][l])
    return {"W": W, "vec": vec, "bmod": bmod, "bc": bc, "sm": np.ascontiguousarray(sm.reshape(128, -1)), "dtab": np.ascontiguousarray(dt.reshape(128, -1))}


NP_MOD, NP_FM, NP_TM, NP_WO, NP_GU, NP_WD = 0, 24, 28, 33, 37, 59
NPIECES = 83


class _Stop(Exception):
    pass


def build(n_layers=2, taps=(), stop=None):
    nc = bass.Bass("TRN2", target_bir_lowering=False)
    dr = {}

    def din(name, shape):
        dr[name] = nc.dram_tensor(name, list(shape), F32, kind="ExternalInput").ap()
        return dr[name]

    x_d = din("x", [SEQ, D])
    ctx_d = din("ctx", [CTX, D])
    cvec_d = din("cvec", [128, 16])
    cbf_d = din("cbf", [128, 6 * 128])
    idf_d = din("idf", [128, 128])
    rope_d = din("rope", [128, 2 * SEQ])
    pm_d = din("pm", [128, 20 * 128])
    L = []
    for l in range(n_layers):
        L.append({
            "W": din(f"W{l}", [NPIECES, 128, PIECE]), "vec": din(f"vec{l}", [128, 26]), "bmod": din(f"bmod{l}", [128, 48]),
            "bc": din(f"bc{l}", [128, 260]), "sm": din(f"sm{l}", [128, 6 * 128]), "dtab": din(f"dtab{l}", [128, 4 * 21 * 64]),
        })
    out_d = nc.dram_tensor("out", [SEQ, D], F32, kind="ExternalOutput").ap()
    tap_d = {}

    st = contextlib.ExitStack()
    with st:
        S = Sched(nc)
        sb = lambda n, s, d: st.enter_context(nc.sbuf_tensor(n, list(s), d))
        ps = lambda n, s, d: st.enter_context(nc.psum_tensor(n, list(s), d))
        X = sb("X", [128, 8, NTOK], F32)
        HB = sb("HB", [128, 8, 1280], BF16)
        RR = sb("RR", [128, 32472], BF16)
        QT = RR[:, 0:9216].rearrange("p (k t) -> p k t", k=4)
        KT = RR[:, 9216:16128].rearrange("p (k t) -> p k t", k=3)
        VA = RR[:, 16128:18504].rearrange("p (t h e) -> p t h e", t=18, h=2)
        VD = RR[:, 18504:23256].rearrange("p (t h e) -> p t h e", t=18, h=4)
        OB = RR[:, 23256:27864].rearrange("p (t c) -> p t c", t=18)
        CY = RR[:, 27864:32472].rearrange("p (t c) -> p t c", t=18)
        ACTB = RR[:, 0:18432].rearrange("p (f t) -> p f t", f=8)
        H2B = RR[:, 18432:28672].rearrange("p (k t) -> p k t", k=8)
        RING = [sb(f"ring{i}", [128, PIECE], BF16) for i in range(4)]
        TAB = sb("TAB", [128, 5376], BF16)
        SCR = sb("SCR", [128, 2048], F32)
        Fs = [SCR[:, i * 512:(i + 1) * 512] for i in range(4)]
        SQ = [sb(f"SQ{i}", [128, 512], BF16) for i in range(2)]
        QG = sb("QG", [128, 512], BF16)
        RQ = sb("RQ", [128, 512], BF16)
        PT = [sb(f"PT{i}", [128, 896], BF16) for i in range(2)]
        OT = [sb(f"OT{i}", [128, 256], BF16) for i in range(2)]
        PC = sb("PC", [128, 2, 512], BF16)
        VN = [sb(f"VN{i}", [128, 256], BF16) for i in range(2)]
        CBF = sb("CBF", [128, 6, 128], BF16)
        IDF = sb("IDF", [128, 128], F32)
        CV = sb("CV", [128, 8, 2], F32)
        CA = sb("CA", [128, 8, 2], BF16)
        SMALL = sb("SMALL", [128, 64], F32)
        IDB = CBF[:, 0, :]
        ONES = CBF[:, 1, :]
        BDONES = CBF[:, 2, :]
        ROTT = CBF[:, 3, :]
        MASKP = CBF[:, 4, :]
        MASKN = CBF[:, 5, :]
        P0 = ps("P0", [128, 1024], F32)
        P1 = ps("P1", [128, 1024], F32)
        P2 = ps("P2", [128, 1024], F32)
        P3 = ps("P3", [128, 512], F32)
        PST = ps("PST", [128, 1024], BF16)
        PSV = {"P0a": P0[:, 0:512], "P0b": P0[:, 512:1024], "P1a": P1[:, 0:512], "P1b": P1[:, 512:1024],
               "P2a": P2[:, 0:512], "P2b": P2[:, 512:1024], "P3": P3[:, :]}

        def mm(out, lhsT, rhs, start, stop, reads, writes):
            S.op("pe", lambda: nc.tensor.matmul(out, lhsT=lhsT, rhs=rhs, start=start, stop=stop), reads, writes)

        def tr(out, in_, ident, reads, writes):
            S.op("pe", lambda: nc.tensor.transpose(out, in_, ident), reads, writes)

        def act(out, in_, func, reads, writes, scale=None, bias=None):
            kw = {}
            if scale is not None:
                kw["scale"] = scale
            if bias is not None:
                kw["bias"] = bias
            S.op("act", lambda: nc.scalar.activation(out=out, in_=in_, func=func, **kw), reads, writes)

        def tt(out, in0, in1, op, reads, writes):
            S.op("dve", lambda: nc.vector.tensor_tensor(out=out, in0=in0, in1=in1, op=op), reads, writes)

        def stt(out, in0, scalar, in1, op0, op1, reads, writes):
            S.op("dve", lambda: nc.vector.scalar_tensor_tensor(out=out, in0=in0, scalar=scalar, in1=in1, op0=op0, op1=op1),
                 reads, writes)

        def ts(out, in0, s1, s2, op0, op1, reads, writes):
            S.op("dve", lambda: nc.vector.tensor_scalar(out=out, in0=in0, scalar1=s1, scalar2=s2, op0=op0, op1=op1),
                 reads, writes)

        def cpv(out, in_, reads, writes):
            S.op("dve", lambda: nc.vector.tensor_copy(out=out, in_=in_), reads, writes)

        def dma_sp(out, in_, reads, writes):
            S.op("sp", lambda: nc.sync.dma_start(out=out, in_=in_), reads, writes, dma=True)

        def dma_pool(out, in_, reads, writes):
            S.op("pool", lambda: nc.gpsimd.dma_start(out=out, in_=in_, max_dma_last_dim=4096), reads, writes, dma=True)

        def stage(name):
            if stop == name:
                S.barrier()
                raise _Stop()

        def tap(name, ap, shape, dtype, reads):
            if name not in taps:
                return
            S.barrier()
            t = nc.dram_tensor("tap_" + name, list(shape), dtype, kind="ExternalOutput").ap()
            tap_d[name] = t
            dma_sp(t, ap, list(reads) + ["tapsrc"], ["tap_" + name])

        ring_i = [0]

        def load_piece(l, idx, nelem=PIECE):
            i = ring_i[0]
            ring_i[0] = (i + 1) % 4
            dma_pool(RING[i][:, 0:nelem], L[l]["W"][idx, :, 0:nelem], [], [("slot", i)])
            return RING[i], ("slot", i)

        dma_sp(IDF[:], idf_d, [], ["IDF"])
        dma_sp(CV[:].rearrange("p k s -> p (k s)"), cvec_d, [], ["CV"])
        dma_pool(CBF[:].rearrange("p a b -> p (a b)"), cbf_d, [], ["CBF"])
        act(CA[:], CV[:], AF.Silu, ["CV"], ["CA"])
        for t in range(18):
            stg = SCR[:, (t % 2) * 1024:(t % 2 + 1) * 1024]
            stg_r = ["F0", "F1"] if t % 2 == 0 else ["F2", "F3"]
            src = x_d[t * 128:(t + 1) * 128, :] if t < 16 else ctx_d[(t - 16) * 128:(t - 15) * 128, :]
            dma_sp(stg, src, [], stg_r)
            Pt = P0 if t % 2 == 0 else P1
            pr = ["P0a", "P0b"] if t % 2 == 0 else ["P1a", "P1b"]
            for k in range(8):
                tr(Pt[:, k * 128:(k + 1) * 128], stg[:, k * 128:(k + 1) * 128], IDF[:], stg_r + ["IDF"], pr)
            dst = X[:, :, t * 128:(t + 1) * 128]
            src_ps = Pt[:].rearrange("p (k t) -> p k t", k=8)
            xr = [("X", t // 4 if t < 16 else 4)]
            if t % 2 == 0:
                act(dst, src_ps, AF.Copy, pr, xr)
            else:
                cpv(dst, src_ps, pr, xr)
        S.barrier()

        def xres(c):
            return ("X", c)

        def sidx(c):
            return 1 if c == 4 else 0

        VEC = sb("VEC", [128, 26], F32)
        BMOD = sb("BMOD", [128, 48], F32)
        BC = sb("BC", [128, 260], F32)
        SM = sb("SM", [128, 6, 128], BF16)
        MOD = sb("MOD", [128, 48, 2], F32)
        A1 = sb("A1", [128, 8, 2], F32)
        A2 = sb("A2", [128, 8, 2], F32)
        GS = sb("GS", [128, 8], F32)

        def run_layers():
          for l in range(n_layers):
              last = l == n_layers - 1
              Ld = L[l]
              dma_sp(VEC[:], Ld["vec"], [], ["VEC"])
              dma_sp(BMOD[:], Ld["bmod"], [], ["BMOD"])
              dma_sp(BC[:], Ld["bc"], [], ["BC"])
              dma_pool(SM[:].rearrange("p a b -> p (a b)"), Ld["sm"], [], ["SM"])
              WST = SM[:, 0:4, :]
              WPBD = SM[:, 4:6, :]

              for jj in range(24):
                  slot, sr = load_piece(l, NP_MOD + jj)
                  sv = slot[:].rearrange("p (k n) -> p k n", k=8)
                  for cc in range(2):
                      j = 2 * jj + cc
                      for k in range(8):
                          mm(P3[:, 2 * j:2 * j + 2], sv[:, k, cc * 128:(cc + 1) * 128], CA[:, k, :], k == 0, k == 7,
                             [sr, "CA"], ["P3"])
              tt(MOD[:], P3[:, 0:96].rearrange("p (j s) -> p j s", s=2), BMOD[:].unsqueeze(2).to_broadcast([128, 48, 2]),
                 ALU.add, ["P3", "BMOD"], ["MOD"])
              stt(A1[:], MOD[:, 8:16, :], 1.0, VEC[:, 0:8].unsqueeze(2).to_broadcast([128, 8, 2]), ALU.add, ALU.mult,
                  ["MOD", "VEC"], ["A1"])
              stt(A2[:], MOD[:, 32:40, :], 1.0, VEC[:, 8:16].unsqueeze(2).to_broadcast([128, 8, 2]), ALU.add, ALU.mult,
                  ["MOD", "VEC"], ["A2"])
              ts(GS[:, 0:1], VEC[:, 16:17], 0.125, None, ALU.mult, ALU.bypass, ["VEC"], ["GS"])
              ts(GS[:, 1:2], VEC[:, 17:18], 1.0, None, ALU.mult, ALU.bypass, ["VEC"], ["GS"])
              ts(GS[:, 2:3], VEC[:, 18:19], 0.125, None, ALU.mult, ALU.bypass, ["VEC"], ["GS"])
              ts(GS[:, 3:4], VEC[:, 19:20], 1.0, None, ALU.mult, ALU.bypass, ["VEC"], ["GS"])
              act(GS[:, 4:8], BC[:, 0:4], AF.Exp, ["BC"], ["GS"])
              SH1, G1, SH2, G2 = MOD[:, 0:8, :], MOD[:, 16:24, :], MOD[:, 24:32, :], MOD[:, 40:48, :]
              stage("modload")
              tap(f"mod{l}", MOD[:], [128, 48, 2], F32, ["MOD"])

              stage("mod")
              def do_norm(c, Amul, SHv, dst_fn, tag):
                  s0, Lc = GROUPS[c]
                  s = sidx(c)
                  psn, psr = (PSV["P2a"], "P2a") if c % 2 == 0 else (PSV["P2b"], "P2b")
                  for k in range(8):
                      sq = SQ[k % 2]
                      act(sq[:, 0:Lc], X[:, k, s0:s0 + Lc], AF.Square, [xres(c)], [f"SQ{k % 2}"])
                      mm(psn[:, 0:Lc], ONES, sq[:, 0:Lc], k == 0, k == 7, [f"SQ{k % 2}", "CBF"], [psr])
                  act(Fs[0][:, 0:Lc], psn[:, 0:Lc], AF.Ln, [psr], ["F0"], scale=1.0 / D, bias=EPS)
                  act(Fs[1][:, 0:Lc], Fs[0][:, 0:Lc], AF.Exp, ["F0"], ["F1"], scale=-0.5)
                  for k in range(8):
                      tmp, tr_ = (Fs[2], "F2") if k % 2 == 0 else (Fs[3], "F3")
                      tt(tmp[:, 0:Lc], X[:, k, s0:s0 + Lc], Fs[1][:, 0:Lc], ALU.mult, [xres(c), "F1"], [tr_])
                      dst, dres = dst_fn(k, c)
                      act(dst, tmp[:, 0:Lc], AF.Identity, [tr_, "MOD", tag], [dres], scale=Amul[:, k, s:s + 1],
                          bias=SHv[:, k, s:s + 1])

              groups_all = [0, 1, 2, 3] if last else [0, 1, 2, 3, 4]

              dma_pool(TAB[:, 0:4096], rope_d, [], ["TAB"])
              COS = TAB[:, 0:2048]
              SIN = TAB[:, 2048:4096]
              S.op("dve", lambda: nc.vector.memset(VA[:, :, :, 64:65], 1.0), [], ["VAones"])
              S.op("dve", lambda: nc.vector.memset(VD[:, :, :, 64:65], 1.0), [], ["VDones"])

              def hcols(c, half):
                  s0, Lc = GROUPS[c]
                  return s0 - HSTART[half], Lc

              for half in range(2):
                  hgroups = HALVES[half]
                  for c in hgroups:
                      h0, Lc = hcols(c, half)
                      do_norm(c, A1, SH1, lambda k, c, h0=h0, Lc=Lc: (HB[:, k, h0:h0 + Lc], ("HB", c)), "A1")
                  if half == 0:
                      tap(f"hT{l}", HB[:, :, 0:1024], [128, 8, 1024], BF16, [("HB", 0), ("HB", 1)])
                  stage(f"norm1_{half}")
                  fcount = 0
                  for a in range(4):
                      js = [2 * a, 2 * a + 1] if a < 3 else [6]
                      slot, sr = load_piece(l, NP_FM + a, 8 * 128 * len(js))
                      sv = slot[:, 0:8 * 128 * len(js)].rearrange("p (k n) -> p k n", k=8)
                      for ji, j in enumerate(js):
                          for c in hgroups:
                              if last and c == 4 and j in (0, 1, 3, 4):
                                  continue
                              s0, Lc = GROUPS[c]
                              h0, _ = hcols(c, half)
                              pfn = ["P0a", "P0b", "P1a"][fcount % 3]
                              phn = ["P1b", "P2a"][fcount % 2]
                              prn = ["P2b", "P3"][fcount % 2]
                              fcount += 1
                              pf, ph, pr_ = PSV[pfn][:, 0:Lc], PSV[phn][:, 0:Lc], PSV[prn][:, 0:Lc]
                              for k in range(8):
                                  mm(pf, sv[:, k, ji * 128:(ji + 1) * 128], HB[:, k, h0:h0 + Lc], k == 0, k == 7,
                                     [sr, ("HB", c)], [pfn])
                              sq = SQ[fcount % 2]
                              sqn = f"SQ{fcount % 2}"
                              act(sq[:, 0:Lc], pf, AF.Square, [pfn], [sqn])
                              mm(ph, BDONES, sq[:, 0:Lc], True, True, [sqn, "CBF"], [phn])
                              act(Fs[0][:, 0:Lc], ph, AF.Ln, [phn], ["F0"], scale=1.0 / 64, bias=EPS)
                              act(Fs[1][:, 0:Lc], Fs[0][:, 0:Lc], AF.Exp, ["F0"], ["F1"], scale=-0.5)
                              gcol = {0: 0, 1: 0, 2: 1, 3: 2, 4: 2, 5: 3, 6: 3}[j]
                              gain = GS[:, gcol:gcol + 1]
                              if j < 2:
                                  dst, dres = QT[:, j, s0:s0 + Lc], ("QT", j, c)
                              elif j == 2:
                                  dst, dres = KT[:, 0, s0:s0 + Lc], ("KT", 0, c)
                              elif j < 5:
                                  dst, dres = QT[:, j - 1, s0:s0 + Lc], ("QT", j - 1, c)
                              else:
                                  dst, dres = KT[:, j - 4, s0:s0 + Lc], ("KT", j - 4, c)
                              if j < 3 and c != 4:
                                  act(QG[:, 0:Lc], pf, AF.Identity, [pfn, "GS"], ["QG"], scale=gain)
                                  mm(pr_, ROTT, QG[:, 0:Lc], True, True, ["QG", "CBF"], [prn])
                                  act(RQ[:, 0:Lc], pr_, AF.Copy, [prn], ["RQ"])
                                  tt(Fs[2][:, 0:Lc], QG[:, 0:Lc], COS[:, s0:s0 + Lc], ALU.mult, ["QG", "TAB"], ["F2"])
                                  tt(Fs[3][:, 0:Lc], RQ[:, 0:Lc], SIN[:, s0:s0 + Lc], ALU.mult, ["RQ", "TAB"], ["F3"])
                                  tt(Fs[2][:, 0:Lc], Fs[2][:, 0:Lc], Fs[3][:, 0:Lc], ALU.add, ["F2", "F3"], ["F2"])
                                  tt(dst, Fs[2][:, 0:Lc], Fs[1][:, 0:Lc], ALU.mult, ["F2", "F1"], [dres])
                              else:
                                  stt(dst, pf, gain, Fs[1][:, 0:Lc], ALU.mult, ALU.mult, [pfn, "GS", "F1"], [dres])
                  stage(f"fm_{half}")
                  tiles_h = []
                  for c in hgroups:
                      s0, Lc = GROUPS[c]
                      tiles_h += [(t, c) for t in range(s0 // 128, (s0 + Lc) // 128)]
                  tcount = 0
                  for pi in range(5):
                      ncol = 128 if pi == 4 else 256
                      slot, sr = load_piece(l, NP_TM + pi, 8 * ncol)
                      sv = slot[:, 0:8 * ncol].rearrange("p (k n) -> p k n", k=8)
                      for (t, c) in tiles_h:
                          if last and c == 4 and pi < 3:
                              continue
                          h0 = t * 128 - HSTART[half]
                          pn = ["P0a", "P0b", "P1a", "P1b"][tcount % 4]
                          tcount += 1
                          pp = PSV[pn][:, 0:ncol]
                          for k in range(8):
                              mm(pp, HB[:, k, h0:h0 + 128], sv[:, k, :], k == 0, k == 7, [sr, ("HB", c)], [pn])
                          if pi == 0:
                              act(OB[:, t, :], pp, AF.Gelu_apprx_tanh, [pn], [("OB", t)])
                          elif pi == 1:
                              gv, gvn = (Fs[0], "F0") if t % 2 == 0 else (Fs[1], "F1")
                              gv = gv[:, 0:256]
                              act(gv, pp, AF.Gelu_apprx_tanh, [pn], [gvn])
                              o = (t % 2) * 16
                              stn = f"SMALL{t % 2}"
                              S.op("dve", lambda gv=gv, o=o: nc.vector.bn_stats(out=SMALL[:, o:o + 6], in_=gv), [gvn], [stn])
                              S.op("dve", lambda o=o: nc.vector.bn_aggr(out=SMALL[:, o + 8:o + 10], in_=SMALL[:, o:o + 6]),
                                   [stn], [stn + "b"])
                              act(SMALL[:, o + 10:o + 11], SMALL[:, o + 9:o + 10], AF.Ln, [stn + "b"], [stn + "c"], bias=EPS)
                              act(SMALL[:, o + 11:o + 12], SMALL[:, o + 10:o + 11], AF.Exp, [stn + "c"], [stn + "d"], scale=-0.5)
                              ts(gv, gv, SMALL[:, o + 8:o + 9], SMALL[:, o + 11:o + 12], ALU.subtract, ALU.mult,
                                 [gvn, stn + "b", stn + "d"], [gvn])
                              vn = VN[t % 2]
                              tt(vn[:], gv, BC[:, 4:260], ALU.mult, [gvn, "BC"], [f"VN{t % 2}"])
                              pzn = "P2a" if t % 2 == 0 else "P2b"
                              pz = PSV[pzn]
                              for g in range(4):
                                  mm(pz[:, g * 64:(g + 1) * 64], WST[:, g, :], vn[:, g * 64:(g + 1) * 64], True, True,
                                     [f"VN{t % 2}", "SM"], [pzn])
                              guf, gufn = (Fs[2], "F2") if t % 2 == 0 else (Fs[3], "F3")
                              act(guf[:, 0:256], OB[:, t, :], AF.Copy, [("OB", t)], [gufn])
                              for g in range(4):
                                  stt(OB[:, t, g * 64:(g + 1) * 64], pz[:, g * 64:(g + 1) * 64], VEC[:, 22 + g:23 + g],
                                      guf[:, g * 64:(g + 1) * 64], ALU.add, ALU.mult, [pzn, "VEC", gufn], [("OB", t)])
                          elif pi == 2:
                              act(CY[:, t, :], pp, AF.Copy, [pn], [("CY", t)])
                          elif pi == 3:
                              cpv(VD[:, t, :, 0:64], pp.rearrange("p (h e) -> p h e", h=4), [pn, "VDones"], [("VD", t)])
                          else:
                              cpv(VA[:, t, :, 0:64], pp.rearrange("p (h e) -> p h e", h=2), [pn, "VAones"], [("VA", t)])
                  S.barrier()
              stage("p1")
              tap(f"qt{l}", QT[:, :, :], [128, 4, NTOK], BF16, [])
              tap(f"kt{l}", KT[:, :, :], [128, 3, NTOK], BF16, [])
              tap(f"ob{l}", OB[:, :, :], [128, 18, 256], BF16, [])
              tap(f"cy{l}", CY[:, :, :], [128, 18, 256], BF16, [])
              tap(f"vd{l}", VD[:, :, :, :], [128, 18, 4, 66], BF16, [])

              for half in range(2):
                  hgroups = [c for c in HALVES[half] if c in groups_all]
                  tiles_h = []
                  for c in hgroups:
                      s0, Lc = GROUPS[c]
                      tiles_h += [(t, c) for t in range(s0 // 128, (s0 + Lc) // 128)]
                  dma_pool(TAB[:, 0:2560], pm_d, [], ["TAB"])
                  PMv = TAB[:, 0:2560].rearrange("p (a t) -> p a t", a=20)
                  for (t, c) in tiles_h:
                      h0 = t * 128 - HSTART[half]
                      pv = PST[:, (t % 2) * 512:(t % 2) * 512 + 256]
                      pvn = "PST"
                      for cc in range(2):
                          tr(pv[:, cc * 128:(cc + 1) * 128], OB[:, t, cc * 128:(cc + 1) * 128], IDB, [("OB", t), "CBF"], [pvn])
                      cpv(HB[:, 2:4, h0:h0 + 128], pv.rearrange("p (c t) -> p c t", c=2), [pvn], [("HB", c)])
                  for c in hgroups:
                      s0, Lc = GROUPS[c]
                      h0, _ = hcols(c, half)
                      first_t = 16 if c == 4 else 0
                      last_t = 17 if c == 4 else 15
                      for t in range(s0 // 128, (s0 + Lc) // 128):
                          lo = (t * 128 - s0)
                          for g in range(4):
                              srcs = []
                              if t > first_t:
                                  srcs.append((t - 1, 3))
                              srcs.append((t, 1 if t == first_t else (2 if t == last_t else 0)))
                              if t < last_t:
                                  srcs.append((t + 1, 4))
                              o = (g % 2) * 64
                              pcn = "P0a" if g < 2 else "P0b"
                              outp = P0[o:o + 64, (g // 2) * 512 + lo:(g // 2) * 512 + lo + 128]
                              for i, (tj, v) in enumerate(srcs):
                                  mm(outp, CY[:, tj, g * 64:(g + 1) * 64], PMv[:, g * 5 + v, :], i == 0, i == len(srcs) - 1,
                                     [("CY", tj), "TAB"], [pcn])
                      for cc in range(2):
                          pcn = "P0a" if cc == 0 else "P0b"
                          pln = "P1a" if cc == 0 else "P1b"
                          act(PC[:, cc, 0:Lc], P0[:, cc * 512:cc * 512 + Lc], AF.Copy, [pcn], [("PC", cc)])
                          mm(P1[:, cc * 512:cc * 512 + Lc], WPBD[:, cc, :], PC[:, cc, 0:Lc], True, True, [("PC", cc), "SM"], [pln])
                          act(HB[:, 4 + cc, h0:h0 + Lc], P1[:, cc * 512:cc * 512 + Lc], AF.Identity, [pln, "VEC"], [("HB", c)],
                              scale=VEC[:, 20 + cc:21 + cc])
                  stage(f"p2a_{half}")
                  acount = [0]

                  def att_scores(job):
                      i = acount[0] % 2
                      acount[0] += 1
                      job["i"] = i
                      Sp = P0 if i == 0 else P1
                      Sn = ["P0a", "P0b"] if i == 0 else ["P1a", "P1b"]
                      chunks = job["chunks"]
                      n = len(chunks)
                      for ci, (kt, bias) in enumerate(chunks):
                          o = Sp[:, ci * 128:(ci + 1) * 128]
                          mm(o, job["K"](kt), job["Q"], True, bias is None, [], Sn)
                          if bias is not None:
                              mm(o, IDB, bias, False, True, ["CBF", "TAB"], Sn)
                      act(PT[i][:, 0:n * 128], Sp[:, 0:n * 128], AF.Exp, Sn, [f"PT{i}"])

                  def att_pv(job):
                      i = job["i"]
                      chunks = job["chunks"]
                      n = len(chunks)
                      h = job["h"]
                      for ci, (kt, bias) in enumerate(chunks):
                          mm(job["O"][:, h * 66:h * 66 + 65], PT[i][:, ci * 128:(ci + 1) * 128], job["V"](kt), ci == 0, ci == n - 1,
                             [f"PT{i}"], [job["On"]])

                  def finish_tile(t, c, Otile, Oname, sink, cat0):
                      h0 = t * 128 - HSTART[half]
                      Ov = Otile[:, 0:264].rearrange("p (h e) -> p h e", h=4)
                      o = 32 + (t % 2) * 8
                      dn = f"DEN{t % 2}"
                      if sink:
                          tt(SMALL[:, o:o + 4], Ov[:, :, 64], GS[:, 4:8], ALU.add, [Oname, "GS"], [dn])
                      else:
                          cpv(SMALL[:, o:o + 4], Ov[:, :, 64], [Oname], [dn])
                      S.op("dve", lambda o=o: nc.vector.reciprocal(out=SMALL[:, o + 4:o + 8], in_=SMALL[:, o:o + 4]), [dn], [dn + "r"])
                      ot = OT[t % 2]
                      otn = f"OT{t % 2}"
                      tt(ot[:].rearrange("p (h e) -> p h e", h=4), Ov[:, :, 0:64],
                         SMALL[:, o + 4:o + 8].unsqueeze(2).to_broadcast([128, 4, 64]), ALU.mult, [Oname, dn + "r"], [otn])
                      pv = PST[:, (t % 2) * 512:(t % 2) * 512 + 256]
                      pvn = "PST"
                      for cc in range(2):
                          tr(pv[:, cc * 128:(cc + 1) * 128], ot[:, cc * 128:(cc + 1) * 128], IDB, [otn, "CBF"], [pvn])
                      cpv(HB[:, cat0:cat0 + 2, h0:h0 + 128], pv.rearrange("p (c t) -> p c t", c=2), [pvn], [("HB", c)])

                  def run_attn(jobs, sink, cat0):
                      prev = None
                      for job in jobs + [None]:
                          if job is not None:
                              att_scores(job)
                          if prev is not None:
                              att_pv(prev)
                              if prev["last"]:
                                  finish_tile(prev["t"], prev["c"], prev["O"], prev["On"], sink, cat0)
                          prev = job

                  def tcols(t):
                      return slice(t * 128, (t + 1) * 128)

                  jobs = []
                  for (t, c) in tiles_h:
                      Ot, On = (PSV["P2a"], "P2a") if t % 2 == 0 else (PSV["P2b"], "P2b")
                      for h in range(4):
                          kv, g = h // 2, h % 2
                          po = kv * 64
                          if c == 4:
                              chunks = [(16, None), (17, None)]
                          else:
                              chunks = []
                              if t > 0:
                                  chunks.append((t - 1, MASKP))
                              chunks.append((t, None))
                              if t < 15:
                                  chunks.append((t + 1, MASKN))
                              chunks += [(16, None), (17, None)]
                          jobs.append({"t": t, "c": c, "h": h, "K": (lambda kt, po=po: KT[po:po + 64, 0, tcols(kt)]),
                                       "Q": QT[po:po + 64, g, tcols(t)], "V": (lambda kt, kv=kv: VA[:, kt, kv, 0:65]),
                                       "chunks": chunks, "O": Ot, "On": On, "last": h == 3})
                  run_attn(jobs, True, 0)
                  stage(f"attA_{half}")
                  dma_pool(TAB[:, 0:5376], Ld["dtab"], [], ["TAB"])
                  DT = TAB[:, 0:5376].rearrange("p (h s q) -> p h s q", h=4, s=21)
                  jobs = []
                  for (t, c) in tiles_h:
                      Ot, On = (PSV["P2a"], "P2a") if t % 2 == 0 else (PSV["P2b"], "P2b")
                      if c == 4:
                          deltas, interior = [], False
                      elif 2 <= t <= 13:
                          deltas, interior = [-2, -1, 0, 1, 2], True
                      elif t == 0:
                          deltas, interior = [0, 1, 2, 3], False
                      elif t == 1:
                          deltas, interior = [-1, 0, 1, 2], False
                      elif t == 14:
                          deltas, interior = [-2, -1, 0, 1], False
                      else:
                          deltas, interior = [-3, -2, -1, 0], False
                      for h in range(4):
                          po = (h % 2) * 64
                          ch = h // 2
                          chunks = []
                          for dl in deltas:
                              i0 = _dslot(interior, 2 * dl + 7)
                              i1 = _dslot(interior, 2 * dl + 6)
                              chunks.append((t + dl, DT[:, h, i0:i1 + 1:(i1 - i0), :]))
                          chunks += [(16, None), (17, None)]
                          jobs.append({"t": t, "c": c, "h": h, "K": (lambda kt, po=po, ch=ch: KT[po:po + 64, 1 + ch, tcols(kt)]),
                                       "Q": QT[po:po + 64, 2 + ch, tcols(t)], "V": (lambda kt, h=h: VD[:, kt, h, 0:65]),
                                       "chunks": chunks, "O": Ot, "On": On, "last": h == 3})
                  run_attn(jobs, False, 6)
                  if half == 0:
                      tap(f"cat{l}", HB[:, :, 0:1024], [128, 8, 1024], BF16, [])
                  stage(f"attD_{half}")
                  wcount = 0
                  for a in range(4):
                      slot, sr = load_piece(l, NP_WO + a)
                      sv = slot[:].rearrange("p (k n) -> p k n", k=8)
                      for di in range(2):
                          dch = 2 * a + di
                          for c in hgroups:
                              s0, Lc = GROUPS[c]
                              h0, _ = hcols(c, half)
                              s = sidx(c)
                              pn = ["P0a", "P0b", "P1a", "P1b"][wcount % 4]
                              wcount += 1
                              pw = PSV[pn][:, 0:Lc]
                              for k in range(8):
                                  mm(pw, sv[:, k, di * 128:(di + 1) * 128], HB[:, k, h0:h0 + Lc], k == 0, k == 7,
                                     [sr, ("HB", c)], [pn])
                              stt(X[:, dch, s0:s0 + Lc], pw, G1[:, dch, s:s + 1], X[:, dch, s0:s0 + Lc], ALU.mult, ALU.add,
                                  [pn, "MOD", xres(c)], [xres(c)])
                  stage(f"wout_{half}")
                  S.barrier()
              stage("wout")
              tap(f"x1_{l}", X[:], [128, 8, NTOK], F32, [])

              def h2dst(k, c):
                  s0, Lc = GROUPS[c]
                  if c < 2:
                      return HB[:, k, s0:s0 + Lc], ("HB", c)
                  return H2B[:, k, s0 - 1024:s0 - 1024 + Lc], ("H2B", c)

              for c in groups_all:
                  do_norm(c, A2, SH2, h2dst, "A2")
              stage("norm2")
              gcount = 0
              dcount = 0
              for bi, blk in enumerate(FFBLOCKS):
                  for fl, f in enumerate(blk):
                      slot, sr = load_piece(l, NP_GU + f)
                      sv = slot[:].rearrange("p (u k n) -> p u k n", u=2, k=8)
                      for c in groups_all:
                          s0, Lc = GROUPS[c]
                          i = gcount % 2
                          gcount += 1
                          pgn, pun = ("P0a", "P0b") if i == 0 else ("P1a", "P1b")
                          pg, pu = PSV[pgn][:, 0:Lc], PSV[pun][:, 0:Lc]
                          for k in range(8):
                              hsrc, hres = h2dst(k, c)
                              mm(pg, sv[:, 0, k, :], hsrc, k == 0, k == 7, [sr, hres], [pgn])
                          for k in range(8):
                              hsrc, hres = h2dst(k, c)
                              mm(pu, sv[:, 1, k, :], hsrc, k == 0, k == 7, [sr, hres], [pun])
                          sg, sgn = (Fs[0], "F0") if i == 0 else (Fs[1], "F1")
                          act(sg[:, 0:Lc], pg, AF.Silu, [pgn], [sgn])
                          tt(ACTB[:, fl, s0:s0 + Lc], pu, sg[:, 0:Lc], ALU.mult, [sgn, pun], [("ACTB", fl, c)])
                  nf = len(blk)
                  for dch in range(8):
                      slot, sr = load_piece(l, NP_WD + bi * 8 + dch, nf * 128)
                      sv = slot[:, 0:nf * 128].rearrange("p (f n) -> p f n", f=nf)
                      for c in groups_all:
                          s0, Lc = GROUPS[c]
                          s = sidx(c)
                          pn = ["P2a", "P2b", "P3"][dcount % 3]
                          dcount += 1
                          pd = PSV[pn][:, 0:Lc]
                          for fl in range(nf):
                              mm(pd, sv[:, fl, :], ACTB[:, fl, s0:s0 + Lc], fl == 0, fl == nf - 1, [sr, ("ACTB", fl, c)], [pn])
                          stt(X[:, dch, s0:s0 + Lc], pd, G2[:, dch, s:s + 1], X[:, dch, s0:s0 + Lc], ALU.mult, ALU.add,
                              [pn, "MOD", xres(c)], [xres(c)])
              S.barrier()
              stage("ffn")
              tap(f"x2_{l}", X[:], [128, 8, NTOK], F32, [])


        try:
            run_layers()
        except _Stop:
            pass
        for t in range(16):
            Pt = P0 if t % 2 == 0 else P1
            pr = ["P0a", "P0b"] if t % 2 == 0 else ["P1a", "P1b"]
            for k in range(8):
                tr(Pt[:, k * 128:(k + 1) * 128], X[:, k, t * 128:(t + 1) * 128], IDF[:], [xres(t // 4), "IDF"], pr)
            stg = SCR[:, (t % 2) * 1024:(t % 2 + 1) * 1024]
            stg_r = ["F0", "F1"] if t % 2 == 0 else ["F2", "F3"]
            if t % 2 == 0:
                act(stg, Pt[:], AF.Copy, pr, stg_r)
            else:
                cpv(stg, Pt[:], pr, stg_r)
            dma_sp(out_d[t * 128:(t + 1) * 128, :], stg, stg_r, [("out", t)])
        S.emit(st)
    return nc, tap_d


_CACHE = {}


def prep_inputs(inp, n_layers=2):
    inp = {k: np.asarray(v, dtype=np.float32) for k, v in inp.items()}
    cbf, rope, pm = _consts()
    shared = {"cbf": np.ascontiguousarray(cbf.reshape(128, -1)), "idf": np.eye(128, dtype=np.float32),
              "rope": np.ascontiguousarray(rope.reshape(128, -1)), "pm": np.ascontiguousarray(pm.reshape(128, -1))}
    for l in range(n_layers):
        la = _layer_arrays(inp, l)
        for k, v in la.items():
            shared[f"{k}{l}"] = v
    maps = []
    for b in range(8):
        m = dict(shared)
        m["x"] = np.ascontiguousarray(inp["x"][b])
        m["ctx"] = np.ascontiguousarray(inp["ctx"][b])
        cv = np.zeros((128, 8, 2), np.float32)
        cv[:, :, 0] = inp["c"][b].reshape(8, 128).T
        cv[:, :, 1] = inp["c_ctx"].reshape(8, 128).T
        m["cvec"] = np.ascontiguousarray(cv.reshape(128, 16))
        maps.append(m)
    return maps


def kernel(**inputs):
    if "nc" not in _CACHE:
        _CACHE["nc"] = build(2)[0]
    nc = _CACHE["nc"]
    maps = prep_inputs(inputs, 2)
    res = run_bass_kernel_spmd(nc, maps, core_ids=list(range(8)))
    return np.stack([np.asarray(r["out"], dtype=np.float32) for r in res.results], 0)
```

```python
import contextlib
import numpy as np
import concourse.bass as bass
import concourse.mybir as mybir
from concourse.bass_utils import run_bass_kernel_spmd

F32 = mybir.dt.float32
BF16 = mybir.dt.bfloat16
AF = mybir.ActivationFunctionType
ALU = mybir.AluOpType

D = 1024
SEQ = 2048
CTX = 256
NTOK = SEQ + CTX
FF = 2816
NFF = 22
NEG = -30000.0
EPS = 1e-6
GROUPS = [(0, 512), (512, 512), (1024, 512), (1536, 512), (2048, 256)]
HALVES = [[0, 1], [2, 3, 4]]
HSTART = [0, 1024]
FFBLOCKS = [list(range(0, 8)), list(range(8, 15)), list(range(15, 22))]
PIECE = 2048
import os as _os
_DBG_DELAY = int(_os.environ.get('DBG_DELAY', '0'))
_DBG_SKIP = set(int(v) for v in _os.environ.get('DBG_SKIP_A', '').split(',') if v)


class _Op:
    __slots__ = ("eng", "fn", "dma", "deps", "has_dep", "sig")

    def __init__(self, eng, fn, dma):
        self.eng = eng
        self.fn = fn
        self.dma = dma
        self.deps = []
        self.has_dep = False
        self.sig = None


class Sched:
    COMPUTE = ("pe", "act", "dve")

    def __init__(self, nc, n_dma_sems=8):
        self.nc = nc
        self.ops = []
        self.res = {}
        self.n_dma_sems = n_dma_sems
        self.last = {}
        self.pending = {}
        self.dma_since = []

    def op(self, eng, fn, reads=(), writes=(), dma=False):
        o = _Op(eng, fn, dma)
        deps = {}
        for r in reads:
            st = self.res.get(r)
            if st is not None and st[0] is not None:
                deps[id(st[0])] = (st[0], True)
        for w in writes:
            st = self.res.get(w)
            if st is not None:
                if st[0] is not None and id(st[0]) not in deps:
                    deps[id(st[0])] = (st[0], False)
                for rd in st[1]:
                    if id(rd) not in deps:
                        deps[id(rd)] = (rd, False)
        for d in self.pending.pop(eng, ()):
            if id(d) not in deps:
                deps[id(d)] = (d, True)
        for d, raw in deps.values():
            if d is o:
                continue
            if not d.dma and not o.dma and d.eng == eng:
                if eng == "pe" or not raw:
                    continue
            o.deps.append(d)
            d.has_dep = True
        for r in reads:
            st = self.res.get(r)
            if st is None:
                self.res[r] = [None, [o]]
            else:
                st[1].append(o)
        for w in writes:
            self.res[w] = [o, []]
        self.ops.append(o)
        self.last[eng] = o
        if dma:
            self.dma_since.append(o)
        return o

    def barrier(self):
        lasts = [o for e, o in self.last.items() if e in self.COMPUTE]
        lasts += self.dma_since
        self.dma_since = []
        for e in self.COMPUTE:
            self.pending[e] = self.pending.get(e, []) + lasts

    def emit(self, stack):
        nc = self.nc
        eobj = {"pe": nc.tensor, "act": nc.scalar, "dve": nc.vector, "pool": nc.gpsimd, "sp": nc.sync}
        semh = {}
        cnt = {}
        waited = {e: {} for e in eobj}
        dma_rr = {}
        dcnt = {}

        def get_sem(key):
            if key not in semh:
                semh[key] = stack.enter_context(nc.semaphore("s_" + "_".join(str(k) for k in key)))
            return semh[key]

        for o in self.ops:
            E = eobj[o.eng]
            need = {}
            for d in o.deps:
                key, val = d.sig
                if need.get(key, 0) < val:
                    need[key] = val
            if o.dma:
                i = dma_rr.get(o.eng, 0)
                dma_rr[o.eng] = (i + 1) % self.n_dma_sems
                k = ("dma", o.eng, i)
                n = dcnt.get(k, 0) + 1
                dcnt[k] = n
                if n > 1 and need.get(k, 0) < 16 * (n - 1):
                    need[k] = 16 * (n - 1)
            for key, val in need.items():
                if waited[o.eng].get(key, 0) < val:
                    E.wait_ge(get_sem(key), val)
                    waited[o.eng][key] = val
            ins = o.fn()
            if o.dma:
                ins.then_inc(get_sem(k), 16)
                o.sig = (k, 16 * n)
            elif o.has_dep:
                key = ("e", o.eng)
                cnt[key] = cnt.get(key, 0) + 1
                ins.then_inc(get_sem(key), 1)
                o.sig = (key, cnt[key])
        for key, n in dcnt.items():
            nc.sync.wait_ge(get_sem(key), 16 * n)
        for key, n in cnt.items():
            nc.sync.wait_ge(get_sem(key), n)


def _kmajor(w):
    K, N = w.shape
    return np.ascontiguousarray(w.reshape(K // 128, 128, N).transpose(1, 0, 2))


def _pad_piece(a):
    a = np.ascontiguousarray(a, dtype=np.float32).reshape(128, -1)
    out = np.zeros((128, PIECE), np.float32)
    out[:, : a.shape[1]] = a
    return out


def _dslot(interior, s):
    if 3 <= s <= 9:
        return 8 + (9 - s)
    if s >= 10:
        return (0 if interior else 4) + (13 - s)
    return (18 if interior else 15) + (2 - s)


def _consts():
    c = np.zeros((128, 6, 128), np.float32)
    c[:, 0, :] = np.eye(128)
    c[:, 1, :] = 1.0
    c[0:64, 2, 0:64] = 1.0
    c[64:128, 2, 64:128] = 1.0
    for m in range(128):
        if (m % 32) < 16:
            c[m + 16, 3, m] = -1.0
        else:
            c[m - 16, 3, m] = 1.0
    j = np.arange(128)[:, None]
    i = np.arange(128)[None, :]
    c[:, 4, :] = np.where(j >= i, 0.0, NEG)
    c[:, 5, :] = np.where(j <= i, 0.0, NEG)
    p = np.arange(128)
    idx = p % 64
    f = (idx % 16).astype(np.float64)
    inv = np.power(10000.0, -f / 16.0)
    t = np.arange(SEQ)
    pos = np.where((idx < 32)[:, None], (t // 64)[None, :], (t % 64)[None, :]).astype(np.float64)
    ang = pos * inv[:, None]
    rope = np.stack([np.cos(ang), np.sin(ang)], axis=1).astype(np.float32)
    pm = np.zeros((128, 20, 128), np.float32)
    jj = np.arange(128)[:, None]
    tt = np.arange(128)[None, :]
    for g, w in enumerate((2, 4, 8, 16)):
        h = w // 2
        eye = (jj == tt).astype(np.float32)
        pm[:, g * 5 + 0, :] = ((jj >= tt - h) & (jj < tt + h)) / w - eye
        lo = np.maximum(tt - h, 0)
        cntf = (tt + h) - lo
        pm[:, g * 5 + 1, :] = ((jj >= lo) & (jj < tt + h)) / cntf - eye
        hi = np.minimum(tt + h, 128)
        cntl = hi - (tt - h)
        pm[:, g * 5 + 2, :] = ((jj >= tt - h) & (jj < hi)) / cntl - eye
        pm[:, g * 5 + 3, :] = (jj >= 128 + tt - h) / w
        pm[:, g * 5 + 4, :] = (jj < tt + h - 128) / w
    return c, rope, pm


def _dtab(rpb):
    kc = np.arange(64)[:, None]
    qc = np.arange(64)[None, :]
    cs = np.clip(qc - 8, 0, 48)
    col_ok = (kc >= cs) & (kc < cs + 16)
    coff = np.clip(kc - qc, -15, 15) + 15
    out = np.full((128, 4, 21, 64), NEG, np.float32)
    for h in range(4):
        for interior in (True, False):
            for s in range(14):
                sl = _dslot(interior, s)
                for kr in range(2):
                    rho = s + kr
                    ok = 0 <= rho <= 14 and ((3 <= rho <= 10) or not interior)
                    if not ok:
                        continue
                    T = np.where(col_ok, rpb[h, rho][coff], NEG)
                    out[kr * 64:(kr + 1) * 64, h, sl, :] = T
    return out


def _layer_arrays(inp, l):
    f32 = np.float32
    w_in = inp["w_in"][l]
    pieces = []
    wm = _kmajor(inp["w_mod"][l])
    for jj in range(24):
        pieces.append(_pad_piece(wm[:, :, jj * 256:(jj + 1) * 256]))
    aq = w_in[:, 0:256]
    fm_cols = [np.concatenate([aq[:, 0:64], aq[:, 128:192]], 1), np.concatenate([aq[:, 64:128], aq[:, 192:256]], 1),
               w_in[:, 256:384], w_in[:, 1280:1408], w_in[:, 1408:1536], w_in[:, 1536:1664], w_in[:, 1664:1792]]
    fm = _kmajor(np.concatenate(fm_cols, 1))
    for a in range(4):
        pieces.append(_pad_piece(fm[:, :, a * 256:min((a + 1) * 256, 896)]))
    tm = [w_in[:, 512:768], w_in[:, 768:1024], w_in[:, 1024:1280], w_in[:, 1792:2048], w_in[:, 384:512]]
    for a in tm:
        pieces.append(_pad_piece(_kmajor(a)))
    wo = _kmajor(inp["w_out"][l])
    for a in range(4):
        pieces.append(_pad_piece(wo[:, :, a * 256:(a + 1) * 256]))
    wg = _kmajor(inp["w_gate"][l])
    wu = _kmajor(inp["w_up"][l])
    for f in range(NFF):
        pieces.append(_pad_piece(np.concatenate([wg[:, :, f * 128:(f + 1) * 128].reshape(128, -1),
                                                 wu[:, :, f * 128:(f + 1) * 128].reshape(128, -1)], 1)))
    wd = _kmajor(inp["w_down"][l])
    for blk in FFBLOCKS:
        for dch in range(8):
            pieces.append(_pad_piece(wd[:, blk[0]:blk[-1] + 1, dch * 128:(dch + 1) * 128]))
    W = np.stack(pieces, 0)
    vec = np.zeros((128, 26), f32)
    vec[:, 0:8] = inp["g_mix"][l].reshape(8, 128).T
    vec[:, 8:16] = inp["g_ffn"][l].reshape(8, 128).T
    vec[:, 16] = np.tile(inp["a_q_gain"][l], 2)
    vec[:, 17] = np.tile(inp["a_k_gain"][l], 2)
    vec[:, 18] = np.tile(inp["d_q_gain"][l], 2)
    vec[:, 19] = np.tile(inp["d_k_gain"][l], 2)
    vec[:, 20:22] = inp["c_scale"][l].reshape(2, 128).T
    vec[:, 22:26] = inp["b_b_s"][l].T
    bmod = np.ascontiguousarray(inp["b_mod"][l].reshape(48, 128).T)
    bc = np.zeros((128, 260), f32)
    bc[:, 0:4] = inp["a_sink"][l][None, :]
    bc[:, 4:260] = inp["b_v_gain"][l][None, :]
    sm = np.zeros((128, 6, 128), f32)
    sm[:, 0:4, :] = inp["b_w_s"][l].transpose(2, 0, 1)
    wp = inp["c_w_pool"][l]
    for g in range(4):
        o = (g % 2) * 64
        sm[o:o + 64, 4 + g // 2, o:o + 64] = wp[g]
    dt = _dtab(inp["d_rpb"][l])
    return {"W": W, "vec": vec, "bmod": bmod, "bc": bc, "sm": np.ascontiguousarray(sm.reshape(128, -1)), "dtab": np.ascontiguousarray(dt.reshape(128, -1))}


NP_MOD, NP_FM, NP_TM, NP_WO, NP_GU, NP_WD = 0, 24, 28, 33, 37, 59
NPIECES = 83


class _Stop(Exception):
    pass


def build(n_layers=2, taps=(), stop=None):
    nc = bass.Bass("TRN2", target_bir_lowering=False)
    dr = {}

    def din(name, shape):
        dr[name] = nc.dram_tensor(name, list(shape), F32, kind="ExternalInput").ap()
        return dr[name]

    x_d = din("x", [SEQ, D])
    ctx_d = din("ctx", [CTX, D])
    cvec_d = din("cvec", [128, 16])
    cbf_d = din("cbf", [128, 6 * 128])
    idf_d = din("idf", [128, 128])
    rope_d = din("rope", [128, 2 * SEQ])
    pm_d = din("pm", [128, 20 * 128])
    L = []
    for l in range(n_layers):
        L.append({
            "W": din(f"W{l}", [NPIECES, 128, PIECE]), "vec": din(f"vec{l}", [128, 26]), "bmod": din(f"bmod{l}", [128, 48]),
            "bc": din(f"bc{l}", [128, 260]), "sm": din(f"sm{l}", [128, 6 * 128]), "dtab": din(f"dtab{l}", [128, 4 * 21 * 64]),
        })
    out_d = nc.dram_tensor("out", [SEQ, D], F32, kind="ExternalOutput").ap()
    tap_d = {}

    st = contextlib.ExitStack()
    with st:
        S = Sched(nc)
        sb = lambda n, s, d: st.enter_context(nc.sbuf_tensor(n, list(s), d))
        ps = lambda n, s, d: st.enter_context(nc.psum_tensor(n, list(s), d))
        X = sb("X", [128, 8, NTOK], F32)
        HB = sb("HB", [128, 8, 1280], BF16)
        RR = sb("RR", [128, 32472], BF16)
        QT = RR[:, 0:9216].rearrange("p (k t) -> p k t", k=4)
        KT = RR[:, 9216:16128].rearrange("p (k t) -> p k t", k=3)
        VA = RR[:, 16128:18504].rearrange("p (t h e) -> p t h e", t=18, h=2)
        VD = RR[:, 18504:23256].rearrange("p (t h e) -> p t h e", t=18, h=4)
        OB = RR[:, 23256:27864].rearrange("p (t c) -> p t c", t=18)
        CY = RR[:, 27864:32472].rearrange("p (t c) -> p t c", t=18)
        ACTB = RR[:, 0:18432].rearrange("p (f t) -> p f t", f=8)
        H2B = RR[:, 18432:28672].rearrange("p (k t) -> p k t", k=8)
        RING = [sb(f"ring{i}", [128, PIECE], BF16) for i in range(4)]
        TAB = sb("TAB", [128, 5376], BF16)
        SCR = sb("SCR", [128, 2048], F32)
        Fs = [SCR[:, i * 512:(i + 1) * 512] for i in range(4)]
        SQ = [sb(f"SQ{i}", [128, 512], BF16) for i in range(2)]
        QG = sb("QG", [128, 512], BF16)
        RQ = sb("RQ", [128, 512], BF16)
        PT = [sb(f"PT{i}", [128, 896], BF16) for i in range(2)]
        OT = [sb(f"OT{i}", [128, 256], BF16) for i in range(2)]
        PC = sb("PC", [128, 2, 512], BF16)
        QZ = PC[:].rearrange("p a (b q) -> p (a b) q", q=128)
        VN = [sb(f"VN{i}", [128, 256], BF16) for i in range(2)]
        CBF = sb("CBF", [128, 6, 128], BF16)
        IDF = sb("IDF", [128, 128], F32)
        CV = sb("CV", [128, 8, 2], F32)
        CA = sb("CA", [128, 8, 2], BF16)
        SMALL = sb("SMALL", [128, 64], F32)
        IDB = CBF[:, 0, :]
        ONES = CBF[:, 1, :]
        BDONES = CBF[:, 2, :]
        ROTT = CBF[:, 3, :]
        MASKP = CBF[:, 4, :]
        MASKN = CBF[:, 5, :]
        P0 = ps("P0", [128, 1024], F32)
        P1 = ps("P1", [128, 1024], F32)
        P2 = ps("P2", [128, 1024], F32)
        P3 = ps("P3", [128, 512], F32)
        PST = ps("PST", [128, 1024], BF16)
        PSV = {"P0a": P0[:, 0:512], "P0b": P0[:, 512:1024], "P1a": P1[:, 0:512], "P1b": P1[:, 512:1024],
               "P2a": P2[:, 0:512], "P2b": P2[:, 512:1024], "P3": P3[:, :]}

        def mm(out, lhsT, rhs, start, stop, reads, writes):
            S.op("pe", lambda: nc.tensor.matmul(out, lhsT=lhsT, rhs=rhs, start=start, stop=stop), reads, writes)

        def tr(out, in_, ident, reads, writes):
            S.op("pe", lambda: nc.tensor.transpose(out, in_, ident), reads, writes)

        def act(out, in_, func, reads, writes, scale=None, bias=None):
            kw = {}
            if scale is not None:
                kw["scale"] = scale
            if bias is not None:
                kw["bias"] = bias
            S.op("act", lambda: nc.scalar.activation(out=out, in_=in_, func=func, **kw), reads, writes)

        def tt(out, in0, in1, op, reads, writes):
            S.op("dve", lambda: nc.vector.tensor_tensor(out=out, in0=in0, in1=in1, op=op), reads, writes)

        def stt(out, in0, scalar, in1, op0, op1, reads, writes):
            S.op("dve", lambda: nc.vector.scalar_tensor_tensor(out=out, in0=in0, scalar=scalar, in1=in1, op0=op0, op1=op1),
                 reads, writes)

        def ts(out, in0, s1, s2, op0, op1, reads, writes):
            S.op("dve", lambda: nc.vector.tensor_scalar(out=out, in0=in0, scalar1=s1, scalar2=s2, op0=op0, op1=op1),
                 reads, writes)

        def cpv(out, in_, reads, writes):
            S.op("dve", lambda: nc.vector.tensor_copy(out=out, in_=in_), reads, writes)

        def dma_sp(out, in_, reads, writes):
            S.op("sp", lambda: nc.sync.dma_start(out=out, in_=in_), reads, writes, dma=True)

        def dma_pool(out, in_, reads, writes):
            S.op("pool", lambda: nc.gpsimd.dma_start(out=out, in_=in_, max_dma_last_dim=4096), reads, writes, dma=True)

        def stage(name):
            if stop == name:
                S.barrier()
                raise _Stop()

        def tap(name, ap, shape, dtype, reads):
            if name not in taps:
                return
            S.barrier()
            t = nc.dram_tensor("tap_" + name, list(shape), dtype, kind="ExternalOutput").ap()
            tap_d[name] = t
            dma_sp(t, ap, list(reads) + ["tapsrc"], ["tap_" + name])

        ring_i = [0]

        def load_piece(l, idx, nelem=PIECE):
            i = ring_i[0]
            ring_i[0] = (i + 1) % 4
            dma_pool(RING[i][:, 0:nelem], L[l]["W"][idx, :, 0:nelem], [], [("slot", i)])
            return RING[i], ("slot", i)

        dma_sp(IDF[:], idf_d, [], ["IDF"])
        dma_sp(CV[:].rearrange("p k s -> p (k s)"), cvec_d, [], ["CV"])
        dma_pool(CBF[:].rearrange("p a b -> p (a b)"), cbf_d, [], ["CBF"])
        act(CA[:], CV[:], AF.Silu, ["CV"], ["CA"])
        for t in range(18):
            stg = SCR[:, (t % 2) * 1024:(t % 2 + 1) * 1024]
            stg_r = ["F0", "F1"] if t % 2 == 0 else ["F2", "F3"]
            src = x_d[t * 128:(t + 1) * 128, :] if t < 16 else ctx_d[(t - 16) * 128:(t - 15) * 128, :]
            dma_sp(stg, src, [], stg_r)
            Pt = P0 if t % 2 == 0 else P1
            pr = ["P0a", "P0b"] if t % 2 == 0 else ["P1a", "P1b"]
            for k in range(8):
                tr(Pt[:, k * 128:(k + 1) * 128], stg[:, k * 128:(k + 1) * 128], IDF[:], stg_r + ["IDF"], pr)
            dst = X[:, :, t * 128:(t + 1) * 128]
            src_ps = Pt[:].rearrange("p (k t) -> p k t", k=8)
            xr = [("X", t // 4 if t < 16 else 4)]
            if t % 2 == 0:
                act(dst, src_ps, AF.Copy, pr, xr)
            else:
                cpv(dst, src_ps, pr, xr)
        S.barrier()

        def xres(c):
            return ("X", c)

        def sidx(c):
            return 1 if c == 4 else 0

        VEC = sb("VEC", [128, 26], F32)
        BMODs = [sb(f"BMOD{i}", [128, 48], F32) for i in range(2)]
        BC = sb("BC", [128, 260], F32)
        SM = sb("SM", [128, 6, 128], BF16)
        MODs = [sb(f"MOD{i}", [128, 48, 2], F32) for i in range(2)]
        A1 = sb("A1", [128, 8, 2], F32)
        A2 = sb("A2", [128, 8, 2], F32)
        GS = sb("GS", [128, 8], F32)

        def mod_gen(l):
            b = l % 2
            dma_sp(BMODs[b][:], L[l]["bmod"], [], [f"BMOD{b}"])
            for jj in range(24):
                slot, sr = load_piece(l, NP_MOD + jj)
                sv = slot[:].rearrange("p (k n) -> p k n", k=8)
                for cc in range(2):
                    j = 2 * jj + cc
                    for k in range(8):
                        mm(P3[:, 2 * j:2 * j + 2], sv[:, k, cc * 128:(cc + 1) * 128], CA[:, k, :], k == 0, k == 7,
                           [sr, "CA"], ["P3"])
                yield
            tt(MODs[b][:], P3[:, 0:96].rearrange("p (j s) -> p j s", s=2),
               BMODs[b][:].unsqueeze(2).to_broadcast([128, 48, 2]), ALU.add, ["P3", f"BMOD{b}"], [f"MOD{b}"])
            yield

        def run_layers():
          for l in range(n_layers):
              last = l == n_layers - 1
              Ld = L[l]
              dma_sp(VEC[:], Ld["vec"], [], ["VEC"])
              dma_sp(BC[:], Ld["bc"], [], ["BC"])
              dma_pool(SM[:].rearrange("p a b -> p (a b)"), Ld["sm"], [], ["SM"])
              WST = SM[:, 0:4, :]
              WPBD = SM[:, 4:6, :]

              MOD = MODs[l % 2]
              MODn = f"MOD{l % 2}"
              if l == 0:
                  for _ in mod_gen(0):
                      pass
              pre = [mod_gen(l + 1) if not last else None]

              def advance_mod(n=1):
                  for _ in range(n):
                      if pre[0] is not None:
                          try:
                              next(pre[0])
                          except StopIteration:
                              pre[0] = None
              stt(A1[:], MOD[:, 8:16, :], 1.0, VEC[:, 0:8].unsqueeze(2).to_broadcast([128, 8, 2]), ALU.add, ALU.mult,
                  [MODn, "VEC"], ["A1"])
              stt(A2[:], MOD[:, 32:40, :], 1.0, VEC[:, 8:16].unsqueeze(2).to_broadcast([128, 8, 2]), ALU.add, ALU.mult,
                  [MODn, "VEC"], ["A2"])
              ts(GS[:, 0:1], VEC[:, 16:17], 0.125, None, ALU.mult, ALU.bypass, ["VEC"], ["GS"])
              ts(GS[:, 1:2], VEC[:, 17:18], 1.0, None, ALU.mult, ALU.bypass, ["VEC"], ["GS"])
              ts(GS[:, 2:3], VEC[:, 18:19], 0.125, None, ALU.mult, ALU.bypass, ["VEC"], ["GS"])
              ts(GS[:, 3:4], VEC[:, 19:20], 1.0, None, ALU.mult, ALU.bypass, ["VEC"], ["GS"])
              act(GS[:, 4:8], BC[:, 0:4], AF.Exp, ["BC"], ["GS"])
              SH1, G1, SH2, G2 = MOD[:, 0:8, :], MOD[:, 16:24, :], MOD[:, 24:32, :], MOD[:, 40:48, :]
              stage("modload")
              tap(f"mod{l}", MOD[:], [128, 48, 2], F32, [MODn])

              stage("mod")
              def do_norm(c, Amul, SHv, dst_fn, tag):
                  s0, Lc = GROUPS[c]
                  s = sidx(c)
                  psn, psr = (PSV["P2a"], "P2a") if c % 2 == 0 else (PSV["P2b"], "P2b")
                  for k in range(8):
                      sq = SQ[k % 2]
                      act(sq[:, 0:Lc], X[:, k, s0:s0 + Lc], AF.Square, [xres(c)], [f"SQ{k % 2}"])
                      mm(psn[:, 0:Lc], ONES, sq[:, 0:Lc], k == 0, k == 7, [f"SQ{k % 2}", "CBF"], [psr])
                  act(Fs[0][:, 0:Lc], psn[:, 0:Lc], AF.Ln, [psr], ["F0"], scale=1.0 / D, bias=EPS)
                  act(Fs[1][:, 0:Lc], Fs[0][:, 0:Lc], AF.Exp, ["F0"], ["F1"], scale=-0.5)
                  for k in range(8):
                      tmp, tr_ = (Fs[2], "F2") if k % 2 == 0 else (Fs[3], "F3")
                      tt(tmp[:, 0:Lc], X[:, k, s0:s0 + Lc], Fs[1][:, 0:Lc], ALU.mult, [xres(c), "F1"], [tr_])
                      dst, dres = dst_fn(k, c)
                      act(dst, tmp[:, 0:Lc], AF.Identity, [tr_, MODn, tag], [dres], scale=Amul[:, k, s:s + 1],
                          bias=SHv[:, k, s:s + 1])

              groups_all = [0, 1, 2, 3] if last else [0, 1, 2, 3, 4]

              dma_pool(TAB[:, 0:4096], rope_d, [], ["TAB"])
              COS = TAB[:, 0:2048]
              SIN = TAB[:, 2048:4096]
              S.op("dve", lambda: nc.vector.memset(VA[:, :, :, 64:65], 1.0), [], ["VAones"])
              S.op("dve", lambda: nc.vector.memset(VD[:, :, :, 64:65], 1.0), [], ["VDones"])

              def hcols(c, half):
                  s0, Lc = GROUPS[c]
                  return s0 - HSTART[half], Lc

              for half in range(2):
                  hgroups = HALVES[half]
                  for c in hgroups:
                      h0, Lc = hcols(c, half)
                      do_norm(c, A1, SH1, lambda k, c, h0=h0, Lc=Lc: (HB[:, k, h0:h0 + Lc], ("HB", c)), "A1")
                  if half == 0:
                      tap(f"hT{l}", HB[:, :, 0:1024], [128, 8, 1024], BF16, [("HB", 0), ("HB", 1)])
                  stage(f"norm1_{half}")
                  fcount = 0
                  for a in range(4):
                      js = [2 * a, 2 * a + 1] if a < 3 else [6]
                      slot, sr = load_piece(l, NP_FM + a, 8 * 128 * len(js))
                      sv = slot[:, 0:8 * 128 * len(js)].rearrange("p (k n) -> p k n", k=8)
                      for ji, j in enumerate(js):
                          for c in hgroups:
                              if last and c == 4 and j in (0, 1, 3, 4):
                                  continue
                              s0, Lc = GROUPS[c]
                              h0, _ = hcols(c, half)
                              pfn = ["P0a", "P0b", "P1a"][fcount % 3]
                              phn = ["P1b", "P2a"][fcount % 2]
                              prn = ["P2b", "P3"][fcount % 2]
                              fcount += 1
                              pf, ph, pr_ = PSV[pfn][:, 0:Lc], PSV[phn][:, 0:Lc], PSV[prn][:, 0:Lc]
                              for k in range(8):
                                  mm(pf, sv[:, k, ji * 128:(ji + 1) * 128], HB[:, k, h0:h0 + Lc], k == 0, k == 7,
                                     [sr, ("HB", c)], [pfn])
                              sq = SQ[fcount % 2]
                              sqn = f"SQ{fcount % 2}"
                              act(sq[:, 0:Lc], pf, AF.Square, [pfn], [sqn])
                              mm(ph, BDONES, sq[:, 0:Lc], True, True, [sqn, "CBF"], [phn])
                              act(Fs[0][:, 0:Lc], ph, AF.Ln, [phn], ["F0"], scale=1.0 / 64, bias=EPS)
                              act(Fs[1][:, 0:Lc], Fs[0][:, 0:Lc], AF.Exp, ["F0"], ["F1"], scale=-0.5)
                              gcol = {0: 0, 1: 0, 2: 1, 3: 2, 4: 2, 5: 3, 6: 3}[j]
                              gain = GS[:, gcol:gcol + 1]
                              if j < 2:
                                  dst, dres = QT[:, j, s0:s0 + Lc], ("QT", j, c)
                              elif j == 2:
                                  dst, dres = KT[:, 0, s0:s0 + Lc], ("KT", 0, c)
                              elif j < 5:
                                  dst, dres = QT[:, j - 1, s0:s0 + Lc], ("QT", j - 1, c)
                              else:
                                  dst, dres = KT[:, j - 4, s0:s0 + Lc], ("KT", j - 4, c)
                              if j < 3 and c != 4:
                                  act(QG[:, 0:Lc], pf, AF.Identity, [pfn, "GS"], ["QG"], scale=gain)
                                  mm(pr_, ROTT, QG[:, 0:Lc], True, True, ["QG", "CBF"], [prn])
                                  act(RQ[:, 0:Lc], pr_, AF.Copy, [prn], ["RQ"])
                                  tt(Fs[2][:, 0:Lc], QG[:, 0:Lc], COS[:, s0:s0 + Lc], ALU.mult, ["QG", "TAB"], ["F2"])
                                  tt(Fs[3][:, 0:Lc], RQ[:, 0:Lc], SIN[:, s0:s0 + Lc], ALU.mult, ["RQ", "TAB"], ["F3"])
                                  tt(Fs[2][:, 0:Lc], Fs[2][:, 0:Lc], Fs[3][:, 0:Lc], ALU.add, ["F2", "F3"], ["F2"])
                                  tt(dst, Fs[2][:, 0:Lc], Fs[1][:, 0:Lc], ALU.mult, ["F2", "F1"], [dres])
                              else:
                                  stt(dst, pf, gain, Fs[1][:, 0:Lc], ALU.mult, ALU.mult, [pfn, "GS", "F1"], [dres])
                  stage(f"fm_{half}")
                  tiles_h = []
                  for c in hgroups:
                      s0, Lc = GROUPS[c]
                      tiles_h += [(t, c) for t in range(s0 // 128, (s0 + Lc) // 128)]
                  tcount = 0
                  for pi in range(5):
                      ncol = 128 if pi == 4 else 256
                      slot, sr = load_piece(l, NP_TM + pi, 8 * ncol)
                      sv = slot[:, 0:8 * ncol].rearrange("p (k n) -> p k n", k=8)
                      for (t, c) in tiles_h:
                          if last and c == 4 and pi < 3:
                              continue
                          h0 = t * 128 - HSTART[half]
                          pn = ["P0a", "P0b", "P1a", "P1b"][tcount % 4]
                          tcount += 1
                          pp = PSV[pn][:, 0:ncol]
                          for k in range(8):
                              mm(pp, HB[:, k, h0:h0 + 128], sv[:, k, :], k == 0, k == 7, [sr, ("HB", c)], [pn])
                          if pi == 0:
                              act(OB[:, t, :], pp, AF.Gelu_apprx_tanh, [pn], [("OB", t)])
                          elif pi == 1:
                              gv, gvn = (Fs[0], "F0") if t % 2 == 0 else (Fs[1], "F1")
                              gv = gv[:, 0:256]
                              act(gv, pp, AF.Gelu_apprx_tanh, [pn], [gvn])
                              o = (t % 2) * 16
                              stn = f"SMALL{t % 2}"
                              S.op("dve", lambda gv=gv, o=o: nc.vector.bn_stats(out=SMALL[:, o:o + 6], in_=gv), [gvn], [stn])
                              S.op("dve", lambda o=o: nc.vector.bn_aggr(out=SMALL[:, o + 8:o + 10], in_=SMALL[:, o:o + 6]),
                                   [stn], [stn + "b"])
                              act(SMALL[:, o + 10:o + 11], SMALL[:, o + 9:o + 10], AF.Ln, [stn + "b"], [stn + "c"], bias=EPS)
                              act(SMALL[:, o + 11:o + 12], SMALL[:, o + 10:o + 11], AF.Exp, [stn + "c"], [stn + "d"], scale=-0.5)
                              ts(gv, gv, SMALL[:, o + 8:o + 9], SMALL[:, o + 11:o + 12], ALU.subtract, ALU.mult,
                                 [gvn, stn + "b", stn + "d"], [gvn])
                              vn = VN[t % 2]
                              tt(vn[:], gv, BC[:, 4:260], ALU.mult, [gvn, "BC"], [f"VN{t % 2}"])
                              pzn = "P2a" if t % 2 == 0 else "P2b"
                              pz = PSV[pzn]
                              for g in range(4):
                                  mm(pz[:, g * 64:(g + 1) * 64], WST[:, g, :], vn[:, g * 64:(g + 1) * 64], True, True,
                                     [f"VN{t % 2}", "SM"], [pzn])
                              guf, gufn = (Fs[2], "F2") if t % 2 == 0 else (Fs[3], "F3")
                              act(guf[:, 0:256], OB[:, t, :], AF.Copy, [("OB", t)], [gufn])
                              for g in range(4):
                                  stt(OB[:, t, g * 64:(g + 1) * 64], pz[:, g * 64:(g + 1) * 64], VEC[:, 22 + g:23 + g],
                                      guf[:, g * 64:(g + 1) * 64], ALU.add, ALU.mult, [pzn, "VEC", gufn], [("OB", t)])
                          elif pi == 2:
                              act(CY[:, t, :], pp, AF.Copy, [pn], [("CY", t)])
                          elif pi == 3:
                              cpv(VD[:, t, :, 0:64], pp.rearrange("p (h e) -> p h e", h=4), [pn, "VDones"], [("VD", t)])
                          else:
                              cpv(VA[:, t, :, 0:64], pp.rearrange("p (h e) -> p h e", h=2), [pn, "VAones"], [("VA", t)])
                  S.barrier()
              stage("p1")
              tap(f"qt{l}", QT[:, :, :], [128, 4, NTOK], BF16, [])
              tap(f"kt{l}", KT[:, :, :], [128, 3, NTOK], BF16, [])
              tap(f"ob{l}", OB[:, :, :], [128, 18, 256], BF16, [])
              tap(f"cy{l}", CY[:, :, :], [128, 18, 256], BF16, [])
              tap(f"vd{l}", VD[:, :, :, :], [128, 18, 4, 66], BF16, [])

              for half in range(2):
                  hgroups = [c for c in HALVES[half] if c in groups_all]
                  tiles_h = []
                  for c in hgroups:
                      s0, Lc = GROUPS[c]
                      tiles_h += [(t, c) for t in range(s0 // 128, (s0 + Lc) // 128)]
                  dma_pool(TAB[:, 0:2560], pm_d, [], ["TAB"])
                  PMv = TAB[:, 0:2560].rearrange("p (a t) -> p a t", a=20)
                  for (t, c) in tiles_h:
                      h0 = t * 128 - HSTART[half]
                      pv = PST[:, (t % 2) * 512:(t % 2) * 512 + 256]
                      pvn = "PST"
                      for cc in range(2):
                          tr(pv[:, cc * 128:(cc + 1) * 128], OB[:, t, cc * 128:(cc + 1) * 128], IDB, [("OB", t), "CBF"], [pvn])
                      cpv(HB[:, 2:4, h0:h0 + 128], pv.rearrange("p (c t) -> p c t", c=2), [pvn], [("HB", c)])
                  for c in hgroups:
                      s0, Lc = GROUPS[c]
                      h0, _ = hcols(c, half)
                      first_t = 16 if c == 4 else 0
                      last_t = 17 if c == 4 else 15
                      for t in range(s0 // 128, (s0 + Lc) // 128):
                          lo = (t * 128 - s0)
                          for g in range(4):
                              srcs = []
                              if t > first_t:
                                  srcs.append((t - 1, 3))
                              srcs.append((t, 1 if t == first_t else (2 if t == last_t else 0)))
                              if t < last_t:
                                  srcs.append((t + 1, 4))
                              o = (g % 2) * 64
                              pcn = "P0a" if g < 2 else "P0b"
                              outp = P0[o:o + 64, (g // 2) * 512 + lo:(g // 2) * 512 + lo + 128]
                              for i, (tj, v) in enumerate(srcs):
                                  mm(outp, CY[:, tj, g * 64:(g + 1) * 64], PMv[:, g * 5 + v, :], i == 0, i == len(srcs) - 1,
                                     [("CY", tj), "TAB"], [pcn])
                      for cc in range(2):
                          pcn = "P0a" if cc == 0 else "P0b"
                          pln = "P1a" if cc == 0 else "P1b"
                          act(PC[:, cc, 0:Lc], P0[:, cc * 512:cc * 512 + Lc], AF.Copy, [pcn], [("PC", cc)])
                          mm(P1[:, cc * 512:cc * 512 + Lc], WPBD[:, cc, :], PC[:, cc, 0:Lc], True, True, [("PC", cc), "SM"], [pln])
                          act(HB[:, 4 + cc, h0:h0 + Lc], P1[:, cc * 512:cc * 512 + Lc], AF.Identity, [pln, "VEC"], [("HB", c)],
                              scale=VEC[:, 20 + cc:21 + cc])
                  stage(f"p2a_{half}")
                  acount = [0]

                  def att_scores(job):
                      i = acount[0] % 2
                      acount[0] += 1
                      job["i"] = i
                      Sp = P0 if i == 0 else P1
                      Sn = ["P0a", "P0b"] if i == 0 else ["P1a", "P1b"]
                      chunks = job["chunks"]
                      n = len(chunks)
                      for ci, (kt, bias) in enumerate(chunks):
                          o = Sp[:, ci * 128:(ci + 1) * 128]
                          mm(o, job["K"](kt), QZ[:, job["slot"], :], True, bias is None, [("QZ", job["slot"])], Sn)
                          if bias is not None:
                              mm(o, IDB, bias, False, True, ["CBF", "TAB"], Sn)
                      act(PT[i][:, 0:n * 128], Sp[:, 0:n * 128], AF.Exp, Sn, [f"PT{i}"])

                  def att_pv(job):
                      i = job["i"]
                      chunks = job["chunks"]
                      n = len(chunks)
                      h = job["h"]
                      for ci, (kt, bias) in enumerate(chunks):
                          mm(job["O"][:, h * 66:h * 66 + 65], PT[i][:, ci * 128:(ci + 1) * 128], job["V"](kt), ci == 0, ci == n - 1,
                             [f"PT{i}"], [job["On"]])

                  def finish_tile(t, c, Otile, Oname, sink, cat0):
                      h0 = t * 128 - HSTART[half]
                      Ov = Otile[:, 0:264].rearrange("p (h e) -> p h e", h=4)
                      o = 32 + (t % 2) * 8
                      dn = f"DEN{t % 2}"
                      if sink:
                          tt(SMALL[:, o:o + 4], Ov[:, :, 64], GS[:, 4:8], ALU.add, [Oname, "GS"], [dn])
                      else:
                          cpv(SMALL[:, o:o + 4], Ov[:, :, 64], [Oname], [dn])
                      S.op("dve", lambda o=o: nc.vector.reciprocal(out=SMALL[:, o + 4:o + 8], in_=SMALL[:, o:o + 4]), [dn], [dn + "r"])
                      ot = OT[t % 2]
                      otn = f"OT{t % 2}"
                      tt(ot[:].rearrange("p (h e) -> p h e", h=4), Ov[:, :, 0:64],
                         SMALL[:, o + 4:o + 8].unsqueeze(2).to_broadcast([128, 4, 64]), ALU.mult, [Oname, dn + "r"], [otn])
                      pv = PST[:, (t % 2) * 512:(t % 2) * 512 + 256]
                      pvn = "PST"
                      for cc in range(2):
                          tr(pv[:, cc * 128:(cc + 1) * 128], ot[:, cc * 128:(cc + 1) * 128], IDB, [otn, "CBF"], [pvn])
                      cpv(HB[:, cat0:cat0 + 2, h0:h0 + 128], pv.rearrange("p (c t) -> p c t", c=2), [pvn], [("HB", c)])

                  def run_attn(jobs, sink, cat0):
                      S.op("dve", lambda: nc.vector.memset(QZ, 0.0), [], [("PC", 0), ("PC", 1)] + [("QZ", i) for i in range(8)])
                      prev = None
                      for job in jobs + [None]:
                          if job is not None:
                              po = job["po"]
                              cpv(QZ[po:po + 64, job["slot"], :], job["Q"], [("QZ", job["slot"])], [("QZ", job["slot"])])
                              att_scores(job)
                          if prev is not None:
                              att_pv(prev)
                              if prev["last"]:
                                  finish_tile(prev["t"], prev["c"], prev["O"], prev["On"], sink, cat0)
                                  advance_mod()
                          prev = job

                  def tcols(t):
                      return slice(t * 128, (t + 1) * 128)

                  jobs = []
                  for (t, c) in tiles_h:
                      Ot, On = (PSV["P2a"], "P2a") if t % 2 == 0 else (PSV["P2b"], "P2b")
                      for h in range(4):
                          kv, g = h // 2, h % 2
                          po = kv * 64
                          if c == 4:
                              chunks = [(16, None), (17, None)]
                          else:
                              chunks = []
                              if t > 0:
                                  chunks.append((t - 1, MASKP))
                              chunks.append((t, None))
                              if t < 15:
                                  chunks.append((t + 1, MASKN))
                              chunks += [(16, None), (17, None)]
                          jobs.append({"t": t, "c": c, "h": h, "K": (lambda kt: KT[:, 0, tcols(kt)]), "po": po,
                                       "slot": (t % 2) * 4 + h,
                                       "Q": QT[po:po + 64, g, tcols(t)], "V": (lambda kt, kv=kv: VA[:, kt, kv, 0:65]),
                                       "chunks": chunks, "O": Ot, "On": On, "last": h == 3})
                  run_attn(jobs, True, 0)
                  stage(f"attA_{half}")
                  dma_pool(TAB[:, 0:5376], Ld["dtab"], [], ["TAB"])
                  DT = TAB[:, 0:5376].rearrange("p (h s q) -> p h s q", h=4, s=21)
                  jobs = []
                  for (t, c) in tiles_h:
                      Ot, On = (PSV["P2a"], "P2a") if t % 2 == 0 else (PSV["P2b"], "P2b")
                      if c == 4:
                          deltas, interior = [], False
                      elif 2 <= t <= 13:
                          deltas, interior = [-2, -1, 0, 1, 2], True
                      elif t == 0:
                          deltas, interior = [0, 1, 2, 3], False
                      elif t == 1:
                          deltas, interior = [-1, 0, 1, 2], False
                      elif t == 14:
                          deltas, interior = [-2, -1, 0, 1], False
                      else:
                          deltas, interior = [-3, -2, -1, 0], False
                      for h in range(4):
                          po = (h % 2) * 64
                          ch = h // 2
                          chunks = []
                          for dl in deltas:
                              i0 = _dslot(interior, 2 * dl + 7)
                              i1 = _dslot(interior, 2 * dl + 6)
                              chunks.append((t + dl, DT[:, h, i0:i1 + 1:(i1 - i0), :]))
                          chunks += [(16, None), (17, None)]
                          jobs.append({"t": t, "c": c, "h": h, "K": (lambda kt, ch=ch: KT[:, 1 + ch, tcols(kt)]), "po": po,
                                       "slot": (t % 2) * 4 + h,
                                       "Q": QT[po:po + 64, 2 + ch, tcols(t)], "V": (lambda kt, h=h: VD[:, kt, h, 0:65]),
                                       "chunks": chunks, "O": Ot, "On": On, "last": h == 3})
                  run_attn(jobs, False, 6)
                  if half == 0:
                      tap(f"cat{l}", HB[:, :, 0:1024], [128, 8, 1024], BF16, [])
                  stage(f"attD_{half}")
                  wcount = 0
                  for a in range(4):
                      slot, sr = load_piece(l, NP_WO + a)
                      sv = slot[:].rearrange("p (k n) -> p k n", k=8)
                      for di in range(2):
                          dch = 2 * a + di
                          for c in hgroups:
                              s0, Lc = GROUPS[c]
                              h0, _ = hcols(c, half)
                              s = sidx(c)
                              pn = ["P0a", "P0b", "P1a", "P1b"][wcount % 4]
                              wcount += 1
                              pw = PSV[pn][:, 0:Lc]
                              for k in range(8):
                                  mm(pw, sv[:, k, di * 128:(di + 1) * 128], HB[:, k, h0:h0 + Lc], k == 0, k == 7,
                                     [sr, ("HB", c)], [pn])
                              stt(X[:, dch, s0:s0 + Lc], pw, G1[:, dch, s:s + 1], X[:, dch, s0:s0 + Lc], ALU.mult, ALU.add,
                                  [pn, MODn, xres(c)], [xres(c)])
                  stage(f"wout_{half}")
                  S.barrier()
              advance_mod(64)
              stage("wout")
              tap(f"x1_{l}", X[:], [128, 8, NTOK], F32, [])

              def h2dst(k, c):
                  s0, Lc = GROUPS[c]
                  if c < 2:
                      return HB[:, k, s0:s0 + Lc], ("HB", c)
                  return H2B[:, k, s0 - 1024:s0 - 1024 + Lc], ("H2B", c)

              for c in groups_all:
                  do_norm(c, A2, SH2, h2dst, "A2")
              stage("norm2")
              gcount = 0
              dcount = 0
              for bi, blk in enumerate(FFBLOCKS):
                  for fl, f in enumerate(blk):
                      slot, sr = load_piece(l, NP_GU + f)
                      sv = slot[:].rearrange("p (u k n) -> p u k n", u=2, k=8)
                      for c in groups_all:
                          s0, Lc = GROUPS[c]
                          i = gcount % 2
                          gcount += 1
                          pgn, pun = ("P0a", "P0b") if i == 0 else ("P1a", "P1b")
                          pg, pu = PSV[pgn][:, 0:Lc], PSV[pun][:, 0:Lc]
                          for k in range(8):
                              hsrc, hres = h2dst(k, c)
                              mm(pg, sv[:, 0, k, :], hsrc, k == 0, k == 7, [sr, hres], [pgn])
                          for k in range(8):
                              hsrc, hres = h2dst(k, c)
                              mm(pu, sv[:, 1, k, :], hsrc, k == 0, k == 7, [sr, hres], [pun])
                          sg, sgn = (Fs[0], "F0") if i == 0 else (Fs[1], "F1")
                          act(sg[:, 0:Lc], pg, AF.Silu, [pgn], [sgn])
                          tt(ACTB[:, fl, s0:s0 + Lc], pu, sg[:, 0:Lc], ALU.mult, [sgn, pun], [("ACTB", fl, c)])
                  nf = len(blk)
                  for dch in range(8):
                      slot, sr = load_piece(l, NP_WD + bi * 8 + dch, nf * 128)
                      sv = slot[:, 0:nf * 128].rearrange("p (f n) -> p f n", f=nf)
                      for c in groups_all:
                          s0, Lc = GROUPS[c]
                          s = sidx(c)
                          pn = ["P2a", "P2b", "P3"][dcount % 3]
                          dcount += 1
                          pd = PSV[pn][:, 0:Lc]
                          for fl in range(nf):
                              mm(pd, sv[:, fl, :], ACTB[:, fl, s0:s0 + Lc], fl == 0, fl == nf - 1, [sr, ("ACTB", fl, c)], [pn])
                          stt(X[:, dch, s0:s0 + Lc], pd, G2[:, dch, s:s + 1], X[:, dch, s0:s0 + Lc], ALU.mult, ALU.add,
                              [pn, MODn, xres(c)], [xres(c)])
              S.barrier()
              stage("ffn")
              tap(f"x2_{l}", X[:], [128, 8, NTOK], F32, [])


        try:
            run_layers()
        except _Stop:
            pass
        for t in range(16):
            Pt = P0 if t % 2 == 0 else P1
            pr = ["P0a", "P0b"] if t % 2 == 0 else ["P1a", "P1b"]
            for k in range(8):
                tr(Pt[:, k * 128:(k + 1) * 128], X[:, k, t * 128:(t + 1) * 128], IDF[:], [xres(t // 4), "IDF"], pr)
            stg = SCR[:, (t % 2) * 1024:(t % 2 + 1) * 1024]
            stg_r = ["F0", "F1"] if t % 2 == 0 else ["F2", "F3"]
            if t % 2 == 0:
                act(stg, Pt[:], AF.Copy, pr, stg_r)
            else:
                cpv(stg, Pt[:], pr, stg_r)
            dma_sp(out_d[t * 128:(t + 1) * 128, :], stg, stg_r, [("out", t)])
        S.emit(st)
    return nc, tap_d


_CACHE = {}


def prep_inputs(inp, n_layers=2):
    inp = {k: np.asarray(v, dtype=np.float32) for k, v in inp.items()}
    cbf, rope, pm = _consts()
    shared = {"cbf": np.ascontiguousarray(cbf.reshape(128, -1)), "idf": np.eye(128, dtype=np.float32),
              "rope": np.ascontiguousarray(rope.reshape(128, -1)), "pm": np.ascontiguousarray(pm.reshape(128, -1))}
    for l in range(n_layers):
        la = _layer_arrays(inp, l)
        for k, v in la.items():
            shared[f"{k}{l}"] = v
    maps = []
    for b in range(8):
        m = dict(shared)
        m["x"] = np.ascontiguousarray(inp["x"][b])
        m["ctx"] = np.ascontiguousarray(inp["ctx"][b])
        cv = np.zeros((128, 8, 2), np.float32)
        cv[:, :, 0] = inp["c"][b].reshape(8, 128).T
        cv[:, :, 1] = inp["c_ctx"].reshape(8, 128).T
        m["cvec"] = np.ascontiguousarray(cv.reshape(128, 16))
        maps.append(m)
    return maps


def kernel(**inputs):
    if "nc" not in _CACHE:
        _CACHE["nc"] = build(2)[0]
    nc = _CACHE["nc"]
    maps = prep_inputs(inputs, 2)
    res = run_bass_kernel_spmd(nc, maps, core_ids=list(range(8)))
    return np.stack([np.asarray(r["out"], dtype=np.float32) for r in res.results], 0)
```

```python
import contextlib
import numpy as np
import concourse.bass as bass
import concourse.mybir as mybir
from concourse.bass_utils import run_bass_kernel_spmd

F32 = mybir.dt.float32
BF16 = mybir.dt.bfloat16
AF = mybir.ActivationFunctionType
ALU = mybir.AluOpType

D = 1024
SEQ = 2048
CTX = 256
NTOK = SEQ + CTX
FF = 2816
NFF = 22
NEG = -30000.0
EPS = 1e-6
GROUPS = [(0, 512), (512, 512), (1024, 512), (1536, 512), (2048, 256)]
HALVES = [[0, 1], [2, 3, 4]]
HSTART = [0, 1024]
FFBLOCKS = [list(range(0, 8)), list(range(8, 15)), list(range(15, 22))]
PIECE = 2048
import os as _os
_DBG_DELAY = int(_os.environ.get('DBG_DELAY', '0'))
_DBG_SKIP = set(int(v) for v in _os.environ.get('DBG_SKIP_A', '').split(',') if v)


class _Op:
    __slots__ = ("eng", "fn", "dma", "deps", "has_dep", "sig")

    def __init__(self, eng, fn, dma):
        self.eng = eng
        self.fn = fn
        self.dma = dma
        self.deps = []
        self.has_dep = False
        self.sig = None


class Sched:
    COMPUTE = ("pe", "act", "dve")

    def __init__(self, nc, n_dma_sems=8):
        self.nc = nc
        self.ops = []
        self.res = {}
        self.n_dma_sems = n_dma_sems
        self.last = {}
        self.pending = {}
        self.dma_since = []

    def op(self, eng, fn, reads=(), writes=(), dma=False):
        o = _Op(eng, fn, dma)
        deps = {}
        for r in reads:
            st = self.res.get(r)
            if st is not None and st[0] is not None:
                deps[id(st[0])] = (st[0], True)
        for w in writes:
            st = self.res.get(w)
            if st is not None:
                if st[0] is not None and id(st[0]) not in deps:
                    deps[id(st[0])] = (st[0], False)
                for rd in st[1]:
                    if id(rd) not in deps:
                        deps[id(rd)] = (rd, False)
        for d in self.pending.pop(eng, ()):
            if id(d) not in deps:
                deps[id(d)] = (d, True)
        for d, raw in deps.values():
            if d is o:
                continue
            if not d.dma and not o.dma and d.eng == eng:
                if eng == "pe" or not raw:
                    continue
            o.deps.append(d)
            d.has_dep = True
        for r in reads:
            st = self.res.get(r)
            if st is None:
                self.res[r] = [None, [o]]
            else:
                st[1].append(o)
        for w in writes:
            self.res[w] = [o, []]
        self.ops.append(o)
        self.last[eng] = o
        if dma:
            self.dma_since.append(o)
        return o

    def barrier(self):
        lasts = [o for e, o in self.last.items() if e in self.COMPUTE]
        lasts += self.dma_since
        self.dma_since = []
        for e in self.COMPUTE:
            self.pending[e] = self.pending.get(e, []) + lasts

    def emit(self, stack):
        nc = self.nc
        eobj = {"pe": nc.tensor, "act": nc.scalar, "dve": nc.vector, "pool": nc.gpsimd, "sp": nc.sync}
        semh = {}
        cnt = {}
        waited = {e: {} for e in eobj}
        dma_rr = {}
        dcnt = {}

        def get_sem(key):
            if key not in semh:
                semh[key] = stack.enter_context(nc.semaphore("s_" + "_".join(str(k) for k in key)))
            return semh[key]

        for o in self.ops:
            E = eobj[o.eng]
            need = {}
            for d in o.deps:
                key, val = d.sig
                if need.get(key, 0) < val:
                    need[key] = val
            if o.dma:
                i = dma_rr.get(o.eng, 0)
                dma_rr[o.eng] = (i + 1) % self.n_dma_sems
                k = ("dma", o.eng, i)
                n = dcnt.get(k, 0) + 1
                dcnt[k] = n
                if n > 1 and need.get(k, 0) < 16 * (n - 1):
                    need[k] = 16 * (n - 1)
            for key, val in need.items():
                if waited[o.eng].get(key, 0) < val:
                    E.wait_ge(get_sem(key), val)
                    waited[o.eng][key] = val
            ins = o.fn()
            if o.dma:
                ins.then_inc(get_sem(k), 16)
                o.sig = (k, 16 * n)
            elif o.has_dep:
                key = ("e", o.eng)
                cnt[key] = cnt.get(key, 0) + 1
                ins.then_inc(get_sem(key), 1)
                o.sig = (key, cnt[key])
        for key, n in dcnt.items():
            nc.sync.wait_ge(get_sem(key), 16 * n)
        for key, n in cnt.items():
            nc.sync.wait_ge(get_sem(key), n)


def _kmajor(w):
    K, N = w.shape
    return np.ascontiguousarray(w.reshape(K // 128, 128, N).transpose(1, 0, 2))


def _pad_piece(a):
    a = np.ascontiguousarray(a, dtype=np.float32).reshape(128, -1)
    out = np.zeros((128, PIECE), np.float32)
    out[:, : a.shape[1]] = a
    return out


def _dslot(interior, s):
    if 3 <= s <= 9:
        return 8 + (9 - s)
    if s >= 10:
        return (0 if interior else 4) + (13 - s)
    return (18 if interior else 15) + (2 - s)


def _consts():
    c = np.zeros((128, 6, 128), np.float32)
    c[:, 0, :] = np.eye(128)
    c[:, 1, :] = 1.0
    c[0:64, 2, 0:64] = 1.0
    c[64:128, 2, 64:128] = 1.0
    for m in range(128):
        if (m % 32) < 16:
            c[m + 16, 3, m] = -1.0
        else:
            c[m - 16, 3, m] = 1.0
    j = np.arange(128)[:, None]
    i = np.arange(128)[None, :]
    c[:, 4, :] = np.where(j >= i, 0.0, NEG)
    c[:, 5, :] = np.where(j <= i, 0.0, NEG)
    p = np.arange(128)
    idx = p % 64
    f = (idx % 16).astype(np.float64)
    inv = np.power(10000.0, -f / 16.0)
    t = np.arange(SEQ)
    pos = np.where((idx < 32)[:, None], (t // 64)[None, :], (t % 64)[None, :]).astype(np.float64)
    ang = pos * inv[:, None]
    rope = np.stack([np.cos(ang), np.sin(ang)], axis=1).astype(np.float32)
    pm = np.zeros((128, 20, 128), np.float32)
    jj = np.arange(128)[:, None]
    tt = np.arange(128)[None, :]
    for g, w in enumerate((2, 4, 8, 16)):
        h = w // 2
        eye = (jj == tt).astype(np.float32)
        pm[:, g * 5 + 0, :] = ((jj >= tt - h) & (jj < tt + h)) / w - eye
        lo = np.maximum(tt - h, 0)
        cntf = (tt + h) - lo
        pm[:, g * 5 + 1, :] = ((jj >= lo) & (jj < tt + h)) / cntf - eye
        hi = np.minimum(tt + h, 128)
        cntl = hi - (tt - h)
        pm[:, g * 5 + 2, :] = ((jj >= tt - h) & (jj < hi)) / cntl - eye
        pm[:, g * 5 + 3, :] = (jj >= 128 + tt - h) / w
        pm[:, g * 5 + 4, :] = (jj < tt + h - 128) / w
    return c, rope, pm


def _dtab(rpb):
    kc = np.arange(64)[:, None]
    qc = np.arange(64)[None, :]
    cs = np.clip(qc - 8, 0, 48)
    col_ok = (kc >= cs) & (kc < cs + 16)
    coff = np.clip(kc - qc, -15, 15) + 15
    out = np.full((128, 4, 21, 64), NEG, np.float32)
    for h in range(4):
        for interior in (True, False):
            for s in range(14):
                sl = _dslot(interior, s)
                for kr in range(2):
                    rho = s + kr
                    ok = 0 <= rho <= 14 and ((3 <= rho <= 10) or not interior)
                    if not ok:
                        continue
                    T = np.where(col_ok, rpb[h, rho][coff], NEG)
                    out[kr * 64:(kr + 1) * 64, h, sl, :] = T
    return out


def _layer_arrays(inp, l):
    f32 = np.float32
    w_in = inp["w_in"][l]
    pieces = []
    wm = _kmajor(inp["w_mod"][l])
    for jj in range(24):
        pieces.append(_pad_piece(wm[:, :, jj * 256:(jj + 1) * 256]))
    aq = w_in[:, 0:256]
    fm_cols = [np.concatenate([aq[:, 0:64], aq[:, 128:192]], 1), np.concatenate([aq[:, 64:128], aq[:, 192:256]], 1),
               w_in[:, 256:384], w_in[:, 1280:1408], w_in[:, 1408:1536], w_in[:, 1536:1664], w_in[:, 1664:1792]]
    fm = _kmajor(np.concatenate(fm_cols, 1))
    for a in range(4):
        pieces.append(_pad_piece(fm[:, :, a * 256:min((a + 1) * 256, 896)]))
    tm = [w_in[:, 512:768], w_in[:, 768:1024], w_in[:, 1024:1280], w_in[:, 1792:2048], w_in[:, 384:512]]
    for a in tm:
        pieces.append(_pad_piece(_kmajor(a)))
    wo = _kmajor(inp["w_out"][l])
    for a in range(4):
        pieces.append(_pad_piece(wo[:, :, a * 256:(a + 1) * 256]))
    wg = _kmajor(inp["w_gate"][l])
    wu = _kmajor(inp["w_up"][l])
    for f in range(NFF):
        pieces.append(_pad_piece(np.concatenate([wg[:, :, f * 128:(f + 1) * 128].reshape(128, -1),
                                                 wu[:, :, f * 128:(f + 1) * 128].reshape(128, -1)], 1)))
    wd = _kmajor(inp["w_down"][l])
    for blk in FFBLOCKS:
        for dch in range(8):
            pieces.append(_pad_piece(wd[:, blk[0]:blk[-1] + 1, dch * 128:(dch + 1) * 128]))
    W = np.stack(pieces, 0)
    vec = np.zeros((128, 26), f32)
    vec[:, 0:8] = inp["g_mix"][l].reshape(8, 128).T
    vec[:, 8:16] = inp["g_ffn"][l].reshape(8, 128).T
    vec[:, 16] = np.tile(inp["a_q_gain"][l], 2)
    vec[:, 17] = np.tile(inp["a_k_gain"][l], 2)
    vec[:, 18] = np.tile(inp["d_q_gain"][l], 2)
    vec[:, 19] = np.tile(inp["d_k_gain"][l], 2)
    vec[:, 20:22] = inp["c_scale"][l].reshape(2, 128).T
    vec[:, 22:26] = inp["b_b_s"][l].T
    bmod = np.ascontiguousarray(inp["b_mod"][l].reshape(48, 128).T)
    bc = np.zeros((128, 260), f32)
    bc[:, 0:4] = inp["a_sink"][l][None, :]
    bc[:, 4:260] = inp["b_v_gain"][l][None, :]
    sm = np.zeros((128, 6, 128), f32)
    sm[:, 0:4, :] = inp["b_w_s"][l].transpose(2, 0, 1)
    wp = inp["c_w_pool"][l]
    for g in range(4):
        o = (g % 2) * 64
        sm[o:o + 64, 4 + g // 2, o:o + 64] = wp[g]
    dt = _dtab(inp["d_rpb"][l])
    return {"W": W, "vec": vec, "bmod": bmod, "bc": bc, "sm": np.ascontiguousarray(sm.reshape(128, -1)), "dtab": np.ascontiguousarray(dt.reshape(128, -1))}


NP_MOD, NP_FM, NP_TM, NP_WO, NP_GU, NP_WD = 0, 24, 28, 33, 37, 59
NPIECES = 83


class _Stop(Exception):
    pass


def build(n_layers=2, taps=(), stop=None):
    nc = bass.Bass("TRN2", target_bir_lowering=False)
    dr = {}

    def din(name, shape):
        dr[name] = nc.dram_tensor(name, list(shape), F32, kind="ExternalInput").ap()
        return dr[name]

    x_d = din("x", [SEQ, D])
    ctx_d = din("ctx", [CTX, D])
    cvec_d = din("cvec", [128, 16])
    cbf_d = din("cbf", [128, 6 * 128])
    idf_d = din("idf", [128, 128])
    rope_d = din("rope", [128, 2 * SEQ])
    pm_d = din("pm", [128, 20 * 128])
    L = []
    for l in range(n_layers):
        L.append({
            "W": din(f"W{l}", [NPIECES, 128, PIECE]), "vec": din(f"vec{l}", [128, 26]), "bmod": din(f"bmod{l}", [128, 48]),
            "bc": din(f"bc{l}", [128, 260]), "sm": din(f"sm{l}", [128, 6 * 128]), "dtab": din(f"dtab{l}", [128, 4 * 21 * 64]),
        })
    out_d = nc.dram_tensor("out", [SEQ, D], F32, kind="ExternalOutput").ap()
    tap_d = {}

    st = contextlib.ExitStack()
    with st:
        S = Sched(nc)
        sb = lambda n, s, d: st.enter_context(nc.sbuf_tensor(n, list(s), d))
        ps = lambda n, s, d: st.enter_context(nc.psum_tensor(n, list(s), d))
        X = sb("X", [128, 8, NTOK], F32)
        HB = sb("HB", [128, 8, 1280], BF16)
        RR = sb("RR", [128, 32472], BF16)
        QT = RR[:, 0:9216].rearrange("p (k t) -> p k t", k=4)
        KT = RR[:, 9216:16128].rearrange("p (k t) -> p k t", k=3)
        VA = RR[:, 16128:18504].rearrange("p (t h e) -> p t h e", t=18, h=2)
        VD = RR[:, 18504:23256].rearrange("p (t h e) -> p t h e", t=18, h=4)
        OB = RR[:, 23256:27864].rearrange("p (t c) -> p t c", t=18)
        CY = RR[:, 27864:32472].rearrange("p (t c) -> p t c", t=18)
        ACTB = RR[:, 0:18432].rearrange("p (f t) -> p f t", f=8)
        H2B = RR[:, 18432:28672].rearrange("p (k t) -> p k t", k=8)
        RING = [sb(f"ring{i}", [128, PIECE], BF16) for i in range(4)]
        TAB = sb("TAB", [128, 5376], BF16)
        SCR = sb("SCR", [128, 2048], F32)
        Fs = [SCR[:, i * 512:(i + 1) * 512] for i in range(4)]
        SQ = [sb(f"SQ{i}", [128, 512], BF16) for i in range(2)]
        QG = sb("QG", [128, 512], BF16)
        RQ = sb("RQ", [128, 512], BF16)
        PT = [sb(f"PT{i}", [128, 896], BF16) for i in range(2)]
        OT = [sb(f"OT{i}", [128, 256], BF16) for i in range(2)]
        PC = sb("PC", [128, 2, 512], BF16)
        QZ = PC[:].rearrange("p a (b q) -> p (a b) q", q=128)
        VN = [sb(f"VN{i}", [128, 256], BF16) for i in range(2)]
        CBF = sb("CBF", [128, 6, 128], BF16)
        IDF = sb("IDF", [128, 128], F32)
        CV = sb("CV", [128, 8, 2], F32)
        CA = sb("CA", [128, 8, 2], BF16)
        SMALL = sb("SMALL", [128, 64], F32)
        IDB = CBF[:, 0, :]
        ONES = CBF[:, 1, :]
        BDONES = CBF[:, 2, :]
        ROTT = CBF[:, 3, :]
        MASKP = CBF[:, 4, :]
        MASKN = CBF[:, 5, :]
        P0 = ps("P0", [128, 1024], F32)
        P1 = ps("P1", [128, 1024], F32)
        P2 = ps("P2", [128, 1024], F32)
        P3 = ps("P3", [128, 512], F32)
        PST = ps("PST", [128, 1024], BF16)
        PSV = {"P0a": P0[:, 0:512], "P0b": P0[:, 512:1024], "P1a": P1[:, 0:512], "P1b": P1[:, 512:1024],
               "P2a": P2[:, 0:512], "P2b": P2[:, 512:1024], "P3": P3[:, :]}

        def mm(out, lhsT, rhs, start, stop, reads, writes):
            S.op("pe", lambda: nc.tensor.matmul(out, lhsT=lhsT, rhs=rhs, start=start, stop=stop), reads, writes)

        def tr(out, in_, ident, reads, writes):
            S.op("pe", lambda: nc.tensor.transpose(out, in_, ident), reads, writes)

        def act(out, in_, func, reads, writes, scale=None, bias=None):
            kw = {}
            if scale is not None:
                kw["scale"] = scale
            if bias is not None:
                kw["bias"] = bias
            S.op("act", lambda: nc.scalar.activation(out=out, in_=in_, func=func, **kw), reads, writes)

        def tt(out, in0, in1, op, reads, writes):
            S.op("dve", lambda: nc.vector.tensor_tensor(out=out, in0=in0, in1=in1, op=op), reads, writes)

        def stt(out, in0, scalar, in1, op0, op1, reads, writes):
            S.op("dve", lambda: nc.vector.scalar_tensor_tensor(out=out, in0=in0, scalar=scalar, in1=in1, op0=op0, op1=op1),
                 reads, writes)

        def ts(out, in0, s1, s2, op0, op1, reads, writes):
            S.op("dve", lambda: nc.vector.tensor_scalar(out=out, in0=in0, scalar1=s1, scalar2=s2, op0=op0, op1=op1),
                 reads, writes)

        def cpv(out, in_, reads, writes):
            S.op("dve", lambda: nc.vector.tensor_copy(out=out, in_=in_), reads, writes)

        def dma_sp(out, in_, reads, writes):
            S.op("sp", lambda: nc.sync.dma_start(out=out, in_=in_), reads, writes, dma=True)

        def dma_pool(out, in_, reads, writes):
            S.op("pool", lambda: nc.gpsimd.dma_start(out=out, in_=in_, max_dma_last_dim=4096), reads, writes, dma=True)

        def stage(name):
            if stop == name:
                S.barrier()
                raise _Stop()

        def tap(name, ap, shape, dtype, reads):
            if name not in taps:
                return
            S.barrier()
            t = nc.dram_tensor("tap_" + name, list(shape), dtype, kind="ExternalOutput").ap()
            tap_d[name] = t
            dma_sp(t, ap, list(reads) + ["tapsrc"], ["tap_" + name])

        ring_i = [0]

        def load_piece(l, idx, nelem=PIECE):
            i = ring_i[0]
            ring_i[0] = (i + 1) % 4
            dma_pool(RING[i][:, 0:nelem], L[l]["W"][idx, :, 0:nelem], [], [("slot", i)])
            return RING[i], ("slot", i)

        dma_sp(IDF[:], idf_d, [], ["IDF"])
        dma_sp(CV[:].rearrange("p k s -> p (k s)"), cvec_d, [], ["CV"])
        dma_pool(CBF[:].rearrange("p a b -> p (a b)"), cbf_d, [], ["CBF"])
        act(CA[:], CV[:], AF.Silu, ["CV"], ["CA"])
        for t in range(18):
            stg = SCR[:, (t % 2) * 1024:(t % 2 + 1) * 1024]
            stg_r = ["F0", "F1"] if t % 2 == 0 else ["F2", "F3"]
            src = x_d[t * 128:(t + 1) * 128, :] if t < 16 else ctx_d[(t - 16) * 128:(t - 15) * 128, :]
            dma_sp(stg, src, [], stg_r)
            Pt = P0 if t % 2 == 0 else P1
            pr = ["P0a", "P0b"] if t % 2 == 0 else ["P1a", "P1b"]
            for k in range(8):
                tr(Pt[:, k * 128:(k + 1) * 128], stg[:, k * 128:(k + 1) * 128], IDF[:], stg_r + ["IDF"], pr)
            dst = X[:, :, t * 128:(t + 1) * 128]
            src_ps = Pt[:].rearrange("p (k t) -> p k t", k=8)
            xr = [("X", t // 4 if t < 16 else 4)]
            if t % 2 == 0:
                act(dst, src_ps, AF.Copy, pr, xr)
            else:
                cpv(dst, src_ps, pr, xr)
        S.barrier()

        def xres(c):
            return ("X", c)

        def sidx(c):
            return 1 if c == 4 else 0

        VEC = sb("VEC", [128, 26], F32)
        BMODs = [sb(f"BMOD{i}", [128, 48], F32) for i in range(2)]
        BC = sb("BC", [128, 260], F32)
        SM = sb("SM", [128, 6, 128], BF16)
        MODs = [sb(f"MOD{i}", [128, 48, 2], F32) for i in range(2)]
        A1 = sb("A1", [128, 8, 2], F32)
        A2 = sb("A2", [128, 8, 2], F32)
        GS = sb("GS", [128, 8], F32)

        def mod_gen(l):
            b = l % 2
            dma_sp(BMODs[b][:], L[l]["bmod"], [], [f"BMOD{b}"])
            for jj in range(24):
                slot, sr = load_piece(l, NP_MOD + jj)
                sv = slot[:].rearrange("p (k n) -> p k n", k=8)
                for cc in range(2):
                    j = 2 * jj + cc
                    for k in range(8):
                        mm(P3[:, 2 * j:2 * j + 2], sv[:, k, cc * 128:(cc + 1) * 128], CA[:, k, :], k == 0, k == 7,
                           [sr, "CA"], ["P3"])
                yield
            tt(MODs[b][:], P3[:, 0:96].rearrange("p (j s) -> p j s", s=2),
               BMODs[b][:].unsqueeze(2).to_broadcast([128, 48, 2]), ALU.add, ["P3", f"BMOD{b}"], [f"MOD{b}"])
            yield

        def run_layers():
          for l in range(n_layers):
              last = l == n_layers - 1
              Ld = L[l]
              dma_sp(VEC[:], Ld["vec"], [], ["VEC"])
              dma_sp(BC[:], Ld["bc"], [], ["BC"])
              dma_pool(SM[:].rearrange("p a b -> p (a b)"), Ld["sm"], [], ["SM"])
              WST = SM[:, 0:4, :]
              WPBD = SM[:, 4:6, :]

              MOD = MODs[l % 2]
              MODn = f"MOD{l % 2}"
              if l == 0:
                  for _ in mod_gen(0):
                      pass
              pre = [mod_gen(l + 1) if not last else None]

              def advance_mod(n=1):
                  for _ in range(n):
                      if pre[0] is not None:
                          try:
                              next(pre[0])
                          except StopIteration:
                              pre[0] = None
              stt(A1[:], MOD[:, 8:16, :], 1.0, VEC[:, 0:8].unsqueeze(2).to_broadcast([128, 8, 2]), ALU.add, ALU.mult,
                  [MODn, "VEC"], ["A1"])
              stt(A2[:], MOD[:, 32:40, :], 1.0, VEC[:, 8:16].unsqueeze(2).to_broadcast([128, 8, 2]), ALU.add, ALU.mult,
                  [MODn, "VEC"], ["A2"])
              ts(GS[:, 0:1], VEC[:, 16:17], 0.125, None, ALU.mult, ALU.bypass, ["VEC"], ["GS"])
              ts(GS[:, 1:2], VEC[:, 17:18], 1.0, None, ALU.mult, ALU.bypass, ["VEC"], ["GS"])
              ts(GS[:, 2:3], VEC[:, 18:19], 0.125, None, ALU.mult, ALU.bypass, ["VEC"], ["GS"])
              ts(GS[:, 3:4], VEC[:, 19:20], 1.0, None, ALU.mult, ALU.bypass, ["VEC"], ["GS"])
              act(GS[:, 4:8], BC[:, 0:4], AF.Exp, ["BC"], ["GS"])
              SH1, G1, SH2, G2 = MOD[:, 0:8, :], MOD[:, 16:24, :], MOD[:, 24:32, :], MOD[:, 40:48, :]
              stage("modload")
              tap(f"mod{l}", MOD[:], [128, 48, 2], F32, [MODn])

              stage("mod")
              def do_norm(c, Amul, SHv, dst_fn, tag):
                  s0, Lc = GROUPS[c]
                  s = sidx(c)
                  psn, psr = (PSV["P2a"], "P2a") if c % 2 == 0 else (PSV["P2b"], "P2b")
                  for k in range(8):
                      sq = SQ[k % 2]
                      act(sq[:, 0:Lc], X[:, k, s0:s0 + Lc], AF.Square, [xres(c)], [f"SQ{k % 2}"])
                      mm(psn[:, 0:Lc], ONES, sq[:, 0:Lc], k == 0, k == 7, [f"SQ{k % 2}", "CBF"], [psr])
                  act(Fs[0][:, 0:Lc], psn[:, 0:Lc], AF.Ln, [psr], ["F0"], scale=1.0 / D, bias=EPS)
                  act(Fs[1][:, 0:Lc], Fs[0][:, 0:Lc], AF.Exp, ["F0"], ["F1"], scale=-0.5)
                  for k in range(8):
                      tmp, tr_ = (Fs[2], "F2") if k % 2 == 0 else (Fs[3], "F3")
                      tt(tmp[:, 0:Lc], X[:, k, s0:s0 + Lc], Fs[1][:, 0:Lc], ALU.mult, [xres(c), "F1"], [tr_])
                      dst, dres = dst_fn(k, c)
                      act(dst, tmp[:, 0:Lc], AF.Identity, [tr_, MODn, tag], [dres], scale=Amul[:, k, s:s + 1],
                          bias=SHv[:, k, s:s + 1])

              groups_all = [0, 1, 2, 3] if last else [0, 1, 2, 3, 4]

              dma_pool(TAB[:, 0:4096], rope_d, [], ["TAB"])
              COS = TAB[:, 0:2048]
              SIN = TAB[:, 2048:4096]
              S.op("dve", lambda: nc.vector.memset(VA[:, :, :, 64:65], 1.0), [], ["VAones"])
              S.op("dve", lambda: nc.vector.memset(VD[:, :, :, 64:65], 1.0), [], ["VDones"])

              def hcols(c, half):
                  s0, Lc = GROUPS[c]
                  return s0 - HSTART[half], Lc

              for half in range(2):
                  hgroups = HALVES[half]
                  for c in hgroups:
                      h0, Lc = hcols(c, half)
                      do_norm(c, A1, SH1, lambda k, c, h0=h0, Lc=Lc: (HB[:, k, h0:h0 + Lc], ("HB", c)), "A1")
                  if half == 0:
                      tap(f"hT{l}", HB[:, :, 0:1024], [128, 8, 1024], BF16, [("HB", 0), ("HB", 1)])
                  stage(f"norm1_{half}")
                  fcount = 0
                  for a in range(4):
                      js = [2 * a, 2 * a + 1] if a < 3 else [6]
                      slot, sr = load_piece(l, NP_FM + a, 8 * 128 * len(js))
                      sv = slot[:, 0:8 * 128 * len(js)].rearrange("p (k n) -> p k n", k=8)
                      for ji, j in enumerate(js):
                          for c in hgroups:
                              if last and c == 4 and j in (0, 1, 3, 4):
                                  continue
                              s0, Lc = GROUPS[c]
                              h0, _ = hcols(c, half)
                              pfn = ["P0a", "P0b", "P1a"][fcount % 3]
                              phn = ["P1b", "P2a"][fcount % 2]
                              prn = ["P2b", "P3"][fcount % 2]
                              fcount += 1
                              pf, ph, pr_ = PSV[pfn][:, 0:Lc], PSV[phn][:, 0:Lc], PSV[prn][:, 0:Lc]
                              for k in range(8):
                                  mm(pf, sv[:, k, ji * 128:(ji + 1) * 128], HB[:, k, h0:h0 + Lc], k == 0, k == 7,
                                     [sr, ("HB", c)], [pfn])
                              sq = SQ[fcount % 2]
                              sqn = f"SQ{fcount % 2}"
                              act(sq[:, 0:Lc], pf, AF.Square, [pfn], [sqn])
                              mm(ph, BDONES, sq[:, 0:Lc], True, True, [sqn, "CBF"], [phn])
                              act(Fs[0][:, 0:Lc], ph, AF.Ln, [phn], ["F0"], scale=1.0 / 64, bias=EPS)
                              act(Fs[1][:, 0:Lc], Fs[0][:, 0:Lc], AF.Exp, ["F0"], ["F1"], scale=-0.5)
                              gcol = {0: 0, 1: 0, 2: 1, 3: 2, 4: 2, 5: 3, 6: 3}[j]
                              gain = GS[:, gcol:gcol + 1]
                              if j < 2:
                                  dst, dres = QT[:, j, s0:s0 + Lc], ("QT", j, c)
                              elif j == 2:
                                  dst, dres = KT[:, 0, s0:s0 + Lc], ("KT", 0, c)
                              elif j < 5:
                                  dst, dres = QT[:, j - 1, s0:s0 + Lc], ("QT", j - 1, c)
                              else:
                                  dst, dres = KT[:, j - 4, s0:s0 + Lc], ("KT", j - 4, c)
                              if j < 3 and c != 4:
                                  act(QG[:, 0:Lc], pf, AF.Identity, [pfn, "GS"], ["QG"], scale=gain)
                                  mm(pr_, ROTT, QG[:, 0:Lc], True, True, ["QG", "CBF"], [prn])
                                  act(RQ[:, 0:Lc], pr_, AF.Copy, [prn], ["RQ"])
                                  tt(Fs[2][:, 0:Lc], QG[:, 0:Lc], COS[:, s0:s0 + Lc], ALU.mult, ["QG", "TAB"], ["F2"])
                                  tt(Fs[3][:, 0:Lc], RQ[:, 0:Lc], SIN[:, s0:s0 + Lc], ALU.mult, ["RQ", "TAB"], ["F3"])
                                  tt(Fs[2][:, 0:Lc], Fs[2][:, 0:Lc], Fs[3][:, 0:Lc], ALU.add, ["F2", "F3"], ["F2"])
                                  tt(dst, Fs[2][:, 0:Lc], Fs[1][:, 0:Lc], ALU.mult, ["F2", "F1"], [dres])
                              else:
                                  stt(dst, pf, gain, Fs[1][:, 0:Lc], ALU.mult, ALU.mult, [pfn, "GS", "F1"], [dres])
                  stage(f"fm_{half}")
                  tiles_h = []
                  for c in hgroups:
                      s0, Lc = GROUPS[c]
                      tiles_h += [(t, c) for t in range(s0 // 128, (s0 + Lc) // 128)]
                  tcount = [0]

                  def tm_proj(t, c, sv, sr, ncol):
                      h0 = t * 128 - HSTART[half]
                      pn = ["P0a", "P0b", "P1a", "P1b"][tcount[0] % 4]
                      tcount[0] += 1
                      pp = PSV[pn][:, 0:ncol]
                      for k in range(8):
                          mm(pp, HB[:, k, h0:h0 + 128], sv[:, k, :], k == 0, k == 7, [sr, ("HB", c)], [pn])
                      return pp, pn

                  def tm_piece(pi):
                      ncol = 128 if pi == 4 else 256
                      slot, sr = load_piece(l, NP_TM + pi, 8 * ncol)
                      return slot[:, 0:8 * ncol].rearrange("p (k n) -> p k n", k=8), sr, ncol

                  svU, srU, _ = tm_piece(0)
                  for (t, c) in tiles_h:
                      if last and c == 4:
                          continue
                      pp, pn = tm_proj(t, c, svU, srU, 256)
                      act(OB[:, t, :], pp, AF.Gelu_apprx_tanh, [pn], [("OB", t)])
                  svV, srV, _ = tm_piece(1)
                  svC, srC, _ = tm_piece(2)
                  svD, srD, _ = tm_piece(3)
                  svA, srA, _ = tm_piece(4)

                  def emit_z(t):
                      vn = VN[t % 2]
                      pzn = "P2a" if t % 2 == 0 else "P2b"
                      pz = PSV[pzn]
                      for g in range(4):
                          mm(pz[:, g * 64:(g + 1) * 64], WST[:, g, :], vn[:, g * 64:(g + 1) * 64], True, True,
                             [f"VN{t % 2}", "SM"], [pzn])
                      guf, gufn = (Fs[2], "F2") if t % 2 == 0 else (Fs[3], "F3")
                      act(guf[:, 0:256], OB[:, t, :], AF.Copy, [("OB", t)], [gufn])
                      for g in range(4):
                          stt(OB[:, t, g * 64:(g + 1) * 64], pz[:, g * 64:(g + 1) * 64], VEC[:, 22 + g:23 + g],
                              guf[:, g * 64:(g + 1) * 64], ALU.add, ALU.mult, [pzn, "VEC", gufn], [("OB", t)])

                  pending_z = None
                  for (t, c) in tiles_h:
                      full = not (last and c == 4)
                      if full:
                          pp, pn = tm_proj(t, c, svV, srV, 256)
                          gv, gvn = (Fs[0], "F0") if t % 2 == 0 else (Fs[1], "F1")
                          gv = gv[:, 0:256]
                          act(gv, pp, AF.Gelu_apprx_tanh, [pn], [gvn])
                          o = (t % 2) * 16
                          stn = f"SMALL{t % 2}"
                          S.op("dve", lambda gv=gv, o=o: nc.vector.bn_stats(out=SMALL[:, o:o + 6], in_=gv), [gvn], [stn])
                          S.op("dve", lambda o=o: nc.vector.bn_aggr(out=SMALL[:, o + 8:o + 10], in_=SMALL[:, o:o + 6]),
                               [stn], [stn + "b"])
                          act(SMALL[:, o + 10:o + 11], SMALL[:, o + 9:o + 10], AF.Ln, [stn + "b"], [stn + "c"], bias=EPS)
                          act(SMALL[:, o + 11:o + 12], SMALL[:, o + 10:o + 11], AF.Exp, [stn + "c"], [stn + "d"], scale=-0.5)
                          ts(gv, gv, SMALL[:, o + 8:o + 9], SMALL[:, o + 11:o + 12], ALU.subtract, ALU.mult,
                             [gvn, stn + "b", stn + "d"], [gvn])
                          tt(VN[t % 2][:], gv, BC[:, 4:260], ALU.mult, [gvn, "BC"], [f"VN{t % 2}"])
                          pp, pn = tm_proj(t, c, svC, srC, 256)
                          act(CY[:, t, :], pp, AF.Copy, [pn], [("CY", t)])
                      if pending_z is not None:
                          emit_z(pending_z)
                          pending_z = None
                      pp, pn = tm_proj(t, c, svD, srD, 256)
                      cpv(VD[:, t, :, 0:64], pp.rearrange("p (h e) -> p h e", h=4), [pn, "VDones"], [("VD", t)])
                      pp, pn = tm_proj(t, c, svA, srA, 128)
                      cpv(VA[:, t, :, 0:64], pp.rearrange("p (h e) -> p h e", h=2), [pn, "VAones"], [("VA", t)])
                      if full:
                          pending_z = t
                  if pending_z is not None:
                      emit_z(pending_z)
                  S.barrier()
              stage("p1")
              tap(f"qt{l}", QT[:, :, :], [128, 4, NTOK], BF16, [])
              tap(f"kt{l}", KT[:, :, :], [128, 3, NTOK], BF16, [])
              tap(f"ob{l}", OB[:, :, :], [128, 18, 256], BF16, [])
              tap(f"cy{l}", CY[:, :, :], [128, 18, 256], BF16, [])
              tap(f"vd{l}", VD[:, :, :, :], [128, 18, 4, 66], BF16, [])

              for half in range(2):
                  hgroups = [c for c in HALVES[half] if c in groups_all]
                  tiles_h = []
                  for c in hgroups:
                      s0, Lc = GROUPS[c]
                      tiles_h += [(t, c) for t in range(s0 // 128, (s0 + Lc) // 128)]
                  dma_pool(TAB[:, 0:2560], pm_d, [], ["TAB"])
                  PMv = TAB[:, 0:2560].rearrange("p (a t) -> p a t", a=20)
                  for (t, c) in tiles_h:
                      h0 = t * 128 - HSTART[half]
                      pv = PST[:, (t % 2) * 512:(t % 2) * 512 + 256]
                      pvn = "PST"
                      for cc in range(2):
                          tr(pv[:, cc * 128:(cc + 1) * 128], OB[:, t, cc * 128:(cc + 1) * 128], IDB, [("OB", t), "CBF"], [pvn])
                      cpv(HB[:, 2:4, h0:h0 + 128], pv.rearrange("p (c t) -> p c t", c=2), [pvn], [("HB", c)])
                  for c in hgroups:
                      s0, Lc = GROUPS[c]
                      h0, _ = hcols(c, half)
                      first_t = 16 if c == 4 else 0
                      last_t = 17 if c == 4 else 15
                      for t in range(s0 // 128, (s0 + Lc) // 128):
                          lo = (t * 128 - s0)
                          for g in range(4):
                              srcs = []
                              if t > first_t:
                                  srcs.append((t - 1, 3))
                              srcs.append((t, 1 if t == first_t else (2 if t == last_t else 0)))
                              if t < last_t:
                                  srcs.append((t + 1, 4))
                              o = (g % 2) * 64
                              pcn = "P0a" if g < 2 else "P0b"
                              outp = P0[o:o + 64, (g // 2) * 512 + lo:(g // 2) * 512 + lo + 128]
                              for i, (tj, v) in enumerate(srcs):
                                  mm(outp, CY[:, tj, g * 64:(g + 1) * 64], PMv[:, g * 5 + v, :], i == 0, i == len(srcs) - 1,
                                     [("CY", tj), "TAB"], [pcn])
                      for cc in range(2):
                          pcn = "P0a" if cc == 0 else "P0b"
                          pln = "P1a" if cc == 0 else "P1b"
                          act(PC[:, cc, 0:Lc], P0[:, cc * 512:cc * 512 + Lc], AF.Copy, [pcn], [("PC", cc)])
                          mm(P1[:, cc * 512:cc * 512 + Lc], WPBD[:, cc, :], PC[:, cc, 0:Lc], True, True, [("PC", cc), "SM"], [pln])
                          act(HB[:, 4 + cc, h0:h0 + Lc], P1[:, cc * 512:cc * 512 + Lc], AF.Identity, [pln, "VEC"], [("HB", c)],
                              scale=VEC[:, 20 + cc:21 + cc])
                  stage(f"p2a_{half}")
                  acount = [0]

                  def att_scores(job):
                      i = acount[0] % 2
                      acount[0] += 1
                      job["i"] = i
                      Sp = P0 if i == 0 else P1
                      Sn = ["P0a", "P0b"] if i == 0 else ["P1a", "P1b"]
                      chunks = job["chunks"]
                      n = len(chunks)
                      for ci, (kt, bias) in enumerate(chunks):
                          o = Sp[:, ci * 128:(ci + 1) * 128]
                          mm(o, job["K"](kt), QZ[:, job["slot"], :], True, bias is None, [("QZ", job["slot"])], Sn)
                          if bias is not None:
                              mm(o, IDB, bias, False, True, ["CBF", "TAB"], Sn)
                      act(PT[i][:, 0:n * 128], Sp[:, 0:n * 128], AF.Exp, Sn, [f"PT{i}"])

                  def att_pv(job):
                      i = job["i"]
                      chunks = job["chunks"]
                      n = len(chunks)
                      h = job["h"]
                      for ci, (kt, bias) in enumerate(chunks):
                          mm(job["O"][:, h * 66:h * 66 + 65], PT[i][:, ci * 128:(ci + 1) * 128], job["V"](kt), ci == 0, ci == n - 1,
                             [f"PT{i}"], [job["On"]])

                  def finish_tile(t, c, Otile, Oname, sink, cat0):
                      h0 = t * 128 - HSTART[half]
                      Ov = Otile[:, 0:264].rearrange("p (h e) -> p h e", h=4)
                      o = 32 + (t % 2) * 8
                      dn = f"DEN{t % 2}"
                      if sink:
                          tt(SMALL[:, o:o + 4], Ov[:, :, 64], GS[:, 4:8], ALU.add, [Oname, "GS"], [dn])
                      else:
                          cpv(SMALL[:, o:o + 4], Ov[:, :, 64], [Oname], [dn])
                      S.op("dve", lambda o=o: nc.vector.reciprocal(out=SMALL[:, o + 4:o + 8], in_=SMALL[:, o:o + 4]), [dn], [dn + "r"])
                      ot = OT[t % 2]
                      otn = f"OT{t % 2}"
                      tt(ot[:].rearrange("p (h e) -> p h e", h=4), Ov[:, :, 0:64],
                         SMALL[:, o + 4:o + 8].unsqueeze(2).to_broadcast([128, 4, 64]), ALU.mult, [Oname, dn + "r"], [otn])
                      pv = PST[:, (t % 2) * 512:(t % 2) * 512 + 256]
                      pvn = "PST"
                      for cc in range(2):
                          tr(pv[:, cc * 128:(cc + 1) * 128], ot[:, cc * 128:(cc + 1) * 128], IDB, [otn, "CBF"], [pvn])
                      cpv(HB[:, cat0:cat0 + 2, h0:h0 + 128], pv.rearrange("p (c t) -> p c t", c=2), [pvn], [("HB", c)])

                  def run_attn(jobs, sink, cat0):
                      S.op("dve", lambda: nc.vector.memset(QZ, 0.0), [], [("PC", 0), ("PC", 1)] + [("QZ", i) for i in range(8)])
                      prev = None
                      for job in jobs + [None]:
                          if job is not None:
                              po = job["po"]
                              cpv(QZ[po:po + 64, job["slot"], :], job["Q"], [("QZ", job["slot"])], [("QZ", job["slot"])])
                              att_scores(job)
                          if prev is not None:
                              att_pv(prev)
                              if prev["last"]:
                                  finish_tile(prev["t"], prev["c"], prev["O"], prev["On"], sink, cat0)
                                  advance_mod()
                          prev = job

                  def tcols(t):
                      return slice(t * 128, (t + 1) * 128)

                  jobs = []
                  for (t, c) in tiles_h:
                      Ot, On = (PSV["P2a"], "P2a") if t % 2 == 0 else (PSV["P2b"], "P2b")
                      for h in range(4):
                          kv, g = h // 2, h % 2
                          po = kv * 64
                          if c == 4:
                              chunks = [(16, None), (17, None)]
                          else:
                              chunks = []
                              if t > 0:
                                  chunks.append((t - 1, MASKP))
                              chunks.append((t, None))
                              if t < 15:
                                  chunks.append((t + 1, MASKN))
                              chunks += [(16, None), (17, None)]
                          jobs.append({"t": t, "c": c, "h": h, "K": (lambda kt: KT[:, 0, tcols(kt)]), "po": po,
                                       "slot": (t % 2) * 4 + h,
                                       "Q": QT[po:po + 64, g, tcols(t)], "V": (lambda kt, kv=kv: VA[:, kt, kv, 0:65]),
                                       "chunks": chunks, "O": Ot, "On": On, "last": h == 3})
                  run_attn(jobs, True, 0)
                  stage(f"attA_{half}")
                  dma_pool(TAB[:, 0:5376], Ld["dtab"], [], ["TAB"])
                  DT = TAB[:, 0:5376].rearrange("p (h s q) -> p h s q", h=4, s=21)
                  jobs = []
                  for (t, c) in tiles_h:
                      Ot, On = (PSV["P2a"], "P2a") if t % 2 == 0 else (PSV["P2b"], "P2b")
                      if c == 4:
                          deltas, interior = [], False
                      elif 2 <= t <= 13:
                          deltas, interior = [-2, -1, 0, 1, 2], True
                      elif t == 0:
                          deltas, interior = [0, 1, 2, 3], False
                      elif t == 1:
                          deltas, interior = [-1, 0, 1, 2], False
                      elif t == 14:
                          deltas, interior = [-2, -1, 0, 1], False
                      else:
                          deltas, interior = [-3, -2, -1, 0], False
                      for h in range(4):
                          po = (h % 2) * 64
                          ch = h // 2
                          chunks = []
                          for dl in deltas:
                              i0 = _dslot(interior, 2 * dl + 7)
                              i1 = _dslot(interior, 2 * dl + 6)
                              chunks.append((t + dl, DT[:, h, i0:i1 + 1:(i1 - i0), :]))
                          chunks += [(16, None), (17, None)]
                          jobs.append({"t": t, "c": c, "h": h, "K": (lambda kt, ch=ch: KT[:, 1 + ch, tcols(kt)]), "po": po,
                                       "slot": (t % 2) * 4 + h,
                                       "Q": QT[po:po + 64, 2 + ch, tcols(t)], "V": (lambda kt, h=h: VD[:, kt, h, 0:65]),
                                       "chunks": chunks, "O": Ot, "On": On, "last": h == 3})
                  run_attn(jobs, False, 6)
                  if half == 0:
                      tap(f"cat{l}", HB[:, :, 0:1024], [128, 8, 1024], BF16, [])
                  stage(f"attD_{half}")
                  wcount = 0
                  for a in range(4):
                      slot, sr = load_piece(l, NP_WO + a)
                      sv = slot[:].rearrange("p (k n) -> p k n", k=8)
                      for di in range(2):
                          dch = 2 * a + di
                          for c in hgroups:
                              s0, Lc = GROUPS[c]
                              h0, _ = hcols(c, half)
                              s = sidx(c)
                              pn = ["P0a", "P0b", "P1a", "P1b"][wcount % 4]
                              wcount += 1
                              pw = PSV[pn][:, 0:Lc]
                              for k in range(8):
                                  mm(pw, sv[:, k, di * 128:(di + 1) * 128], HB[:, k, h0:h0 + Lc], k == 0, k == 7,
                                     [sr, ("HB", c)], [pn])
                              stt(X[:, dch, s0:s0 + Lc], pw, G1[:, dch, s:s + 1], X[:, dch, s0:s0 + Lc], ALU.mult, ALU.add,
                                  [pn, MODn, xres(c)], [xres(c)])
                  stage(f"wout_{half}")
                  S.barrier()
              advance_mod(64)
              stage("wout")
              tap(f"x1_{l}", X[:], [128, 8, NTOK], F32, [])

              def h2dst(k, c):
                  s0, Lc = GROUPS[c]
                  if c < 2:
                      return HB[:, k, s0:s0 + Lc], ("HB", c)
                  return H2B[:, k, s0 - 1024:s0 - 1024 + Lc], ("H2B", c)

              for c in groups_all:
                  do_norm(c, A2, SH2, h2dst, "A2")
              stage("norm2")
              gcount = 0
              dcount = 0
              for bi, blk in enumerate(FFBLOCKS):
                  for fl, f in enumerate(blk):
                      slot, sr = load_piece(l, NP_GU + f)
                      sv = slot[:].rearrange("p (u k n) -> p u k n", u=2, k=8)
                      for c in groups_all:
                          s0, Lc = GROUPS[c]
                          i = gcount % 2
                          gcount += 1
                          pgn, pun = ("P0a", "P0b") if i == 0 else ("P1a", "P1b")
                          pg, pu = PSV[pgn][:, 0:Lc], PSV[pun][:, 0:Lc]
                          for k in range(8):
                              hsrc, hres = h2dst(k, c)
                              mm(pg, sv[:, 0, k, :], hsrc, k == 0, k == 7, [sr, hres], [pgn])
                          for k in range(8):
                              hsrc, hres = h2dst(k, c)
                              mm(pu, sv[:, 1, k, :], hsrc, k == 0, k == 7, [sr, hres], [pun])
                          sg, sgn = (Fs[0], "F0") if i == 0 else (Fs[1], "F1")
                          act(sg[:, 0:Lc], pg, AF.Silu, [pgn], [sgn])
                          tt(ACTB[:, fl, s0:s0 + Lc], pu, sg[:, 0:Lc], ALU.mult, [sgn, pun], [("ACTB", fl, c)])
                  nf = len(blk)
                  for dch in range(8):
                      slot, sr = load_piece(l, NP_WD + bi * 8 + dch, nf * 128)
                      sv = slot[:, 0:nf * 128].rearrange("p (f n) -> p f n", f=nf)
                      for c in groups_all:
                          s0, Lc = GROUPS[c]
                          s = sidx(c)
                          pn = ["P2a", "P2b", "P3"][dcount % 3]
                          dcount += 1
                          pd = PSV[pn][:, 0:Lc]
                          for fl in range(nf):
                              mm(pd, sv[:, fl, :], ACTB[:, fl, s0:s0 + Lc], fl == 0, fl == nf - 1, [sr, ("ACTB", fl, c)], [pn])
                          stt(X[:, dch, s0:s0 + Lc], pd, G2[:, dch, s:s + 1], X[:, dch, s0:s0 + Lc], ALU.mult, ALU.add,
                              [pn, MODn, xres(c)], [xres(c)])
              S.barrier()
              stage("ffn")
              tap(f"x2_{l}", X[:], [128, 8, NTOK], F32, [])


        try:
            run_layers()
        except _Stop:
            pass
        for t in range(16):
            Pt = P0 if t % 2 == 0 else P1
            pr = ["P0a", "P0b"] if t % 2 == 0 else ["P1a", "P1b"]
            for k in range(8):
                tr(Pt[:, k * 128:(k + 1) * 128], X[:, k, t * 128:(t + 1) * 128], IDF[:], [xres(t // 4), "IDF"], pr)
            stg = SCR[:, (t % 2) * 1024:(t % 2 + 1) * 1024]
            stg_r = ["F0", "F1"] if t % 2 == 0 else ["F2", "F3"]
            if t % 2 == 0:
                act(stg, Pt[:], AF.Copy, pr, stg_r)
            else:
                cpv(stg, Pt[:], pr, stg_r)
            dma_sp(out_d[t * 128:(t + 1) * 128, :], stg, stg_r, [("out", t)])
        S.emit(st)
    return nc, tap_d


_CACHE = {}


def prep_inputs(inp, n_layers=2):
    inp = {k: np.asarray(v, dtype=np.float32) for k, v in inp.items()}
    cbf, rope, pm = _consts()
    shared = {"cbf": np.ascontiguousarray(cbf.reshape(128, -1)), "idf": np.eye(128, dtype=np.float32),
              "rope": np.ascontiguousarray(rope.reshape(128, -1)), "pm": np.ascontiguousarray(pm.reshape(128, -1))}
    for l in range(n_layers):
        la = _layer_arrays(inp, l)
        for k, v in la.items():
            shared[f"{k}{l}"] = v
    maps = []
    for b in range(8):
        m = dict(shared)
        m["x"] = np.ascontiguousarray(inp["x"][b])
        m["ctx"] = np.ascontiguousarray(inp["ctx"][b])
        cv = np.zeros((128, 8, 2), np.float32)
        cv[:, :, 0] = inp["c"][b].reshape(8, 128).T
        cv[:, :, 1] = inp["c_ctx"].reshape(8, 128).T
        m["cvec"] = np.ascontiguousarray(cv.reshape(128, 16))
        maps.append(m)
    return maps


def kernel(**inputs):
    if "nc" not in _CACHE:
        _CACHE["nc"] = build(2)[0]
    nc = _CACHE["nc"]
    maps = prep_inputs(inputs, 2)
    res = run_bass_kernel_spmd(nc, maps, core_ids=list(range(8)))
    return np.stack([np.asarray(r["out"], dtype=np.float32) for r in res.results], 0)
```

```python
import contextlib
import numpy as np
import concourse.bass as bass
import concourse.mybir as mybir
from concourse.bass_utils import run_bass_kernel_spmd

F32 = mybir.dt.float32
BF16 = mybir.dt.bfloat16
AF = mybir.ActivationFunctionType
ALU = mybir.AluOpType

D = 1024
SEQ = 2048
CTX = 256
NTOK = SEQ + CTX
FF = 2816
NFF = 22
NEG = -30000.0
EPS = 1e-6
GROUPS = [(0, 512), (512, 512), (1024, 512), (1536, 512), (2048, 256)]
HALVES = [[0, 1], [2, 3, 4]]
HSTART = [0, 1024]
FFBLOCKS = [list(range(0, 8)), list(range(8, 15)), list(range(15, 22))]
PIECE = 2048
import os as _os
_DBG_DELAY = int(_os.environ.get('DBG_DELAY', '0'))
_DBG_SKIP = set(int(v) for v in _os.environ.get('DBG_SKIP_A', '').split(',') if v)


class _Op:
    __slots__ = ("eng", "fn", "dma", "deps", "has_dep", "sig")

    def __init__(self, eng, fn, dma):
        self.eng = eng
        self.fn = fn
        self.dma = dma
        self.deps = []
        self.has_dep = False
        self.sig = None


class Sched:
    COMPUTE = ("pe", "act", "dve")

    def __init__(self, nc, n_dma_sems=8):
        self.nc = nc
        self.ops = []
        self.res = {}
        self.n_dma_sems = n_dma_sems
        self.last = {}
        self.pending = {}
        self.dma_since = []

    def op(self, eng, fn, reads=(), writes=(), dma=False):
        o = _Op(eng, fn, dma)
        deps = {}
        for r in reads:
            st = self.res.get(r)
            if st is not None and st[0] is not None:
                deps[id(st[0])] = (st[0], True)
        for w in writes:
            st = self.res.get(w)
            if st is not None:
                if st[0] is not None and id(st[0]) not in deps:
                    deps[id(st[0])] = (st[0], False)
                for rd in st[1]:
                    if id(rd) not in deps:
                        deps[id(rd)] = (rd, False)
        for d in self.pending.pop(eng, ()):
            if id(d) not in deps:
                deps[id(d)] = (d, True)
        for d, raw in deps.values():
            if d is o:
                continue
            if not d.dma and not o.dma and d.eng == eng:
                if eng == "pe":
                    continue
            o.deps.append(d)
            d.has_dep = True
        for r in reads:
            st = self.res.get(r)
            if st is None:
                self.res[r] = [None, [o]]
            else:
                st[1].append(o)
        for w in writes:
            self.res[w] = [o, []]
        self.ops.append(o)
        self.last[eng] = o
        if dma:
            self.dma_since.append(o)
        return o

    def barrier(self):
        lasts = [o for e, o in self.last.items() if e in self.COMPUTE]
        lasts += self.dma_since
        self.dma_since = []
        for e in self.COMPUTE:
            self.pending[e] = self.pending.get(e, []) + lasts

    def emit(self, stack):
        nc = self.nc
        eobj = {"pe": nc.tensor, "act": nc.scalar, "dve": nc.vector, "pool": nc.gpsimd, "sp": nc.sync}
        semh = {}
        cnt = {}
        waited = {e: {} for e in eobj}
        dma_rr = {}
        dcnt = {}

        def get_sem(key):
            if key not in semh:
                semh[key] = stack.enter_context(nc.semaphore("s_" + "_".join(str(k) for k in key)))
            return semh[key]

        for o in self.ops:
            E = eobj[o.eng]
            need = {}
            for d in o.deps:
                key, val = d.sig
                if need.get(key, 0) < val:
                    need[key] = val
            if o.dma:
                i = dma_rr.get(o.eng, 0)
                dma_rr[o.eng] = (i + 1) % self.n_dma_sems
                k = ("dma", o.eng, i)
                n = dcnt.get(k, 0) + 1
                dcnt[k] = n
                if n > 1 and need.get(k, 0) < 16 * (n - 1):
                    need[k] = 16 * (n - 1)
            for key, val in need.items():
                if waited[o.eng].get(key, 0) < val:
                    E.wait_ge(get_sem(key), val)
                    waited[o.eng][key] = val
            ins = o.fn()
            if o.dma:
                ins.then_inc(get_sem(k), 16)
                o.sig = (k, 16 * n)
            elif o.has_dep:
                key = ("e", o.eng)
                cnt[key] = cnt.get(key, 0) + 1
                ins.then_inc(get_sem(key), 1)
                o.sig = (key, cnt[key])
        for key, n in dcnt.items():
            nc.sync.wait_ge(get_sem(key), 16 * n)
        for key, n in cnt.items():
            nc.sync.wait_ge(get_sem(key), n)


def _kmajor(w):
    K, N = w.shape
    return np.ascontiguousarray(w.reshape(K // 128, 128, N).transpose(1, 0, 2))


def _pad_piece(a):
    a = np.ascontiguousarray(a, dtype=np.float32).reshape(128, -1)
    out = np.zeros((128, PIECE), np.float32)
    out[:, : a.shape[1]] = a
    return out


def _dslot(interior, s):
    if 3 <= s <= 9:
        return 8 + (9 - s)
    if s >= 10:
        return (0 if interior else 4) + (13 - s)
    return (18 if interior else 15) + (2 - s)


def _consts():
    c = np.zeros((128, 6, 128), np.float32)
    c[:, 0, :] = np.eye(128)
    c[:, 1, :] = 1.0
    c[0:64, 2, 0:64] = 1.0
    c[64:128, 2, 64:128] = 1.0
    for m in range(128):
        if (m % 32) < 16:
            c[m + 16, 3, m] = -1.0
        else:
            c[m - 16, 3, m] = 1.0
    j = np.arange(128)[:, None]
    i = np.arange(128)[None, :]
    c[:, 4, :] = np.where(j >= i, 0.0, NEG)
    c[:, 5, :] = np.where(j <= i, 0.0, NEG)
    p = np.arange(128)
    idx = p % 64
    f = (idx % 16).astype(np.float64)
    inv = np.power(10000.0, -f / 16.0)
    t = np.arange(SEQ)
    pos = np.where((idx < 32)[:, None], (t // 64)[None, :], (t % 64)[None, :]).astype(np.float64)
    ang = pos * inv[:, None]
    rope = np.stack([np.cos(ang), np.sin(ang)], axis=1).astype(np.float32)
    pm = np.zeros((128, 20, 128), np.float32)
    jj = np.arange(128)[:, None]
    tt = np.arange(128)[None, :]
    for g, w in enumerate((2, 4, 8, 16)):
        h = w // 2
        eye = (jj == tt).astype(np.float32)
        pm[:, g * 5 + 0, :] = ((jj >= tt - h) & (jj < tt + h)) / w - eye
        lo = np.maximum(tt - h, 0)
        cntf = (tt + h) - lo
        pm[:, g * 5 + 1, :] = ((jj >= lo) & (jj < tt + h)) / cntf - eye
        hi = np.minimum(tt + h, 128)
        cntl = hi - (tt - h)
        pm[:, g * 5 + 2, :] = ((jj >= tt - h) & (jj < hi)) / cntl - eye
        pm[:, g * 5 + 3, :] = (jj >= 128 + tt - h) / w
        pm[:, g * 5 + 4, :] = (jj < tt + h - 128) / w
    return c, rope, pm


def _dtab(rpb):
    kc = np.arange(64)[:, None]
    qc = np.arange(64)[None, :]
    cs = np.clip(qc - 8, 0, 48)
    col_ok = (kc >= cs) & (kc < cs + 16)
    coff = np.clip(kc - qc, -15, 15) + 15
    out = np.full((128, 4, 21, 64), NEG, np.float32)
    for h in range(4):
        for interior in (True, False):
            for s in range(14):
                sl = _dslot(interior, s)
                for kr in range(2):
                    rho = s + kr
                    ok = 0 <= rho <= 14 and ((3 <= rho <= 10) or not interior)
                    if not ok:
                        continue
                    T = np.where(col_ok, rpb[h, rho][coff], NEG)
                    out[kr * 64:(kr + 1) * 64, h, sl, :] = T
    return out


def _layer_arrays(inp, l):
    f32 = np.float32
    w_in = inp["w_in"][l]
    pieces = []
    wm = _kmajor(inp["w_mod"][l])
    for jj in range(24):
        pieces.append(_pad_piece(wm[:, :, jj * 256:(jj + 1) * 256]))
    aq = w_in[:, 0:256]
    fm_cols = [np.concatenate([aq[:, 0:64], aq[:, 128:192]], 1), np.concatenate([aq[:, 64:128], aq[:, 192:256]], 1),
               w_in[:, 256:384], w_in[:, 1280:1408], w_in[:, 1408:1536], w_in[:, 1536:1664], w_in[:, 1664:1792]]
    fm = _kmajor(np.concatenate(fm_cols, 1))
    for a in range(4):
        pieces.append(_pad_piece(fm[:, :, a * 256:min((a + 1) * 256, 896)]))
    tm = [w_in[:, 512:768], w_in[:, 768:1024], w_in[:, 1024:1280], w_in[:, 1792:2048], w_in[:, 384:512]]
    for a in tm:
        pieces.append(_pad_piece(_kmajor(a)))
    wo = _kmajor(inp["w_out"][l])
    for a in range(4):
        pieces.append(_pad_piece(wo[:, :, a * 256:(a + 1) * 256]))
    wg = _kmajor(inp["w_gate"][l])
    wu = _kmajor(inp["w_up"][l])
    for f in range(NFF):
        pieces.append(_pad_piece(np.concatenate([wg[:, :, f * 128:(f + 1) * 128].reshape(128, -1),
                                                 wu[:, :, f * 128:(f + 1) * 128].reshape(128, -1)], 1)))
    wd = _kmajor(inp["w_down"][l])
    for blk in FFBLOCKS:
        for dch in range(8):
            pieces.append(_pad_piece(wd[:, blk[0]:blk[-1] + 1, dch * 128:(dch + 1) * 128]))
    W = np.stack(pieces, 0)
    vec = np.zeros((128, 26), f32)
    vec[:, 0:8] = inp["g_mix"][l].reshape(8, 128).T
    vec[:, 8:16] = inp["g_ffn"][l].reshape(8, 128).T
    vec[:, 16] = np.tile(inp["a_q_gain"][l], 2)
    vec[:, 17] = np.tile(inp["a_k_gain"][l], 2)
    vec[:, 18] = np.tile(inp["d_q_gain"][l], 2)
    vec[:, 19] = np.tile(inp["d_k_gain"][l], 2)
    vec[:, 20:22] = inp["c_scale"][l].reshape(2, 128).T
    vec[:, 22:26] = inp["b_b_s"][l].T
    bmod = np.ascontiguousarray(inp["b_mod"][l].reshape(48, 128).T)
    bc = np.zeros((128, 260), f32)
    bc[:, 0:4] = inp["a_sink"][l][None, :]
    bc[:, 4:260] = inp["b_v_gain"][l][None, :]
    sm = np.zeros((128, 6, 128), f32)
    sm[:, 0:4, :] = inp["b_w_s"][l].transpose(2, 0, 1)
    wp = inp["c_w_pool"][l]
    for g in range(4):
        o = (g % 2) * 64
        sm[o:o + 64, 4 + g // 2, o:o + 64] = wp[g]
    dt = _dtab(inp["d_rpb"][l])
    return {"W": W, "vec": vec, "bmod": bmod, "bc": bc, "sm": np.ascontiguousarray(sm.reshape(128, -1)), "dtab": np.ascontiguousarray(dt.reshape(128, -1))}


NP_MOD, NP_FM, NP_TM, NP_WO, NP_GU, NP_WD = 0, 24, 28, 33, 37, 59
NPIECES = 83


class _Stop(Exception):
    pass


def build(n_layers=2, taps=(), stop=None):
    nc = bass.Bass("TRN2", target_bir_lowering=False)
    dr = {}

    def din(name, shape):
        dr[name] = nc.dram_tensor(name, list(shape), F32, kind="ExternalInput").ap()
        return dr[name]

    x_d = din("x", [SEQ, D])
    ctx_d = din("ctx", [CTX, D])
    cvec_d = din("cvec", [128, 16])
    cbf_d = din("cbf", [128, 6 * 128])
    idf_d = din("idf", [128, 128])
    rope_d = din("rope", [128, 2 * SEQ])
    pm_d = din("pm", [128, 20 * 128])
    L = []
    for l in range(n_layers):
        L.append({
            "W": din(f"W{l}", [NPIECES, 128, PIECE]), "vec": din(f"vec{l}", [128, 26]), "bmod": din(f"bmod{l}", [128, 48]),
            "bc": din(f"bc{l}", [128, 260]), "sm": din(f"sm{l}", [128, 6 * 128]), "dtab": din(f"dtab{l}", [128, 4 * 21 * 64]),
        })
    out_d = nc.dram_tensor("out", [SEQ, D], F32, kind="ExternalOutput").ap()
    tap_d = {}

    st = contextlib.ExitStack()
    with st:
        S = Sched(nc)
        sb = lambda n, s, d: st.enter_context(nc.sbuf_tensor(n, list(s), d))
        ps = lambda n, s, d: st.enter_context(nc.psum_tensor(n, list(s), d))
        X = sb("X", [128, 8, NTOK], F32)
        HB = sb("HB", [128, 8, 1280], BF16)
        RR = sb("RR", [128, 32472], BF16)
        QT = RR[:, 0:9216].rearrange("p (k t) -> p k t", k=4)
        KT = RR[:, 9216:16128].rearrange("p (k t) -> p k t", k=3)
        VA = RR[:, 16128:18504].rearrange("p (t h e) -> p t h e", t=18, h=2)
        VD = RR[:, 18504:23256].rearrange("p (t h e) -> p t h e", t=18, h=4)
        OB = RR[:, 23256:27864].rearrange("p (t c) -> p t c", t=18)
        CY = RR[:, 27864:32472].rearrange("p (t c) -> p t c", t=18)
        ACTB = RR[:, 0:18432].rearrange("p (f t) -> p f t", f=8)
        H2B = RR[:, 18432:28672].rearrange("p (k t) -> p k t", k=8)
        RING = [sb(f"ring{i}", [128, PIECE], BF16) for i in range(4)]
        TAB = sb("TAB", [128, 5376], BF16)
        SCR = sb("SCR", [128, 2048], F32)
        Fs = [SCR[:, i * 512:(i + 1) * 512] for i in range(4)]
        SQ = [sb(f"SQ{i}", [128, 512], BF16) for i in range(2)]
        QG = sb("QG", [128, 512], BF16)
        RQ = sb("RQ", [128, 512], BF16)
        PT = [sb(f"PT{i}", [128, 896], BF16) for i in range(2)]
        OT = [sb(f"OT{i}", [128, 256], BF16) for i in range(2)]
        PC = sb("PC", [128, 2, 512], BF16)
        QZ = PC[:].rearrange("p a (b q) -> p (a b) q", q=128)
        VN = [sb(f"VN{i}", [128, 256], BF16) for i in range(2)]
        CBF = sb("CBF", [128, 6, 128], BF16)
        IDF = sb("IDF", [128, 128], F32)
        CV = sb("CV", [128, 8, 2], F32)
        CA = sb("CA", [128, 8, 2], BF16)
        SMALL = sb("SMALL", [128, 64], F32)
        IDB = CBF[:, 0, :]
        ONES = CBF[:, 1, :]
        BDONES = CBF[:, 2, :]
        ROTT = CBF[:, 3, :]
        MASKP = CBF[:, 4, :]
        MASKN = CBF[:, 5, :]
        P0 = ps("P0", [128, 1024], F32)
        P1 = ps("P1", [128, 1024], F32)
        P2 = ps("P2", [128, 1024], F32)
        P3 = ps("P3", [128, 512], F32)
        PST = ps("PST", [128, 1024], BF16)
        PSV = {"P0a": P0[:, 0:512], "P0b": P0[:, 512:1024], "P1a": P1[:, 0:512], "P1b": P1[:, 512:1024],
               "P2a": P2[:, 0:512], "P2b": P2[:, 512:1024], "P3": P3[:, :]}

        def mm(out, lhsT, rhs, start, stop, reads, writes):
            S.op("pe", lambda: nc.tensor.matmul(out, lhsT=lhsT, rhs=rhs, start=start, stop=stop), reads, writes)

        def tr(out, in_, ident, reads, writes):
            S.op("pe", lambda: nc.tensor.transpose(out, in_, ident), reads, writes)

        def act(out, in_, func, reads, writes, scale=None, bias=None):
            kw = {}
            if scale is not None:
                kw["scale"] = scale
            if bias is not None:
                kw["bias"] = bias
            S.op("act", lambda: nc.scalar.activation(out=out, in_=in_, func=func, **kw), reads, writes)

        def tt(out, in0, in1, op, reads, writes):
            S.op("dve", lambda: nc.vector.tensor_tensor(out=out, in0=in0, in1=in1, op=op), reads, writes)

        def stt(out, in0, scalar, in1, op0, op1, reads, writes):
            S.op("dve", lambda: nc.vector.scalar_tensor_tensor(out=out, in0=in0, scalar=scalar, in1=in1, op0=op0, op1=op1),
                 reads, writes)

        def ts(out, in0, s1, s2, op0, op1, reads, writes):
            S.op("dve", lambda: nc.vector.tensor_scalar(out=out, in0=in0, scalar1=s1, scalar2=s2, op0=op0, op1=op1),
                 reads, writes)

        def cpv(out, in_, reads, writes):
            S.op("dve", lambda: nc.vector.tensor_copy(out=out, in_=in_), reads, writes)

        def dma_sp(out, in_, reads, writes):
            S.op("sp", lambda: nc.sync.dma_start(out=out, in_=in_), reads, writes, dma=True)

        def dma_pool(out, in_, reads, writes):
            S.op("pool", lambda: nc.gpsimd.dma_start(out=out, in_=in_, max_dma_last_dim=4096), reads, writes, dma=True)

        def stage(name):
            if stop == name:
                S.barrier()
                raise _Stop()

        def tap(name, ap, shape, dtype, reads):
            if name not in taps:
                return
            S.barrier()
            t = nc.dram_tensor("tap_" + name, list(shape), dtype, kind="ExternalOutput").ap()
            tap_d[name] = t
            dma_sp(t, ap, list(reads) + ["tapsrc"], ["tap_" + name])

        ring_i = [0]

        def load_piece(l, idx, nelem=PIECE):
            i = ring_i[0]
            ring_i[0] = (i + 1) % 4
            dma_pool(RING[i][:, 0:nelem], L[l]["W"][idx, :, 0:nelem], [], [("slot", i)])
            return RING[i], ("slot", i)

        dma_sp(IDF[:], idf_d, [], ["IDF"])
        dma_sp(CV[:].rearrange("p k s -> p (k s)"), cvec_d, [], ["CV"])
        dma_pool(CBF[:].rearrange("p a b -> p (a b)"), cbf_d, [], ["CBF"])
        act(CA[:], CV[:], AF.Silu, ["CV"], ["CA"])
        for t in range(18):
            stg = SCR[:, (t % 2) * 1024:(t % 2 + 1) * 1024]
            stg_r = ["F0", "F1"] if t % 2 == 0 else ["F2", "F3"]
            src = x_d[t * 128:(t + 1) * 128, :] if t < 16 else ctx_d[(t - 16) * 128:(t - 15) * 128, :]
            dma_sp(stg, src, [], stg_r)
            Pt = P0 if t % 2 == 0 else P1
            pr = ["P0a", "P0b"] if t % 2 == 0 else ["P1a", "P1b"]
            for k in range(8):
                tr(Pt[:, k * 128:(k + 1) * 128], stg[:, k * 128:(k + 1) * 128], IDF[:], stg_r + ["IDF"], pr)
            dst = X[:, :, t * 128:(t + 1) * 128]
            src_ps = Pt[:].rearrange("p (k t) -> p k t", k=8)
            xr = [("X", t // 4 if t < 16 else 4)]
            if t % 2 == 0:
                act(dst, src_ps, AF.Copy, pr, xr)
            else:
                cpv(dst, src_ps, pr, xr)
        S.barrier()

        def xres(c):
            return ("X", c)

        def hbres(c):
            return ("HB", 2) if c == 4 else ("HB", c % 2)

        def grp_of_tile(t):
            return t // 4 if t < 16 else 4

        def sidx(c):
            return 1 if c == 4 else 0

        VEC = sb("VEC", [128, 26], F32)
        BMODs = [sb(f"BMOD{i}", [128, 48], F32) for i in range(2)]
        BC = sb("BC", [128, 260], F32)
        SM = sb("SM", [128, 6, 128], BF16)
        MODs = [sb(f"MOD{i}", [128, 48, 2], F32) for i in range(2)]
        A1 = sb("A1", [128, 8, 2], F32)
        A2 = sb("A2", [128, 8, 2], F32)
        GS = sb("GS", [128, 8], F32)

        def mod_gen(l):
            b = l % 2
            dma_sp(BMODs[b][:], L[l]["bmod"], [], [f"BMOD{b}"])
            for jj in range(24):
                slot, sr = load_piece(l, NP_MOD + jj)
                sv = slot[:].rearrange("p (k n) -> p k n", k=8)
                for cc in range(2):
                    j = 2 * jj + cc
                    for k in range(8):
                        mm(P3[:, 2 * j:2 * j + 2], sv[:, k, cc * 128:(cc + 1) * 128], CA[:, k, :], k == 0, k == 7,
                           [sr, "CA"], ["P3"])
                yield
            tt(MODs[b][:], P3[:, 0:96].rearrange("p (j s) -> p j s", s=2),
               BMODs[b][:].unsqueeze(2).to_broadcast([128, 48, 2]), ALU.add, ["P3", f"BMOD{b}"], [f"MOD{b}"])
            yield

        def run_layers():
          for l in range(n_layers):
              last = l == n_layers - 1
              Ld = L[l]
              dma_sp(VEC[:], Ld["vec"], [], ["VEC"])
              dma_sp(BC[:], Ld["bc"], [], ["BC"])
              dma_pool(SM[:].rearrange("p a b -> p (a b)"), Ld["sm"], [], ["SM"])
              WST = SM[:, 0:4, :]
              WPBD = SM[:, 4:6, :]

              MOD = MODs[l % 2]
              MODn = f"MOD{l % 2}"
              if l == 0:
                  for _ in mod_gen(0):
                      pass
              pre = [mod_gen(l + 1) if not last else None]

              def advance_mod(n=1):
                  for _ in range(n):
                      if pre[0] is not None:
                          try:
                              next(pre[0])
                          except StopIteration:
                              pre[0] = None
              stt(A1[:], MOD[:, 8:16, :], 1.0, VEC[:, 0:8].unsqueeze(2).to_broadcast([128, 8, 2]), ALU.add, ALU.mult,
                  [MODn, "VEC"], ["A1"])
              stt(A2[:], MOD[:, 32:40, :], 1.0, VEC[:, 8:16].unsqueeze(2).to_broadcast([128, 8, 2]), ALU.add, ALU.mult,
                  [MODn, "VEC"], ["A2"])
              ts(GS[:, 0:1], VEC[:, 16:17], 0.125, None, ALU.mult, ALU.bypass, ["VEC"], ["GS"])
              ts(GS[:, 1:2], VEC[:, 17:18], 1.0, None, ALU.mult, ALU.bypass, ["VEC"], ["GS"])
              ts(GS[:, 2:3], VEC[:, 18:19], 0.125, None, ALU.mult, ALU.bypass, ["VEC"], ["GS"])
              ts(GS[:, 3:4], VEC[:, 19:20], 1.0, None, ALU.mult, ALU.bypass, ["VEC"], ["GS"])
              act(GS[:, 4:8], BC[:, 0:4], AF.Exp, ["BC"], ["GS"])
              SH1, G1, SH2, G2 = MOD[:, 0:8, :], MOD[:, 16:24, :], MOD[:, 24:32, :], MOD[:, 40:48, :]
              stage("modload")
              tap(f"mod{l}", MOD[:], [128, 48, 2], F32, [MODn])

              stage("mod")
              def do_norm(c, Amul, SHv, dst_fn, tag):
                  s0, Lc = GROUPS[c]
                  s = sidx(c)
                  psn, psr = (PSV["P2a"], "P2a") if c % 2 == 0 else (PSV["P2b"], "P2b")
                  for k in range(8):
                      sq = SQ[k % 2]
                      act(sq[:, 0:Lc], X[:, k, s0:s0 + Lc], AF.Square, [xres(c)], [f"SQ{k % 2}"])
                      mm(psn[:, 0:Lc], ONES, sq[:, 0:Lc], k == 0, k == 7, [f"SQ{k % 2}", "CBF"], [psr])
                  act(Fs[0][:, 0:Lc], psn[:, 0:Lc], AF.Ln, [psr], ["F0"], scale=1.0 / D, bias=EPS)
                  act(Fs[1][:, 0:Lc], Fs[0][:, 0:Lc], AF.Exp, ["F0"], ["F1"], scale=-0.5)
                  for k in range(8):
                      tmp, tr_ = (Fs[2], "F2") if k % 2 == 0 else (Fs[3], "F3")
                      tt(tmp[:, 0:Lc], X[:, k, s0:s0 + Lc], Fs[1][:, 0:Lc], ALU.mult, [xres(c), "F1"], [tr_])
                      dst, dres = dst_fn(k, c)
                      act(dst, tmp[:, 0:Lc], AF.Identity, [tr_, MODn, tag], [dres], scale=Amul[:, k, s:s + 1],
                          bias=SHv[:, k, s:s + 1])

              groups_all = [0, 1, 2, 3] if last else [0, 1, 2, 3, 4]

              dma_pool(TAB[:, 0:4096], rope_d, [], ["TAB"])
              COS = TAB[:, 0:2048]
              SIN = TAB[:, 2048:4096]
              S.op("dve", lambda: nc.vector.memset(VA[:, :, :, 64:65], 1.0), [], ["VAones"])
              S.op("dve", lambda: nc.vector.memset(VD[:, :, :, 64:65], 1.0), [], ["VDones"])

              def hcols(c, half):
                  s0, Lc = GROUPS[c]
                  return s0 - HSTART[half], Lc

              for half in range(2):
                  hgroups = HALVES[half]
                  for c in hgroups:
                      h0, Lc = hcols(c, half)
                      do_norm(c, A1, SH1, lambda k, c, h0=h0, Lc=Lc: (HB[:, k, h0:h0 + Lc], hbres(c)), "A1")
                  if half == 0:
                      tap(f"hT{l}", HB[:, :, 0:1024], [128, 8, 1024], BF16, [("HB", 0), ("HB", 1)])
                  stage(f"norm1_{half}")
                  combos = []
                  for a in range(4):
                      js = [2 * a, 2 * a + 1] if a < 3 else [6]
                      for ji, j in enumerate(js):
                          for c in hgroups:
                              if last and c == 4 and j in (0, 1, 3, 4):
                                  continue
                              combos.append((a, len(js), ji, j, c))
                  fm_slot = {}

                  def fm_piece(a, njs):
                      if a not in fm_slot:
                          slot, sr = load_piece(l, NP_FM + a, 8 * 128 * njs)
                          fm_slot[a] = (slot[:, 0:8 * 128 * njs].rearrange("p (k n) -> p k n", k=8), sr)
                      return fm_slot[a]

                  def fm_info(k):
                      a, njs, ji, j, c = combos[k]
                      s0, Lc = GROUPS[c]
                      h0, _ = hcols(c, half)
                      pfn = ["P0a", "P0b", "P1a"][k % 3]
                      phn = ["P1b", "P2a"][k % 2]
                      prn = ["P2b", "P3"][k % 2]
                      gcol = {0: 0, 1: 0, 2: 1, 3: 2, 4: 2, 5: 3, 6: 3}[j]
                      if j < 2:
                          dst, dres = QT[:, j, s0:s0 + Lc], ("QT", j, c)
                      elif j == 2:
                          dst, dres = KT[:, 0, s0:s0 + Lc], ("KT", 0, c)
                      elif j < 5:
                          dst, dres = QT[:, j - 1, s0:s0 + Lc], ("QT", j - 1, c)
                      else:
                          dst, dres = KT[:, j - 4, s0:s0 + Lc], ("KT", j - 4, c)
                      return dict(a=a, njs=njs, ji=ji, j=j, c=c, s0=s0, Lc=Lc, h0=h0, pfn=pfn, phn=phn, prn=prn,
                                  pf=PSV[pfn][:, 0:Lc], ph=PSV[phn][:, 0:Lc], pr=PSV[prn][:, 0:Lc],
                                  gain=GS[:, gcol:gcol + 1], dst=dst, dres=dres, rope=(j < 3 and c != 4),
                                  sq=SQ[k % 2], sqn=f"SQ{k % 2}")

                  def fm_head(k):
                      f = fm_info(k)
                      sv, sr = fm_piece(f["a"], f["njs"])
                      for kk in range(8):
                          mm(f["pf"], sv[:, kk, f["ji"] * 128:(f["ji"] + 1) * 128], HB[:, kk, f["h0"]:f["h0"] + f["Lc"]],
                             kk == 0, kk == 7, [sr, ("HB", f["c"])], [f["pfn"]])
                      act(f["sq"][:, 0:f["Lc"]], f["pf"], AF.Square, [f["pfn"]], [f["sqn"]])

                  def fm_qg(k):
                      f = fm_info(k)
                      if f["rope"]:
                          act(QG[:, 0:f["Lc"]], f["pf"], AF.Identity, [f["pfn"], "GS"], ["QG"], scale=f["gain"])

                  def fm_mid(k):
                      f = fm_info(k)
                      Lc, s0 = f["Lc"], f["s0"]
                      mm(f["ph"], BDONES, f["sq"][:, 0:Lc], True, True, [f["sqn"], "CBF"], [f["phn"]])
                      if f["rope"]:
                          mm(f["pr"], ROTT, QG[:, 0:Lc], True, True, ["QG", "CBF"], [f["prn"]])
                          tt(Fs[2][:, 0:Lc], QG[:, 0:Lc], COS[:, s0:s0 + Lc], ALU.mult, ["QG", "TAB"], ["F2"])

                  def fm_tail(k):
                      f = fm_info(k)
                      Lc, s0 = f["Lc"], f["s0"]
                      act(Fs[0][:, 0:Lc], f["ph"], AF.Ln, [f["phn"]], ["F0"], scale=1.0 / 64, bias=EPS)
                      act(Fs[1][:, 0:Lc], Fs[0][:, 0:Lc], AF.Exp, ["F0"], ["F1"], scale=-0.5)
                      if f["rope"]:
                          act(RQ[:, 0:Lc], f["pr"], AF.Copy, [f["prn"]], ["RQ"])
                          tt(Fs[3][:, 0:Lc], RQ[:, 0:Lc], SIN[:, s0:s0 + Lc], ALU.mult, ["RQ", "TAB"], ["F3"])
                          tt(Fs[2][:, 0:Lc], Fs[2][:, 0:Lc], Fs[3][:, 0:Lc], ALU.add, ["F2", "F3"], ["F2"])
                          tt(f["dst"], Fs[2][:, 0:Lc], Fs[1][:, 0:Lc], ALU.mult, ["F2", "F1"], [f["dres"]])
                      else:
                          stt(f["dst"], f["pf"], f["gain"], Fs[1][:, 0:Lc], ALU.mult, ALU.mult, [f["pfn"], "GS", "F1"], [f["dres"]])

                  nfm = len(combos)
                  fm_head(0)
                  fm_qg(0)
                  for k in range(nfm):
                      if k + 1 < nfm:
                          fm_head(k + 1)
                      fm_mid(k)
                      if k + 1 < nfm:
                          fm_qg(k + 1)
                      fm_tail(k)
                  stage(f"fm_{half}")
                  tiles_h = []
                  for c in hgroups:
                      s0, Lc = GROUPS[c]
                      tiles_h += [(t, c) for t in range(s0 // 128, (s0 + Lc) // 128)]
                  tcount = [0]

                  def tm_proj(t, c, sv, sr, ncol):
                      h0 = t * 128 - HSTART[half]
                      pn = ["P0a", "P0b", "P1a", "P1b"][tcount[0] % 4]
                      tcount[0] += 1
                      pp = PSV[pn][:, 0:ncol]
                      for k in range(8):
                          mm(pp, HB[:, k, h0:h0 + 128], sv[:, k, :], k == 0, k == 7, [sr, hbres(c)], [pn])
                      return pp, pn

                  def tm_piece(pi):
                      ncol = 128 if pi == 4 else 256
                      slot, sr = load_piece(l, NP_TM + pi, 8 * ncol)
                      return slot[:, 0:8 * ncol].rearrange("p (k n) -> p k n", k=8), sr, ncol

                  svU, srU, _ = tm_piece(0)
                  for (t, c) in tiles_h:
                      if last and c == 4:
                          continue
                      pp, pn = tm_proj(t, c, svU, srU, 256)
                      act(OB[:, t, :], pp, AF.Gelu_apprx_tanh, [pn], [("OB", t)])
                  svV, srV, _ = tm_piece(1)
                  svC, srC, _ = tm_piece(2)
                  svD, srD, _ = tm_piece(3)
                  svA, srA, _ = tm_piece(4)

                  def emit_z(t):
                      vn = VN[t % 2]
                      pzn = "P2a" if t % 2 == 0 else "P2b"
                      pz = PSV[pzn]
                      for g in range(4):
                          mm(pz[:, g * 64:(g + 1) * 64], WST[:, g, :], vn[:, g * 64:(g + 1) * 64], True, True,
                             [f"VN{t % 2}", "SM"], [pzn])
                      guf, gufn = (Fs[2], "F2") if t % 2 == 0 else (Fs[3], "F3")
                      act(guf[:, 0:256], OB[:, t, :], AF.Copy, [("OB", t)], [gufn])
                      for g in range(4):
                          stt(OB[:, t, g * 64:(g + 1) * 64], pz[:, g * 64:(g + 1) * 64], VEC[:, 22 + g:23 + g],
                              guf[:, g * 64:(g + 1) * 64], ALU.add, ALU.mult, [pzn, "VEC", gufn], [("OB", t)])

                  pending_z = None
                  for (t, c) in tiles_h:
                      full = not (last and c == 4)
                      if full:
                          pp, pn = tm_proj(t, c, svV, srV, 256)
                          gv, gvn = (Fs[0], "F0") if t % 2 == 0 else (Fs[1], "F1")
                          gv = gv[:, 0:256]
                          act(gv, pp, AF.Gelu_apprx_tanh, [pn], [gvn])
                          o = (t % 2) * 16
                          stn = f"SMALL{t % 2}"
                          S.op("dve", lambda gv=gv, o=o: nc.vector.bn_stats(out=SMALL[:, o:o + 6], in_=gv), [gvn], [stn])
                          S.op("dve", lambda o=o: nc.vector.bn_aggr(out=SMALL[:, o + 8:o + 10], in_=SMALL[:, o:o + 6]),
                               [stn], [stn + "b"])
                          act(SMALL[:, o + 10:o + 11], SMALL[:, o + 9:o + 10], AF.Ln, [stn + "b"], [stn + "c"], bias=EPS)
                          act(SMALL[:, o + 11:o + 12], SMALL[:, o + 10:o + 11], AF.Exp, [stn + "c"], [stn + "d"], scale=-0.5)
                          ts(gv, gv, SMALL[:, o + 8:o + 9], SMALL[:, o + 11:o + 12], ALU.subtract, ALU.mult,
                             [gvn, stn + "b", stn + "d"], [gvn])
                          tt(VN[t % 2][:], gv, BC[:, 4:260], ALU.mult, [gvn, "BC"], [f"VN{t % 2}"])
                          pp, pn = tm_proj(t, c, svC, srC, 256)
                          act(CY[:, t, :], pp, AF.Copy, [pn], [("CY", t)])
                      if pending_z is not None:
                          emit_z(pending_z)
                          pending_z = None
                      pp, pn = tm_proj(t, c, svD, srD, 256)
                      cpv(VD[:, t, :, 0:64], pp.rearrange("p (h e) -> p h e", h=4), [pn, "VDones"], [("VD", t)])
                      pp, pn = tm_proj(t, c, svA, srA, 128)
                      cpv(VA[:, t, :, 0:64], pp.rearrange("p (h e) -> p h e", h=2), [pn, "VAones"], [("VA", t)])
                      if full:
                          pending_z = t
                  if pending_z is not None:
                      emit_z(pending_z)
              stage("p1")
              tap(f"qt{l}", QT[:, :, :], [128, 4, NTOK], BF16, [])
              tap(f"kt{l}", KT[:, :, :], [128, 3, NTOK], BF16, [])
              tap(f"ob{l}", OB[:, :, :], [128, 18, 256], BF16, [])
              tap(f"cy{l}", CY[:, :, :], [128, 18, 256], BF16, [])
              tap(f"vd{l}", VD[:, :, :, :], [128, 18, 4, 66], BF16, [])

              for half in range(2):
                  hgroups = [c for c in HALVES[half] if c in groups_all]
                  tiles_h = []
                  for c in hgroups:
                      s0, Lc = GROUPS[c]
                      tiles_h += [(t, c) for t in range(s0 // 128, (s0 + Lc) // 128)]
                  dma_pool(TAB[:, 0:2560], pm_d, [], ["TAB"])
                  PMv = TAB[:, 0:2560].rearrange("p (a t) -> p a t", a=20)
                  for (t, c) in tiles_h:
                      h0 = t * 128 - HSTART[half]
                      pv = PST[:, (t % 2) * 512:(t % 2) * 512 + 256]
                      pvn = "PST"
                      for cc in range(2):
                          tr(pv[:, cc * 128:(cc + 1) * 128], OB[:, t, cc * 128:(cc + 1) * 128], IDB, [("OB", t), "CBF"], [pvn])
                      cpv(HB[:, 2:4, h0:h0 + 128], pv.rearrange("p (c t) -> p c t", c=2), [pvn], [hbres(c)])
                  for c in hgroups:
                      s0, Lc = GROUPS[c]
                      h0, _ = hcols(c, half)
                      first_t = 16 if c == 4 else 0
                      last_t = 17 if c == 4 else 15
                      for t in range(s0 // 128, (s0 + Lc) // 128):
                          lo = (t * 128 - s0)
                          for g in range(4):
                              srcs = []
                              if t > first_t:
                                  srcs.append((t - 1, 3))
                              srcs.append((t, 1 if t == first_t else (2 if t == last_t else 0)))
                              if t < last_t:
                                  srcs.append((t + 1, 4))
                              o = (g % 2) * 64
                              pcn = "P0a" if g < 2 else "P0b"
                              outp = P0[o:o + 64, (g // 2) * 512 + lo:(g // 2) * 512 + lo + 128]
                              for i, (tj, v) in enumerate(srcs):
                                  mm(outp, CY[:, tj, g * 64:(g + 1) * 64], PMv[:, g * 5 + v, :], i == 0, i == len(srcs) - 1,
                                     [("CY", tj), "TAB"], [pcn])
                      for cc in range(2):
                          pcn = "P0a" if cc == 0 else "P0b"
                          pln = "P1a" if cc == 0 else "P1b"
                          act(PC[:, cc, 0:Lc], P0[:, cc * 512:cc * 512 + Lc], AF.Copy, [pcn], [("PC", cc)])
                          mm(P1[:, cc * 512:cc * 512 + Lc], WPBD[:, cc, :], PC[:, cc, 0:Lc], True, True, [("PC", cc), "SM"], [pln])
                          act(HB[:, 4 + cc, h0:h0 + Lc], P1[:, cc * 512:cc * 512 + Lc], AF.Identity, [pln, "VEC"], [hbres(c)],
                              scale=VEC[:, 20 + cc:21 + cc])
                  stage(f"p2a_{half}")
                  acount = [0]

                  def att_scores(job):
                      i = acount[0] % 2
                      acount[0] += 1
                      job["i"] = i
                      Sp = P0 if i == 0 else P1
                      Sn = ["P0a", "P0b"] if i == 0 else ["P1a", "P1b"]
                      chunks = job["chunks"]
                      n = len(chunks)
                      for ci, (kt, bias) in enumerate(chunks):
                          o = Sp[:, ci * 128:(ci + 1) * 128]
                          mm(o, job["K"](kt), QZ[:, job["slot"], :], True, bias is None,
                             [("QZ", job["slot"]), ("KT", job["kidx"], grp_of_tile(kt))], Sn)
                          if bias is not None:
                              mm(o, IDB, bias, False, True, ["CBF", "TAB"], Sn)
                      act(PT[i][:, 0:n * 128], Sp[:, 0:n * 128], AF.Exp, Sn, [f"PT{i}"])

                  def att_pv(job):
                      i = job["i"]
                      chunks = job["chunks"]
                      n = len(chunks)
                      h = job["h"]
                      for ci, (kt, bias) in enumerate(chunks):
                          mm(job["O"][:, h * 66:h * 66 + 65], PT[i][:, ci * 128:(ci + 1) * 128], job["V"](kt), ci == 0, ci == n - 1,
                             [f"PT{i}", (job["vname"], kt), job["vname"] + "ones"], [job["On"]])

                  def finish_tile(t, c, Otile, Oname, sink, cat0):
                      h0 = t * 128 - HSTART[half]
                      Ov = Otile[:, 0:264].rearrange("p (h e) -> p h e", h=4)
                      o = 32 + (t % 2) * 8
                      dn = f"DEN{t % 2}"
                      if sink:
                          tt(SMALL[:, o:o + 4], Ov[:, :, 64], GS[:, 4:8], ALU.add, [Oname, "GS"], [dn])
                      else:
                          cpv(SMALL[:, o:o + 4], Ov[:, :, 64], [Oname], [dn])
                      S.op("dve", lambda o=o: nc.vector.reciprocal(out=SMALL[:, o + 4:o + 8], in_=SMALL[:, o:o + 4]), [dn], [dn + "r"])
                      ot = OT[t % 2]
                      otn = f"OT{t % 2}"
                      tt(ot[:].rearrange("p (h e) -> p h e", h=4), Ov[:, :, 0:64],
                         SMALL[:, o + 4:o + 8].unsqueeze(2).to_broadcast([128, 4, 64]), ALU.mult, [Oname, dn + "r"], [otn])
                      pv = PST[:, (t % 2) * 512:(t % 2) * 512 + 256]
                      pvn = "PST"
                      for cc in range(2):
                          tr(pv[:, cc * 128:(cc + 1) * 128], ot[:, cc * 128:(cc + 1) * 128], IDB, [otn, "CBF"], [pvn])
                      cpv(HB[:, cat0:cat0 + 2, h0:h0 + 128], pv.rearrange("p (c t) -> p c t", c=2), [pvn], [hbres(c)])

                  def run_attn(jobs, sink, cat0):
                      S.op("dve", lambda: nc.vector.memset(QZ, 0.0), [], [("PC", 0), ("PC", 1)] + [("QZ", i) for i in range(8)])
                      prev = None
                      for job in jobs + [None]:
                          if job is not None:
                              po = job["po"]
                              cpv(QZ[po:po + 64, job["slot"], :], job["Q"], [("QZ", job["slot"]), job["qres"]], [("QZ", job["slot"])])
                              att_scores(job)
                          if prev is not None:
                              att_pv(prev)
                              if prev["last"]:
                                  finish_tile(prev["t"], prev["c"], prev["O"], prev["On"], sink, cat0)
                                  advance_mod()
                          prev = job

                  def tcols(t):
                      return slice(t * 128, (t + 1) * 128)

                  jobs = []
                  for (t, c) in tiles_h:
                      Ot, On = (PSV["P2a"], "P2a") if t % 2 == 0 else (PSV["P2b"], "P2b")
                      for h in range(4):
                          kv, g = h // 2, h % 2
                          po = kv * 64
                          if c == 4:
                              chunks = [(16, None), (17, None)]
                          else:
                              chunks = []
                              if t > 0:
                                  chunks.append((t - 1, MASKP))
                              chunks.append((t, None))
                              if t < 15:
                                  chunks.append((t + 1, MASKN))
                              chunks += [(16, None), (17, None)]
                          jobs.append({"t": t, "c": c, "h": h, "K": (lambda kt: KT[:, 0, tcols(kt)]), "po": po,
                                       "kidx": 0, "qres": ("QT", g, c), "vname": "VA",
                                       "slot": (t % 2) * 4 + h,
                                       "Q": QT[po:po + 64, g, tcols(t)], "V": (lambda kt, kv=kv: VA[:, kt, kv, 0:65]),
                                       "chunks": chunks, "O": Ot, "On": On, "last": h == 3})
                  run_attn(jobs, True, 0)
                  stage(f"attA_{half}")
                  dma_pool(TAB[:, 0:5376], Ld["dtab"], [], ["TAB"])
                  DT = TAB[:, 0:5376].rearrange("p (h s q) -> p h s q", h=4, s=21)
                  jobs = []
                  for (t, c) in tiles_h:
                      Ot, On = (PSV["P2a"], "P2a") if t % 2 == 0 else (PSV["P2b"], "P2b")
                      if c == 4:
                          deltas, interior = [], False
                      elif 2 <= t <= 13:
                          deltas, interior = [-2, -1, 0, 1, 2], True
                      elif t == 0:
                          deltas, interior = [0, 1, 2, 3], False
                      elif t == 1:
                          deltas, interior = [-1, 0, 1, 2], False
                      elif t == 14:
                          deltas, interior = [-2, -1, 0, 1], False
                      else:
                          deltas, interior = [-3, -2, -1, 0], False
                      for h in range(4):
                          po = (h % 2) * 64
                          ch = h // 2
                          chunks = []
                          for dl in deltas:
                              i0 = _dslot(interior, 2 * dl + 7)
                              i1 = _dslot(interior, 2 * dl + 6)
                              chunks.append((t + dl, DT[:, h, i0:i1 + 1:(i1 - i0), :]))
                          chunks += [(16, None), (17, None)]
                          jobs.append({"t": t, "c": c, "h": h, "K": (lambda kt, ch=ch: KT[:, 1 + ch, tcols(kt)]), "po": po,
                                       "kidx": 1 + ch, "qres": ("QT", 2 + ch, c), "vname": "VD",
                                       "slot": (t % 2) * 4 + h,
                                       "Q": QT[po:po + 64, 2 + ch, tcols(t)], "V": (lambda kt, h=h: VD[:, kt, h, 0:65]),
                                       "chunks": chunks, "O": Ot, "On": On, "last": h == 3})
                  run_attn(jobs, False, 6)
                  if half == 0:
                      tap(f"cat{l}", HB[:, :, 0:1024], [128, 8, 1024], BF16, [])
                  stage(f"attD_{half}")
                  wcount = 0
                  for a in range(4):
                      slot, sr = load_piece(l, NP_WO + a)
                      sv = slot[:].rearrange("p (k n) -> p k n", k=8)
                      for di in range(2):
                          dch = 2 * a + di
                          for c in hgroups:
                              s0, Lc = GROUPS[c]
                              h0, _ = hcols(c, half)
                              s = sidx(c)
                              pn = ["P0a", "P0b", "P1a", "P1b"][wcount % 4]
                              wcount += 1
                              pw = PSV[pn][:, 0:Lc]
                              for k in range(8):
                                  mm(pw, sv[:, k, di * 128:(di + 1) * 128], HB[:, k, h0:h0 + Lc], k == 0, k == 7,
                                     [sr, hbres(c)], [pn])
                              stt(X[:, dch, s0:s0 + Lc], pw, G1[:, dch, s:s + 1], X[:, dch, s0:s0 + Lc], ALU.mult, ALU.add,
                                  [pn, MODn, xres(c)], [xres(c)])
                  stage(f"wout_{half}")
                  if half == 1:
                      S.barrier()
              advance_mod(64)
              stage("wout")
              tap(f"x1_{l}", X[:], [128, 8, NTOK], F32, [])

              def h2dst(k, c):
                  s0, Lc = GROUPS[c]
                  if c < 2:
                      return HB[:, k, s0:s0 + Lc], hbres(c)
                  return H2B[:, k, s0 - 1024:s0 - 1024 + Lc], ("H2B", c)

              for c in groups_all:
                  do_norm(c, A2, SH2, h2dst, "A2")
              stage("norm2")
              gcount = 0
              dcount = 0
              for bi, blk in enumerate(FFBLOCKS):
                  for fl, f in enumerate(blk):
                      slot, sr = load_piece(l, NP_GU + f)
                      sv = slot[:].rearrange("p (u k n) -> p u k n", u=2, k=8)
                      for c in groups_all:
                          s0, Lc = GROUPS[c]
                          i = gcount % 2
                          gcount += 1
                          pgn, pun = ("P0a", "P0b") if i == 0 else ("P1a", "P1b")
                          pg, pu = PSV[pgn][:, 0:Lc], PSV[pun][:, 0:Lc]
                          for k in range(8):
                              hsrc, hres = h2dst(k, c)
                              mm(pg, sv[:, 0, k, :], hsrc, k == 0, k == 7, [sr, hres], [pgn])
                          for k in range(8):
                              hsrc, hres = h2dst(k, c)
                              mm(pu, sv[:, 1, k, :], hsrc, k == 0, k == 7, [sr, hres], [pun])
                          sg, sgn = (Fs[0], "F0") if i == 0 else (Fs[1], "F1")
                          act(sg[:, 0:Lc], pg, AF.Silu, [pgn], [sgn])
                          tt(ACTB[:, fl, s0:s0 + Lc], pu, sg[:, 0:Lc], ALU.mult, [sgn, pun], [("ACTB", fl, c)])
                  nf = len(blk)
                  for dch in range(8):
                      slot, sr = load_piece(l, NP_WD + bi * 8 + dch, nf * 128)
                      sv = slot[:, 0:nf * 128].rearrange("p (f n) -> p f n", f=nf)
                      for c in groups_all:
                          s0, Lc = GROUPS[c]
                          s = sidx(c)
                          pn = ["P2a", "P2b", "P3"][dcount % 3]
                          dcount += 1
                          pd = PSV[pn][:, 0:Lc]
                          for fl in range(nf):
                              mm(pd, sv[:, fl, :], ACTB[:, fl, s0:s0 + Lc], fl == 0, fl == nf - 1, [sr, ("ACTB", fl, c)], [pn])
                          stt(X[:, dch, s0:s0 + Lc], pd, G2[:, dch, s:s + 1], X[:, dch, s0:s0 + Lc], ALU.mult, ALU.add,
                              [pn, MODn, xres(c)], [xres(c)])
              S.barrier()
              stage("ffn")
              tap(f"x2_{l}", X[:], [128, 8, NTOK], F32, [])


        try:
            run_layers()
        except _Stop:
            pass
        for t in range(16):
            Pt = P0 if t % 2 == 0 else P1
            pr = ["P0a", "P0b"] if t % 2 == 0 else ["P1a", "P1b"]
            for k in range(8):
                tr(Pt[:, k * 128:(k + 1) * 128], X[:, k, t * 128:(t + 1) * 128], IDF[:], [xres(t // 4), "IDF"], pr)
            stg = SCR[:, (t % 2) * 1024:(t % 2 + 1) * 1024]
            stg_r = ["F0", "F1"] if t % 2 == 0 else ["F2", "F3"]
            if t % 2 == 0:
                act(stg, Pt[:], AF.Copy, pr, stg_r)
            else:
                cpv(stg, Pt[:], pr, stg_r)
            dma_sp(out_d[t * 128:(t + 1) * 128, :], stg, stg_r, [("out", t)])
        S.emit(st)
    return nc, tap_d


_CACHE = {}


def prep_inputs(inp, n_layers=2):
    inp = {k: np.asarray(v, dtype=np.float32) for k, v in inp.items()}
    cbf, rope, pm = _consts()
    shared = {"cbf": np.ascontiguousarray(cbf.reshape(128, -1)), "idf": np.eye(128, dtype=np.float32),
              "rope": np.ascontiguousarray(rope.reshape(128, -1)), "pm": np.ascontiguousarray(pm.reshape(128, -1))}
    for l in range(n_layers):
        la = _layer_arrays(inp, l)
        for k, v in la.items():
            shared[f"{k}{l}"] = v
    maps = []
    for b in range(8):
        m = dict(shared)
        m["x"] = np.ascontiguousarray(inp["x"][b])
        m["ctx"] = np.ascontiguousarray(inp["ctx"][b])
        cv = np.zeros((128, 8, 2), np.float32)
        cv[:, :, 0] = inp["c"][b].reshape(8, 128).T
        cv[:, :, 1] = inp["c_ctx"].reshape(8, 128).T
        m["cvec"] = np.ascontiguousarray(cv.reshape(128, 16))
        maps.append(m)
    return maps


def kernel(**inputs):
    if "nc" not in _CACHE:
        _CACHE["nc"] = build(2)[0]
    nc = _CACHE["nc"]
    maps = prep_inputs(inputs, 2)
    res = run_bass_kernel_spmd(nc, maps, core_ids=list(range(8)))
    return np.stack([np.asarray(r["out"], dtype=np.float32) for r in res.results], 0)
```

```python
import contextlib
import numpy as np
import concourse.bass as bass
import concourse.mybir as mybir
from concourse.bass_utils import run_bass_kernel_spmd

F32 = mybir.dt.float32
BF16 = mybir.dt.bfloat16
AF = mybir.ActivationFunctionType
ALU = mybir.AluOpType

D = 1024
SEQ = 2048
CTX = 256
NTOK = SEQ + CTX
FF = 2816
NFF = 22
NEG = -30000.0
EPS = 1e-6
GROUPS = [(0, 512), (512, 512), (1024, 512), (1536, 512), (2048, 256)]
HALVES = [[0, 1], [2, 3, 4]]
HSTART = [0, 1024]
FFBLOCKS = [list(range(0, 8)), list(range(8, 15)), list(range(15, 22))]
PIECE = 2048
import os as _os
_DBG_DELAY = int(_os.environ.get('DBG_DELAY', '0'))
_DBG_SKIP = set(int(v) for v in _os.environ.get('DBG_SKIP_A', '').split(',') if v)


class _Op:
    __slots__ = ("eng", "fn", "dma", "deps", "has_dep", "sig")

    def __init__(self, eng, fn, dma):
        self.eng = eng
        self.fn = fn
        self.dma = dma
        self.deps = []
        self.has_dep = False
        self.sig = None


class Sched:
    COMPUTE = ("pe", "act", "dve")

    def __init__(self, nc, n_dma_sems=8):
        self.nc = nc
        self.ops = []
        self.res = {}
        self.n_dma_sems = n_dma_sems
        self.last = {}
        self.pending = {}
        self.dma_since = []

    def op(self, eng, fn, reads=(), writes=(), dma=False):
        o = _Op(eng, fn, dma)
        deps = {}
        for r in reads:
            st = self.res.get(r)
            if st is not None and st[0] is not None:
                deps[id(st[0])] = (st[0], True)
        for w in writes:
            st = self.res.get(w)
            if st is not None:
                if st[0] is not None and id(st[0]) not in deps:
                    deps[id(st[0])] = (st[0], False)
                for rd in st[1]:
                    if id(rd) not in deps:
                        deps[id(rd)] = (rd, False)
        for d in self.pending.pop(eng, ()):
            if id(d) not in deps:
                deps[id(d)] = (d, True)
        for d, raw in deps.values():
            if d is o:
                continue
            if not d.dma and not o.dma and d.eng == eng:
                if eng == "pe":
                    continue
            o.deps.append(d)
            d.has_dep = True
        for r in reads:
            st = self.res.get(r)
            if st is None:
                self.res[r] = [None, [o]]
            else:
                st[1].append(o)
        for w in writes:
            self.res[w] = [o, []]
        self.ops.append(o)
        self.last[eng] = o
        if dma:
            self.dma_since.append(o)
        return o

    def barrier(self):
        lasts = [o for e, o in self.last.items() if e in self.COMPUTE]
        lasts += self.dma_since
        self.dma_since = []
        for e in self.COMPUTE:
            self.pending[e] = self.pending.get(e, []) + lasts

    def emit(self, stack):
        nc = self.nc
        eobj = {"pe": nc.tensor, "act": nc.scalar, "dve": nc.vector, "pool": nc.gpsimd, "sp": nc.sync}
        semh = {}
        cnt = {}
        waited = {e: {} for e in eobj}
        dma_rr = {}
        dcnt = {}

        def get_sem(key):
            if key not in semh:
                semh[key] = stack.enter_context(nc.semaphore("s_" + "_".join(str(k) for k in key)))
            return semh[key]

        for o in self.ops:
            E = eobj[o.eng]
            need = {}
            for d in o.deps:
                key, val = d.sig
                if need.get(key, 0) < val:
                    need[key] = val
            if o.dma:
                i = dma_rr.get(o.eng, 0)
                dma_rr[o.eng] = (i + 1) % self.n_dma_sems
                k = ("dma", o.eng, i)
                n = dcnt.get(k, 0) + 1
                dcnt[k] = n
                if n > 1 and need.get(k, 0) < 16 * (n - 1):
                    need[k] = 16 * (n - 1)
            for key, val in need.items():
                if waited[o.eng].get(key, 0) < val:
                    E.wait_ge(get_sem(key), val)
                    waited[o.eng][key] = val
            ins = o.fn()
            if o.dma:
                ins.then_inc(get_sem(k), 16)
                o.sig = (k, 16 * n)
            elif o.has_dep:
                key = ("e", o.eng)
                cnt[key] = cnt.get(key, 0) + 1
                ins.then_inc(get_sem(key), 1)
                o.sig = (key, cnt[key])
        for key, n in dcnt.items():
            nc.sync.wait_ge(get_sem(key), 16 * n)
        for key, n in cnt.items():
            nc.sync.wait_ge(get_sem(key), n)


def _kmajor(w):
    K, N = w.shape
    return np.ascontiguousarray(w.reshape(K // 128, 128, N).transpose(1, 0, 2))


def _pad_piece(a):
    a = np.ascontiguousarray(a, dtype=np.float32).reshape(128, -1)
    out = np.zeros((128, PIECE), np.float32)
    out[:, : a.shape[1]] = a
    return out


def _dslot(interior, s):
    if 3 <= s <= 9:
        return 8 + (9 - s)
    if s >= 10:
        return (0 if interior else 4) + (13 - s)
    return (18 if interior else 15) + (2 - s)


def _consts():
    c = np.zeros((128, 6, 128), np.float32)
    c[:, 0, :] = np.eye(128)
    c[:, 1, :] = 1.0
    c[0:64, 2, 0:64] = 1.0
    c[64:128, 2, 64:128] = 1.0
    for m in range(128):
        if (m % 32) < 16:
            c[m + 16, 3, m] = -1.0
        else:
            c[m - 16, 3, m] = 1.0
    j = np.arange(128)[:, None]
    i = np.arange(128)[None, :]
    c[:, 4, :] = np.where(j >= i, 0.0, NEG)
    c[:, 5, :] = np.where(j <= i, 0.0, NEG)
    p = np.arange(128)
    idx = p % 64
    f = (idx % 16).astype(np.float64)
    inv = np.power(10000.0, -f / 16.0)
    t = np.arange(SEQ)
    pos = np.where((idx < 32)[:, None], (t // 64)[None, :], (t % 64)[None, :]).astype(np.float64)
    ang = pos * inv[:, None]
    rope = np.stack([np.cos(ang), np.sin(ang)], axis=1).astype(np.float32)
    pm = np.zeros((128, 20, 128), np.float32)
    jj = np.arange(128)[:, None]
    tt = np.arange(128)[None, :]
    for g, w in enumerate((2, 4, 8, 16)):
        h = w // 2
        eye = (jj == tt).astype(np.float32)
        pm[:, g * 5 + 0, :] = ((jj >= tt - h) & (jj < tt + h)) / w - eye
        lo = np.maximum(tt - h, 0)
        cntf = (tt + h) - lo
        pm[:, g * 5 + 1, :] = ((jj >= lo) & (jj < tt + h)) / cntf - eye
        hi = np.minimum(tt + h, 128)
        cntl = hi - (tt - h)
        pm[:, g * 5 + 2, :] = ((jj >= tt - h) & (jj < hi)) / cntl - eye
        pm[:, g * 5 + 3, :] = (jj >= 128 + tt - h) / w
        pm[:, g * 5 + 4, :] = (jj < tt + h - 128) / w
    return c, rope, pm


def _dtab(rpb):
    kc = np.arange(64)[:, None]
    qc = np.arange(64)[None, :]
    cs = np.clip(qc - 8, 0, 48)
    col_ok = (kc >= cs) & (kc < cs + 16)
    coff = np.clip(kc - qc, -15, 15) + 15
    out = np.full((128, 4, 21, 64), NEG, np.float32)
    for h in range(4):
        for interior in (True, False):
            for s in range(14):
                sl = _dslot(interior, s)
                for kr in range(2):
                    rho = s + kr
                    ok = 0 <= rho <= 14 and ((3 <= rho <= 10) or not interior)
                    if not ok:
                        continue
                    T = np.where(col_ok, rpb[h, rho][coff], NEG)
                    out[kr * 64:(kr + 1) * 64, h, sl, :] = T
    return out


def _layer_arrays(inp, l):
    f32 = np.float32
    w_in = inp["w_in"][l]
    pieces = []
    wm = _kmajor(inp["w_mod"][l])
    for jj in range(24):
        pieces.append(_pad_piece(wm[:, :, jj * 256:(jj + 1) * 256]))
    aq = w_in[:, 0:256]
    fm_cols = [np.concatenate([aq[:, 0:64], aq[:, 128:192]], 1), np.concatenate([aq[:, 64:128], aq[:, 192:256]], 1),
               w_in[:, 256:384], w_in[:, 1280:1408], w_in[:, 1408:1536], w_in[:, 1536:1664], w_in[:, 1664:1792]]
    fm = _kmajor(np.concatenate(fm_cols, 1))
    for a in range(4):
        pieces.append(_pad_piece(fm[:, :, a * 256:min((a + 1) * 256, 896)]))
    tm = [w_in[:, 512:768], w_in[:, 768:1024], w_in[:, 1024:1280], w_in[:, 1792:2048], w_in[:, 384:512]]
    for a in tm:
        pieces.append(_pad_piece(_kmajor(a)))
    wo = _kmajor(inp["w_out"][l])
    for a in range(4):
        pieces.append(_pad_piece(wo[:, :, a * 256:(a + 1) * 256]))
    wg = _kmajor(inp["w_gate"][l])
    wu = _kmajor(inp["w_up"][l])
    for f in range(NFF):
        pieces.append(_pad_piece(np.concatenate([wg[:, :, f * 128:(f + 1) * 128].reshape(128, -1),
                                                 wu[:, :, f * 128:(f + 1) * 128].reshape(128, -1)], 1)))
    wd = _kmajor(inp["w_down"][l])
    for blk in FFBLOCKS:
        for dch in range(8):
            pieces.append(_pad_piece(wd[:, blk[0]:blk[-1] + 1, dch * 128:(dch + 1) * 128]))
    W = np.stack(pieces, 0)
    vec = np.zeros((128, 26), f32)
    vec[:, 0:8] = inp["g_mix"][l].reshape(8, 128).T
    vec[:, 8:16] = inp["g_ffn"][l].reshape(8, 128).T
    vec[:, 16] = np.tile(inp["a_q_gain"][l], 2)
    vec[:, 17] = np.tile(inp["a_k_gain"][l], 2)
    vec[:, 18] = np.tile(inp["d_q_gain"][l], 2)
    vec[:, 19] = np.tile(inp["d_k_gain"][l], 2)
    vec[:, 20:22] = inp["c_scale"][l].reshape(2, 128).T
    vec[:, 22:26] = inp["b_b_s"][l].T
    bmod = np.ascontiguousarray(inp["b_mod"][l].reshape(48, 128).T)
    bc = np.zeros((128, 260), f32)
    bc[:, 0:4] = inp["a_sink"][l][None, :]
    bc[:, 4:260] = inp["b_v_gain"][l][None, :]
    sm = np.zeros((128, 6, 128), f32)
    sm[:, 0:4, :] = inp["b_w_s"][l].transpose(2, 0, 1)
    wp = inp["c_w_pool"][l]
    for g in range(4):
        o = (g % 2) * 64
        sm[o:o + 64, 4 + g // 2, o:o + 64] = wp[g]
    dt = _dtab(inp["d_rpb"][l])
    return {"W": W, "vec": vec, "bmod": bmod, "bc": bc, "sm": np.ascontiguousarray(sm.reshape(128, -1)), "dtab": np.ascontiguousarray(dt.reshape(128, -1))}


NP_MOD, NP_FM, NP_TM, NP_WO, NP_GU, NP_WD = 0, 24, 28, 33, 37, 59
NPIECES = 83


class _Stop(Exception):
    pass


def build(n_layers=2, taps=(), stop=None):
    nc = bass.Bass("TRN2", target_bir_lowering=False)
    dr = {}

    def din(name, shape):
        dr[name] = nc.dram_tensor(name, list(shape), F32, kind="ExternalInput").ap()
        return dr[name]

    x_d = din("x", [SEQ, D])
    ctx_d = din("ctx", [CTX, D])
    cvec_d = din("cvec", [128, 16])
    cbf_d = din("cbf", [128, 6 * 128])
    idf_d = din("idf", [128, 128])
    rope_d = din("rope", [128, 2 * SEQ])
    pm_d = din("pm", [128, 20 * 128])
    L = []
    for l in range(n_layers):
        L.append({
            "W": din(f"W{l}", [NPIECES, 128, PIECE]), "vec": din(f"vec{l}", [128, 26]), "bmod": din(f"bmod{l}", [128, 48]),
            "bc": din(f"bc{l}", [128, 260]), "sm": din(f"sm{l}", [128, 6 * 128]), "dtab": din(f"dtab{l}", [128, 4 * 21 * 64]),
        })
    out_d = nc.dram_tensor("out", [SEQ, D], F32, kind="ExternalOutput").ap()
    tap_d = {}

    st = contextlib.ExitStack()
    with st:
        S = Sched(nc)
        sb = lambda n, s, d: st.enter_context(nc.sbuf_tensor(n, list(s), d))
        ps = lambda n, s, d: st.enter_context(nc.psum_tensor(n, list(s), d))
        X = sb("X", [128, 8, NTOK], F32)
        HB = sb("HB", [128, 8, 1280], BF16)
        RR = sb("RR", [128, 32472], BF16)
        QT = RR[:, 0:9216].rearrange("p (k t) -> p k t", k=4)
        KT = RR[:, 9216:16128].rearrange("p (k t) -> p k t", k=3)
        VA = RR[:, 16128:18504].rearrange("p (t h e) -> p t h e", t=18, h=2)
        VD = RR[:, 18504:23256].rearrange("p (t h e) -> p t h e", t=18, h=4)
        OB = RR[:, 23256:27864].rearrange("p (t c) -> p t c", t=18)
        CY = RR[:, 27864:32472].rearrange("p (t c) -> p t c", t=18)
        ACTB = RR[:, 0:18432].rearrange("p (f t) -> p f t", f=8)
        H2B = RR[:, 18432:28672].rearrange("p (k t) -> p k t", k=8)
        RING = [sb(f"ring{i}", [128, PIECE], BF16) for i in range(4)]
        TAB = sb("TAB", [128, 5376], BF16)
        SCR = sb("SCR", [128, 2048], F32)
        Fs = [SCR[:, i * 512:(i + 1) * 512] for i in range(4)]
        SQ = [sb(f"SQ{i}", [128, 512], BF16) for i in range(2)]
        QG = sb("QG", [128, 512], BF16)
        RQ = sb("RQ", [128, 512], BF16)
        PT = [sb(f"PT{i}", [128, 896], BF16) for i in range(2)]
        OT = [sb(f"OT{i}", [128, 256], BF16) for i in range(2)]
        PC = sb("PC", [128, 2, 512], BF16)
        QZ = PC[:].rearrange("p a (b q) -> p (a b) q", q=128)
        VN = [sb(f"VN{i}", [128, 256], BF16) for i in range(2)]
        CBF = sb("CBF", [128, 6, 128], BF16)
        IDF = sb("IDF", [128, 128], F32)
        CV = sb("CV", [128, 8, 2], F32)
        CA = sb("CA", [128, 8, 2], BF16)
        SMALL = sb("SMALL", [128, 64], F32)
        IDB = CBF[:, 0, :]
        ONES = CBF[:, 1, :]
        BDONES = CBF[:, 2, :]
        ROTT = CBF[:, 3, :]
        MASKP = CBF[:, 4, :]
        MASKN = CBF[:, 5, :]
        P0 = ps("P0", [128, 1024], F32)
        P1 = ps("P1", [128, 1024], F32)
        P2 = ps("P2", [128, 1024], F32)
        P3 = ps("P3", [128, 512], F32)
        PST = ps("PST", [128, 1024], BF16)
        PSV = {"P0a": P0[:, 0:512], "P0b": P0[:, 512:1024], "P1a": P1[:, 0:512], "P1b": P1[:, 512:1024],
               "P2a": P2[:, 0:512], "P2b": P2[:, 512:1024], "P3": P3[:, :]}

        def mm(out, lhsT, rhs, start, stop, reads, writes):
            S.op("pe", lambda: nc.tensor.matmul(out, lhsT=lhsT, rhs=rhs, start=start, stop=stop), reads, writes)

        def tr(out, in_, ident, reads, writes):
            S.op("pe", lambda: nc.tensor.transpose(out, in_, ident), reads, writes)

        def act(out, in_, func, reads, writes, scale=None, bias=None):
            kw = {}
            if scale is not None:
                kw["scale"] = scale
            if bias is not None:
                kw["bias"] = bias
            S.op("act", lambda: nc.scalar.activation(out=out, in_=in_, func=func, **kw), reads, writes)

        def tt(out, in0, in1, op, reads, writes):
            S.op("dve", lambda: nc.vector.tensor_tensor(out=out, in0=in0, in1=in1, op=op), reads, writes)

        def stt(out, in0, scalar, in1, op0, op1, reads, writes):
            S.op("dve", lambda: nc.vector.scalar_tensor_tensor(out=out, in0=in0, scalar=scalar, in1=in1, op0=op0, op1=op1),
                 reads, writes)

        def ts(out, in0, s1, s2, op0, op1, reads, writes):
            S.op("dve", lambda: nc.vector.tensor_scalar(out=out, in0=in0, scalar1=s1, scalar2=s2, op0=op0, op1=op1),
                 reads, writes)

        def cpv(out, in_, reads, writes):
            S.op("dve", lambda: nc.vector.tensor_copy(out=out, in_=in_), reads, writes)

        def dma_sp(out, in_, reads, writes):
            S.op("sp", lambda: nc.sync.dma_start(out=out, in_=in_), reads, writes, dma=True)

        def dma_pool(out, in_, reads, writes):
            S.op("pool", lambda: nc.gpsimd.dma_start(out=out, in_=in_, max_dma_last_dim=4096), reads, writes, dma=True)

        def stage(name):
            if stop == name:
                S.barrier()
                raise _Stop()

        def tap(name, ap, shape, dtype, reads):
            if name not in taps:
                return
            S.barrier()
            t = nc.dram_tensor("tap_" + name, list(shape), dtype, kind="ExternalOutput").ap()
            tap_d[name] = t
            dma_sp(t, ap, list(reads) + ["tapsrc"], ["tap_" + name])

        ring_i = [0]
        ring_serial = {}

        def load_piece(l, idx, nelem=PIECE):
            i = ring_i[0]
            ring_i[0] = (i + 1) % 4
            dma_pool(RING[i][:, 0:nelem], L[l]["W"][idx, :, 0:nelem], [], [("slot", i)])
            ring_serial[("slot", i)] = ring_serial.get(("slot", i), 0) + 1
            return RING[i], ("slot", i)

        dma_sp(IDF[:], idf_d, [], ["IDF"])
        dma_sp(CV[:].rearrange("p k s -> p (k s)"), cvec_d, [], ["CV"])
        dma_pool(CBF[:].rearrange("p a b -> p (a b)"), cbf_d, [], ["CBF"])
        act(CA[:], CV[:], AF.Silu, ["CV"], ["CA"])
        for t in range(18):
            stg = SCR[:, (t % 2) * 1024:(t % 2 + 1) * 1024]
            stg_r = ["F0", "F1"] if t % 2 == 0 else ["F2", "F3"]
            src = x_d[t * 128:(t + 1) * 128, :] if t < 16 else ctx_d[(t - 16) * 128:(t - 15) * 128, :]
            dma_sp(stg, src, [], stg_r)
            Pt = P0 if t % 2 == 0 else P1
            pr = ["P0a", "P0b"] if t % 2 == 0 else ["P1a", "P1b"]
            for k in range(8):
                tr(Pt[:, k * 128:(k + 1) * 128], stg[:, k * 128:(k + 1) * 128], IDF[:], stg_r + ["IDF"], pr)
            dst = X[:, :, t * 128:(t + 1) * 128]
            src_ps = Pt[:].rearrange("p (k t) -> p k t", k=8)
            xr = [("X", t // 4 if t < 16 else 4)]
            if t % 2 == 0:
                act(dst, src_ps, AF.Copy, pr, xr)
            else:
                cpv(dst, src_ps, pr, xr)
        S.barrier()

        def xres(c):
            return ("X", c)

        def hbres(c):
            return ("HB", 2) if c == 4 else ("HB", c % 2)

        def grp_of_tile(t):
            return t // 4 if t < 16 else 4

        def sidx(c):
            return 1 if c == 4 else 0

        VEC = sb("VEC", [128, 26], F32)
        BMODs = [sb(f"BMOD{i}", [128, 48], F32) for i in range(2)]
        BC = sb("BC", [128, 260], F32)
        SM = sb("SM", [128, 6, 128], BF16)
        MODs = [sb(f"MOD{i}", [128, 48, 2], F32) for i in range(2)]
        A1 = sb("A1", [128, 8, 2], F32)
        A2 = sb("A2", [128, 8, 2], F32)
        GS = sb("GS", [128, 8], F32)

        def mod_gen(l, jj0=0, jj1=24):
            b = l % 2
            if jj0 == 0:
                dma_sp(BMODs[b][:], L[l]["bmod"], [], [f"BMOD{b}"])
            for jj in range(jj0, jj1):
                slot, sr = load_piece(l, NP_MOD + jj)
                sv = slot[:].rearrange("p (k n) -> p k n", k=8)
                for cc in range(2):
                    j = 2 * jj + cc
                    for k in range(8):
                        mm(P3[:, 2 * j:2 * j + 2], sv[:, k, cc * 128:(cc + 1) * 128], CA[:, k, :], k == 0, k == 7,
                           [sr, "CA"], ["P3"])
                yield
            c0, c1 = 2 * jj0, 2 * jj1
            tt(MODs[b][:, c0:c1, :], P3[:, 2 * c0:2 * c1].rearrange("p (j s) -> p j s", s=2),
               BMODs[b][:, c0:c1].unsqueeze(2).to_broadcast([128, c1 - c0, 2]), ALU.add, ["P3", f"BMOD{b}"], [f"MOD{b}"])
            yield

        def run_layers():
          for l in range(n_layers):
              last = l == n_layers - 1
              Ld = L[l]
              dma_sp(VEC[:], Ld["vec"], [], ["VEC"])
              dma_sp(BC[:], Ld["bc"], [], ["BC"])
              dma_pool(SM[:].rearrange("p a b -> p (a b)"), Ld["sm"], [], ["SM"])
              WST = SM[:, 0:4, :]
              WPBD = SM[:, 4:6, :]

              MOD = MODs[l % 2]
              MODn = f"MOD{l % 2}"
              late0 = [None]
              if l == 0:
                  for _ in mod_gen(0, 0, 8):
                      pass
                  late0[0] = mod_gen(0, 8, 24)
              pre = [mod_gen(l + 1) if not last else None]

              def _adv(box, n):
                  for _ in range(n):
                      if box[0] is not None:
                          try:
                              next(box[0])
                          except StopIteration:
                              box[0] = None

              def advance_mod(n=1):
                  _adv(pre, n)
              stt(A1[:], MOD[:, 8:16, :], 1.0, VEC[:, 0:8].unsqueeze(2).to_broadcast([128, 8, 2]), ALU.add, ALU.mult,
                  [MODn, "VEC"], ["A1"])
              ts(GS[:, 0:1], VEC[:, 16:17], 0.125, None, ALU.mult, ALU.bypass, ["VEC"], ["GS"])
              ts(GS[:, 1:2], VEC[:, 17:18], 1.0, None, ALU.mult, ALU.bypass, ["VEC"], ["GS"])
              ts(GS[:, 2:3], VEC[:, 18:19], 0.125, None, ALU.mult, ALU.bypass, ["VEC"], ["GS"])
              ts(GS[:, 3:4], VEC[:, 19:20], 1.0, None, ALU.mult, ALU.bypass, ["VEC"], ["GS"])
              act(GS[:, 4:8], BC[:, 0:4], AF.Exp, ["BC"], ["GS"])
              SH1, G1, SH2, G2 = MOD[:, 0:8, :], MOD[:, 16:24, :], MOD[:, 24:32, :], MOD[:, 40:48, :]
              stage("modload")
              tap(f"mod{l}", MOD[:], [128, 48, 2], F32, [MODn])

              stage("mod")
              def do_norm(c, Amul, SHv, dst_fn, tag):
                  s0, Lc = GROUPS[c]
                  s = sidx(c)
                  psn, psr = (PSV["P2a"], "P2a") if c % 2 == 0 else (PSV["P2b"], "P2b")
                  for k in range(8):
                      sq = SQ[k % 2]
                      act(sq[:, 0:Lc], X[:, k, s0:s0 + Lc], AF.Square, [xres(c)], [f"SQ{k % 2}"])
                      mm(psn[:, 0:Lc], ONES, sq[:, 0:Lc], k == 0, k == 7, [f"SQ{k % 2}", "CBF"], [psr])
                  act(Fs[0][:, 0:Lc], psn[:, 0:Lc], AF.Ln, [psr], ["F0"], scale=1.0 / D, bias=EPS)
                  act(Fs[1][:, 0:Lc], Fs[0][:, 0:Lc], AF.Exp, ["F0"], ["F1"], scale=-0.5)
                  for k in range(8):
                      tmp, tr_ = (Fs[2], "F2") if k % 2 == 0 else (Fs[3], "F3")
                      tt(tmp[:, 0:Lc], X[:, k, s0:s0 + Lc], Fs[1][:, 0:Lc], ALU.mult, [xres(c), "F1"], [tr_])
                      dst, dres = dst_fn(k, c)
                      act(dst, tmp[:, 0:Lc], AF.Identity, [tr_, MODn, tag], [dres], scale=Amul[:, k, s:s + 1],
                          bias=SHv[:, k, s:s + 1])

              groups_all = [0, 1, 2, 3] if last else [0, 1, 2, 3, 4]

              dma_pool(TAB[:, 0:4096], rope_d, [], ["TAB"])
              COS = TAB[:, 0:2048]
              SIN = TAB[:, 2048:4096]
              S.op("dve", lambda: nc.vector.memset(VA[:, :, :, 64:65], 1.0), [], ["VAones"])
              S.op("dve", lambda: nc.vector.memset(VD[:, :, :, 64:65], 1.0), [], ["VDones"])

              def hcols(c, half):
                  s0, Lc = GROUPS[c]
                  return s0 - HSTART[half], Lc

              for half in range(2):
                  hgroups = HALVES[half]
                  for c in hgroups:
                      h0, Lc = hcols(c, half)
                      do_norm(c, A1, SH1, lambda k, c, h0=h0, Lc=Lc: (HB[:, k, h0:h0 + Lc], hbres(c)), "A1")
                  if half == 0:
                      tap(f"hT{l}", HB[:, :, 0:1024], [128, 8, 1024], BF16, [("HB", 0), ("HB", 1)])
                  stage(f"norm1_{half}")
                  combos = []
                  for a in range(4):
                      js = [2 * a, 2 * a + 1] if a < 3 else [6]
                      for ji, j in enumerate(js):
                          for c in hgroups:
                              if last and c == 4 and j in (0, 1, 3, 4):
                                  continue
                              combos.append((a, len(js), ji, j, c))
                  fm_slot = {}

                  def fm_piece(a, njs):
                      if a not in fm_slot:
                          slot, sr = load_piece(l, NP_FM + a, 8 * 128 * njs)
                          fm_slot[a] = (slot[:, 0:8 * 128 * njs].rearrange("p (k n) -> p k n", k=8), sr, ring_serial[sr])
                      assert ring_serial[fm_slot[a][1]] == fm_slot[a][2], "FM weight slot recycled while still in use"
                      return fm_slot[a][0], fm_slot[a][1]

                  def fm_info(k):
                      a, njs, ji, j, c = combos[k]
                      s0, Lc = GROUPS[c]
                      h0, _ = hcols(c, half)
                      pfn = ["P0a", "P0b", "P1a"][k % 3]
                      phn = ["P1b", "P2a"][k % 2]
                      prn = "P2b"
                      gcol = {0: 0, 1: 0, 2: 1, 3: 2, 4: 2, 5: 3, 6: 3}[j]
                      if j < 2:
                          dst, dres = QT[:, j, s0:s0 + Lc], ("QT", j, c)
                      elif j == 2:
                          dst, dres = KT[:, 0, s0:s0 + Lc], ("KT", 0, c)
                      elif j < 5:
                          dst, dres = QT[:, j - 1, s0:s0 + Lc], ("QT", j - 1, c)
                      else:
                          dst, dres = KT[:, j - 4, s0:s0 + Lc], ("KT", j - 4, c)
                      return dict(a=a, njs=njs, ji=ji, j=j, c=c, s0=s0, Lc=Lc, h0=h0, pfn=pfn, phn=phn, prn=prn,
                                  pf=PSV[pfn][:, 0:Lc], ph=PSV[phn][:, 0:Lc], pr=PSV[prn][:, 0:Lc],
                                  gain=GS[:, gcol:gcol + 1], dst=dst, dres=dres, rope=(j < 3 and c != 4),
                                  sq=SQ[k % 2], sqn=f"SQ{k % 2}")

                  def fm_head(k):
                      f = fm_info(k)
                      sv, sr = fm_piece(f["a"], f["njs"])
                      for kk in range(8):
                          mm(f["pf"], sv[:, kk, f["ji"] * 128:(f["ji"] + 1) * 128], HB[:, kk, f["h0"]:f["h0"] + f["Lc"]],
                             kk == 0, kk == 7, [sr, ("HB", f["c"])], [f["pfn"]])
                      act(f["sq"][:, 0:f["Lc"]], f["pf"], AF.Square, [f["pfn"]], [f["sqn"]])

                  def fm_qg(k):
                      f = fm_info(k)
                      if f["rope"]:
                          act(QG[:, 0:f["Lc"]], f["pf"], AF.Identity, [f["pfn"], "GS"], ["QG"], scale=f["gain"])

                  def fm_mid(k):
                      f = fm_info(k)
                      Lc, s0 = f["Lc"], f["s0"]
                      mm(f["ph"], BDONES, f["sq"][:, 0:Lc], True, True, [f["sqn"], "CBF"], [f["phn"]])
                      if f["rope"]:
                          mm(f["pr"], ROTT, QG[:, 0:Lc], True, True, ["QG", "CBF"], [f["prn"]])
                          tt(Fs[2][:, 0:Lc], QG[:, 0:Lc], COS[:, s0:s0 + Lc], ALU.mult, ["QG", "TAB"], ["F2"])

                  def fm_tail(k):
                      f = fm_info(k)
                      Lc, s0 = f["Lc"], f["s0"]
                      act(Fs[0][:, 0:Lc], f["ph"], AF.Ln, [f["phn"]], ["F0"], scale=1.0 / 64, bias=EPS)
                      act(Fs[1][:, 0:Lc], Fs[0][:, 0:Lc], AF.Exp, ["F0"], ["F1"], scale=-0.5)
                      if f["rope"]:
                          act(RQ[:, 0:Lc], f["pr"], AF.Copy, [f["prn"]], ["RQ"])
                          tt(Fs[3][:, 0:Lc], RQ[:, 0:Lc], SIN[:, s0:s0 + Lc], ALU.mult, ["RQ", "TAB"], ["F3"])
                          tt(Fs[2][:, 0:Lc], Fs[2][:, 0:Lc], Fs[3][:, 0:Lc], ALU.add, ["F2", "F3"], ["F2"])
                          tt(f["dst"], Fs[2][:, 0:Lc], Fs[1][:, 0:Lc], ALU.mult, ["F2", "F1"], [f["dres"]])
                      else:
                          stt(f["dst"], f["pf"], f["gain"], Fs[1][:, 0:Lc], ALU.mult, ALU.mult, [f["pfn"], "GS", "F1"], [f["dres"]])

                  nfm = len(combos)
                  fm_head(0)
                  fm_qg(0)
                  for k in range(nfm):
                      if k + 1 < nfm:
                          fm_head(k + 1)
                      fm_mid(k)
                      if k + 1 < nfm:
                          fm_qg(k + 1)
                      fm_tail(k)
                      if half == 0:
                          _adv(late0, 1)
                  _adv(late0, 64)
                  stage(f"fm_{half}")
                  tiles_h = []
                  for c in hgroups:
                      s0, Lc = GROUPS[c]
                      tiles_h += [(t, c) for t in range(s0 // 128, (s0 + Lc) // 128)]
                  tcount = [0]

                  def tm_proj(t, c, sv, srs, ncol):
                      sr, ser = srs
                      assert ring_serial[sr] == ser, "TM weight slot recycled while still in use"
                      h0 = t * 128 - HSTART[half]
                      pn = ["P0a", "P0b", "P1a", "P1b"][tcount[0] % 4]
                      tcount[0] += 1
                      pp = PSV[pn][:, 0:ncol]
                      for k in range(8):
                          mm(pp, HB[:, k, h0:h0 + 128], sv[:, k, :], k == 0, k == 7, [sr, hbres(c)], [pn])
                      return pp, pn

                  def tm_piece(pi):
                      ncol = 128 if pi == 4 else 256
                      slot, sr = load_piece(l, NP_TM + pi, 8 * ncol)
                      return slot[:, 0:8 * ncol].rearrange("p (k n) -> p k n", k=8), (sr, ring_serial[sr]), ncol

                  svU, srU, _ = tm_piece(0)
                  for (t, c) in tiles_h:
                      if last and c == 4:
                          continue
                      pp, pn = tm_proj(t, c, svU, srU, 256)
                      act(OB[:, t, :], pp, AF.Gelu_apprx_tanh, [pn], [("OB", t)])
                  svV, srV, _ = tm_piece(1)
                  svC, srC, _ = tm_piece(2)
                  svD, srD, _ = tm_piece(3)
                  svA, srA, _ = tm_piece(4)

                  def emit_z(t):
                      vn = VN[t % 2]
                      pzn = "P2a" if t % 2 == 0 else "P2b"
                      pz = PSV[pzn]
                      for g in range(4):
                          mm(pz[:, g * 64:(g + 1) * 64], WST[:, g, :], vn[:, g * 64:(g + 1) * 64], True, True,
                             [f"VN{t % 2}", "SM"], [pzn])
                      guf, gufn = (Fs[2], "F2") if t % 2 == 0 else (Fs[3], "F3")
                      act(guf[:, 0:256], OB[:, t, :], AF.Copy, [("OB", t)], [gufn])
                      for g in range(4):
                          stt(OB[:, t, g * 64:(g + 1) * 64], pz[:, g * 64:(g + 1) * 64], VEC[:, 22 + g:23 + g],
                              guf[:, g * 64:(g + 1) * 64], ALU.add, ALU.mult, [pzn, "VEC", gufn], [("OB", t)])

                  pending_z = None
                  for (t, c) in tiles_h:
                      full = not (last and c == 4)
                      if full:
                          pp, pn = tm_proj(t, c, svV, srV, 256)
                          gv, gvn = (Fs[0], "F0") if t % 2 == 0 else (Fs[1], "F1")
                          gv = gv[:, 0:256]
                          act(gv, pp, AF.Gelu_apprx_tanh, [pn], [gvn])
                          o = (t % 2) * 16
                          stn = f"SMALL{t % 2}"
                          S.op("dve", lambda gv=gv, o=o: nc.vector.bn_stats(out=SMALL[:, o:o + 6], in_=gv), [gvn], [stn])
                          S.op("dve", lambda o=o: nc.vector.bn_aggr(out=SMALL[:, o + 8:o + 10], in_=SMALL[:, o:o + 6]),
                               [stn], [stn + "b"])
                          act(SMALL[:, o + 10:o + 11], SMALL[:, o + 9:o + 10], AF.Ln, [stn + "b"], [stn + "c"], bias=EPS)
                          act(SMALL[:, o + 11:o + 12], SMALL[:, o + 10:o + 11], AF.Exp, [stn + "c"], [stn + "d"], scale=-0.5)
                          ts(gv, gv, SMALL[:, o + 8:o + 9], SMALL[:, o + 11:o + 12], ALU.subtract, ALU.mult,
                             [gvn, stn + "b", stn + "d"], [gvn])
                          tt(VN[t % 2][:], gv, BC[:, 4:260], ALU.mult, [gvn, "BC"], [f"VN{t % 2}"])
                          pp, pn = tm_proj(t, c, svC, srC, 256)
                          act(CY[:, t, :], pp, AF.Copy, [pn], [("CY", t)])
                      if pending_z is not None:
                          emit_z(pending_z)
                          pending_z = None
                      pp, pn = tm_proj(t, c, svD, srD, 256)
                      cpv(VD[:, t, :, 0:64], pp.rearrange("p (h e) -> p h e", h=4), [pn, "VDones"], [("VD", t)])
                      pp, pn = tm_proj(t, c, svA, srA, 128)
                      cpv(VA[:, t, :, 0:64], pp.rearrange("p (h e) -> p h e", h=2), [pn, "VAones"], [("VA", t)])
                      if full:
                          pending_z = t
                  if pending_z is not None:
                      emit_z(pending_z)
              _adv(late0, 64)
              stage("p1")
              tap(f"qt{l}", QT[:, :, :], [128, 4, NTOK], BF16, [])
              tap(f"kt{l}", KT[:, :, :], [128, 3, NTOK], BF16, [])
              tap(f"ob{l}", OB[:, :, :], [128, 18, 256], BF16, [])
              tap(f"cy{l}", CY[:, :, :], [128, 18, 256], BF16, [])
              tap(f"vd{l}", VD[:, :, :, :], [128, 18, 4, 66], BF16, [])

              for half in range(2):
                  hgroups = [c for c in HALVES[half] if c in groups_all]
                  tiles_h = []
                  for c in hgroups:
                      s0, Lc = GROUPS[c]
                      tiles_h += [(t, c) for t in range(s0 // 128, (s0 + Lc) // 128)]
                  dma_pool(TAB[:, 0:2560], pm_d, [], ["TAB"])
                  PMv = TAB[:, 0:2560].rearrange("p (a t) -> p a t", a=20)
                  for (t, c) in tiles_h:
                      h0 = t * 128 - HSTART[half]
                      pv = PST[:, (t % 2) * 512:(t % 2) * 512 + 256]
                      pvn = "PST"
                      for cc in range(2):
                          tr(pv[:, cc * 128:(cc + 1) * 128], OB[:, t, cc * 128:(cc + 1) * 128], IDB, [("OB", t), "CBF"], [pvn])
                      cpv(HB[:, 2:4, h0:h0 + 128], pv.rearrange("p (c t) -> p c t", c=2), [pvn], [hbres(c)])
                  for c in hgroups:
                      s0, Lc = GROUPS[c]
                      h0, _ = hcols(c, half)
                      first_t = 16 if c == 4 else 0
                      last_t = 17 if c == 4 else 15
                      for t in range(s0 // 128, (s0 + Lc) // 128):
                          lo = (t * 128 - s0)
                          for g in range(4):
                              srcs = []
                              if t > first_t:
                                  srcs.append((t - 1, 3))
                              srcs.append((t, 1 if t == first_t else (2 if t == last_t else 0)))
                              if t < last_t:
                                  srcs.append((t + 1, 4))
                              o = (g % 2) * 64
                              pcn = "P0a" if g < 2 else "P0b"
                              outp = P0[o:o + 64, (g // 2) * 512 + lo:(g // 2) * 512 + lo + 128]
                              for i, (tj, v) in enumerate(srcs):
                                  mm(outp, CY[:, tj, g * 64:(g + 1) * 64], PMv[:, g * 5 + v, :], i == 0, i == len(srcs) - 1,
                                     [("CY", tj), "TAB"], [pcn])
                      for cc in range(2):
                          pcn = "P0a" if cc == 0 else "P0b"
                          pln = "P1a" if cc == 0 else "P1b"
                          act(PC[:, cc, 0:Lc], P0[:, cc * 512:cc * 512 + Lc], AF.Copy, [pcn], [("PC", cc)])
                          mm(P1[:, cc * 512:cc * 512 + Lc], WPBD[:, cc, :], PC[:, cc, 0:Lc], True, True, [("PC", cc), "SM"], [pln])
                          act(HB[:, 4 + cc, h0:h0 + Lc], P1[:, cc * 512:cc * 512 + Lc], AF.Identity, [pln, "VEC"], [hbres(c)],
                              scale=VEC[:, 20 + cc:21 + cc])
                  stage(f"p2a_{half}")
                  acount = [0]

                  def att_scores(job):
                      i = acount[0] % 2
                      acount[0] += 1
                      job["i"] = i
                      Sp = P0 if i == 0 else P1
                      Sn = ["P0a", "P0b"] if i == 0 else ["P1a", "P1b"]
                      chunks = job["chunks"]
                      n = len(chunks)
                      for ci, (kt, bias) in enumerate(chunks):
                          o = Sp[:, ci * 128:(ci + 1) * 128]
                          mm(o, job["K"](kt), QZ[:, job["slot"], :], True, bias is None,
                             [("QZ", job["slot"]), ("KT", job["kidx"], grp_of_tile(kt))], Sn)
                          if bias is not None:
                              mm(o, IDB, bias, False, True, ["CBF", "TAB"], Sn)
                      act(PT[i][:, 0:n * 128], Sp[:, 0:n * 128], AF.Exp, Sn, [f"PT{i}"])

                  def att_pv(job):
                      i = job["i"]
                      chunks = job["chunks"]
                      n = len(chunks)
                      h = job["h"]
                      for ci, (kt, bias) in enumerate(chunks):
                          mm(job["O"][:, h * 66:h * 66 + 65], PT[i][:, ci * 128:(ci + 1) * 128], job["V"](kt), ci == 0, ci == n - 1,
                             [f"PT{i}", (job["vname"], kt), job["vname"] + "ones"], [job["On"]])

                  def finish_tile(t, c, Otile, Oname, sink, cat0):
                      h0 = t * 128 - HSTART[half]
                      Ov = Otile[:, 0:264].rearrange("p (h e) -> p h e", h=4)
                      o = 32 + (t % 2) * 8
                      dn = f"DEN{t % 2}"
                      if sink:
                          tt(SMALL[:, o:o + 4], Ov[:, :, 64], GS[:, 4:8], ALU.add, [Oname, "GS"], [dn])
                      else:
                          cpv(SMALL[:, o:o + 4], Ov[:, :, 64], [Oname], [dn])
                      S.op("dve", lambda o=o: nc.vector.reciprocal(out=SMALL[:, o + 4:o + 8], in_=SMALL[:, o:o + 4]), [dn], [dn + "r"])
                      ot = OT[t % 2]
                      otn = f"OT{t % 2}"
                      tt(ot[:].rearrange("p (h e) -> p h e", h=4), Ov[:, :, 0:64],
                         SMALL[:, o + 4:o + 8].unsqueeze(2).to_broadcast([128, 4, 64]), ALU.mult, [Oname, dn + "r"], [otn])
                      pv = PST[:, (t % 2) * 512:(t % 2) * 512 + 256]
                      pvn = "PST"
                      for cc in range(2):
                          tr(pv[:, cc * 128:(cc + 1) * 128], ot[:, cc * 128:(cc + 1) * 128], IDB, [otn, "CBF"], [pvn])
                      cpv(HB[:, cat0:cat0 + 2, h0:h0 + 128], pv.rearrange("p (c t) -> p c t", c=2), [pvn], [hbres(c)])

                  def run_attn(jobs, sink, cat0):
                      S.op("dve", lambda: nc.vector.memset(QZ, 0.0), [], [("PC", 0), ("PC", 1)] + [("QZ", i) for i in range(8)])
                      prev = None
                      for job in jobs + [None]:
                          if job is not None:
                              po = job["po"]
                              cpv(QZ[po:po + 64, job["slot"], :], job["Q"], [("QZ", job["slot"]), job["qres"]], [("QZ", job["slot"])])
                              att_scores(job)
                          if prev is not None:
                              att_pv(prev)
                              if prev["last"]:
                                  finish_tile(prev["t"], prev["c"], prev["O"], prev["On"], sink, cat0)
                                  advance_mod()
                          prev = job

                  def tcols(t):
                      return slice(t * 128, (t + 1) * 128)

                  jobs = []
                  for (t, c) in tiles_h:
                      Ot, On = (PSV["P2a"], "P2a") if t % 2 == 0 else (PSV["P2b"], "P2b")
                      for h in range(4):
                          kv, g = h // 2, h % 2
                          po = kv * 64
                          if c == 4:
                              chunks = [(16, None), (17, None)]
                          else:
                              chunks = []
                              if t > 0:
                                  chunks.append((t - 1, MASKP))
                              chunks.append((t, None))
                              if t < 15:
                                  chunks.append((t + 1, MASKN))
                              chunks += [(16, None), (17, None)]
                          jobs.append({"t": t, "c": c, "h": h, "K": (lambda kt: KT[:, 0, tcols(kt)]), "po": po,
                                       "kidx": 0, "qres": ("QT", g, c), "vname": "VA",
                                       "slot": (t % 2) * 4 + h,
                                       "Q": QT[po:po + 64, g, tcols(t)], "V": (lambda kt, kv=kv: VA[:, kt, kv, 0:65]),
                                       "chunks": chunks, "O": Ot, "On": On, "last": h == 3})
                  run_attn(jobs, True, 0)
                  stage(f"attA_{half}")
                  dma_pool(TAB[:, 0:5376], Ld["dtab"], [], ["TAB"])
                  DT = TAB[:, 0:5376].rearrange("p (h s q) -> p h s q", h=4, s=21)
                  jobs = []
                  for (t, c) in tiles_h:
                      Ot, On = (PSV["P2a"], "P2a") if t % 2 == 0 else (PSV["P2b"], "P2b")
                      if c == 4:
                          deltas, interior = [], False
                      elif 2 <= t <= 13:
                          deltas, interior = [-2, -1, 0, 1, 2], True
                      elif t == 0:
                          deltas, interior = [0, 1, 2, 3], False
                      elif t == 1:
                          deltas, interior = [-1, 0, 1, 2], False
                      elif t == 14:
                          deltas, interior = [-2, -1, 0, 1], False
                      else:
                          deltas, interior = [-3, -2, -1, 0], False
                      for h in range(4):
                          po = (h % 2) * 64
                          ch = h // 2
                          chunks = []
                          for dl in deltas:
                              i0 = _dslot(interior, 2 * dl + 7)
                              i1 = _dslot(interior, 2 * dl + 6)
                              chunks.append((t + dl, DT[:, h, i0:i1 + 1:(i1 - i0), :]))
                          chunks += [(16, None), (17, None)]
                          jobs.append({"t": t, "c": c, "h": h, "K": (lambda kt, ch=ch: KT[:, 1 + ch, tcols(kt)]), "po": po,
                                       "kidx": 1 + ch, "qres": ("QT", 2 + ch, c), "vname": "VD",
                                       "slot": (t % 2) * 4 + h,
                                       "Q": QT[po:po + 64, 2 + ch, tcols(t)], "V": (lambda kt, h=h: VD[:, kt, h, 0:65]),
                                       "chunks": chunks, "O": Ot, "On": On, "last": h == 3})
                  run_attn(jobs, False, 6)
                  if half == 0:
                      tap(f"cat{l}", HB[:, :, 0:1024], [128, 8, 1024], BF16, [])
                  stage(f"attD_{half}")
                  wcount = 0
                  for a in range(4):
                      slot, sr = load_piece(l, NP_WO + a)
                      sv = slot[:].rearrange("p (k n) -> p k n", k=8)
                      for di in range(2):
                          dch = 2 * a + di
                          for c in hgroups:
                              s0, Lc = GROUPS[c]
                              h0, _ = hcols(c, half)
                              s = sidx(c)
                              pn = ["P0a", "P0b", "P1a", "P1b"][wcount % 4]
                              wcount += 1
                              pw = PSV[pn][:, 0:Lc]
                              for k in range(8):
                                  mm(pw, sv[:, k, di * 128:(di + 1) * 128], HB[:, k, h0:h0 + Lc], k == 0, k == 7,
                                     [sr, hbres(c)], [pn])
                              stt(X[:, dch, s0:s0 + Lc], pw, G1[:, dch, s:s + 1], X[:, dch, s0:s0 + Lc], ALU.mult, ALU.add,
                                  [pn, MODn, xres(c)], [xres(c)])
                  stage(f"wout_{half}")
                  if half == 1:
                      S.barrier()
              advance_mod(64)
              stage("wout")
              tap(f"x1_{l}", X[:], [128, 8, NTOK], F32, [])

              stt(A2[:], MOD[:, 32:40, :], 1.0, VEC[:, 8:16].unsqueeze(2).to_broadcast([128, 8, 2]), ALU.add, ALU.mult,
                  [MODn, "VEC"], ["A2"])
              def h2dst(k, c):
                  s0, Lc = GROUPS[c]
                  if c < 2:
                      return HB[:, k, s0:s0 + Lc], hbres(c)
                  return H2B[:, k, s0 - 1024:s0 - 1024 + Lc], ("H2B", c)

              for c in groups_all:
                  do_norm(c, A2, SH2, h2dst, "A2")
              stage("norm2")
              gcount = 0
              dcount = 0
              for bi, blk in enumerate(FFBLOCKS):
                  for fl, f in enumerate(blk):
                      slot, sr = load_piece(l, NP_GU + f)
                      sv = slot[:].rearrange("p (u k n) -> p u k n", u=2, k=8)
                      for c in groups_all:
                          s0, Lc = GROUPS[c]
                          i = gcount % 2
                          gcount += 1
                          pgn, pun = ("P0a", "P0b") if i == 0 else ("P1a", "P1b")
                          pg, pu = PSV[pgn][:, 0:Lc], PSV[pun][:, 0:Lc]
                          for k in range(8):
                              hsrc, hres = h2dst(k, c)
                              mm(pg, sv[:, 0, k, :], hsrc, k == 0, k == 7, [sr, hres], [pgn])
                          for k in range(8):
                              hsrc, hres = h2dst(k, c)
                              mm(pu, sv[:, 1, k, :], hsrc, k == 0, k == 7, [sr, hres], [pun])
                          sg, sgn = (Fs[0], "F0") if i == 0 else (Fs[1], "F1")
                          act(sg[:, 0:Lc], pg, AF.Silu, [pgn], [sgn])
                          tt(ACTB[:, fl, s0:s0 + Lc], pu, sg[:, 0:Lc], ALU.mult, [sgn, pun], [("ACTB", fl, c)])
                  nf = len(blk)
                  for dch in range(8):
                      slot, sr = load_piece(l, NP_WD + bi * 8 + dch, nf * 128)
                      sv = slot[:, 0:nf * 128].rearrange("p (f n) -> p f n", f=nf)
                      for c in groups_all:
                          s0, Lc = GROUPS[c]
                          s = sidx(c)
                          pn = ["P2a", "P2b", "P3"][dcount % 3]
                          dcount += 1
                          pd = PSV[pn][:, 0:Lc]
                          for fl in range(nf):
                              mm(pd, sv[:, fl, :], ACTB[:, fl, s0:s0 + Lc], fl == 0, fl == nf - 1, [sr, ("ACTB", fl, c)], [pn])
                          stt(X[:, dch, s0:s0 + Lc], pd, G2[:, dch, s:s + 1], X[:, dch, s0:s0 + Lc], ALU.mult, ALU.add,
                              [pn, MODn, xres(c)], [xres(c)])
              S.barrier()
              stage("ffn")
              tap(f"x2_{l}", X[:], [128, 8, NTOK], F32, [])


        try:
            run_layers()
        except _Stop:
            pass
        for t in range(16):
            Pt = P0 if t % 2 == 0 else P1
            pr = ["P0a", "P0b"] if t % 2 == 0 else ["P1a", "P1b"]
            for k in range(8):
                tr(Pt[:, k * 128:(k + 1) * 128], X[:, k, t * 128:(t + 1) * 128], IDF[:], [xres(t // 4), "IDF"], pr)
            stg = SCR[:, (t % 2) * 1024:(t % 2 + 1) * 1024]
            stg_r = ["F0", "F1"] if t % 2 == 0 else ["F2", "F3"]
            if t % 2 == 0:
                act(stg, Pt[:], AF.Copy, pr, stg_r)
            else:
                cpv(stg, Pt[:], pr, stg_r)
            dma_sp(out_d[t * 128:(t + 1) * 128, :], stg, stg_r, [("out", t)])
        S.emit(st)
    return nc, tap_d


_CACHE = {}


def prep_inputs(inp, n_layers=2):
    inp = {k: np.asarray(v, dtype=np.float32) for k, v in inp.items()}
    cbf, rope, pm = _consts()
    shared = {"cbf": np.ascontiguousarray(cbf.reshape(128, -1)), "idf": np.eye(128, dtype=np.float32),
              "rope": np.ascontiguousarray(rope.reshape(128, -1)), "pm": np.ascontiguousarray(pm.reshape(128, -1))}
    for l in range(n_layers):
        la = _layer_arrays(inp, l)
        for k, v in la.items():
            shared[f"{k}{l}"] = v
    maps = []
    for b in range(8):
        m = dict(shared)
        m["x"] = np.ascontiguousarray(inp["x"][b])
        m["ctx"] = np.ascontiguousarray(inp["ctx"][b])
        cv = np.zeros((128, 8, 2), np.float32)
        cv[:, :, 0] = inp["c"][b].reshape(8, 128).T
        cv[:, :, 1] = inp["c_ctx"].reshape(8, 128).T
        m["cvec"] = np.ascontiguousarray(cv.reshape(128, 16))
        maps.append(m)
    return maps


def kernel(**inputs):
    if "nc" not in _CACHE:
        _CACHE["nc"] = build(2)[0]
    nc = _CACHE["nc"]
    maps = prep_inputs(inputs, 2)
    res = run_bass_kernel_spmd(nc, maps, core_ids=list(range(8)))
    return np.stack([np.asarray(r["out"], dtype=np.float32) for r in res.results], 0)
```

```python
import contextlib
import numpy as np
import concourse.bass as bass
import concourse.mybir as mybir
from concourse.bass_utils import run_bass_kernel_spmd

F32 = mybir.dt.float32
BF16 = mybir.dt.bfloat16
AF = mybir.ActivationFunctionType
ALU = mybir.AluOpType

D = 1024
SEQ = 2048
CTX = 256
NTOK = SEQ + CTX
FF = 2816
NFF = 22
NEG = -30000.0
EPS = 1e-6
GROUPS = [(0, 512), (512, 512), (1024, 512), (1536, 512), (2048, 256)]
HALVES = [[0, 1], [2, 3, 4]]
HSTART = [0, 1024]
FFBLOCKS = [list(range(0, 8)), list(range(8, 15)), list(range(15, 22))]
PIECE = 2048
import os as _os
_DBG_DELAY = int(_os.environ.get('DBG_DELAY', '0'))
_DBG_SKIP = set(int(v) for v in _os.environ.get('DBG_SKIP_A', '').split(',') if v)


class _Op:
    __slots__ = ("eng", "fn", "dma", "deps", "has_dep", "sig")

    def __init__(self, eng, fn, dma):
        self.eng = eng
        self.fn = fn
        self.dma = dma
        self.deps = []
        self.has_dep = False
        self.sig = None


class Sched:
    COMPUTE = ("pe", "act", "dve")

    def __init__(self, nc, n_dma_sems=8):
        self.nc = nc
        self.ops = []
        self.res = {}
        self.n_dma_sems = n_dma_sems
        self.last = {}
        self.pending = {}
        self.dma_since = []

    def op(self, eng, fn, reads=(), writes=(), dma=False):
        o = _Op(eng, fn, dma)
        deps = {}
        for r in reads:
            st = self.res.get(r)
            if st is not None and st[0] is not None:
                deps[id(st[0])] = (st[0], True)
        for w in writes:
            st = self.res.get(w)
            if st is not None:
                if st[0] is not None and id(st[0]) not in deps:
                    deps[id(st[0])] = (st[0], False)
                for rd in st[1]:
                    if id(rd) not in deps:
                        deps[id(rd)] = (rd, False)
        for d in self.pending.pop(eng, ()):
            if id(d) not in deps:
                deps[id(d)] = (d, True)
        for d, raw in deps.values():
            if d is o:
                continue
            if not d.dma and not o.dma and d.eng == eng:
                if eng == "pe":
                    continue
            o.deps.append(d)
            d.has_dep = True
        for r in reads:
            st = self.res.get(r)
            if st is None:
                self.res[r] = [None, [o]]
            else:
                st[1].append(o)
        for w in writes:
            self.res[w] = [o, []]
        self.ops.append(o)
        self.last[eng] = o
        if dma:
            self.dma_since.append(o)
        return o

    def barrier(self):
        lasts = [o for e, o in self.last.items() if e in self.COMPUTE]
        lasts += self.dma_since
        self.dma_since = []
        for e in self.COMPUTE:
            self.pending[e] = self.pending.get(e, []) + lasts

    def emit(self, stack):
        nc = self.nc
        eobj = {"pe": nc.tensor, "act": nc.scalar, "dve": nc.vector, "pool": nc.gpsimd, "sp": nc.sync}
        semh = {}
        cnt = {}
        waited = {e: {} for e in eobj}
        dma_rr = {}
        dcnt = {}

        def get_sem(key):
            if key not in semh:
                semh[key] = stack.enter_context(nc.semaphore("s_" + "_".join(str(k) for k in key)))
            return semh[key]

        for o in self.ops:
            E = eobj[o.eng]
            need = {}
            for d in o.deps:
                key, val = d.sig
                if need.get(key, 0) < val:
                    need[key] = val
            if o.dma:
                i = dma_rr.get(o.eng, 0)
                dma_rr[o.eng] = (i + 1) % self.n_dma_sems
                k = ("dma", o.eng, i)
                n = dcnt.get(k, 0) + 1
                dcnt[k] = n
                if n > 1 and need.get(k, 0) < 16 * (n - 1):
                    need[k] = 16 * (n - 1)
            for key, val in need.items():
                if waited[o.eng].get(key, 0) < val:
                    E.wait_ge(get_sem(key), val)
                    waited[o.eng][key] = val
            ins = o.fn()
            if o.dma:
                ins.then_inc(get_sem(k), 16)
                o.sig = (k, 16 * n)
            elif o.has_dep:
                key = ("e", o.eng)
                cnt[key] = cnt.get(key, 0) + 1
                ins.then_inc(get_sem(key), 1)
                o.sig = (key, cnt[key])
        for key, n in dcnt.items():
            nc.sync.wait_ge(get_sem(key), 16 * n)
        for key, n in cnt.items():
            nc.sync.wait_ge(get_sem(key), n)


def _kmajor(w):
    K, N = w.shape
    return np.ascontiguousarray(w.reshape(K // 128, 128, N).transpose(1, 0, 2))


def _pad_piece(a):
    a = np.ascontiguousarray(a, dtype=np.float32).reshape(128, -1)
    out = np.zeros((128, PIECE), np.float32)
    out[:, : a.shape[1]] = a
    return out


def _dslot(interior, s):
    if 3 <= s <= 9:
        return 8 + (9 - s)
    if s >= 10:
        return (0 if interior else 4) + (13 - s)
    return (18 if interior else 15) + (2 - s)


def _consts():
    c = np.zeros((128, 6, 128), np.float32)
    c[:, 0, :] = np.eye(128)
    c[:, 1, :] = 1.0
    c[0:64, 2, 0:64] = 1.0
    c[64:128, 2, 64:128] = 1.0
    for m in range(128):
        if (m % 32) < 16:
            c[m + 16, 3, m] = -1.0
        else:
            c[m - 16, 3, m] = 1.0
    j = np.arange(128)[:, None]
    i = np.arange(128)[None, :]
    c[:, 4, :] = np.where(j >= i, 0.0, NEG)
    c[:, 5, :] = np.where(j <= i, 0.0, NEG)
    p = np.arange(128)
    idx = p % 64
    f = (idx % 16).astype(np.float64)
    inv = np.power(10000.0, -f / 16.0)
    t = np.arange(SEQ)
    pos = np.where((idx < 32)[:, None], (t // 64)[None, :], (t % 64)[None, :]).astype(np.float64)
    ang = pos * inv[:, None]
    rope = np.stack([np.cos(ang), np.sin(ang)], axis=1).astype(np.float32)
    pm = np.zeros((128, 20, 128), np.float32)
    jj = np.arange(128)[:, None]
    tt = np.arange(128)[None, :]
    for g, w in enumerate((2, 4, 8, 16)):
        h = w // 2
        eye = (jj == tt).astype(np.float32)
        pm[:, g * 5 + 0, :] = ((jj >= tt - h) & (jj < tt + h)) / w - eye
        lo = np.maximum(tt - h, 0)
        cntf = (tt + h) - lo
        pm[:, g * 5 + 1, :] = ((jj >= lo) & (jj < tt + h)) / cntf - eye
        hi = np.minimum(tt + h, 128)
        cntl = hi - (tt - h)
        pm[:, g * 5 + 2, :] = ((jj >= tt - h) & (jj < hi)) / cntl - eye
        pm[:, g * 5 + 3, :] = (jj >= 128 + tt - h) / w
        pm[:, g * 5 + 4, :] = (jj < tt + h - 128) / w
    return c, rope, pm


def _dtab(rpb):
    kc = np.arange(64)[:, None]
    qc = np.arange(64)[None, :]
    cs = np.clip(qc - 8, 0, 48)
    col_ok = (kc >= cs) & (kc < cs + 16)
    coff = np.clip(kc - qc, -15, 15) + 15
    out = np.full((128, 4, 21, 64), NEG, np.float32)
    for h in range(4):
        for interior in (True, False):
            for s in range(14):
                sl = _dslot(interior, s)
                for kr in range(2):
                    rho = s + kr
                    ok = 0 <= rho <= 14 and ((3 <= rho <= 10) or not interior)
                    if not ok:
                        continue
                    T = np.where(col_ok, rpb[h, rho][coff], NEG)
                    out[kr * 64:(kr + 1) * 64, h, sl, :] = T
    return out


def _layer_arrays(inp, l):
    f32 = np.float32
    w_in = inp["w_in"][l]
    pieces = []
    wm = _kmajor(inp["w_mod"][l])
    for jj in range(24):
        pieces.append(_pad_piece(wm[:, :, jj * 256:(jj + 1) * 256]))
    aq = w_in[:, 0:256]
    fm_cols = [np.concatenate([aq[:, 0:64], aq[:, 128:192]], 1), np.concatenate([aq[:, 64:128], aq[:, 192:256]], 1),
               w_in[:, 256:384], w_in[:, 1280:1408], w_in[:, 1408:1536], w_in[:, 1536:1664], w_in[:, 1664:1792]]
    fm = _kmajor(np.concatenate(fm_cols, 1))
    for a in range(4):
        pieces.append(_pad_piece(fm[:, :, a * 256:min((a + 1) * 256, 896)]))
    tm = [w_in[:, 512:768], w_in[:, 768:1024], w_in[:, 1024:1280], w_in[:, 1792:2048], w_in[:, 384:512]]
    for a in tm:
        pieces.append(_pad_piece(_kmajor(a)))
    wo = _kmajor(inp["w_out"][l])
    for a in range(4):
        pieces.append(_pad_piece(wo[:, :, a * 256:(a + 1) * 256]))
    wg = _kmajor(inp["w_gate"][l])
    wu = _kmajor(inp["w_up"][l])
    for f in range(NFF):
        pieces.append(_pad_piece(np.concatenate([wg[:, :, f * 128:(f + 1) * 128].reshape(128, -1),
                                                 wu[:, :, f * 128:(f + 1) * 128].reshape(128, -1)], 1)))
    wd = _kmajor(inp["w_down"][l])
    for blk in FFBLOCKS:
        for dch in range(8):
            pieces.append(_pad_piece(wd[:, blk[0]:blk[-1] + 1, dch * 128:(dch + 1) * 128]))
    W = np.stack(pieces, 0)
    vec = np.zeros((128, 26), f32)
    vec[:, 0:8] = inp["g_mix"][l].reshape(8, 128).T
    vec[:, 8:16] = inp["g_ffn"][l].reshape(8, 128).T
    vec[:, 16] = np.tile(inp["a_q_gain"][l], 2)
    vec[:, 17] = np.tile(inp["a_k_gain"][l], 2)
    vec[:, 18] = np.tile(inp["d_q_gain"][l], 2)
    vec[:, 19] = np.tile(inp["d_k_gain"][l], 2)
    vec[:, 20:22] = inp["c_scale"][l].reshape(2, 128).T
    vec[:, 22:26] = inp["b_b_s"][l].T
    bmod = np.ascontiguousarray(inp["b_mod"][l].reshape(48, 128).T)
    bc = np.zeros((128, 260), f32)
    bc[:, 0:4] = inp["a_sink"][l][None, :]
    bc[:, 4:260] = inp["b_v_gain"][l][None, :]
    sm = np.zeros((128, 6, 128), f32)
    sm[:, 0:4, :] = inp["b_w_s"][l].transpose(2, 0, 1)
    wp = inp["c_w_pool"][l]
    for g in range(4):
        o = (g % 2) * 64
        sm[o:o + 64, 4 + g // 2, o:o + 64] = wp[g]
    dt = _dtab(inp["d_rpb"][l])
    return {"W": W, "vec": vec, "bmod": bmod, "bc": bc, "sm": np.ascontiguousarray(sm.reshape(128, -1)), "dtab": np.ascontiguousarray(dt.reshape(128, -1))}


NP_MOD, NP_FM, NP_TM, NP_WO, NP_GU, NP_WD = 0, 24, 28, 33, 37, 59
NPIECES = 83


class _Stop(Exception):
    pass


def build(n_layers=2, taps=(), stop=None):
    nc = bass.Bass("TRN2", target_bir_lowering=False)
    dr = {}

    def din(name, shape):
        dr[name] = nc.dram_tensor(name, list(shape), F32, kind="ExternalInput").ap()
        return dr[name]

    x_d = din("x", [SEQ, D])
    ctx_d = din("ctx", [CTX, D])
    cvec_d = din("cvec", [128, 16])
    cbf_d = din("cbf", [128, 6 * 128])
    idf_d = din("idf", [128, 128])
    rope_d = din("rope", [128, 2 * SEQ])
    pm_d = din("pm", [128, 20 * 128])
    L = []
    for l in range(n_layers):
        L.append({
            "W": din(f"W{l}", [NPIECES, 128, PIECE]), "vec": din(f"vec{l}", [128, 26]), "bmod": din(f"bmod{l}", [128, 48]),
            "bc": din(f"bc{l}", [128, 260]), "sm": din(f"sm{l}", [128, 6 * 128]), "dtab": din(f"dtab{l}", [128, 4 * 21 * 64]),
        })
    out_d = nc.dram_tensor("out", [SEQ, D], F32, kind="ExternalOutput").ap()
    tap_d = {}

    st = contextlib.ExitStack()
    with st:
        S = Sched(nc)
        sb = lambda n, s, d: st.enter_context(nc.sbuf_tensor(n, list(s), d))
        ps = lambda n, s, d: st.enter_context(nc.psum_tensor(n, list(s), d))
        X = sb("X", [128, 8, NTOK], F32)
        HB = sb("HB", [128, 8, 1280], BF16)
        RR = sb("RR", [128, 32472], BF16)
        QT = RR[:, 0:9216].rearrange("p (k t) -> p k t", k=4)
        KT = RR[:, 9216:16128].rearrange("p (k t) -> p k t", k=3)
        VA = RR[:, 16128:18504].rearrange("p (t h e) -> p t h e", t=18, h=2)
        VD = RR[:, 18504:23256].rearrange("p (t h e) -> p t h e", t=18, h=4)
        OB = RR[:, 23256:27864].rearrange("p (t c) -> p t c", t=18)
        CY = RR[:, 27864:32472].rearrange("p (t c) -> p t c", t=18)
        ACTB = RR[:, 0:18432].rearrange("p (f t) -> p f t", f=8)
        H2B = RR[:, 18432:28672].rearrange("p (k t) -> p k t", k=8)
        RING = [sb(f"ring{i}", [128, PIECE], BF16) for i in range(4)]
        TAB = sb("TAB", [128, 5376], BF16)
        SCR = sb("SCR", [128, 2048], F32)
        Fs = [SCR[:, i * 512:(i + 1) * 512] for i in range(4)]
        SQ = [sb(f"SQ{i}", [128, 512], BF16) for i in range(2)]
        QG = sb("QG", [128, 512], BF16)
        RQ = sb("RQ", [128, 512], BF16)
        PT = [sb(f"PT{i}", [128, 896], BF16) for i in range(2)]
        OT = [sb(f"OT{i}", [128, 256], BF16) for i in range(2)]
        PC = sb("PC", [128, 2, 512], BF16)
        QZ = PC[:].rearrange("p a (b q) -> p (a b) q", q=128)
        VN = [sb(f"VN{i}", [128, 256], BF16) for i in range(2)]
        CBF = sb("CBF", [128, 6, 128], BF16)
        IDF = sb("IDF", [128, 128], F32)
        CV = sb("CV", [128, 8, 2], F32)
        CA = sb("CA", [128, 8, 2], BF16)
        SMALL = sb("SMALL", [128, 64], F32)
        IDB = CBF[:, 0, :]
        ONES = CBF[:, 1, :]
        BDONES = CBF[:, 2, :]
        ROTT = CBF[:, 3, :]
        MASKP = CBF[:, 4, :]
        MASKN = CBF[:, 5, :]
        P0 = ps("P0", [128, 1024], F32)
        P1 = ps("P1", [128, 1024], F32)
        P2 = ps("P2", [128, 1024], F32)
        P3 = ps("P3", [128, 512], F32)
        PST = ps("PST", [128, 1024], BF16)
        PSV = {"P0a": P0[:, 0:512], "P0b": P0[:, 512:1024], "P1a": P1[:, 0:512], "P1b": P1[:, 512:1024],
               "P2a": P2[:, 0:512], "P2b": P2[:, 512:1024], "P3": P3[:, :]}

        def mm(out, lhsT, rhs, start, stop, reads, writes):
            S.op("pe", lambda: nc.tensor.matmul(out, lhsT=lhsT, rhs=rhs, start=start, stop=stop), reads, writes)

        def tr(out, in_, ident, reads, writes):
            S.op("pe", lambda: nc.tensor.transpose(out, in_, ident), reads, writes)

        def act(out, in_, func, reads, writes, scale=None, bias=None):
            kw = {}
            if scale is not None:
                kw["scale"] = scale
            if bias is not None:
                kw["bias"] = bias
            S.op("act", lambda: nc.scalar.activation(out=out, in_=in_, func=func, **kw), reads, writes)

        def tt(out, in0, in1, op, reads, writes):
            S.op("dve", lambda: nc.vector.tensor_tensor(out=out, in0=in0, in1=in1, op=op), reads, writes)

        def stt(out, in0, scalar, in1, op0, op1, reads, writes):
            S.op("dve", lambda: nc.vector.scalar_tensor_tensor(out=out, in0=in0, scalar=scalar, in1=in1, op0=op0, op1=op1),
                 reads, writes)

        def ts(out, in0, s1, s2, op0, op1, reads, writes):
            S.op("dve", lambda: nc.vector.tensor_scalar(out=out, in0=in0, scalar1=s1, scalar2=s2, op0=op0, op1=op1),
                 reads, writes)

        def cpv(out, in_, reads, writes):
            S.op("dve", lambda: nc.vector.tensor_copy(out=out, in_=in_), reads, writes)

        def dma_sp(out, in_, reads, writes):
            S.op("sp", lambda: nc.sync.dma_start(out=out, in_=in_), reads, writes, dma=True)

        def dma_pool(out, in_, reads, writes):
            S.op("pool", lambda: nc.gpsimd.dma_start(out=out, in_=in_, max_dma_last_dim=4096), reads, writes, dma=True)

        def stage(name):
            if stop == name:
                S.barrier()
                raise _Stop()

        def tap(name, ap, shape, dtype, reads):
            if name not in taps:
                return
            S.barrier()
            t = nc.dram_tensor("tap_" + name, list(shape), dtype, kind="ExternalOutput").ap()
            tap_d[name] = t
            dma_sp(t, ap, list(reads) + ["tapsrc"], ["tap_" + name])

        ring_i = [0]
        ring_serial = {}

        def load_piece(l, idx, nelem=PIECE):
            i = ring_i[0]
            ring_i[0] = (i + 1) % 4
            dma_pool(RING[i][:, 0:nelem], L[l]["W"][idx, :, 0:nelem], [], [("slot", i)])
            ring_serial[("slot", i)] = ring_serial.get(("slot", i), 0) + 1
            return RING[i], ("slot", i)

        dma_sp(IDF[:], idf_d, [], ["IDF"])
        dma_sp(CV[:].rearrange("p k s -> p (k s)"), cvec_d, [], ["CV"])
        dma_pool(CBF[:].rearrange("p a b -> p (a b)"), cbf_d, [], ["CBF"])
        act(CA[:], CV[:], AF.Silu, ["CV"], ["CA"])
        for t in range(18):
            stg = SCR[:, (t % 2) * 1024:(t % 2 + 1) * 1024]
            stg_r = ["F0", "F1"] if t % 2 == 0 else ["F2", "F3"]
            src = x_d[t * 128:(t + 1) * 128, :] if t < 16 else ctx_d[(t - 16) * 128:(t - 15) * 128, :]
            dma_sp(stg, src, [], stg_r)
            Pt = P0 if t % 2 == 0 else P1
            pr = ["P0a", "P0b"] if t % 2 == 0 else ["P1a", "P1b"]
            for k in range(8):
                tr(Pt[:, k * 128:(k + 1) * 128], stg[:, k * 128:(k + 1) * 128], IDF[:], stg_r + ["IDF"], pr)
            dst = X[:, :, t * 128:(t + 1) * 128]
            src_ps = Pt[:].rearrange("p (k t) -> p k t", k=8)
            xr = [("X", t // 4 if t < 16 else 4)]
            if t % 2 == 0:
                act(dst, src_ps, AF.Copy, pr, xr)
            else:
                cpv(dst, src_ps, pr, xr)
        S.barrier()

        def xres(c):
            return ("X", c)

        def hbres(c):
            return ("HB", 2) if c == 4 else ("HB", c % 2)

        def grp_of_tile(t):
            return t // 4 if t < 16 else 4

        def sidx(c):
            return 1 if c == 4 else 0

        VEC = sb("VEC", [128, 26], F32)
        BMODs = [sb(f"BMOD{i}", [128, 48], F32) for i in range(2)]
        BC = sb("BC", [128, 260], F32)
        SM = sb("SM", [128, 6, 128], BF16)
        MODs = [sb(f"MOD{i}", [128, 48, 2], F32) for i in range(2)]
        A1 = sb("A1", [128, 8, 2], F32)
        A2 = sb("A2", [128, 8, 2], F32)
        GS = sb("GS", [128, 8], F32)

        def mod_gen(l, jj0=0, jj1=24):
            b = l % 2
            if jj0 == 0:
                dma_sp(BMODs[b][:], L[l]["bmod"], [], [f"BMOD{b}"])
            for jj in range(jj0, jj1):
                slot, sr = load_piece(l, NP_MOD + jj)
                sv = slot[:].rearrange("p (k n) -> p k n", k=8)
                for cc in range(2):
                    j = 2 * jj + cc
                    for k in range(8):
                        mm(P3[:, 2 * j:2 * j + 2], sv[:, k, cc * 128:(cc + 1) * 128], CA[:, k, :], k == 0, k == 7,
                           [sr, "CA"], ["P3"])
                yield
            c0, c1 = 2 * jj0, 2 * jj1
            tt(MODs[b][:, c0:c1, :], P3[:, 2 * c0:2 * c1].rearrange("p (j s) -> p j s", s=2),
               BMODs[b][:, c0:c1].unsqueeze(2).to_broadcast([128, c1 - c0, 2]), ALU.add, ["P3", f"BMOD{b}"], [f"MOD{b}"])
            yield

        def run_layers():
          for l in range(n_layers):
              last = l == n_layers - 1
              Ld = L[l]
              dma_sp(VEC[:], Ld["vec"], [], ["VEC"])
              dma_sp(BC[:], Ld["bc"], [], ["BC"])
              dma_pool(SM[:].rearrange("p a b -> p (a b)"), Ld["sm"], [], ["SM"])
              WST = SM[:, 0:4, :]
              WPBD = SM[:, 4:6, :]

              MOD = MODs[l % 2]
              MODn = f"MOD{l % 2}"
              late0 = [None]
              if l == 0:
                  for _ in mod_gen(0, 0, 8):
                      pass
                  late0[0] = mod_gen(0, 8, 24)
              pre = [mod_gen(l + 1) if not last else None]

              def _adv(box, n):
                  for _ in range(n):
                      if box[0] is not None:
                          try:
                              next(box[0])
                          except StopIteration:
                              box[0] = None

              def advance_mod(n=1):
                  _adv(pre, n)
              stt(A1[:], MOD[:, 8:16, :], 1.0, VEC[:, 0:8].unsqueeze(2).to_broadcast([128, 8, 2]), ALU.add, ALU.mult,
                  [MODn, "VEC"], ["A1"])
              ts(GS[:, 0:1], VEC[:, 16:17], 0.125, None, ALU.mult, ALU.bypass, ["VEC"], ["GS"])
              ts(GS[:, 1:2], VEC[:, 17:18], 1.0, None, ALU.mult, ALU.bypass, ["VEC"], ["GS"])
              ts(GS[:, 2:3], VEC[:, 18:19], 0.125, None, ALU.mult, ALU.bypass, ["VEC"], ["GS"])
              ts(GS[:, 3:4], VEC[:, 19:20], 1.0, None, ALU.mult, ALU.bypass, ["VEC"], ["GS"])
              act(GS[:, 4:8], BC[:, 0:4], AF.Exp, ["BC"], ["GS"])
              SH1, G1, SH2, G2 = MOD[:, 0:8, :], MOD[:, 16:24, :], MOD[:, 24:32, :], MOD[:, 40:48, :]
              stage("modload")
              tap(f"mod{l}", MOD[:], [128, 48, 2], F32, [MODn])

              stage("mod")
              def do_norm(c, Amul, SHv, dst_fn, tag):
                  s0, Lc = GROUPS[c]
                  s = sidx(c)
                  psn, psr = (PSV["P2a"], "P2a") if c % 2 == 0 else (PSV["P2b"], "P2b")
                  for k in range(8):
                      sq = SQ[k % 2]
                      act(sq[:, 0:Lc], X[:, k, s0:s0 + Lc], AF.Square, [xres(c)], [f"SQ{k % 2}"])
                      mm(psn[:, 0:Lc], ONES, sq[:, 0:Lc], k == 0, k == 7, [f"SQ{k % 2}", "CBF"], [psr])
                  act(Fs[0][:, 0:Lc], psn[:, 0:Lc], AF.Ln, [psr], ["F0"], scale=1.0 / D, bias=EPS)
                  act(Fs[1][:, 0:Lc], Fs[0][:, 0:Lc], AF.Exp, ["F0"], ["F1"], scale=-0.5)
                  for k in range(8):
                      tmp, tr_ = (Fs[2], "F2") if k % 2 == 0 else (Fs[3], "F3")
                      tt(tmp[:, 0:Lc], X[:, k, s0:s0 + Lc], Fs[1][:, 0:Lc], ALU.mult, [xres(c), "F1"], [tr_])
                      dst, dres = dst_fn(k, c)
                      act(dst, tmp[:, 0:Lc], AF.Identity, [tr_, MODn, tag], [dres], scale=Amul[:, k, s:s + 1],
                          bias=SHv[:, k, s:s + 1])

              groups_all = [0, 1, 2, 3] if last else [0, 1, 2, 3, 4]

              dma_pool(TAB[:, 0:4096], rope_d, [], ["TAB"])
              COS = TAB[:, 0:2048]
              SIN = TAB[:, 2048:4096]
              S.op("dve", lambda: nc.vector.memset(VA[:, :, :, 64:65], 1.0), [], ["VAones"])
              S.op("dve", lambda: nc.vector.memset(VD[:, :, :, 64:65], 1.0), [], ["VDones"])

              def hcols(c, half):
                  s0, Lc = GROUPS[c]
                  return s0 - HSTART[half], Lc

              for half in range(2):
                  hgroups = HALVES[half]
                  for c in hgroups:
                      h0, Lc = hcols(c, half)
                      do_norm(c, A1, SH1, lambda k, c, h0=h0, Lc=Lc: (HB[:, k, h0:h0 + Lc], hbres(c)), "A1")
                  if half == 0:
                      tap(f"hT{l}", HB[:, :, 0:1024], [128, 8, 1024], BF16, [("HB", 0), ("HB", 1)])
                  stage(f"norm1_{half}")
                  combos = []
                  for a in range(4):
                      js = [2 * a, 2 * a + 1] if a < 3 else [6]
                      for ji, j in enumerate(js):
                          for c in hgroups:
                              if last and c == 4 and j in (0, 1, 3, 4):
                                  continue
                              combos.append((a, len(js), ji, j, c))
                  fm_slot = {}

                  def fm_piece(a, njs):
                      if a not in fm_slot:
                          slot, sr = load_piece(l, NP_FM + a, 8 * 128 * njs)
                          fm_slot[a] = (slot[:, 0:8 * 128 * njs].rearrange("p (k n) -> p k n", k=8), sr, ring_serial[sr])
                      assert ring_serial[fm_slot[a][1]] == fm_slot[a][2], "FM weight slot recycled while still in use"
                      return fm_slot[a][0], fm_slot[a][1]

                  def fm_info(k):
                      a, njs, ji, j, c = combos[k]
                      s0, Lc = GROUPS[c]
                      h0, _ = hcols(c, half)
                      pfn = ["P0a", "P0b", "P1a"][k % 3]
                      phn = ["P1b", "P2a"][k % 2]
                      prn = "P2b"
                      gcol = {0: 0, 1: 0, 2: 1, 3: 2, 4: 2, 5: 3, 6: 3}[j]
                      if j < 2:
                          dst, dres = QT[:, j, s0:s0 + Lc], ("QT", j, c)
                      elif j == 2:
                          dst, dres = KT[:, 0, s0:s0 + Lc], ("KT", 0, c)
                      elif j < 5:
                          dst, dres = QT[:, j - 1, s0:s0 + Lc], ("QT", j - 1, c)
                      else:
                          dst, dres = KT[:, j - 4, s0:s0 + Lc], ("KT", j - 4, c)
                      return dict(a=a, njs=njs, ji=ji, j=j, c=c, s0=s0, Lc=Lc, h0=h0, pfn=pfn, phn=phn, prn=prn,
                                  pf=PSV[pfn][:, 0:Lc], ph=PSV[phn][:, 0:Lc], pr=PSV[prn][:, 0:Lc],
                                  gain=GS[:, gcol:gcol + 1], dst=dst, dres=dres, rope=(j < 3 and c != 4),
                                  sq=SQ[k % 2], sqn=f"SQ{k % 2}")

                  def fm_head(k):
                      f = fm_info(k)
                      sv, sr = fm_piece(f["a"], f["njs"])
                      for kk in range(8):
                          mm(f["pf"], sv[:, kk, f["ji"] * 128:(f["ji"] + 1) * 128], HB[:, kk, f["h0"]:f["h0"] + f["Lc"]],
                             kk == 0, kk == 7, [sr, ("HB", f["c"])], [f["pfn"]])
                      act(f["sq"][:, 0:f["Lc"]], f["pf"], AF.Square, [f["pfn"]], [f["sqn"]])

                  def fm_qg(k):
                      f = fm_info(k)
                      if f["rope"]:
                          act(QG[:, 0:f["Lc"]], f["pf"], AF.Identity, [f["pfn"], "GS"], ["QG"], scale=f["gain"])

                  def fm_mid(k):
                      f = fm_info(k)
                      Lc, s0 = f["Lc"], f["s0"]
                      mm(f["ph"], BDONES, f["sq"][:, 0:Lc], True, True, [f["sqn"], "CBF"], [f["phn"]])
                      if f["rope"]:
                          mm(f["pr"], ROTT, QG[:, 0:Lc], True, True, ["QG", "CBF"], [f["prn"]])
                          tt(Fs[2][:, 0:Lc], QG[:, 0:Lc], COS[:, s0:s0 + Lc], ALU.mult, ["QG", "TAB"], ["F2"])

                  def fm_tail(k):
                      f = fm_info(k)
                      Lc, s0 = f["Lc"], f["s0"]
                      act(Fs[0][:, 0:Lc], f["ph"], AF.Ln, [f["phn"]], ["F0"], scale=1.0 / 64, bias=EPS)
                      act(Fs[1][:, 0:Lc], Fs[0][:, 0:Lc], AF.Exp, ["F0"], ["F1"], scale=-0.5)
                      if f["rope"]:
                          act(RQ[:, 0:Lc], f["pr"], AF.Copy, [f["prn"]], ["RQ"])
                          tt(Fs[3][:, 0:Lc], RQ[:, 0:Lc], SIN[:, s0:s0 + Lc], ALU.mult, ["RQ", "TAB"], ["F3"])
                          tt(Fs[2][:, 0:Lc], Fs[2][:, 0:Lc], Fs[3][:, 0:Lc], ALU.add, ["F2", "F3"], ["F2"])
                          tt(f["dst"], Fs[2][:, 0:Lc], Fs[1][:, 0:Lc], ALU.mult, ["F2", "F1"], [f["dres"]])
                      else:
                          stt(f["dst"], f["pf"], f["gain"], Fs[1][:, 0:Lc], ALU.mult, ALU.mult, [f["pfn"], "GS", "F1"], [f["dres"]])

                  nfm = len(combos)
                  fm_head(0)
                  fm_qg(0)
                  for k in range(nfm):
                      if k + 1 < nfm:
                          fm_head(k + 1)
                      fm_mid(k)
                      if k + 1 < nfm:
                          fm_qg(k + 1)
                      fm_tail(k)
                      if half == 0:
                          _adv(late0, 1)
                  _adv(late0, 64)
                  stage(f"fm_{half}")
                  tiles_h = []
                  for c in hgroups:
                      s0, Lc = GROUPS[c]
                      tiles_h += [(t, c) for t in range(s0 // 128, (s0 + Lc) // 128)]
                  tcount = [0]

                  def tm_proj(t, c, sv, srs, ncol):
                      sr, ser = srs
                      assert ring_serial[sr] == ser, "TM weight slot recycled while still in use"
                      h0 = t * 128 - HSTART[half]
                      pn = ["P0a", "P0b", "P1a", "P1b"][tcount[0] % 4]
                      tcount[0] += 1
                      pp = PSV[pn][:, 0:ncol]
                      for k in range(8):
                          mm(pp, HB[:, k, h0:h0 + 128], sv[:, k, :], k == 0, k == 7, [sr, hbres(c)], [pn])
                      return pp, pn

                  def tm_piece(pi):
                      ncol = 128 if pi == 4 else 256
                      slot, sr = load_piece(l, NP_TM + pi, 8 * ncol)
                      return slot[:, 0:8 * ncol].rearrange("p (k n) -> p k n", k=8), (sr, ring_serial[sr]), ncol

                  svU, srU, _ = tm_piece(0)
                  for (t, c) in tiles_h:
                      if last and c == 4:
                          continue
                      pp, pn = tm_proj(t, c, svU, srU, 256)
                      act(OB[:, t, :], pp, AF.Gelu_apprx_tanh, [pn], [("OB", t)])
                  svV, srV, _ = tm_piece(1)
                  svC, srC, _ = tm_piece(2)
                  svD, srD, _ = tm_piece(3)
                  svA, srA, _ = tm_piece(4)

                  def emit_z(t):
                      vn = VN[t % 2]
                      pzn = "P2a" if t % 2 == 0 else "P2b"
                      pz = PSV[pzn]
                      for g in range(4):
                          mm(pz[:, g * 64:(g + 1) * 64], WST[:, g, :], vn[:, g * 64:(g + 1) * 64], True, True,
                             [f"VN{t % 2}", "SM"], [pzn])
                      guf, gufn = (Fs[2], "F2") if t % 2 == 0 else (Fs[3], "F3")
                      act(guf[:, 0:256], OB[:, t, :], AF.Copy, [("OB", t)], [gufn])
                      for g in range(4):
                          stt(OB[:, t, g * 64:(g + 1) * 64], pz[:, g * 64:(g + 1) * 64], VEC[:, 22 + g:23 + g],
                              guf[:, g * 64:(g + 1) * 64], ALU.add, ALU.mult, [pzn, "VEC", gufn], [("OB", t)])

                  pending_z = []
                  for pi0 in range(0, len(tiles_h), 2):
                      pair = tiles_h[pi0:pi0 + 2]
                      fulls = [(t, c) for (t, c) in pair if not (last and c == 4)]
                      for (t, c) in fulls:
                          pp, pn = tm_proj(t, c, svV, srV, 256)
                          gv, gvn = (Fs[0], "F0") if t % 2 == 0 else (Fs[1], "F1")
                          gv = gv[:, 0:256]
                          act(gv, pp, AF.Gelu_apprx_tanh, [pn], [gvn])
                          o = (t % 2) * 16
                          stn = f"SMALL{t % 2}"
                          S.op("dve", lambda gv=gv, o=o: nc.vector.bn_stats(out=SMALL[:, o:o + 6], in_=gv), [gvn], [stn])
                          S.op("dve", lambda o=o: nc.vector.bn_aggr(out=SMALL[:, o + 8:o + 10], in_=SMALL[:, o:o + 6]),
                               [stn], [stn + "b"])
                      for tz in pending_z:
                          emit_z(tz)
                      pending_z = []
                      if len(fulls) == 2:
                          act(SMALL[:, 10:27:16], SMALL[:, 9:26:16], AF.Ln, ["SMALL0b", "SMALL1b"], ["SMALL0c", "SMALL1c"], bias=EPS)
                          act(SMALL[:, 11:28:16], SMALL[:, 10:27:16], AF.Exp, ["SMALL0c", "SMALL1c"], ["SMALL0d", "SMALL1d"], scale=-0.5)
                      for (t, c) in fulls:
                          o = (t % 2) * 16
                          stn = f"SMALL{t % 2}"
                          gv, gvn = (Fs[0], "F0") if t % 2 == 0 else (Fs[1], "F1")
                          gv = gv[:, 0:256]
                          if len(fulls) != 2:
                              act(SMALL[:, o + 10:o + 11], SMALL[:, o + 9:o + 10], AF.Ln, [stn + "b"], [stn + "c"], bias=EPS)
                              act(SMALL[:, o + 11:o + 12], SMALL[:, o + 10:o + 11], AF.Exp, [stn + "c"], [stn + "d"], scale=-0.5)
                          ts(gv, gv, SMALL[:, o + 8:o + 9], SMALL[:, o + 11:o + 12], ALU.subtract, ALU.mult,
                             [gvn, stn + "b", stn + "d"], [gvn])
                          tt(VN[t % 2][:], gv, BC[:, 4:260], ALU.mult, [gvn, "BC"], [f"VN{t % 2}"])
                      for (t, c) in fulls:
                          pp, pn = tm_proj(t, c, svC, srC, 256)
                          act(CY[:, t, :], pp, AF.Copy, [pn], [("CY", t)])
                      for (t, c) in pair:
                          pp, pn = tm_proj(t, c, svD, srD, 256)
                          cpv(VD[:, t, :, 0:64], pp.rearrange("p (h e) -> p h e", h=4), [pn, "VDones"], [("VD", t)])
                          pp, pn = tm_proj(t, c, svA, srA, 128)
                          cpv(VA[:, t, :, 0:64], pp.rearrange("p (h e) -> p h e", h=2), [pn, "VAones"], [("VA", t)])
                      pending_z = [t for (t, c) in fulls]
                  for tz in pending_z:
                      emit_z(tz)
              _adv(late0, 64)
              stage("p1")
              tap(f"qt{l}", QT[:, :, :], [128, 4, NTOK], BF16, [])
              tap(f"kt{l}", KT[:, :, :], [128, 3, NTOK], BF16, [])
              tap(f"ob{l}", OB[:, :, :], [128, 18, 256], BF16, [])
              tap(f"cy{l}", CY[:, :, :], [128, 18, 256], BF16, [])
              tap(f"vd{l}", VD[:, :, :, :], [128, 18, 4, 66], BF16, [])

              for half in range(2):
                  hgroups = [c for c in HALVES[half] if c in groups_all]
                  tiles_h = []
                  for c in hgroups:
                      s0, Lc = GROUPS[c]
                      tiles_h += [(t, c) for t in range(s0 // 128, (s0 + Lc) // 128)]
                  dma_pool(TAB[:, 0:2560], pm_d, [], ["TAB"])
                  PMv = TAB[:, 0:2560].rearrange("p (a t) -> p a t", a=20)
                  for (t, c) in tiles_h:
                      h0 = t * 128 - HSTART[half]
                      pv = PST[:, (t % 2) * 512:(t % 2) * 512 + 256]
                      pvn = "PST"
                      for cc in range(2):
                          tr(pv[:, cc * 128:(cc + 1) * 128], OB[:, t, cc * 128:(cc + 1) * 128], IDB, [("OB", t), "CBF"], [pvn])
                      cpv(HB[:, 2:4, h0:h0 + 128], pv.rearrange("p (c t) -> p c t", c=2), [pvn], [hbres(c)])
                  for c in hgroups:
                      s0, Lc = GROUPS[c]
                      h0, _ = hcols(c, half)
                      first_t = 16 if c == 4 else 0
                      last_t = 17 if c == 4 else 15
                      for t in range(s0 // 128, (s0 + Lc) // 128):
                          lo = (t * 128 - s0)
                          for g in range(4):
                              srcs = []
                              if t > first_t:
                                  srcs.append((t - 1, 3))
                              srcs.append((t, 1 if t == first_t else (2 if t == last_t else 0)))
                              if t < last_t:
                                  srcs.append((t + 1, 4))
                              o = (g % 2) * 64
                              pcn = "P0a" if g < 2 else "P0b"
                              outp = P0[o:o + 64, (g // 2) * 512 + lo:(g // 2) * 512 + lo + 128]
                              for i, (tj, v) in enumerate(srcs):
                                  mm(outp, CY[:, tj, g * 64:(g + 1) * 64], PMv[:, g * 5 + v, :], i == 0, i == len(srcs) - 1,
                                     [("CY", tj), "TAB"], [pcn])
                      for cc in range(2):
                          pcn = "P0a" if cc == 0 else "P0b"
                          pln = "P1a" if cc == 0 else "P1b"
                          act(PC[:, cc, 0:Lc], P0[:, cc * 512:cc * 512 + Lc], AF.Copy, [pcn], [("PC", cc)])
                          mm(P1[:, cc * 512:cc * 512 + Lc], WPBD[:, cc, :], PC[:, cc, 0:Lc], True, True, [("PC", cc), "SM"], [pln])
                          act(HB[:, 4 + cc, h0:h0 + Lc], P1[:, cc * 512:cc * 512 + Lc], AF.Identity, [pln, "VEC"], [hbres(c)],
                              scale=VEC[:, 20 + cc:21 + cc])
                  stage(f"p2a_{half}")
                  acount = [0]

                  def att_scores(job):
                      i = acount[0] % 2
                      acount[0] += 1
                      job["i"] = i
                      Sp = P0 if i == 0 else P1
                      Sn = ["P0a", "P0b"] if i == 0 else ["P1a", "P1b"]
                      chunks = job["chunks"]
                      n = len(chunks)
                      for ci, (kt, bias) in enumerate(chunks):
                          o = Sp[:, ci * 128:(ci + 1) * 128]
                          mm(o, job["K"](kt), QZ[:, job["slot"], :], True, bias is None,
                             [("QZ", job["slot"]), ("KT", job["kidx"], grp_of_tile(kt))], Sn)
                          if bias is not None:
                              mm(o, IDB, bias, False, True, ["CBF", "TAB"], Sn)
                      act(PT[i][:, 0:n * 128], Sp[:, 0:n * 128], AF.Exp, Sn, [f"PT{i}"])

                  def att_pv(job):
                      i = job["i"]
                      chunks = job["chunks"]
                      n = len(chunks)
                      h = job["h"]
                      for ci, (kt, bias) in enumerate(chunks):
                          mm(job["O"][:, h * 66:h * 66 + 65], PT[i][:, ci * 128:(ci + 1) * 128], job["V"](kt), ci == 0, ci == n - 1,
                             [f"PT{i}", (job["vname"], kt), job["vname"] + "ones"], [job["On"]])

                  def finish_tile(t, c, Otile, Oname, sink, cat0):
                      h0 = t * 128 - HSTART[half]
                      Ov = Otile[:, 0:264].rearrange("p (h e) -> p h e", h=4)
                      o = 32 + (t % 2) * 8
                      dn = f"DEN{t % 2}"
                      if sink:
                          tt(SMALL[:, o:o + 4], Ov[:, :, 64], GS[:, 4:8], ALU.add, [Oname, "GS"], [dn])
                      else:
                          cpv(SMALL[:, o:o + 4], Ov[:, :, 64], [Oname], [dn])
                      S.op("dve", lambda o=o: nc.vector.reciprocal(out=SMALL[:, o + 4:o + 8], in_=SMALL[:, o:o + 4]), [dn], [dn + "r"])
                      ot = OT[t % 2]
                      otn = f"OT{t % 2}"
                      tt(ot[:].rearrange("p (h e) -> p h e", h=4), Ov[:, :, 0:64],
                         SMALL[:, o + 4:o + 8].unsqueeze(2).to_broadcast([128, 4, 64]), ALU.mult, [Oname, dn + "r"], [otn])
                      pv = PST[:, (t % 2) * 512:(t % 2) * 512 + 256]
                      pvn = "PST"
                      for cc in range(2):
                          tr(pv[:, cc * 128:(cc + 1) * 128], ot[:, cc * 128:(cc + 1) * 128], IDB, [otn, "CBF"], [pvn])
                      cpv(HB[:, cat0:cat0 + 2, h0:h0 + 128], pv.rearrange("p (c t) -> p c t", c=2), [pvn], [hbres(c)])

                  def run_attn(jobs, sink, cat0):
                      S.op("dve", lambda: nc.vector.memset(QZ, 0.0), [], [("PC", 0), ("PC", 1)] + [("QZ", i) for i in range(8)])
                      prev = None
                      for job in jobs + [None]:
                          if job is not None:
                              po = job["po"]
                              cpv(QZ[po:po + 64, job["slot"], :], job["Q"], [("QZ", job["slot"]), job["qres"]], [("QZ", job["slot"])])
                              att_scores(job)
                          if prev is not None:
                              att_pv(prev)
                              if prev["last"]:
                                  finish_tile(prev["t"], prev["c"], prev["O"], prev["On"], sink, cat0)
                                  advance_mod()
                          prev = job

                  def tcols(t):
                      return slice(t * 128, (t + 1) * 128)

                  jobs = []
                  for (t, c) in tiles_h:
                      Ot, On = (PSV["P2a"], "P2a") if t % 2 == 0 else (PSV["P2b"], "P2b")
                      for h in range(4):
                          kv, g = h // 2, h % 2
                          po = kv * 64
                          if c == 4:
                              chunks = [(16, None), (17, None)]
                          else:
                              chunks = []
                              if t > 0:
                                  chunks.append((t - 1, MASKP))
                              chunks.append((t, None))
                              if t < 15:
                                  chunks.append((t + 1, MASKN))
                              chunks += [(16, None), (17, None)]
                          jobs.append({"t": t, "c": c, "h": h, "K": (lambda kt: KT[:, 0, tcols(kt)]), "po": po,
                                       "kidx": 0, "qres": ("QT", g, c), "vname": "VA",
                                       "slot": (t % 2) * 4 + h,
                                       "Q": QT[po:po + 64, g, tcols(t)], "V": (lambda kt, kv=kv: VA[:, kt, kv, 0:65]),
                                       "chunks": chunks, "O": Ot, "On": On, "last": h == 3})
                  run_attn(jobs, True, 0)
                  stage(f"attA_{half}")
                  dma_pool(TAB[:, 0:5376], Ld["dtab"], [], ["TAB"])
                  DT = TAB[:, 0:5376].rearrange("p (h s q) -> p h s q", h=4, s=21)
                  jobs = []
                  for (t, c) in tiles_h:
                      Ot, On = (PSV["P2a"], "P2a") if t % 2 == 0 else (PSV["P2b"], "P2b")
                      if c == 4:
                          deltas, interior = [], False
                      elif 2 <= t <= 13:
                          deltas, interior = [-2, -1, 0, 1, 2], True
                      elif t == 0:
                          deltas, interior = [0, 1, 2, 3], False
                      elif t == 1:
                          deltas, interior = [-1, 0, 1, 2], False
                      elif t == 14:
                          deltas, interior = [-2, -1, 0, 1], False
                      else:
                          deltas, interior = [-3, -2, -1, 0], False
                      for h in range(4):
                          po = (h % 2) * 64
                          ch = h // 2
                          chunks = []
                          for dl in deltas:
                              i0 = _dslot(interior, 2 * dl + 7)
                              i1 = _dslot(interior, 2 * dl + 6)
                              chunks.append((t + dl, DT[:, h, i0:i1 + 1:(i1 - i0), :]))
                          chunks += [(16, None), (17, None)]
                          jobs.append({"t": t, "c": c, "h": h, "K": (lambda kt, ch=ch: KT[:, 1 + ch, tcols(kt)]), "po": po,
                                       "kidx": 1 + ch, "qres": ("QT", 2 + ch, c), "vname": "VD",
                                       "slot": (t % 2) * 4 + h,
                                       "Q": QT[po:po + 64, 2 + ch, tcols(t)], "V": (lambda kt, h=h: VD[:, kt, h, 0:65]),
                                       "chunks": chunks, "O": Ot, "On": On, "last": h == 3})
                  run_attn(jobs, False, 6)
                  if half == 0:
                      tap(f"cat{l}", HB[:, :, 0:1024], [128, 8, 1024], BF16, [])
                  stage(f"attD_{half}")
                  wcount = 0
                  for a in range(4):
                      slot, sr = load_piece(l, NP_WO + a)
                      sv = slot[:].rearrange("p (k n) -> p k n", k=8)
                      for di in range(2):
                          dch = 2 * a + di
                          for c in hgroups:
                              s0, Lc = GROUPS[c]
                              h0, _ = hcols(c, half)
                              s = sidx(c)
                              pn = ["P0a", "P0b", "P1a", "P1b"][wcount % 4]
                              wcount += 1
                              pw = PSV[pn][:, 0:Lc]
                              for k in range(8):
                                  mm(pw, sv[:, k, di * 128:(di + 1) * 128], HB[:, k, h0:h0 + Lc], k == 0, k == 7,
                                     [sr, hbres(c)], [pn])
                              stt(X[:, dch, s0:s0 + Lc], pw, G1[:, dch, s:s + 1], X[:, dch, s0:s0 + Lc], ALU.mult, ALU.add,
                                  [pn, MODn, xres(c)], [xres(c)])
                  stage(f"wout_{half}")
                  if half == 1:
                      S.barrier()
              advance_mod(64)
              stage("wout")
              tap(f"x1_{l}", X[:], [128, 8, NTOK], F32, [])

              stt(A2[:], MOD[:, 32:40, :], 1.0, VEC[:, 8:16].unsqueeze(2).to_broadcast([128, 8, 2]), ALU.add, ALU.mult,
                  [MODn, "VEC"], ["A2"])
              def h2dst(k, c):
                  s0, Lc = GROUPS[c]
                  if c < 2:
                      return HB[:, k, s0:s0 + Lc], hbres(c)
                  return H2B[:, k, s0 - 1024:s0 - 1024 + Lc], ("H2B", c)

              for c in groups_all:
                  do_norm(c, A2, SH2, h2dst, "A2")
              stage("norm2")
              gcount = 0
              dcount = 0
              for bi, blk in enumerate(FFBLOCKS):
                  for fl, f in enumerate(blk):
                      slot, sr = load_piece(l, NP_GU + f)
                      sv = slot[:].rearrange("p (u k n) -> p u k n", u=2, k=8)
                      for c in groups_all:
                          s0, Lc = GROUPS[c]
                          i = gcount % 2
                          gcount += 1
                          pgn, pun = ("P0a", "P0b") if i == 0 else ("P1a", "P1b")
                          pg, pu = PSV[pgn][:, 0:Lc], PSV[pun][:, 0:Lc]
                          for k in range(8):
                              hsrc, hres = h2dst(k, c)
                              mm(pg, sv[:, 0, k, :], hsrc, k == 0, k == 7, [sr, hres], [pgn])
                          for k in range(8):
                              hsrc, hres = h2dst(k, c)
                              mm(pu, sv[:, 1, k, :], hsrc, k == 0, k == 7, [sr, hres], [pun])
                          sg, sgn = (Fs[0], "F0") if i == 0 else (Fs[1], "F1")
                          act(sg[:, 0:Lc], pg, AF.Silu, [pgn], [sgn])
                          tt(ACTB[:, fl, s0:s0 + Lc], pu, sg[:, 0:Lc], ALU.mult, [sgn, pun], [("ACTB", fl, c)])
                  nf = len(blk)
                  for dch in range(8):
                      slot, sr = load_piece(l, NP_WD + bi * 8 + dch, nf * 128)
                      sv = slot[:, 0:nf * 128].rearrange("p (f n) -> p f n", f=nf)
                      for c in groups_all:
                          s0, Lc = GROUPS[c]
                          s = sidx(c)
                          pn = ["P2a", "P2b", "P3"][dcount % 3]
                          dcount += 1
                          pd = PSV[pn][:, 0:Lc]
                          for fl in range(nf):
                              mm(pd, sv[:, fl, :], ACTB[:, fl, s0:s0 + Lc], fl == 0, fl == nf - 1, [sr, ("ACTB", fl, c)], [pn])
                          stt(X[:, dch, s0:s0 + Lc], pd, G2[:, dch, s:s + 1], X[:, dch, s0:s0 + Lc], ALU.mult, ALU.add,
                              [pn, MODn, xres(c)], [xres(c)])
              S.barrier()
              stage("ffn")
              tap(f"x2_{l}", X[:], [128, 8, NTOK], F32, [])


        try:
            run_layers()
        except _Stop:
            pass
        for t in range(16):
            Pt = P0 if t % 2 == 0 else P1
            pr = ["P0a", "P0b"] if t % 2 == 0 else ["P1a", "P1b"]
            for k in range(8):
                tr(Pt[:, k * 128:(k + 1) * 128], X[:, k, t * 128:(t + 1) * 128], IDF[:], [xres(t // 4), "IDF"], pr)
            stg = SCR[:, (t % 2) * 1024:(t % 2 + 1) * 1024]
            stg_r = ["F0", "F1"] if t % 2 == 0 else ["F2", "F3"]
            if t % 2 == 0:
                act(stg, Pt[:], AF.Copy, pr, stg_r)
            else:
                cpv(stg, Pt[:], pr, stg_r)
            dma_sp(out_d[t * 128:(t + 1) * 128, :], stg, stg_r, [("out", t)])
        S.emit(st)
    return nc, tap_d


_CACHE = {}


def prep_inputs(inp, n_layers=2):
    inp = {k: np.asarray(v, dtype=np.float32) for k, v in inp.items()}
    cbf, rope, pm = _consts()
    shared = {"cbf": np.ascontiguousarray(cbf.reshape(128, -1)), "idf": np.eye(128, dtype=np.float32),
              "rope": np.ascontiguousarray(rope.reshape(128, -1)), "pm": np.ascontiguousarray(pm.reshape(128, -1))}
    for l in range(n_layers):
        la = _layer_arrays(inp, l)
        for k, v in la.items():
            shared[f"{k}{l}"] = v
    maps = []
    for b in range(8):
        m = dict(shared)
        m["x"] = np.ascontiguousarray(inp["x"][b])
        m["ctx"] = np.ascontiguousarray(inp["ctx"][b])
        cv = np.zeros((128, 8, 2), np.float32)
        cv[:, :, 0] = inp["c"][b].reshape(8, 128).T
        cv[:, :, 1] = inp["c_ctx"].reshape(8, 128).T
        m["cvec"] = np.ascontiguousarray(cv.reshape(128, 16))
        maps.append(m)
    return maps


def kernel(**inputs):
    if "nc" not in _CACHE:
        _CACHE["nc"] = build(2)[0]
    nc = _CACHE["nc"]
    maps = prep_inputs(inputs, 2)
    res = run_bass_kernel_spmd(nc, maps, core_ids=list(range(8)))
    return np.stack([np.asarray(r["out"], dtype=np.float32) for r in res.results], 0)
```

```python
import contextlib
import numpy as np
import concourse.bass as bass
import concourse.mybir as mybir
from concourse.bass_utils import run_bass_kernel_spmd

F32 = mybir.dt.float32
BF16 = mybir.dt.bfloat16
AF = mybir.ActivationFunctionType
ALU = mybir.AluOpType

D = 1024
SEQ = 2048
CTX = 256
NTOK = SEQ + CTX
FF = 2816
NFF = 22
NEG = -30000.0
EPS = 1e-6
GROUPS = [(0, 512), (512, 512), (1024, 512), (1536, 512), (2048, 256)]
HALVES = [[0, 1], [2, 3, 4]]
HSTART = [0, 1024]
FFBLOCKS = [list(range(0, 8)), list(range(8, 15)), list(range(15, 22))]
PIECE = 2048
import os as _os
_DBG_DELAY = int(_os.environ.get('DBG_DELAY', '0'))
_DBG_SKIP = set(int(v) for v in _os.environ.get('DBG_SKIP_A', '').split(',') if v)


class _Op:
    __slots__ = ("eng", "fn", "dma", "deps", "has_dep", "sig")

    def __init__(self, eng, fn, dma):
        self.eng = eng
        self.fn = fn
        self.dma = dma
        self.deps = []
        self.has_dep = False
        self.sig = None


class Sched:
    COMPUTE = ("pe", "act", "dve")

    def __init__(self, nc, n_dma_sems=8):
        self.nc = nc
        self.ops = []
        self.res = {}
        self.n_dma_sems = n_dma_sems
        self.last = {}
        self.pending = {}
        self.dma_since = []

    def op(self, eng, fn, reads=(), writes=(), dma=False):
        o = _Op(eng, fn, dma)
        deps = {}
        for r in reads:
            st = self.res.get(r)
            if st is not None and st[0] is not None:
                deps[id(st[0])] = (st[0], True)
        for w in writes:
            st = self.res.get(w)
            if st is not None:
                if st[0] is not None and id(st[0]) not in deps:
                    deps[id(st[0])] = (st[0], False)
                for rd in st[1]:
                    if id(rd) not in deps:
                        deps[id(rd)] = (rd, False)
        for d in self.pending.pop(eng, ()):
            if id(d) not in deps:
                deps[id(d)] = (d, True)
        for d, raw in deps.values():
            if d is o:
                continue
            if not d.dma and not o.dma and d.eng == eng:
                if eng == "pe":
                    continue
            o.deps.append(d)
            d.has_dep = True
        for r in reads:
            st = self.res.get(r)
            if st is None:
                self.res[r] = [None, [o]]
            else:
                st[1].append(o)
        for w in writes:
            self.res[w] = [o, []]
        self.ops.append(o)
        self.last[eng] = o
        if dma:
            self.dma_since.append(o)
        return o

    def barrier(self):
        lasts = [o for e, o in self.last.items() if e in self.COMPUTE]
        lasts += self.dma_since
        self.dma_since = []
        for e in self.COMPUTE:
            self.pending[e] = self.pending.get(e, []) + lasts

    def emit(self, stack):
        nc = self.nc
        eobj = {"pe": nc.tensor, "act": nc.scalar, "dve": nc.vector, "pool": nc.gpsimd, "sp": nc.sync}
        semh = {}
        cnt = {}
        waited = {e: {} for e in eobj}
        dma_rr = {}
        dcnt = {}

        def get_sem(key):
            if key not in semh:
                semh[key] = stack.enter_context(nc.semaphore("s_" + "_".join(str(k) for k in key)))
            return semh[key]

        for o in self.ops:
            E = eobj[o.eng]
            need = {}
            for d in o.deps:
                key, val = d.sig
                if need.get(key, 0) < val:
                    need[key] = val
            if o.dma:
                i = dma_rr.get(o.eng, 0)
                dma_rr[o.eng] = (i + 1) % self.n_dma_sems
                k = ("dma", o.eng, i)
                n = dcnt.get(k, 0) + 1
                dcnt[k] = n
                if n > 1 and need.get(k, 0) < 16 * (n - 1):
                    need[k] = 16 * (n - 1)
            for key, val in need.items():
                if waited[o.eng].get(key, 0) < val:
                    E.wait_ge(get_sem(key), val)
                    waited[o.eng][key] = val
            ins = o.fn()
            if o.dma:
                ins.then_inc(get_sem(k), 16)
                o.sig = (k, 16 * n)
            elif o.has_dep:
                key = ("e", o.eng)
                cnt[key] = cnt.get(key, 0) + 1
                ins.then_inc(get_sem(key), 1)
                o.sig = (key, cnt[key])
        for key, n in dcnt.items():
            nc.sync.wait_ge(get_sem(key), 16 * n)
        for key, n in cnt.items():
            nc.sync.wait_ge(get_sem(key), n)


def _kmajor(w):
    K, N = w.shape
    return np.ascontiguousarray(w.reshape(K // 128, 128, N).transpose(1, 0, 2))


def _pad_piece(a):
    a = np.ascontiguousarray(a, dtype=np.float32).reshape(128, -1)
    out = np.zeros((128, PIECE), np.float32)
    out[:, : a.shape[1]] = a
    return out


def _dslot(interior, s):
    if 3 <= s <= 9:
        return 8 + (9 - s)
    if s >= 10:
        return (0 if interior else 4) + (13 - s)
    return (18 if interior else 15) + (2 - s)


def _consts():
    c = np.zeros((128, 6, 128), np.float32)
    c[:, 0, :] = np.eye(128)
    c[:, 1, :] = 1.0
    c[0:64, 2, 0:64] = 1.0
    c[64:128, 2, 64:128] = 1.0
    for m in range(128):
        if (m % 32) < 16:
            c[m + 16, 3, m] = -1.0
        else:
            c[m - 16, 3, m] = 1.0
    j = np.arange(128)[:, None]
    i = np.arange(128)[None, :]
    c[:, 4, :] = np.where(j >= i, 0.0, NEG)
    c[:, 5, :] = np.where(j <= i, 0.0, NEG)
    p = np.arange(128)
    idx = p % 64
    f = (idx % 16).astype(np.float64)
    inv = np.power(10000.0, -f / 16.0)
    t = np.arange(SEQ)
    pos = np.where((idx < 32)[:, None], (t // 64)[None, :], (t % 64)[None, :]).astype(np.float64)
    ang = pos * inv[:, None]
    rope = np.stack([np.cos(ang), np.sin(ang)], axis=1).astype(np.float32)
    pm = np.zeros((128, 20, 128), np.float32)
    jj = np.arange(128)[:, None]
    tt = np.arange(128)[None, :]
    for g, w in enumerate((2, 4, 8, 16)):
        h = w // 2
        eye = (jj == tt).astype(np.float32)
        pm[:, g * 5 + 0, :] = ((jj >= tt - h) & (jj < tt + h)) / w - eye
        lo = np.maximum(tt - h, 0)
        cntf = (tt + h) - lo
        pm[:, g * 5 + 1, :] = ((jj >= lo) & (jj < tt + h)) / cntf - eye
        hi = np.minimum(tt + h, 128)
        cntl = hi - (tt - h)
        pm[:, g * 5 + 2, :] = ((jj >= tt - h) & (jj < hi)) / cntl - eye
        pm[:, g * 5 + 3, :] = (jj >= 128 + tt - h) / w
        pm[:, g * 5 + 4, :] = (jj < tt + h - 128) / w
    return c, rope, pm


def _dtab(rpb):
    kc = np.arange(64)[:, None]
    qc = np.arange(64)[None, :]
    cs = np.clip(qc - 8, 0, 48)
    col_ok = (kc >= cs) & (kc < cs + 16)
    coff = np.clip(kc - qc, -15, 15) + 15
    out = np.full((128, 4, 21, 64), NEG, np.float32)
    for h in range(4):
        for interior in (True, False):
            for s in range(14):
                sl = _dslot(interior, s)
                for kr in range(2):
                    rho = s + kr
                    ok = 0 <= rho <= 14 and ((3 <= rho <= 10) or not interior)
                    if not ok:
                        continue
                    T = np.where(col_ok, rpb[h, rho][coff], NEG)
                    out[kr * 64:(kr + 1) * 64, h, sl, :] = T
    return out


def _layer_arrays(inp, l):
    f32 = np.float32
    w_in = inp["w_in"][l]
    pieces = []
    wm = _kmajor(inp["w_mod"][l])
    for jj in range(24):
        pieces.append(_pad_piece(wm[:, :, jj * 256:(jj + 1) * 256]))
    aq = w_in[:, 0:256]
    fm_cols = [np.concatenate([aq[:, 0:64], aq[:, 128:192]], 1), np.concatenate([aq[:, 64:128], aq[:, 192:256]], 1),
               w_in[:, 256:384], w_in[:, 1280:1408], w_in[:, 1408:1536], w_in[:, 1536:1664], w_in[:, 1664:1792]]
    fm = _kmajor(np.concatenate(fm_cols, 1))
    for a in range(4):
        pieces.append(_pad_piece(fm[:, :, a * 256:min((a + 1) * 256, 896)]))
    tm = [w_in[:, 512:768], w_in[:, 768:1024], w_in[:, 1024:1280], w_in[:, 1792:2048], w_in[:, 384:512]]
    for a in tm:
        pieces.append(_pad_piece(_kmajor(a)))
    wo = _kmajor(inp["w_out"][l])
    for a in range(4):
        pieces.append(_pad_piece(wo[:, :, a * 256:(a + 1) * 256]))
    wg = _kmajor(inp["w_gate"][l])
    wu = _kmajor(inp["w_up"][l])
    for f in range(NFF):
        pieces.append(_pad_piece(np.concatenate([wg[:, :, f * 128:(f + 1) * 128].reshape(128, -1),
                                                 wu[:, :, f * 128:(f + 1) * 128].reshape(128, -1)], 1)))
    wd = _kmajor(inp["w_down"][l])
    for blk in FFBLOCKS:
        for dch in range(8):
            pieces.append(_pad_piece(wd[:, blk[0]:blk[-1] + 1, dch * 128:(dch + 1) * 128]))
    W = np.stack(pieces, 0)
    vec = np.zeros((128, 26), f32)
    vec[:, 0:8] = inp["g_mix"][l].reshape(8, 128).T
    vec[:, 8:16] = inp["g_ffn"][l].reshape(8, 128).T
    vec[:, 16] = np.tile(inp["a_q_gain"][l], 2)
    vec[:, 17] = np.tile(inp["a_k_gain"][l], 2)
    vec[:, 18] = np.tile(inp["d_q_gain"][l], 2)
    vec[:, 19] = np.tile(inp["d_k_gain"][l], 2)
    vec[:, 20:22] = inp["c_scale"][l].reshape(2, 128).T
    vec[:, 22:26] = inp["b_b_s"][l].T
    bmod = np.ascontiguousarray(inp["b_mod"][l].reshape(48, 128).T)
    bc = np.zeros((128, 260), f32)
    bc[:, 0:4] = inp["a_sink"][l][None, :]
    bc[:, 4:260] = inp["b_v_gain"][l][None, :]
    sm = np.zeros((128, 6, 128), f32)
    sm[:, 0:4, :] = inp["b_w_s"][l].transpose(2, 0, 1)
    wp = inp["c_w_pool"][l]
    for g in range(4):
        o = (g % 2) * 64
        sm[o:o + 64, 4 + g // 2, o:o + 64] = wp[g]
    dt = _dtab(inp["d_rpb"][l])
    return {"W": W, "vec": vec, "bmod": bmod, "bc": bc, "sm": np.ascontiguousarray(sm.reshape(128, -1)), "dtab": np.ascontiguousarray(dt.reshape(128, -1))}


NP_MOD, NP_FM, NP_TM, NP_WO, NP_GU, NP_WD = 0, 24, 28, 33, 37, 59
NPIECES = 83


class _Stop(Exception):
    pass


def build(n_layers=2, taps=(), stop=None):
    nc = bass.Bass("TRN2", target_bir_lowering=False)
    dr = {}

    def din(name, shape):
        dr[name] = nc.dram_tensor(name, list(shape), F32, kind="ExternalInput").ap()
        return dr[name]

    x_d = din("x", [SEQ, D])
    ctx_d = din("ctx", [CTX, D])
    cvec_d = din("cvec", [128, 16])
    cbf_d = din("cbf", [128, 6 * 128])
    idf_d = din("idf", [128, 128])
    rope_d = din("rope", [128, 2 * SEQ])
    pm_d = din("pm", [128, 20 * 128])
    L = []
    for l in range(n_layers):
        L.append({
            "W": din(f"W{l}", [NPIECES, 128, PIECE]), "vec": din(f"vec{l}", [128, 26]), "bmod": din(f"bmod{l}", [128, 48]),
            "bc": din(f"bc{l}", [128, 260]), "sm": din(f"sm{l}", [128, 6 * 128]), "dtab": din(f"dtab{l}", [128, 4 * 21 * 64]),
        })
    out_d = nc.dram_tensor("out", [SEQ, D], F32, kind="ExternalOutput").ap()
    tap_d = {}

    st = contextlib.ExitStack()
    with st:
        S = Sched(nc)
        sb = lambda n, s, d: st.enter_context(nc.sbuf_tensor(n, list(s), d))
        ps = lambda n, s, d: st.enter_context(nc.psum_tensor(n, list(s), d))
        X = sb("X", [128, 8, NTOK], F32)
        HB = sb("HB", [128, 8, 1280], BF16)
        RR = sb("RR", [128, 32472], BF16)
        QT = RR[:, 0:9216].rearrange("p (k t) -> p k t", k=4)
        KT = RR[:, 9216:16128].rearrange("p (k t) -> p k t", k=3)
        VA = RR[:, 16128:18504].rearrange("p (t h e) -> p t h e", t=18, h=2)
        VD = RR[:, 18504:23256].rearrange("p (t h e) -> p t h e", t=18, h=4)
        OB = RR[:, 23256:27864].rearrange("p (t c) -> p t c", t=18)
        CY = RR[:, 27864:32472].rearrange("p (t c) -> p t c", t=18)
        ACTB = RR[:, 0:18432].rearrange("p (f t) -> p f t", f=8)
        H2B = RR[:, 18432:28672].rearrange("p (k t) -> p k t", k=8)
        RING = [sb(f"ring{i}", [128, PIECE], BF16) for i in range(4)]
        TAB = sb("TAB", [128, 5376], BF16)
        SCR = sb("SCR", [128, 2048], F32)
        Fs = [SCR[:, i * 512:(i + 1) * 512] for i in range(4)]
        SQ = [sb(f"SQ{i}", [128, 512], BF16) for i in range(2)]
        QG = sb("QG", [128, 512], BF16)
        RQ = sb("RQ", [128, 512], BF16)
        PT = [sb(f"PT{i}", [128, 896], BF16) for i in range(2)]
        OT = [sb(f"OT{i}", [128, 256], BF16) for i in range(2)]
        PC = sb("PC", [128, 2, 512], BF16)
        QZ = PC[:].rearrange("p a (b q) -> p (a b) q", q=128)
        VN = [sb(f"VN{i}", [128, 256], BF16) for i in range(2)]
        CBF = sb("CBF", [128, 6, 128], BF16)
        IDF = sb("IDF", [128, 128], F32)
        CV = sb("CV", [128, 8, 2], F32)
        CA = sb("CA", [128, 8, 2], BF16)
        SMALL = sb("SMALL", [128, 64], F32)
        IDB = CBF[:, 0, :]
        ONES = CBF[:, 1, :]
        BDONES = CBF[:, 2, :]
        ROTT = CBF[:, 3, :]
        MASKP = CBF[:, 4, :]
        MASKN = CBF[:, 5, :]
        P0 = ps("P0", [128, 1024], F32)
        P1 = ps("P1", [128, 1024], F32)
        P2 = ps("P2", [128, 1024], F32)
        P3 = ps("P3", [128, 512], F32)
        PST = ps("PST", [128, 1024], BF16)
        PSV = {"P0a": P0[:, 0:512], "P0b": P0[:, 512:1024], "P1a": P1[:, 0:512], "P1b": P1[:, 512:1024],
               "P2a": P2[:, 0:512], "P2b": P2[:, 512:1024], "P3": P3[:, :]}

        def mm(out, lhsT, rhs, start, stop, reads, writes):
            S.op("pe", lambda: nc.tensor.matmul(out, lhsT=lhsT, rhs=rhs, start=start, stop=stop), reads, writes)

        def tr(out, in_, ident, reads, writes):
            S.op("pe", lambda: nc.tensor.transpose(out, in_, ident), reads, writes)

        def act(out, in_, func, reads, writes, scale=None, bias=None):
            kw = {}
            if scale is not None:
                kw["scale"] = scale
            if bias is not None:
                kw["bias"] = bias
            S.op("act", lambda: nc.scalar.activation(out=out, in_=in_, func=func, **kw), reads, writes)

        def tt(out, in0, in1, op, reads, writes):
            S.op("dve", lambda: nc.vector.tensor_tensor(out=out, in0=in0, in1=in1, op=op), reads, writes)

        def stt(out, in0, scalar, in1, op0, op1, reads, writes):
            S.op("dve", lambda: nc.vector.scalar_tensor_tensor(out=out, in0=in0, scalar=scalar, in1=in1, op0=op0, op1=op1),
                 reads, writes)

        def ts(out, in0, s1, s2, op0, op1, reads, writes):
            S.op("dve", lambda: nc.vector.tensor_scalar(out=out, in0=in0, scalar1=s1, scalar2=s2, op0=op0, op1=op1),
                 reads, writes)

        def cpv(out, in_, reads, writes):
            S.op("dve", lambda: nc.vector.tensor_copy(out=out, in_=in_), reads, writes)

        def dma_sp(out, in_, reads, writes):
            S.op("sp", lambda: nc.sync.dma_start(out=out, in_=in_), reads, writes, dma=True)

        def dma_pool(out, in_, reads, writes):
            S.op("pool", lambda: nc.gpsimd.dma_start(out=out, in_=in_, max_dma_last_dim=4096), reads, writes, dma=True)

        def stage(name):
            if stop == name:
                S.barrier()
                raise _Stop()

        def tap(name, ap, shape, dtype, reads):
            if name not in taps:
                return
            S.barrier()
            t = nc.dram_tensor("tap_" + name, list(shape), dtype, kind="ExternalOutput").ap()
            tap_d[name] = t
            dma_sp(t, ap, list(reads) + ["tapsrc"], ["tap_" + name])

        ring_i = [0]
        ring_serial = {}

        def load_piece(l, idx, nelem=PIECE):
            i = ring_i[0]
            ring_i[0] = (i + 1) % 4
            dma_pool(RING[i][:, 0:nelem], L[l]["W"][idx, :, 0:nelem], [], [("slot", i)])
            ring_serial[("slot", i)] = ring_serial.get(("slot", i), 0) + 1
            return RING[i], ("slot", i)

        dma_sp(IDF[:], idf_d, [], ["IDF"])
        dma_sp(CV[:].rearrange("p k s -> p (k s)"), cvec_d, [], ["CV"])
        dma_pool(CBF[:].rearrange("p a b -> p (a b)"), cbf_d, [], ["CBF"])
        act(CA[:], CV[:], AF.Silu, ["CV"], ["CA"])
        for t in range(18):
            stg = SCR[:, (t % 2) * 1024:(t % 2 + 1) * 1024]
            stg_r = ["F0", "F1"] if t % 2 == 0 else ["F2", "F3"]
            src = x_d[t * 128:(t + 1) * 128, :] if t < 16 else ctx_d[(t - 16) * 128:(t - 15) * 128, :]
            dma_sp(stg, src, [], stg_r)
            Pt = P0 if t % 2 == 0 else P1
            pr = ["P0a", "P0b"] if t % 2 == 0 else ["P1a", "P1b"]
            for k in range(8):
                tr(Pt[:, k * 128:(k + 1) * 128], stg[:, k * 128:(k + 1) * 128], IDF[:], stg_r + ["IDF"], pr)
            dst = X[:, :, t * 128:(t + 1) * 128]
            src_ps = Pt[:].rearrange("p (k t) -> p k t", k=8)
            xr = [("X", t // 4 if t < 16 else 4)]
            if t % 2 == 0:
                act(dst, src_ps, AF.Copy, pr, xr)
            else:
                cpv(dst, src_ps, pr, xr)
        S.barrier()

        def xres(c):
            return ("X", c)

        def hbres(c):
            return ("HB", 2) if c == 4 else ("HB", c % 2)

        def grp_of_tile(t):
            return t // 4 if t < 16 else 4

        def sidx(c):
            return 1 if c == 4 else 0

        VEC = sb("VEC", [128, 26], F32)
        BMODs = [sb(f"BMOD{i}", [128, 48], F32) for i in range(2)]
        BC = sb("BC", [128, 260], F32)
        SM = sb("SM", [128, 6, 128], BF16)
        MODs = [sb(f"MOD{i}", [128, 48, 2], F32) for i in range(2)]
        A1 = sb("A1", [128, 8, 2], F32)
        A2 = sb("A2", [128, 8, 2], F32)
        GS = sb("GS", [128, 8], F32)

        def mod_gen(l, jj0=0, jj1=24):
            b = l % 2
            if jj0 == 0:
                dma_sp(BMODs[b][:], L[l]["bmod"], [], [f"BMOD{b}"])
            for jj in range(jj0, jj1):
                slot, sr = load_piece(l, NP_MOD + jj)
                sv = slot[:].rearrange("p (k n) -> p k n", k=8)
                for cc in range(2):
                    j = 2 * jj + cc
                    for k in range(8):
                        mm(P3[:, 2 * j:2 * j + 2], sv[:, k, cc * 128:(cc + 1) * 128], CA[:, k, :], k == 0, k == 7,
                           [sr, "CA"], ["P3"])
                yield
            c0, c1 = 2 * jj0, 2 * jj1
            tt(MODs[b][:, c0:c1, :], P3[:, 2 * c0:2 * c1].rearrange("p (j s) -> p j s", s=2),
               BMODs[b][:, c0:c1].unsqueeze(2).to_broadcast([128, c1 - c0, 2]), ALU.add, ["P3", f"BMOD{b}"], [f"MOD{b}"])
            yield

        def run_layers():
          for l in range(n_layers):
              last = l == n_layers - 1
              Ld = L[l]
              dma_sp(VEC[:], Ld["vec"], [], ["VEC"])
              dma_sp(BC[:], Ld["bc"], [], ["BC"])
              dma_pool(SM[:].rearrange("p a b -> p (a b)"), Ld["sm"], [], ["SM"])
              WST = SM[:, 0:4, :]
              WPBD = SM[:, 4:6, :]

              MOD = MODs[l % 2]
              MODn = f"MOD{l % 2}"
              late0 = [None]
              if l == 0:
                  for _ in mod_gen(0, 0, 8):
                      pass
                  late0[0] = mod_gen(0, 8, 24)
              pre = [mod_gen(l + 1) if not last else None]

              def _adv(box, n):
                  for _ in range(n):
                      if box[0] is not None:
                          try:
                              next(box[0])
                          except StopIteration:
                              box[0] = None

              def advance_mod(n=1):
                  _adv(pre, n)
              stt(A1[:], MOD[:, 8:16, :], 1.0, VEC[:, 0:8].unsqueeze(2).to_broadcast([128, 8, 2]), ALU.add, ALU.mult,
                  [MODn, "VEC"], ["A1"])
              ts(GS[:, 0:1], VEC[:, 16:17], 0.125, None, ALU.mult, ALU.bypass, ["VEC"], ["GS"])
              ts(GS[:, 1:2], VEC[:, 17:18], 1.0, None, ALU.mult, ALU.bypass, ["VEC"], ["GS"])
              ts(GS[:, 2:3], VEC[:, 18:19], 0.125, None, ALU.mult, ALU.bypass, ["VEC"], ["GS"])
              ts(GS[:, 3:4], VEC[:, 19:20], 1.0, None, ALU.mult, ALU.bypass, ["VEC"], ["GS"])
              act(GS[:, 4:8], BC[:, 0:4], AF.Exp, ["BC"], ["GS"])
              SH1, G1, SH2, G2 = MOD[:, 0:8, :], MOD[:, 16:24, :], MOD[:, 24:32, :], MOD[:, 40:48, :]
              stage("modload")
              tap(f"mod{l}", MOD[:], [128, 48, 2], F32, [MODn])

              stage("mod")
              def do_norms(groups, Amul, SHv, dst_fn, tag):
                  def bank(c):
                      return (PSV["P2a"], "P2a") if c % 2 == 0 else (PSV["P2b"], "P2b")

                  def stat_k(c, k):
                      s0, Lc = GROUPS[c]
                      psn, psr = bank(c)
                      sq = SQ[k % 2]
                      act(sq[:, 0:Lc], X[:, k, s0:s0 + Lc], AF.Square, [xres(c)], [f"SQ{k % 2}"])
                      mm(psn[:, 0:Lc], ONES, sq[:, 0:Lc], k == 0, k == 7, [f"SQ{k % 2}", "CBF"], [psr])

                  def stat_fin(c):
                      s0, Lc = GROUPS[c]
                      psn, psr = bank(c)
                      act(Fs[0][:, 0:Lc], psn[:, 0:Lc], AF.Ln, [psr], ["F0"], scale=1.0 / D, bias=EPS)
                      act(psn[:, 0:Lc], Fs[0][:, 0:Lc], AF.Exp, ["F0"], [psr], scale=-0.5)

                  def apply_k(c, k):
                      s0, Lc = GROUPS[c]
                      s = sidx(c)
                      psn, psr = bank(c)
                      tmp, tr_ = (Fs[2], "F2") if k % 2 == 0 else (Fs[3], "F3")
                      tt(tmp[:, 0:Lc], psn[:, 0:Lc], X[:, k, s0:s0 + Lc], ALU.mult, [xres(c), psr], [tr_])
                      dst, dres = dst_fn(k, c)
                      act(dst, tmp[:, 0:Lc], AF.Identity, [tr_, MODn, tag], [dres], scale=Amul[:, k, s:s + 1],
                          bias=SHv[:, k, s:s + 1])

                  n = len(groups)
                  for k in range(8):
                      stat_k(groups[0], k)
                  stat_fin(groups[0])
                  for i in range(n):
                      for k in range(8):
                          if i + 1 < n:
                              stat_k(groups[i + 1], k)
                          apply_k(groups[i], k)
                      if i + 1 < n:
                          stat_fin(groups[i + 1])

              groups_all = [0, 1, 2, 3] if last else [0, 1, 2, 3, 4]

              dma_pool(TAB[:, 0:4096], rope_d, [], ["TAB"])
              COS = TAB[:, 0:2048]
              SIN = TAB[:, 2048:4096]
              S.op("dve", lambda: nc.vector.memset(VA[:, :, :, 64:65], 1.0), [], ["VAones"])
              S.op("dve", lambda: nc.vector.memset(VD[:, :, :, 64:65], 1.0), [], ["VDones"])

              def hcols(c, half):
                  s0, Lc = GROUPS[c]
                  return s0 - HSTART[half], Lc

              for half in range(2):
                  hgroups = HALVES[half]
                  def hdst(k, c):
                      h0, Lc = hcols(c, half)
                      return HB[:, k, h0:h0 + Lc], hbres(c)

                  do_norms(hgroups, A1, SH1, hdst, "A1")
                  if half == 0:
                      tap(f"hT{l}", HB[:, :, 0:1024], [128, 8, 1024], BF16, [("HB", 0), ("HB", 1)])
                  stage(f"norm1_{half}")
                  combos = []
                  for a in range(4):
                      js = [2 * a, 2 * a + 1] if a < 3 else [6]
                      for ji, j in enumerate(js):
                          for c in hgroups:
                              if last and c == 4 and j in (0, 1, 3, 4):
                                  continue
                              combos.append((a, len(js), ji, j, c))
                  fm_slot = {}

                  def fm_piece(a, njs):
                      if a not in fm_slot:
                          slot, sr = load_piece(l, NP_FM + a, 8 * 128 * njs)
                          fm_slot[a] = (slot[:, 0:8 * 128 * njs].rearrange("p (k n) -> p k n", k=8), sr, ring_serial[sr])
                      assert ring_serial[fm_slot[a][1]] == fm_slot[a][2], "FM weight slot recycled while still in use"
                      return fm_slot[a][0], fm_slot[a][1]

                  def fm_info(k):
                      a, njs, ji, j, c = combos[k]
                      s0, Lc = GROUPS[c]
                      h0, _ = hcols(c, half)
                      pfn = ["P0a", "P0b", "P1a"][k % 3]
                      phn = ["P1b", "P2a"][k % 2]
                      prn = "P2b"
                      gcol = {0: 0, 1: 0, 2: 1, 3: 2, 4: 2, 5: 3, 6: 3}[j]
                      if j < 2:
                          dst, dres = QT[:, j, s0:s0 + Lc], ("QT", j, c)
                      elif j == 2:
                          dst, dres = KT[:, 0, s0:s0 + Lc], ("KT", 0, c)
                      elif j < 5:
                          dst, dres = QT[:, j - 1, s0:s0 + Lc], ("QT", j - 1, c)
                      else:
                          dst, dres = KT[:, j - 4, s0:s0 + Lc], ("KT", j - 4, c)
                      return dict(a=a, njs=njs, ji=ji, j=j, c=c, s0=s0, Lc=Lc, h0=h0, pfn=pfn, phn=phn, prn=prn,
                                  pf=PSV[pfn][:, 0:Lc], ph=PSV[phn][:, 0:Lc], pr=PSV[prn][:, 0:Lc],
                                  gain=GS[:, gcol:gcol + 1], dst=dst, dres=dres, rope=(j < 3 and c != 4),
                                  sq=SQ[k % 2], sqn=f"SQ{k % 2}")

                  def fm_head(k):
                      f = fm_info(k)
                      sv, sr = fm_piece(f["a"], f["njs"])
                      for kk in range(8):
                          mm(f["pf"], sv[:, kk, f["ji"] * 128:(f["ji"] + 1) * 128], HB[:, kk, f["h0"]:f["h0"] + f["Lc"]],
                             kk == 0, kk == 7, [sr, ("HB", f["c"])], [f["pfn"]])
                      act(f["sq"][:, 0:f["Lc"]], f["pf"], AF.Square, [f["pfn"]], [f["sqn"]])

                  def fm_qg(k):
                      f = fm_info(k)
                      if f["rope"]:
                          act(QG[:, 0:f["Lc"]], f["pf"], AF.Identity, [f["pfn"], "GS"], ["QG"], scale=f["gain"])

                  def fm_mid(k):
                      f = fm_info(k)
                      Lc, s0 = f["Lc"], f["s0"]
                      mm(f["ph"], BDONES, f["sq"][:, 0:Lc], True, True, [f["sqn"], "CBF"], [f["phn"]])
                      if f["rope"]:
                          mm(f["pr"], ROTT, QG[:, 0:Lc], True, True, ["QG", "CBF"], [f["prn"]])
                          tt(Fs[2][:, 0:Lc], QG[:, 0:Lc], COS[:, s0:s0 + Lc], ALU.mult, ["QG", "TAB"], ["F2"])

                  def fm_tail(k):
                      f = fm_info(k)
                      Lc, s0 = f["Lc"], f["s0"]
                      act(Fs[0][:, 0:Lc], f["ph"], AF.Ln, [f["phn"]], ["F0"], scale=1.0 / 64, bias=EPS)
                      act(Fs[1][:, 0:Lc], Fs[0][:, 0:Lc], AF.Exp, ["F0"], ["F1"], scale=-0.5)
                      if f["rope"]:
                          act(RQ[:, 0:Lc], f["pr"], AF.Copy, [f["prn"]], ["RQ"])
                          tt(Fs[3][:, 0:Lc], RQ[:, 0:Lc], SIN[:, s0:s0 + Lc], ALU.mult, ["RQ", "TAB"], ["F3"])
                          tt(Fs[2][:, 0:Lc], Fs[2][:, 0:Lc], Fs[3][:, 0:Lc], ALU.add, ["F2", "F3"], ["F2"])
                          tt(f["dst"], Fs[2][:, 0:Lc], Fs[1][:, 0:Lc], ALU.mult, ["F2", "F1"], [f["dres"]])
                      else:
                          stt(f["dst"], f["pf"], f["gain"], Fs[1][:, 0:Lc], ALU.mult, ALU.mult, [f["pfn"], "GS", "F1"], [f["dres"]])

                  nfm = len(combos)
                  fm_head(0)
                  fm_qg(0)
                  for k in range(nfm):
                      if k + 1 < nfm:
                          fm_head(k + 1)
                      fm_mid(k)
                      if k + 1 < nfm:
                          fm_qg(k + 1)
                      fm_tail(k)
                      if half == 0:
                          _adv(late0, 1)
                  _adv(late0, 64)
                  stage(f"fm_{half}")
                  tiles_h = []
                  for c in hgroups:
                      s0, Lc = GROUPS[c]
                      tiles_h += [(t, c) for t in range(s0 // 128, (s0 + Lc) // 128)]
                  tcount = [0]

                  def tm_proj(t, c, sv, srs, ncol):
                      sr, ser = srs
                      assert ring_serial[sr] == ser, "TM weight slot recycled while still in use"
                      h0 = t * 128 - HSTART[half]
                      pn = ["P0a", "P0b", "P1a", "P1b"][tcount[0] % 4]
                      tcount[0] += 1
                      pp = PSV[pn][:, 0:ncol]
                      for k in range(8):
                          mm(pp, HB[:, k, h0:h0 + 128], sv[:, k, :], k == 0, k == 7, [sr, hbres(c)], [pn])
                      return pp, pn

                  def tm_piece(pi):
                      ncol = 128 if pi == 4 else 256
                      slot, sr = load_piece(l, NP_TM + pi, 8 * ncol)
                      return slot[:, 0:8 * ncol].rearrange("p (k n) -> p k n", k=8), (sr, ring_serial[sr]), ncol

                  svU, srU, _ = tm_piece(0)
                  for (t, c) in tiles_h:
                      if last and c == 4:
                          continue
                      pp, pn = tm_proj(t, c, svU, srU, 256)
                      act(OB[:, t, :], pp, AF.Gelu_apprx_tanh, [pn], [("OB", t)])
                  svV, srV, _ = tm_piece(1)
                  svC, srC, _ = tm_piece(2)
                  svD, srD, _ = tm_piece(3)
                  svA, srA, _ = tm_piece(4)

                  def emit_z(t):
                      vn = VN[t % 2]
                      pzn = "P2a" if t % 2 == 0 else "P2b"
                      pz = PSV[pzn]
                      for g in range(4):
                          mm(pz[:, g * 64:(g + 1) * 64], WST[:, g, :], vn[:, g * 64:(g + 1) * 64], True, True,
                             [f"VN{t % 2}", "SM"], [pzn])
                      guf, gufn = (Fs[2], "F2") if t % 2 == 0 else (Fs[3], "F3")
                      act(guf[:, 0:256], OB[:, t, :], AF.Copy, [("OB", t)], [gufn])
                      for g in range(4):
                          stt(OB[:, t, g * 64:(g + 1) * 64], pz[:, g * 64:(g + 1) * 64], VEC[:, 22 + g:23 + g],
                              guf[:, g * 64:(g + 1) * 64], ALU.add, ALU.mult, [pzn, "VEC", gufn], [("OB", t)])

                  pending_z = []
                  for pi0 in range(0, len(tiles_h), 2):
                      pair = tiles_h[pi0:pi0 + 2]
                      fulls = [(t, c) for (t, c) in pair if not (last and c == 4)]
                      for (t, c) in fulls:
                          pp, pn = tm_proj(t, c, svV, srV, 256)
                          gv, gvn = (Fs[0], "F0") if t % 2 == 0 else (Fs[1], "F1")
                          gv = gv[:, 0:256]
                          act(gv, pp, AF.Gelu_apprx_tanh, [pn], [gvn])
                          o = (t % 2) * 16
                          stn = f"SMALL{t % 2}"
                          S.op("dve", lambda gv=gv, o=o: nc.vector.bn_stats(out=SMALL[:, o:o + 6], in_=gv), [gvn], [stn])
                          S.op("dve", lambda o=o: nc.vector.bn_aggr(out=SMALL[:, o + 8:o + 10], in_=SMALL[:, o:o + 6]),
                               [stn], [stn + "b"])
                      for tz in pending_z:
                          emit_z(tz)
                      pending_z = []
                      if len(fulls) == 2:
                          act(SMALL[:, 10:27:16], SMALL[:, 9:26:16], AF.Ln, ["SMALL0b", "SMALL1b"], ["SMALL0c", "SMALL1c"], bias=EPS)
                          act(SMALL[:, 11:28:16], SMALL[:, 10:27:16], AF.Exp, ["SMALL0c", "SMALL1c"], ["SMALL0d", "SMALL1d"], scale=-0.5)
                      for (t, c) in fulls:
                          o = (t % 2) * 16
                          stn = f"SMALL{t % 2}"
                          gv, gvn = (Fs[0], "F0") if t % 2 == 0 else (Fs[1], "F1")
                          gv = gv[:, 0:256]
                          if len(fulls) != 2:
                              act(SMALL[:, o + 10:o + 11], SMALL[:, o + 9:o + 10], AF.Ln, [stn + "b"], [stn + "c"], bias=EPS)
                              act(SMALL[:, o + 11:o + 12], SMALL[:, o + 10:o + 11], AF.Exp, [stn + "c"], [stn + "d"], scale=-0.5)
                          ts(gv, gv, SMALL[:, o + 8:o + 9], SMALL[:, o + 11:o + 12], ALU.subtract, ALU.mult,
                             [gvn, stn + "b", stn + "d"], [gvn])
                          tt(VN[t % 2][:], gv, BC[:, 4:260], ALU.mult, [gvn, "BC"], [f"VN{t % 2}"])
                      for (t, c) in fulls:
                          pp, pn = tm_proj(t, c, svC, srC, 256)
                          act(CY[:, t, :], pp, AF.Copy, [pn], [("CY", t)])
                      for (t, c) in pair:
                          pp, pn = tm_proj(t, c, svD, srD, 256)
                          cpv(VD[:, t, :, 0:64], pp.rearrange("p (h e) -> p h e", h=4), [pn, "VDones"], [("VD", t)])
                          pp, pn = tm_proj(t, c, svA, srA, 128)
                          cpv(VA[:, t, :, 0:64], pp.rearrange("p (h e) -> p h e", h=2), [pn, "VAones"], [("VA", t)])
                      pending_z = [t for (t, c) in fulls]
                  for tz in pending_z:
                      emit_z(tz)
              _adv(late0, 64)
              stage("p1")
              tap(f"qt{l}", QT[:, :, :], [128, 4, NTOK], BF16, [])
              tap(f"kt{l}", KT[:, :, :], [128, 3, NTOK], BF16, [])
              tap(f"ob{l}", OB[:, :, :], [128, 18, 256], BF16, [])
              tap(f"cy{l}", CY[:, :, :], [128, 18, 256], BF16, [])
              tap(f"vd{l}", VD[:, :, :, :], [128, 18, 4, 66], BF16, [])

              for half in range(2):
                  hgroups = [c for c in HALVES[half] if c in groups_all]
                  tiles_h = []
                  for c in hgroups:
                      s0, Lc = GROUPS[c]
                      tiles_h += [(t, c) for t in range(s0 // 128, (s0 + Lc) // 128)]
                  dma_pool(TAB[:, 0:2560], pm_d, [], ["TAB"])
                  PMv = TAB[:, 0:2560].rearrange("p (a t) -> p a t", a=20)
                  for (t, c) in tiles_h:
                      h0 = t * 128 - HSTART[half]
                      pv = PST[:, (t % 2) * 512:(t % 2) * 512 + 256]
                      pvn = "PST"
                      for cc in range(2):
                          tr(pv[:, cc * 128:(cc + 1) * 128], OB[:, t, cc * 128:(cc + 1) * 128], IDB, [("OB", t), "CBF"], [pvn])
                      cpv(HB[:, 2:4, h0:h0 + 128], pv.rearrange("p (c t) -> p c t", c=2), [pvn], [hbres(c)])
                  for c in hgroups:
                      s0, Lc = GROUPS[c]
                      h0, _ = hcols(c, half)
                      first_t = 16 if c == 4 else 0
                      last_t = 17 if c == 4 else 15
                      for t in range(s0 // 128, (s0 + Lc) // 128):
                          lo = (t * 128 - s0)
                          for g in range(4):
                              srcs = []
                              if t > first_t:
                                  srcs.append((t - 1, 3))
                              srcs.append((t, 1 if t == first_t else (2 if t == last_t else 0)))
                              if t < last_t:
                                  srcs.append((t + 1, 4))
                              o = (g % 2) * 64
                              pcn = "P0a" if g < 2 else "P0b"
                              outp = P0[o:o + 64, (g // 2) * 512 + lo:(g // 2) * 512 + lo + 128]
                              for i, (tj, v) in enumerate(srcs):
                                  mm(outp, CY[:, tj, g * 64:(g + 1) * 64], PMv[:, g * 5 + v, :], i == 0, i == len(srcs) - 1,
                                     [("CY", tj), "TAB"], [pcn])
                      for cc in range(2):
                          pcn = "P0a" if cc == 0 else "P0b"
                          pln = "P1a" if cc == 0 else "P1b"
                          act(PC[:, cc, 0:Lc], P0[:, cc * 512:cc * 512 + Lc], AF.Copy, [pcn], [("PC", cc)])
                          mm(P1[:, cc * 512:cc * 512 + Lc], WPBD[:, cc, :], PC[:, cc, 0:Lc], True, True, [("PC", cc), "SM"], [pln])
                          act(HB[:, 4 + cc, h0:h0 + Lc], P1[:, cc * 512:cc * 512 + Lc], AF.Identity, [pln, "VEC"], [hbres(c)],
                              scale=VEC[:, 20 + cc:21 + cc])
                  stage(f"p2a_{half}")
                  acount = [0]

                  def att_scores(job):
                      i = acount[0] % 2
                      acount[0] += 1
                      job["i"] = i
                      Sp = P0 if i == 0 else P1
                      Sn = ["P0a", "P0b"] if i == 0 else ["P1a", "P1b"]
                      chunks = job["chunks"]
                      n = len(chunks)
                      for ci, (kt, bias) in enumerate(chunks):
                          o = Sp[:, ci * 128:(ci + 1) * 128]
                          mm(o, job["K"](kt), QZ[:, job["slot"], :], True, bias is None,
                             [("QZ", job["slot"]), ("KT", job["kidx"], grp_of_tile(kt))], Sn)
                          if bias is not None:
                              mm(o, IDB, bias, False, True, ["CBF", "TAB"], Sn)
                      act(PT[i][:, 0:n * 128], Sp[:, 0:n * 128], AF.Exp, Sn, [f"PT{i}"])

                  def att_pv(job):
                      i = job["i"]
                      chunks = job["chunks"]
                      n = len(chunks)
                      h = job["h"]
                      for ci, (kt, bias) in enumerate(chunks):
                          mm(job["O"][:, h * 66:h * 66 + 65], PT[i][:, ci * 128:(ci + 1) * 128], job["V"](kt), ci == 0, ci == n - 1,
                             [f"PT{i}", (job["vname"], kt), job["vname"] + "ones"], [job["On"]])

                  def finish_tile(t, c, Otile, Oname, sink, cat0):
                      h0 = t * 128 - HSTART[half]
                      Ov = Otile[:, 0:264].rearrange("p (h e) -> p h e", h=4)
                      o = 32 + (t % 2) * 8
                      dn = f"DEN{t % 2}"
                      if sink:
                          tt(SMALL[:, o:o + 4], Ov[:, :, 64], GS[:, 4:8], ALU.add, [Oname, "GS"], [dn])
                      else:
                          cpv(SMALL[:, o:o + 4], Ov[:, :, 64], [Oname], [dn])
                      S.op("dve", lambda o=o: nc.vector.reciprocal(out=SMALL[:, o + 4:o + 8], in_=SMALL[:, o:o + 4]), [dn], [dn + "r"])
                      ot = OT[t % 2]
                      otn = f"OT{t % 2}"
                      tt(ot[:].rearrange("p (h e) -> p h e", h=4), Ov[:, :, 0:64],
                         SMALL[:, o + 4:o + 8].unsqueeze(2).to_broadcast([128, 4, 64]), ALU.mult, [Oname, dn + "r"], [otn])
                      pv = PST[:, (t % 2) * 512:(t % 2) * 512 + 256]
                      pvn = "PST"
                      for cc in range(2):
                          tr(pv[:, cc * 128:(cc + 1) * 128], ot[:, cc * 128:(cc + 1) * 128], IDB, [otn, "CBF"], [pvn])
                      cpv(HB[:, cat0:cat0 + 2, h0:h0 + 128], pv.rearrange("p (c t) -> p c t", c=2), [pvn], [hbres(c)])

                  def run_attn(jobs, sink, cat0):
                      S.op("dve", lambda: nc.vector.memset(QZ, 0.0), [], [("PC", 0), ("PC", 1)] + [("QZ", i) for i in range(8)])
                      prev = None
                      for job in jobs + [None]:
                          if job is not None:
                              po = job["po"]
                              cpv(QZ[po:po + 64, job["slot"], :], job["Q"], [("QZ", job["slot"]), job["qres"]], [("QZ", job["slot"])])
                              att_scores(job)
                          if prev is not None:
                              att_pv(prev)
                              if prev["last"]:
                                  finish_tile(prev["t"], prev["c"], prev["O"], prev["On"], sink, cat0)
                                  advance_mod()
                          prev = job

                  def tcols(t):
                      return slice(t * 128, (t + 1) * 128)

                  jobs = []
                  for (t, c) in tiles_h:
                      Ot, On = (PSV["P2a"], "P2a") if t % 2 == 0 else (PSV["P2b"], "P2b")
                      for h in range(4):
                          kv, g = h // 2, h % 2
                          po = kv * 64
                          if c == 4:
                              chunks = [(16, None), (17, None)]
                          else:
                              chunks = []
                              if t > 0:
                                  chunks.append((t - 1, MASKP))
                              chunks.append((t, None))
                              if t < 15:
                                  chunks.append((t + 1, MASKN))
                              chunks += [(16, None), (17, None)]
                          jobs.append({"t": t, "c": c, "h": h, "K": (lambda kt: KT[:, 0, tcols(kt)]), "po": po,
                                       "kidx": 0, "qres": ("QT", g, c), "vname": "VA",
                                       "slot": (t % 2) * 4 + h,
                                       "Q": QT[po:po + 64, g, tcols(t)], "V": (lambda kt, kv=kv: VA[:, kt, kv, 0:65]),
                                       "chunks": chunks, "O": Ot, "On": On, "last": h == 3})
                  run_attn(jobs, True, 0)
                  stage(f"attA_{half}")
                  dma_pool(TAB[:, 0:5376], Ld["dtab"], [], ["TAB"])
                  DT = TAB[:, 0:5376].rearrange("p (h s q) -> p h s q", h=4, s=21)
                  jobs = []
                  for (t, c) in tiles_h:
                      Ot, On = (PSV["P2a"], "P2a") if t % 2 == 0 else (PSV["P2b"], "P2b")
                      if c == 4:
                          deltas, interior = [], False
                      elif 2 <= t <= 13:
                          deltas, interior = [-2, -1, 0, 1, 2], True
                      elif t == 0:
                          deltas, interior = [0, 1, 2, 3], False
                      elif t == 1:
                          deltas, interior = [-1, 0, 1, 2], False
                      elif t == 14:
                          deltas, interior = [-2, -1, 0, 1], False
                      else:
                          deltas, interior = [-3, -2, -1, 0], False
                      for h in range(4):
                          po = (h % 2) * 64
                          ch = h // 2
                          chunks = []
                          for dl in deltas:
                              i0 = _dslot(interior, 2 * dl + 7)
                              i1 = _dslot(interior, 2 * dl + 6)
                              chunks.append((t + dl, DT[:, h, i0:i1 + 1:(i1 - i0), :]))
                          chunks += [(16, None), (17, None)]
                          jobs.append({"t": t, "c": c, "h": h, "K": (lambda kt, ch=ch: KT[:, 1 + ch, tcols(kt)]), "po": po,
                                       "kidx": 1 + ch, "qres": ("QT", 2 + ch, c), "vname": "VD",
                                       "slot": (t % 2) * 4 + h,
                                       "Q": QT[po:po + 64, 2 + ch, tcols(t)], "V": (lambda kt, h=h: VD[:, kt, h, 0:65]),
                                       "chunks": chunks, "O": Ot, "On": On, "last": h == 3})
                  run_attn(jobs, False, 6)
                  if half == 0:
                      tap(f"cat{l}", HB[:, :, 0:1024], [128, 8, 1024], BF16, [])
                  stage(f"attD_{half}")
                  wcount = 0
                  for a in range(4):
                      slot, sr = load_piece(l, NP_WO + a)
                      sv = slot[:].rearrange("p (k n) -> p k n", k=8)
                      for di in range(2):
                          dch = 2 * a + di
                          for c in hgroups:
                              s0, Lc = GROUPS[c]
                              h0, _ = hcols(c, half)
                              s = sidx(c)
                              pn = ["P0a", "P0b", "P1a", "P1b"][wcount % 4]
                              wcount += 1
                              pw = PSV[pn][:, 0:Lc]
                              for k in range(8):
                                  mm(pw, sv[:, k, di * 128:(di + 1) * 128], HB[:, k, h0:h0 + Lc], k == 0, k == 7,
                                     [sr, hbres(c)], [pn])
                              stt(X[:, dch, s0:s0 + Lc], pw, G1[:, dch, s:s + 1], X[:, dch, s0:s0 + Lc], ALU.mult, ALU.add,
                                  [pn, MODn, xres(c)], [xres(c)])
                  stage(f"wout_{half}")
                  if half == 1:
                      S.barrier()
              advance_mod(64)
              stage("wout")
              tap(f"x1_{l}", X[:], [128, 8, NTOK], F32, [])

              stt(A2[:], MOD[:, 32:40, :], 1.0, VEC[:, 8:16].unsqueeze(2).to_broadcast([128, 8, 2]), ALU.add, ALU.mult,
                  [MODn, "VEC"], ["A2"])
              def h2dst(k, c):
                  s0, Lc = GROUPS[c]
                  if c < 2:
                      return HB[:, k, s0:s0 + Lc], hbres(c)
                  return H2B[:, k, s0 - 1024:s0 - 1024 + Lc], ("H2B", c)

              do_norms(groups_all, A2, SH2, h2dst, "A2")
              stage("norm2")
              gcount = 0
              dcount = 0
              for bi, blk in enumerate(FFBLOCKS):
                  for fl, f in enumerate(blk):
                      slot, sr = load_piece(l, NP_GU + f)
                      sv = slot[:].rearrange("p (u k n) -> p u k n", u=2, k=8)
                      for c in groups_all:
                          s0, Lc = GROUPS[c]
                          i = gcount % 2
                          gcount += 1
                          pgn, pun = ("P0a", "P0b") if i == 0 else ("P1a", "P1b")
                          pg, pu = PSV[pgn][:, 0:Lc], PSV[pun][:, 0:Lc]
                          for k in range(8):
                              hsrc, hres = h2dst(k, c)
                              mm(pg, sv[:, 0, k, :], hsrc, k == 0, k == 7, [sr, hres], [pgn])
                          for k in range(8):
                              hsrc, hres = h2dst(k, c)
                              mm(pu, sv[:, 1, k, :], hsrc, k == 0, k == 7, [sr, hres], [pun])
                          sg, sgn = (Fs[0], "F0") if i == 0 else (Fs[1], "F1")
                          act(sg[:, 0:Lc], pg, AF.Silu, [pgn], [sgn])
                          tt(ACTB[:, fl, s0:s0 + Lc], pu, sg[:, 0:Lc], ALU.mult, [sgn, pun], [("ACTB", fl, c)])
                  nf = len(blk)
                  for dch in range(8):
                      slot, sr = load_piece(l, NP_WD + bi * 8 + dch, nf * 128)
                      sv = slot[:, 0:nf * 128].rearrange("p (f n) -> p f n", f=nf)
                      for c in groups_all:
                          s0, Lc = GROUPS[c]
                          s = sidx(c)
                          pn = ["P2a", "P2b", "P3"][dcount % 3]
                          dcount += 1
                          pd = PSV[pn][:, 0:Lc]
                          for fl in range(nf):
                              mm(pd, sv[:, fl, :], ACTB[:, fl, s0:s0 + Lc], fl == 0, fl == nf - 1, [sr, ("ACTB", fl, c)], [pn])
                          stt(X[:, dch, s0:s0 + Lc], pd, G2[:, dch, s:s + 1], X[:, dch, s0:s0 + Lc], ALU.mult, ALU.add,
                              [pn, MODn, xres(c)], [xres(c)])
              S.barrier()
              stage("ffn")
              tap(f"x2_{l}", X[:], [128, 8, NTOK], F32, [])


        try:
            run_layers()
        except _Stop:
            pass
        for t in range(16):
            Pt = P0 if t % 2 == 0 else P1
            pr = ["P0a", "P0b"] if t % 2 == 0 else ["P1a", "P1b"]
            for k in range(8):
                tr(Pt[:, k * 128:(k + 1) * 128], X[:, k, t * 128:(t + 1) * 128], IDF[:], [xres(t // 4), "IDF"], pr)
            stg = SCR[:, (t % 2) * 1024:(t % 2 + 1) * 1024]
            stg_r = ["F0", "F1"] if t % 2 == 0 else ["F2", "F3"]
            if t % 2 == 0:
                act(stg, Pt[:], AF.Copy, pr, stg_r)
            else:
                cpv(stg, Pt[:], pr, stg_r)
            dma_sp(out_d[t * 128:(t + 1) * 128, :], stg, stg_r, [("out", t)])
        S.emit(st)
    return nc, tap_d


_CACHE = {}


def prep_inputs(inp, n_layers=2):
    inp = {k: np.asarray(v, dtype=np.float32) for k, v in inp.items()}
    cbf, rope, pm = _consts()
    shared = {"cbf": np.ascontiguousarray(cbf.reshape(128, -1)), "idf": np.eye(128, dtype=np.float32),
              "rope": np.ascontiguousarray(rope.reshape(128, -1)), "pm": np.ascontiguousarray(pm.reshape(128, -1))}
    for l in range(n_layers):
        la = _layer_arrays(inp, l)
        for k, v in la.items():
            shared[f"{k}{l}"] = v
    maps = []
    for b in range(8):
        m = dict(shared)
        m["x"] = np.ascontiguousarray(inp["x"][b])
        m["ctx"] = np.ascontiguousarray(inp["ctx"][b])
        cv = np.zeros((128, 8, 2), np.float32)
        cv[:, :, 0] = inp["c"][b].reshape(8, 128).T
        cv[:, :, 1] = inp["c_ctx"].reshape(8, 128).T
        m["cvec"] = np.ascontiguousarray(cv.reshape(128, 16))
        maps.append(m)
    return maps


def kernel(**inputs):
    if "nc" not in _CACHE:
        _CACHE["nc"] = build(2)[0]
    nc = _CACHE["nc"]
    maps = prep_inputs(inputs, 2)
    res = run_bass_kernel_spmd(nc, maps, core_ids=list(range(8)))
    return np.stack([np.asarray(r["out"], dtype=np.float32) for r in res.results], 0)
```

```python
import contextlib
import numpy as np
import concourse.bass as bass
import concourse.mybir as mybir
from concourse.bass_utils import run_bass_kernel_spmd

F32 = mybir.dt.float32
BF16 = mybir.dt.bfloat16
AF = mybir.ActivationFunctionType
ALU = mybir.AluOpType

D = 1024
SEQ = 2048
CTX = 256
NTOK = SEQ + CTX
FF = 2816
NFF = 22
NEG = -30000.0
EPS = 1e-6
GROUPS = [(0, 512), (512, 512), (1024, 512), (1536, 512), (2048, 256)]
HALVES = [[0, 1], [2, 3, 4]]
HSTART = [0, 1024]
FFBLOCKS = [list(range(0, 8)), list(range(8, 15)), list(range(15, 22))]
PIECE = 2048
import os as _os
_DBG_DELAY = int(_os.environ.get('DBG_DELAY', '0'))
_DBG_SKIP = set(int(v) for v in _os.environ.get('DBG_SKIP_A', '').split(',') if v)


class _Op:
    __slots__ = ("eng", "fn", "dma", "deps", "has_dep", "sig")

    def __init__(self, eng, fn, dma):
        self.eng = eng
        self.fn = fn
        self.dma = dma
        self.deps = []
        self.has_dep = False
        self.sig = None


class Sched:
    COMPUTE = ("pe", "act", "dve")

    def __init__(self, nc, n_dma_sems=8):
        self.nc = nc
        self.ops = []
        self.res = {}
        self.n_dma_sems = n_dma_sems
        self.last = {}
        self.pending = {}
        self.dma_since = []

    def op(self, eng, fn, reads=(), writes=(), dma=False):
        o = _Op(eng, fn, dma)
        deps = {}
        for r in reads:
            st = self.res.get(r)
            if st is not None and st[0] is not None:
                deps[id(st[0])] = (st[0], True)
        for w in writes:
            st = self.res.get(w)
            if st is not None:
                if st[0] is not None and id(st[0]) not in deps:
                    deps[id(st[0])] = (st[0], False)
                for rd in st[1]:
                    if id(rd) not in deps:
                        deps[id(rd)] = (rd, False)
        for d in self.pending.pop(eng, ()):
            if id(d) not in deps:
                deps[id(d)] = (d, True)
        for d, raw in deps.values():
            if d is o:
                continue
            if not d.dma and not o.dma and d.eng == eng:
                if eng == "pe":
                    continue
            o.deps.append(d)
            d.has_dep = True
        for r in reads:
            st = self.res.get(r)
            if st is None:
                self.res[r] = [None, [o]]
            else:
                st[1].append(o)
        for w in writes:
            self.res[w] = [o, []]
        self.ops.append(o)
        self.last[eng] = o
        if dma:
            self.dma_since.append(o)
        return o

    def barrier(self):
        lasts = [o for e, o in self.last.items() if e in self.COMPUTE]
        lasts += self.dma_since
        self.dma_since = []
        for e in self.COMPUTE:
            self.pending[e] = self.pending.get(e, []) + lasts

    def emit(self, stack):
        nc = self.nc
        eobj = {"pe": nc.tensor, "act": nc.scalar, "dve": nc.vector, "pool": nc.gpsimd, "sp": nc.sync}
        semh = {}
        cnt = {}
        waited = {e: {} for e in eobj}
        dma_rr = {}
        dcnt = {}

        def get_sem(key):
            if key not in semh:
                semh[key] = stack.enter_context(nc.semaphore("s_" + "_".join(str(k) for k in key)))
            return semh[key]

        for o in self.ops:
            E = eobj[o.eng]
            need = {}
            for d in o.deps:
                key, val = d.sig
                if need.get(key, 0) < val:
                    need[key] = val
            if o.dma:
                i = dma_rr.get(o.eng, 0)
                dma_rr[o.eng] = (i + 1) % self.n_dma_sems
                k = ("dma", o.eng, i)
                n = dcnt.get(k, 0) + 1
                dcnt[k] = n
                if n > 1 and need.get(k, 0) < 16 * (n - 1):
                    need[k] = 16 * (n - 1)
            for key, val in need.items():
                if waited[o.eng].get(key, 0) < val:
                    E.wait_ge(get_sem(key), val)
                    waited[o.eng][key] = val
            ins = o.fn()
            if o.dma:
                ins.then_inc(get_sem(k), 16)
                o.sig = (k, 16 * n)
            elif o.has_dep:
                key = ("e", o.eng)
                cnt[key] = cnt.get(key, 0) + 1
                ins.then_inc(get_sem(key), 1)
                o.sig = (key, cnt[key])
        for key, n in dcnt.items():
            nc.sync.wait_ge(get_sem(key), 16 * n)
        for key, n in cnt.items():
            nc.sync.wait_ge(get_sem(key), n)


def _kmajor(w):
    K, N = w.shape
    return np.ascontiguousarray(w.reshape(K // 128, 128, N).transpose(1, 0, 2))


def _pad_piece(a):
    a = np.ascontiguousarray(a, dtype=np.float32).reshape(128, -1)
    out = np.zeros((128, PIECE), np.float32)
    out[:, : a.shape[1]] = a
    return out


def _dslot(interior, s):
    if 3 <= s <= 9:
        return 8 + (9 - s)
    if s >= 10:
        return (0 if interior else 4) + (13 - s)
    return (18 if interior else 15) + (2 - s)


def _consts():
    c = np.zeros((128, 6, 128), np.float32)
    c[:, 0, :] = np.eye(128)
    c[:, 1, :] = 1.0
    c[0:64, 2, 0:64] = 1.0
    c[64:128, 2, 64:128] = 1.0
    for m in range(128):
        if (m % 32) < 16:
            c[m + 16, 3, m] = -1.0
        else:
            c[m - 16, 3, m] = 1.0
    j = np.arange(128)[:, None]
    i = np.arange(128)[None, :]
    c[:, 4, :] = np.where(j >= i, 0.0, NEG)
    c[:, 5, :] = np.where(j <= i, 0.0, NEG)
    p = np.arange(128)
    idx = p % 64
    f = (idx % 16).astype(np.float64)
    inv = np.power(10000.0, -f / 16.0)
    t = np.arange(SEQ)
    pos = np.where((idx < 32)[:, None], (t // 64)[None, :], (t % 64)[None, :]).astype(np.float64)
    ang = pos * inv[:, None]
    rope = np.stack([np.cos(ang), np.sin(ang)], axis=1).astype(np.float32)
    pm = np.zeros((128, 20, 128), np.float32)
    jj = np.arange(128)[:, None]
    tt = np.arange(128)[None, :]
    for g, w in enumerate((2, 4, 8, 16)):
        h = w // 2
        eye = (jj == tt).astype(np.float32)
        pm[:, g * 5 + 0, :] = ((jj >= tt - h) & (jj < tt + h)) / w - eye
        lo = np.maximum(tt - h, 0)
        cntf = (tt + h) - lo
        pm[:, g * 5 + 1, :] = ((jj >= lo) & (jj < tt + h)) / cntf - eye
        hi = np.minimum(tt + h, 128)
        cntl = hi - (tt - h)
        pm[:, g * 5 + 2, :] = ((jj >= tt - h) & (jj < hi)) / cntl - eye
        pm[:, g * 5 + 3, :] = (jj >= 128 + tt - h) / w
        pm[:, g * 5 + 4, :] = (jj < tt + h - 128) / w
    return c, rope, pm


def _dtab(rpb):
    kc = np.arange(64)[:, None]
    qc = np.arange(64)[None, :]
    cs = np.clip(qc - 8, 0, 48)
    col_ok = (kc >= cs) & (kc < cs + 16)
    coff = np.clip(kc - qc, -15, 15) + 15
    out = np.full((128, 4, 21, 64), NEG, np.float32)
    for h in range(4):
        for interior in (True, False):
            for s in range(14):
                sl = _dslot(interior, s)
                for kr in range(2):
                    rho = s + kr
                    ok = 0 <= rho <= 14 and ((3 <= rho <= 10) or not interior)
                    if not ok:
                        continue
                    T = np.where(col_ok, rpb[h, rho][coff], NEG)
                    out[kr * 64:(kr + 1) * 64, h, sl, :] = T
    return out


def _layer_arrays(inp, l):
    f32 = np.float32
    w_in = inp["w_in"][l]
    pieces = []
    wm = _kmajor(inp["w_mod"][l])
    for jj in range(24):
        pieces.append(_pad_piece(wm[:, :, jj * 256:(jj + 1) * 256]))
    aq = w_in[:, 0:256]
    fm_cols = [np.concatenate([aq[:, 0:64], aq[:, 128:192]], 1), np.concatenate([aq[:, 64:128], aq[:, 192:256]], 1),
               w_in[:, 256:384], w_in[:, 1280:1408], w_in[:, 1408:1536], w_in[:, 1536:1664], w_in[:, 1664:1792]]
    fm = _kmajor(np.concatenate(fm_cols, 1))
    for a in range(4):
        pieces.append(_pad_piece(fm[:, :, a * 256:min((a + 1) * 256, 896)]))
    tm = [w_in[:, 512:768], w_in[:, 768:1024], w_in[:, 1024:1280], w_in[:, 1792:2048], w_in[:, 384:512]]
    for a in tm:
        pieces.append(_pad_piece(_kmajor(a)))
    wo = _kmajor(inp["w_out"][l])
    for a in range(4):
        pieces.append(_pad_piece(wo[:, :, a * 256:(a + 1) * 256]))
    wg = _kmajor(inp["w_gate"][l])
    wu = _kmajor(inp["w_up"][l])
    for f in range(NFF):
        pieces.append(_pad_piece(np.concatenate([wg[:, :, f * 128:(f + 1) * 128].reshape(128, -1),
                                                 wu[:, :, f * 128:(f + 1) * 128].reshape(128, -1)], 1)))
    wd = _kmajor(inp["w_down"][l])
    for blk in FFBLOCKS:
        for dch in range(8):
            pieces.append(_pad_piece(wd[:, blk[0]:blk[-1] + 1, dch * 128:(dch + 1) * 128]))
    W = np.stack(pieces, 0)
    vec = np.zeros((128, 26), f32)
    vec[:, 0:8] = inp["g_mix"][l].reshape(8, 128).T
    vec[:, 8:16] = inp["g_ffn"][l].reshape(8, 128).T
    vec[:, 16] = np.tile(inp["a_q_gain"][l], 2)
    vec[:, 17] = np.tile(inp["a_k_gain"][l], 2)
    vec[:, 18] = np.tile(inp["d_q_gain"][l], 2)
    vec[:, 19] = np.tile(inp["d_k_gain"][l], 2)
    vec[:, 20:22] = inp["c_scale"][l].reshape(2, 128).T
    vec[:, 22:26] = inp["b_b_s"][l].T
    bmod = np.ascontiguousarray(inp["b_mod"][l].reshape(48, 128).T)
    bc = np.zeros((128, 260), f32)
    bc[:, 0:4] = inp["a_sink"][l][None, :]
    bc[:, 4:260] = inp["b_v_gain"][l][None, :]
    sm = np.zeros((128, 6, 128), f32)
    sm[:, 0:4, :] = inp["b_w_s"][l].transpose(2, 0, 1)
    wp = inp["c_w_pool"][l]
    for g in range(4):
        o = (g % 2) * 64
        sm[o:o + 64, 4 + g // 2, o:o + 64] = wp[g]
    dt = _dtab(inp["d_rpb"][l])
    return {"W": W, "vec": vec, "bmod": bmod, "bc": bc, "sm": np.ascontiguousarray(sm.reshape(128, -1)), "dtab": np.ascontiguousarray(dt.reshape(128, -1))}


NP_MOD, NP_FM, NP_TM, NP_WO, NP_GU, NP_WD = 0, 24, 28, 33, 37, 59
NPIECES = 83


class _Stop(Exception):
    pass


def build(n_layers=2, taps=(), stop=None):
    nc = bass.Bass("TRN2", target_bir_lowering=False)
    dr = {}

    def din(name, shape):
        dr[name] = nc.dram_tensor(name, list(shape), F32, kind="ExternalInput").ap()
        return dr[name]

    x_d = din("x", [SEQ, D])
    ctx_d = din("ctx", [CTX, D])
    cvec_d = din("cvec", [128, 16])
    cbf_d = din("cbf", [128, 6 * 128])
    idf_d = din("idf", [128, 128])
    rope_d = din("rope", [128, 2 * SEQ])
    pm_d = din("pm", [128, 20 * 128])
    L = []
    for l in range(n_layers):
        L.append({
            "W": din(f"W{l}", [NPIECES, 128, PIECE]), "vec": din(f"vec{l}", [128, 26]), "bmod": din(f"bmod{l}", [128, 48]),
            "bc": din(f"bc{l}", [128, 260]), "sm": din(f"sm{l}", [128, 6 * 128]), "dtab": din(f"dtab{l}", [128, 4 * 21 * 64]),
        })
    out_d = nc.dram_tensor("out", [SEQ, D], F32, kind="ExternalOutput").ap()
    tap_d = {}

    st = contextlib.ExitStack()
    with st:
        S = Sched(nc)
        sb = lambda n, s, d: st.enter_context(nc.sbuf_tensor(n, list(s), d))
        ps = lambda n, s, d: st.enter_context(nc.psum_tensor(n, list(s), d))
        X = sb("X", [128, 8, NTOK], F32)
        HB = sb("HB", [128, 8, 1280], BF16)
        RR = sb("RR", [128, 32472], BF16)
        QT = RR[:, 0:9216].rearrange("p (k t) -> p k t", k=4)
        KT = RR[:, 9216:16128].rearrange("p (k t) -> p k t", k=3)
        VA = RR[:, 16128:18504].rearrange("p (t h e) -> p t h e", t=18, h=2)
        VD = RR[:, 18504:23256].rearrange("p (t h e) -> p t h e", t=18, h=4)
        OB = RR[:, 23256:27864].rearrange("p (t c) -> p t c", t=18)
        CY = RR[:, 27864:32472].rearrange("p (t c) -> p t c", t=18)
        ACTB = RR[:, 0:18432].rearrange("p (f t) -> p f t", f=8)
        H2B = RR[:, 18432:28672].rearrange("p (k t) -> p k t", k=8)
        RING = [sb(f"ring{i}", [128, PIECE], BF16) for i in range(4)]
        TAB = sb("TAB", [128, 5376], BF16)
        SCR = sb("SCR", [128, 2048], F32)
        Fs = [SCR[:, i * 512:(i + 1) * 512] for i in range(4)]
        SQ = [sb(f"SQ{i}", [128, 512], BF16) for i in range(2)]
        QG = sb("QG", [128, 512], BF16)
        RQ = sb("RQ", [128, 512], BF16)
        PT = [sb(f"PT{i}", [128, 896], BF16) for i in range(2)]
        OT = [sb(f"OT{i}", [128, 256], BF16) for i in range(2)]
        PC = sb("PC", [128, 2, 512], BF16)
        QZ = PC[:].rearrange("p a (b q) -> p (a b) q", q=128)
        VN = [sb(f"VN{i}", [128, 256], BF16) for i in range(2)]
        CBF = sb("CBF", [128, 6, 128], BF16)
        IDF = sb("IDF", [128, 128], F32)
        CV = sb("CV", [128, 8, 2], F32)
        CA = sb("CA", [128, 8, 2], BF16)
        SMALL = sb("SMALL", [128, 64], F32)
        IDB = CBF[:, 0, :]
        ONES = CBF[:, 1, :]
        BDONES = CBF[:, 2, :]
        ROTT = CBF[:, 3, :]
        MASKP = CBF[:, 4, :]
        MASKN = CBF[:, 5, :]
        P0 = ps("P0", [128, 1024], F32)
        P1 = ps("P1", [128, 1024], F32)
        P2 = ps("P2", [128, 1024], F32)
        P3 = ps("P3", [128, 512], F32)
        PST = ps("PST", [128, 1024], BF16)
        PSV = {"P0a": P0[:, 0:512], "P0b": P0[:, 512:1024], "P1a": P1[:, 0:512], "P1b": P1[:, 512:1024],
               "P2a": P2[:, 0:512], "P2b": P2[:, 512:1024], "P3": P3[:, :]}

        def mm(out, lhsT, rhs, start, stop, reads, writes):
            S.op("pe", lambda: nc.tensor.matmul(out, lhsT=lhsT, rhs=rhs, start=start, stop=stop), reads, writes)

        def tr(out, in_, ident, reads, writes):
            S.op("pe", lambda: nc.tensor.transpose(out, in_, ident), reads, writes)

        def act(out, in_, func, reads, writes, scale=None, bias=None):
            kw = {}
            if scale is not None:
                kw["scale"] = scale
            if bias is not None:
                kw["bias"] = bias
            S.op("act", lambda: nc.scalar.activation(out=out, in_=in_, func=func, **kw), reads, writes)

        def tt(out, in0, in1, op, reads, writes):
            S.op("dve", lambda: nc.vector.tensor_tensor(out=out, in0=in0, in1=in1, op=op), reads, writes)

        def stt(out, in0, scalar, in1, op0, op1, reads, writes):
            S.op("dve", lambda: nc.vector.scalar_tensor_tensor(out=out, in0=in0, scalar=scalar, in1=in1, op0=op0, op1=op1),
                 reads, writes)

        def ts(out, in0, s1, s2, op0, op1, reads, writes):
            S.op("dve", lambda: nc.vector.tensor_scalar(out=out, in0=in0, scalar1=s1, scalar2=s2, op0=op0, op1=op1),
                 reads, writes)

        def cpv(out, in_, reads, writes):
            S.op("dve", lambda: nc.vector.tensor_copy(out=out, in_=in_), reads, writes)

        def dma_sp(out, in_, reads, writes):
            S.op("sp", lambda: nc.sync.dma_start(out=out, in_=in_), reads, writes, dma=True)

        def dma_pool(out, in_, reads, writes):
            S.op("pool", lambda: nc.gpsimd.dma_start(out=out, in_=in_, max_dma_last_dim=4096), reads, writes, dma=True)

        def stage(name):
            if stop == name:
                S.barrier()
                raise _Stop()

        def tap(name, ap, shape, dtype, reads):
            if name not in taps:
                return
            S.barrier()
            t = nc.dram_tensor("tap_" + name, list(shape), dtype, kind="ExternalOutput").ap()
            tap_d[name] = t
            dma_sp(t, ap, list(reads) + ["tapsrc"], ["tap_" + name])

        ring_i = [0]
        ring_serial = {}

        def load_piece(l, idx, nelem=PIECE):
            i = ring_i[0]
            ring_i[0] = (i + 1) % 4
            dma_pool(RING[i][:, 0:nelem], L[l]["W"][idx, :, 0:nelem], [], [("slot", i)])
            ring_serial[("slot", i)] = ring_serial.get(("slot", i), 0) + 1
            return RING[i], ("slot", i)

        dma_sp(IDF[:], idf_d, [], ["IDF"])
        dma_sp(CV[:].rearrange("p k s -> p (k s)"), cvec_d, [], ["CV"])
        dma_pool(CBF[:].rearrange("p a b -> p (a b)"), cbf_d, [], ["CBF"])
        act(CA[:], CV[:], AF.Silu, ["CV"], ["CA"])
        for t in range(18):
            stg = SCR[:, (t % 2) * 1024:(t % 2 + 1) * 1024]
            stg_r = ["F0", "F1"] if t % 2 == 0 else ["F2", "F3"]
            src = x_d[t * 128:(t + 1) * 128, :] if t < 16 else ctx_d[(t - 16) * 128:(t - 15) * 128, :]
            dma_sp(stg, src, [], stg_r)
            Pt = P0 if t % 2 == 0 else P1
            pr = ["P0a", "P0b"] if t % 2 == 0 else ["P1a", "P1b"]
            for k in range(8):
                tr(Pt[:, k * 128:(k + 1) * 128], stg[:, k * 128:(k + 1) * 128], IDF[:], stg_r + ["IDF"], pr)
            dst = X[:, :, t * 128:(t + 1) * 128]
            src_ps = Pt[:].rearrange("p (k t) -> p k t", k=8)
            xr = [("X", t // 4 if t < 16 else 4)]
            if t % 2 == 0:
                act(dst, src_ps, AF.Copy, pr, xr)
            else:
                cpv(dst, src_ps, pr, xr)
        S.barrier()

        def xres(c):
            return ("X", c)

        def hbres(c):
            return ("HB", 2) if c == 4 else ("HB", c % 2)

        def grp_of_tile(t):
            return t // 4 if t < 16 else 4

        def sidx(c):
            return 1 if c == 4 else 0

        VEC = sb("VEC", [128, 26], F32)
        BMODs = [sb(f"BMOD{i}", [128, 48], F32) for i in range(2)]
        BC = sb("BC", [128, 260], F32)
        SM = sb("SM", [128, 6, 128], BF16)
        MODs = [sb(f"MOD{i}", [128, 48, 2], F32) for i in range(2)]
        A1 = sb("A1", [128, 8, 2], F32)
        A2 = sb("A2", [128, 8, 2], F32)
        GS = sb("GS", [128, 8], F32)

        def mod_gen(l, jj0=0, jj1=24):
            b = l % 2
            if jj0 == 0:
                dma_sp(BMODs[b][:], L[l]["bmod"], [], [f"BMOD{b}"])
            for jj in range(jj0, jj1):
                slot, sr = load_piece(l, NP_MOD + jj)
                sv = slot[:].rearrange("p (k n) -> p k n", k=8)
                for cc in range(2):
                    j = 2 * jj + cc
                    for k in range(8):
                        mm(P3[:, 2 * j:2 * j + 2], sv[:, k, cc * 128:(cc + 1) * 128], CA[:, k, :], k == 0, k == 7,
                           [sr, "CA"], ["P3"])
                yield
            c0, c1 = 2 * jj0, 2 * jj1
            tt(MODs[b][:, c0:c1, :], P3[:, 2 * c0:2 * c1].rearrange("p (j s) -> p j s", s=2),
               BMODs[b][:, c0:c1].unsqueeze(2).to_broadcast([128, c1 - c0, 2]), ALU.add, ["P3", f"BMOD{b}"], [f"MOD{b}"])
            yield

        def run_layers():
          for l in range(n_layers):
              last = l == n_layers - 1
              Ld = L[l]
              dma_sp(VEC[:], Ld["vec"], [], ["VEC"])
              dma_sp(BC[:], Ld["bc"], [], ["BC"])
              dma_pool(SM[:].rearrange("p a b -> p (a b)"), Ld["sm"], [], ["SM"])
              WST = SM[:, 0:4, :]
              WPBD = SM[:, 4:6, :]

              MOD = MODs[l % 2]
              MODn = f"MOD{l % 2}"
              late0 = [None]
              if l == 0:
                  for _ in mod_gen(0, 0, 8):
                      pass
                  late0[0] = mod_gen(0, 8, 24)
              pre = [mod_gen(l + 1) if not last else None]

              def _adv(box, n):
                  for _ in range(n):
                      if box[0] is not None:
                          try:
                              next(box[0])
                          except StopIteration:
                              box[0] = None

              def advance_mod(n=1):
                  _adv(pre, n)
              stt(A1[:], MOD[:, 8:16, :], 1.0, VEC[:, 0:8].unsqueeze(2).to_broadcast([128, 8, 2]), ALU.add, ALU.mult,
                  [MODn, "VEC"], ["A1"])
              ts(GS[:, 0:1], VEC[:, 16:17], 0.125, None, ALU.mult, ALU.bypass, ["VEC"], ["GS"])
              ts(GS[:, 1:2], VEC[:, 17:18], 1.0, None, ALU.mult, ALU.bypass, ["VEC"], ["GS"])
              ts(GS[:, 2:3], VEC[:, 18:19], 0.125, None, ALU.mult, ALU.bypass, ["VEC"], ["GS"])
              ts(GS[:, 3:4], VEC[:, 19:20], 1.0, None, ALU.mult, ALU.bypass, ["VEC"], ["GS"])
              act(GS[:, 4:8], BC[:, 0:4], AF.Exp, ["BC"], ["GS"])
              SH1, G1, SH2, G2 = MOD[:, 0:8, :], MOD[:, 16:24, :], MOD[:, 24:32, :], MOD[:, 40:48, :]
              stage("modload")
              tap(f"mod{l}", MOD[:], [128, 48, 2], F32, [MODn])

              stage("mod")
              def do_norms(groups, Amul, SHv, dst_fn, tag):
                  def bank(c):
                      return (PSV["P2a"], "P2a") if c % 2 == 0 else (PSV["P2b"], "P2b")

                  def stat_k(c, k):
                      s0, Lc = GROUPS[c]
                      psn, psr = bank(c)
                      sq = SQ[k % 2]
                      act(sq[:, 0:Lc], X[:, k, s0:s0 + Lc], AF.Square, [xres(c)], [f"SQ{k % 2}"])
                      mm(psn[:, 0:Lc], ONES, sq[:, 0:Lc], k == 0, k == 7, [f"SQ{k % 2}", "CBF"], [psr])

                  def stat_fin(c):
                      s0, Lc = GROUPS[c]
                      psn, psr = bank(c)
                      act(Fs[0][:, 0:Lc], psn[:, 0:Lc], AF.Ln, [psr], ["F0"], scale=1.0 / D, bias=EPS)
                      act(psn[:, 0:Lc], Fs[0][:, 0:Lc], AF.Exp, ["F0"], [psr], scale=-0.5)

                  def apply_k(c, k):
                      s0, Lc = GROUPS[c]
                      s = sidx(c)
                      psn, psr = bank(c)
                      tmp, tr_ = (Fs[2], "F2") if k % 2 == 0 else (Fs[3], "F3")
                      tt(tmp[:, 0:Lc], psn[:, 0:Lc], X[:, k, s0:s0 + Lc], ALU.mult, [xres(c), psr], [tr_])
                      dst, dres = dst_fn(k, c)
                      act(dst, tmp[:, 0:Lc], AF.Identity, [tr_, MODn, tag], [dres], scale=Amul[:, k, s:s + 1],
                          bias=SHv[:, k, s:s + 1])

                  n = len(groups)
                  for k in range(8):
                      stat_k(groups[0], k)
                  stat_fin(groups[0])
                  for i in range(n):
                      for k in range(8):
                          if i + 1 < n:
                              stat_k(groups[i + 1], k)
                          apply_k(groups[i], k)
                      if i + 1 < n:
                          stat_fin(groups[i + 1])

              groups_all = [0, 1, 2, 3] if last else [0, 1, 2, 3, 4]

              dma_pool(TAB[:, 0:4096], rope_d, [], ["TAB"])
              COS = TAB[:, 0:2048]
              SIN = TAB[:, 2048:4096]
              S.op("dve", lambda: nc.vector.memset(VA[:, :, :, 64:65], 1.0), [], ["VAones"])
              S.op("dve", lambda: nc.vector.memset(VD[:, :, :, 64:65], 1.0), [], ["VDones"])

              def hcols(c, half):
                  s0, Lc = GROUPS[c]
                  return s0 - HSTART[half], Lc

              for half in range(2):
                  hgroups = HALVES[half]
                  def hdst(k, c):
                      h0, Lc = hcols(c, half)
                      return HB[:, k, h0:h0 + Lc], hbres(c)

                  do_norms(hgroups, A1, SH1, hdst, "A1")
                  if half == 0:
                      tap(f"hT{l}", HB[:, :, 0:1024], [128, 8, 1024], BF16, [("HB", 0), ("HB", 1)])
                  stage(f"norm1_{half}")
                  combos = []
                  for a in range(4):
                      js = [2 * a, 2 * a + 1] if a < 3 else [6]
                      for ji, j in enumerate(js):
                          for c in hgroups:
                              if last and c == 4 and j in (0, 1, 3, 4):
                                  continue
                              combos.append((a, len(js), ji, j, c))
                  fm_slot = {}

                  def fm_piece(a, njs):
                      if a not in fm_slot:
                          slot, sr = load_piece(l, NP_FM + a, 8 * 128 * njs)
                          fm_slot[a] = (slot[:, 0:8 * 128 * njs].rearrange("p (k n) -> p k n", k=8), sr, ring_serial[sr])
                      assert ring_serial[fm_slot[a][1]] == fm_slot[a][2], "FM weight slot recycled while still in use"
                      return fm_slot[a][0], fm_slot[a][1]

                  def fm_info(k):
                      a, njs, ji, j, c = combos[k]
                      s0, Lc = GROUPS[c]
                      h0, _ = hcols(c, half)
                      pfn = ["P0a", "P0b", "P1a"][k % 3]
                      phn = ["P1b", "P2a"][k % 2]
                      prn = "P2b"
                      gcol = {0: 0, 1: 0, 2: 1, 3: 2, 4: 2, 5: 3, 6: 3}[j]
                      if j < 2:
                          dst, dres = QT[:, j, s0:s0 + Lc], ("QT", j, c)
                      elif j == 2:
                          dst, dres = KT[:, 0, s0:s0 + Lc], ("KT", 0, c)
                      elif j < 5:
                          dst, dres = QT[:, j - 1, s0:s0 + Lc], ("QT", j - 1, c)
                      else:
                          dst, dres = KT[:, j - 4, s0:s0 + Lc], ("KT", j - 4, c)
                      return dict(a=a, njs=njs, ji=ji, j=j, c=c, s0=s0, Lc=Lc, h0=h0, pfn=pfn, phn=phn, prn=prn,
                                  pf=PSV[pfn][:, 0:Lc], ph=PSV[phn][:, 0:Lc], pr=PSV[prn][:, 0:Lc],
                                  gain=GS[:, gcol:gcol + 1], dst=dst, dres=dres, rope=(j < 3 and c != 4),
                                  sq=SQ[k % 2], sqn=f"SQ{k % 2}")

                  def fm_head(k):
                      f = fm_info(k)
                      sv, sr = fm_piece(f["a"], f["njs"])
                      for kk in range(8):
                          mm(f["pf"], sv[:, kk, f["ji"] * 128:(f["ji"] + 1) * 128], HB[:, kk, f["h0"]:f["h0"] + f["Lc"]],
                             kk == 0, kk == 7, [sr, ("HB", f["c"])], [f["pfn"]])
                      act(f["sq"][:, 0:f["Lc"]], f["pf"], AF.Square, [f["pfn"]], [f["sqn"]])

                  def fm_qg(k):
                      f = fm_info(k)
                      if f["rope"]:
                          act(QG[:, 0:f["Lc"]], f["pf"], AF.Identity, [f["pfn"], "GS"], ["QG"], scale=f["gain"])

                  def fm_mid(k):
                      f = fm_info(k)
                      Lc, s0 = f["Lc"], f["s0"]
                      mm(f["ph"], BDONES, f["sq"][:, 0:Lc], True, True, [f["sqn"], "CBF"], [f["phn"]])
                      if f["rope"]:
                          mm(f["pr"], ROTT, QG[:, 0:Lc], True, True, ["QG", "CBF"], [f["prn"]])
                          tt(Fs[2][:, 0:Lc], QG[:, 0:Lc], COS[:, s0:s0 + Lc], ALU.mult, ["QG", "TAB"], ["F2"])

                  def fm_tail(k):
                      f = fm_info(k)
                      Lc, s0 = f["Lc"], f["s0"]
                      act(Fs[0][:, 0:Lc], f["ph"], AF.Ln, [f["phn"]], ["F0"], scale=1.0 / 64, bias=EPS)
                      act(Fs[1][:, 0:Lc], Fs[0][:, 0:Lc], AF.Exp, ["F0"], ["F1"], scale=-0.5)
                      if f["rope"]:
                          act(RQ[:, 0:Lc], f["pr"], AF.Copy, [f["prn"]], ["RQ"])
                          tt(Fs[3][:, 0:Lc], RQ[:, 0:Lc], SIN[:, s0:s0 + Lc], ALU.mult, ["RQ", "TAB"], ["F3"])
                          tt(Fs[2][:, 0:Lc], Fs[2][:, 0:Lc], Fs[3][:, 0:Lc], ALU.add, ["F2", "F3"], ["F2"])
                          tt(f["dst"], Fs[2][:, 0:Lc], Fs[1][:, 0:Lc], ALU.mult, ["F2", "F1"], [f["dres"]])
                      else:
                          stt(f["dst"], f["pf"], f["gain"], Fs[1][:, 0:Lc], ALU.mult, ALU.mult, [f["pfn"], "GS", "F1"], [f["dres"]])

                  nfm = len(combos)
                  fm_head(0)
                  fm_qg(0)
                  for k in range(nfm):
                      if k + 1 < nfm:
                          fm_head(k + 1)
                      fm_mid(k)
                      if k + 1 < nfm:
                          fm_qg(k + 1)
                      fm_tail(k)
                      if half == 0:
                          _adv(late0, 1)
                  _adv(late0, 64)
                  stage(f"fm_{half}")
                  tiles_h = []
                  for c in hgroups:
                      s0, Lc = GROUPS[c]
                      tiles_h += [(t, c) for t in range(s0 // 128, (s0 + Lc) // 128)]
                  tcount = [0]

                  def tm_proj(t, c, sv, srs, ncol):
                      sr, ser = srs
                      assert ring_serial[sr] == ser, "TM weight slot recycled while still in use"
                      h0 = t * 128 - HSTART[half]
                      pn = ["P0a", "P0b", "P1a", "P1b"][tcount[0] % 4]
                      tcount[0] += 1
                      pp = PSV[pn][:, 0:ncol]
                      for k in range(8):
                          mm(pp, HB[:, k, h0:h0 + 128], sv[:, k, :], k == 0, k == 7, [sr, hbres(c)], [pn])
                      return pp, pn

                  def tm_piece(pi):
                      ncol = 128 if pi == 4 else 256
                      slot, sr = load_piece(l, NP_TM + pi, 8 * ncol)
                      return slot[:, 0:8 * ncol].rearrange("p (k n) -> p k n", k=8), (sr, ring_serial[sr]), ncol

                  svU, srU, _ = tm_piece(0)
                  for (t, c) in tiles_h:
                      if last and c == 4:
                          continue
                      pp, pn = tm_proj(t, c, svU, srU, 256)
                      act(OB[:, t, :], pp, AF.Gelu_apprx_tanh, [pn], [("OB", t)])
                  svV, srV, _ = tm_piece(1)
                  svC, srC, _ = tm_piece(2)
                  svD, srD, _ = tm_piece(3)
                  svA, srA, _ = tm_piece(4)

                  def emit_z(t):
                      vn = VN[t % 2]
                      pzn = "P2a" if t % 2 == 0 else "P2b"
                      pz = PSV[pzn]
                      for g in range(4):
                          mm(pz[:, g * 64:(g + 1) * 64], WST[:, g, :], vn[:, g * 64:(g + 1) * 64], True, True,
                             [f"VN{t % 2}", "SM"], [pzn])
                      guf, gufn = (Fs[2], "F2") if t % 2 == 0 else (Fs[3], "F3")
                      act(guf[:, 0:256], OB[:, t, :], AF.Copy, [("OB", t)], [gufn])
                      for g in range(4):
                          stt(OB[:, t, g * 64:(g + 1) * 64], pz[:, g * 64:(g + 1) * 64], VEC[:, 22 + g:23 + g],
                              guf[:, g * 64:(g + 1) * 64], ALU.add, ALU.mult, [pzn, "VEC", gufn], [("OB", t)])

                  pending_z = []
                  for pi0 in range(0, len(tiles_h), 2):
                      pair = tiles_h[pi0:pi0 + 2]
                      fulls = [(t, c) for (t, c) in pair if not (last and c == 4)]
                      for (t, c) in fulls:
                          pp, pn = tm_proj(t, c, svV, srV, 256)
                          gv, gvn = (Fs[0], "F0") if t % 2 == 0 else (Fs[1], "F1")
                          gv = gv[:, 0:256]
                          act(gv, pp, AF.Gelu_apprx_tanh, [pn], [gvn])
                          o = (t % 2) * 16
                          stn = f"SMALL{t % 2}"
                          S.op("dve", lambda gv=gv, o=o: nc.vector.bn_stats(out=SMALL[:, o:o + 6], in_=gv), [gvn], [stn])
                          S.op("dve", lambda o=o: nc.vector.bn_aggr(out=SMALL[:, o + 8:o + 10], in_=SMALL[:, o:o + 6]),
                               [stn], [stn + "b"])
                      for tz in pending_z:
                          emit_z(tz)
                      pending_z = []
                      if len(fulls) == 2:
                          act(SMALL[:, 10:27:16], SMALL[:, 9:26:16], AF.Ln, ["SMALL0b", "SMALL1b"], ["SMALL0c", "SMALL1c"], bias=EPS)
                          act(SMALL[:, 11:28:16], SMALL[:, 10:27:16], AF.Exp, ["SMALL0c", "SMALL1c"], ["SMALL0d", "SMALL1d"], scale=-0.5)
                      for (t, c) in fulls:
                          o = (t % 2) * 16
                          stn = f"SMALL{t % 2}"
                          gv, gvn = (Fs[0], "F0") if t % 2 == 0 else (Fs[1], "F1")
                          gv = gv[:, 0:256]
                          if len(fulls) != 2:
                              act(SMALL[:, o + 10:o + 11], SMALL[:, o + 9:o + 10], AF.Ln, [stn + "b"], [stn + "c"], bias=EPS)
                              act(SMALL[:, o + 11:o + 12], SMALL[:, o + 10:o + 11], AF.Exp, [stn + "c"], [stn + "d"], scale=-0.5)
                          ts(gv, gv, SMALL[:, o + 8:o + 9], SMALL[:, o + 11:o + 12], ALU.subtract, ALU.mult,
                             [gvn, stn + "b", stn + "d"], [gvn])
                          tt(VN[t % 2][:], gv, BC[:, 4:260], ALU.mult, [gvn, "BC"], [f"VN{t % 2}"])
                      for (t, c) in fulls:
                          pp, pn = tm_proj(t, c, svC, srC, 256)
                          act(CY[:, t, :], pp, AF.Copy, [pn], [("CY", t)])
                      for (t, c) in pair:
                          pp, pn = tm_proj(t, c, svD, srD, 256)
                          cpv(VD[:, t, :, 0:64], pp.rearrange("p (h e) -> p h e", h=4), [pn, "VDones"], [("VD", t)])
                          pp, pn = tm_proj(t, c, svA, srA, 128)
                          cpv(VA[:, t, :, 0:64], pp.rearrange("p (h e) -> p h e", h=2), [pn, "VAones"], [("VA", t)])
                      pending_z = [t for (t, c) in fulls]
                  for tz in pending_z:
                      emit_z(tz)
              _adv(late0, 64)
              stage("p1")
              tap(f"qt{l}", QT[:, :, :], [128, 4, NTOK], BF16, [])
              tap(f"kt{l}", KT[:, :, :], [128, 3, NTOK], BF16, [])
              tap(f"ob{l}", OB[:, :, :], [128, 18, 256], BF16, [])
              tap(f"cy{l}", CY[:, :, :], [128, 18, 256], BF16, [])
              tap(f"vd{l}", VD[:, :, :, :], [128, 18, 4, 66], BF16, [])

              for half in range(2):
                  hgroups = [c for c in HALVES[half] if c in groups_all]
                  tiles_h = []
                  for c in hgroups:
                      s0, Lc = GROUPS[c]
                      tiles_h += [(t, c) for t in range(s0 // 128, (s0 + Lc) // 128)]
                  dma_pool(TAB[:, 0:2560], pm_d, [], ["TAB"])
                  PMv = TAB[:, 0:2560].rearrange("p (a t) -> p a t", a=20)
                  for (t, c) in tiles_h:
                      h0 = t * 128 - HSTART[half]
                      pv = PST[:, (t % 2) * 512:(t % 2) * 512 + 256]
                      pvn = "PST"
                      for cc in range(2):
                          tr(pv[:, cc * 128:(cc + 1) * 128], OB[:, t, cc * 128:(cc + 1) * 128], IDB, [("OB", t), "CBF"], [pvn])
                      cpv(HB[:, 2:4, h0:h0 + 128], pv.rearrange("p (c t) -> p c t", c=2), [pvn], [hbres(c)])
                  for c in hgroups:
                      s0, Lc = GROUPS[c]
                      h0, _ = hcols(c, half)
                      first_t = 16 if c == 4 else 0
                      last_t = 17 if c == 4 else 15
                      for t in range(s0 // 128, (s0 + Lc) // 128):
                          lo = (t * 128 - s0)
                          for g in range(4):
                              srcs = []
                              if t > first_t:
                                  srcs.append((t - 1, 3))
                              srcs.append((t, 1 if t == first_t else (2 if t == last_t else 0)))
                              if t < last_t:
                                  srcs.append((t + 1, 4))
                              o = (g % 2) * 64
                              pcn = "P0a" if g < 2 else "P0b"
                              outp = P0[o:o + 64, (g // 2) * 512 + lo:(g // 2) * 512 + lo + 128]
                              for i, (tj, v) in enumerate(srcs):
                                  mm(outp, CY[:, tj, g * 64:(g + 1) * 64], PMv[:, g * 5 + v, :], i == 0, i == len(srcs) - 1,
                                     [("CY", tj), "TAB"], [pcn])
                      for cc in range(2):
                          pcn = "P0a" if cc == 0 else "P0b"
                          pln = "P1a" if cc == 0 else "P1b"
                          act(PC[:, cc, 0:Lc], P0[:, cc * 512:cc * 512 + Lc], AF.Copy, [pcn], [("PC", cc)])
                          mm(P1[:, cc * 512:cc * 512 + Lc], WPBD[:, cc, :], PC[:, cc, 0:Lc], True, True, [("PC", cc), "SM"], [pln])
                          act(HB[:, 4 + cc, h0:h0 + Lc], P1[:, cc * 512:cc * 512 + Lc], AF.Identity, [pln, "VEC"], [hbres(c)],
                              scale=VEC[:, 20 + cc:21 + cc])
                  stage(f"p2a_{half}")
                  acount = [0]

                  def att_scores(job):
                      i = acount[0] % 2
                      acount[0] += 1
                      job["i"] = i
                      Sp = P0 if i == 0 else P1
                      Sn = ["P0a", "P0b"] if i == 0 else ["P1a", "P1b"]
                      chunks = job["chunks"]
                      n = len(chunks)
                      for ci, (kt, bias) in enumerate(chunks):
                          o = Sp[:, ci * 128:(ci + 1) * 128]
                          mm(o, job["K"](kt), QZ[:, job["slot"], :], True, bias is None,
                             [("QZ", job["slot"]), ("KT", job["kidx"], grp_of_tile(kt))], Sn)
                          if bias is not None:
                              mm(o, IDB, bias, False, True, ["CBF", "TAB"], Sn)
                      act(PT[i][:, 0:n * 128], Sp[:, 0:n * 128], AF.Exp, Sn, [f"PT{i}"])

                  def att_pv(job):
                      i = job["i"]
                      chunks = job["chunks"]
                      n = len(chunks)
                      h = job["h"]
                      for ci, (kt, bias) in enumerate(chunks):
                          mm(job["O"][:, h * 66:h * 66 + 65], PT[i][:, ci * 128:(ci + 1) * 128], job["V"](kt), ci == 0, ci == n - 1,
                             [f"PT{i}", (job["vname"], kt), job["vname"] + "ones"], [job["On"]])

                  def finish_tile(t, c, Otile, Oname, sink, cat0):
                      h0 = t * 128 - HSTART[half]
                      Ov = Otile[:, 0:264].rearrange("p (h e) -> p h e", h=4)
                      o = 32 + (t % 2) * 8
                      dn = f"DEN{t % 2}"
                      if sink:
                          tt(SMALL[:, o:o + 4], Ov[:, :, 64], GS[:, 4:8], ALU.add, [Oname, "GS"], [dn])
                      else:
                          cpv(SMALL[:, o:o + 4], Ov[:, :, 64], [Oname], [dn])
                      S.op("dve", lambda o=o: nc.vector.reciprocal(out=SMALL[:, o + 4:o + 8], in_=SMALL[:, o:o + 4]), [dn], [dn + "r"])
                      ot = OT[t % 2]
                      otn = f"OT{t % 2}"
                      tt(ot[:].rearrange("p (h e) -> p h e", h=4), Ov[:, :, 0:64],
                         SMALL[:, o + 4:o + 8].unsqueeze(2).to_broadcast([128, 4, 64]), ALU.mult, [Oname, dn + "r"], [otn])
                      pv = PST[:, (t % 2) * 512:(t % 2) * 512 + 256]
                      pvn = "PST"
                      for cc in range(2):
                          tr(pv[:, cc * 128:(cc + 1) * 128], ot[:, cc * 128:(cc + 1) * 128], IDB, [otn, "CBF"], [pvn])
                      cpv(HB[:, cat0:cat0 + 2, h0:h0 + 128], pv.rearrange("p (c t) -> p c t", c=2), [pvn], [hbres(c)])

                  def run_attn(jobs, sink, cat0):
                      S.op("dve", lambda: nc.vector.memset(QZ, 0.0), [], [("PC", 0), ("PC", 1)] + [("QZ", i) for i in range(8)])
                      prev = None
                      for job in jobs + [None]:
                          if job is not None:
                              if job["h"] == 0:
                                  base = (job["t"] % 2) * 4
                                  names = [("QZ", base + i) for i in range(4)]
                                  for (dst_ap, src_ap, qr) in job["qcopies"]:
                                      cpv(dst_ap, src_ap, names + qr, names)
                              att_scores(job)
                          if prev is not None:
                              att_pv(prev)
                              if prev["last"]:
                                  finish_tile(prev["t"], prev["c"], prev["O"], prev["On"], sink, cat0)
                                  advance_mod()
                          prev = job

                  def tcols(t):
                      return slice(t * 128, (t + 1) * 128)

                  jobs = []
                  for (t, c) in tiles_h:
                      Ot, On = (PSV["P2a"], "P2a") if t % 2 == 0 else (PSV["P2b"], "P2b")
                      for h in range(4):
                          kv, g = h // 2, h % 2
                          po = kv * 64
                          if c == 4:
                              chunks = [(16, None), (17, None)]
                          else:
                              chunks = []
                              if t > 0:
                                  chunks.append((t - 1, MASKP))
                              chunks.append((t, None))
                              if t < 15:
                                  chunks.append((t + 1, MASKN))
                              chunks += [(16, None), (17, None)]
                          jobs.append({"t": t, "c": c, "h": h, "K": (lambda kt: KT[:, 0, tcols(kt)]), "po": po,
                                       "kidx": 0, "qres": ("QT", g, c), "vname": "VA",
                                       "qcopies": [(QZ[0:64, (t % 2) * 4:(t % 2) * 4 + 2, :], QT[0:64, 0:2, tcols(t)],
                                                    [("QT", 0, c), ("QT", 1, c)]),
                                                   (QZ[64:128, (t % 2) * 4 + 2:(t % 2) * 4 + 4, :], QT[64:128, 0:2, tcols(t)],
                                                    [("QT", 0, c), ("QT", 1, c)])],
                                       "slot": (t % 2) * 4 + h,
                                       "Q": QT[po:po + 64, g, tcols(t)], "V": (lambda kt, kv=kv: VA[:, kt, kv, 0:65]),
                                       "chunks": chunks, "O": Ot, "On": On, "last": h == 3})
                  run_attn(jobs, True, 0)
                  stage(f"attA_{half}")
                  dma_pool(TAB[:, 0:5376], Ld["dtab"], [], ["TAB"])
                  DT = TAB[:, 0:5376].rearrange("p (h s q) -> p h s q", h=4, s=21)
                  jobs = []
                  for (t, c) in tiles_h:
                      Ot, On = (PSV["P2a"], "P2a") if t % 2 == 0 else (PSV["P2b"], "P2b")
                      if c == 4:
                          deltas, interior = [], False
                      elif 2 <= t <= 13:
                          deltas, interior = [-2, -1, 0, 1, 2], True
                      elif t == 0:
                          deltas, interior = [0, 1, 2, 3], False
                      elif t == 1:
                          deltas, interior = [-1, 0, 1, 2], False
                      elif t == 14:
                          deltas, interior = [-2, -1, 0, 1], False
                      else:
                          deltas, interior = [-3, -2, -1, 0], False
                      for h in range(4):
                          po = (h % 2) * 64
                          ch = h // 2
                          chunks = []
                          for dl in deltas:
                              i0 = _dslot(interior, 2 * dl + 7)
                              i1 = _dslot(interior, 2 * dl + 6)
                              chunks.append((t + dl, DT[:, h, i0:i1 + 1:(i1 - i0), :]))
                          chunks += [(16, None), (17, None)]
                          jobs.append({"t": t, "c": c, "h": h, "K": (lambda kt, ch=ch: KT[:, 1 + ch, tcols(kt)]), "po": po,
                                       "kidx": 1 + ch, "qres": ("QT", 2 + ch, c), "vname": "VD",
                                       "qcopies": [(QZ[0:64, (t % 2) * 4:(t % 2) * 4 + 4:2, :], QT[0:64, 2:4, tcols(t)],
                                                    [("QT", 2, c), ("QT", 3, c)]),
                                                   (QZ[64:128, (t % 2) * 4 + 1:(t % 2) * 4 + 4:2, :], QT[64:128, 2:4, tcols(t)],
                                                    [("QT", 2, c), ("QT", 3, c)])],
                                       "slot": (t % 2) * 4 + h,
                                       "Q": QT[po:po + 64, 2 + ch, tcols(t)], "V": (lambda kt, h=h: VD[:, kt, h, 0:65]),
                                       "chunks": chunks, "O": Ot, "On": On, "last": h == 3})
                  run_attn(jobs, False, 6)
                  if half == 0:
                      tap(f"cat{l}", HB[:, :, 0:1024], [128, 8, 1024], BF16, [])
                  stage(f"attD_{half}")
                  wcount = 0
                  for a in range(4):
                      slot, sr = load_piece(l, NP_WO + a)
                      sv = slot[:].rearrange("p (k n) -> p k n", k=8)
                      for di in range(2):
                          dch = 2 * a + di
                          for c in hgroups:
                              s0, Lc = GROUPS[c]
                              h0, _ = hcols(c, half)
                              s = sidx(c)
                              pn = ["P0a", "P0b", "P1a", "P1b"][wcount % 4]
                              wcount += 1
                              pw = PSV[pn][:, 0:Lc]
                              for k in range(8):
                                  mm(pw, sv[:, k, di * 128:(di + 1) * 128], HB[:, k, h0:h0 + Lc], k == 0, k == 7,
                                     [sr, hbres(c)], [pn])
                              stt(X[:, dch, s0:s0 + Lc], pw, G1[:, dch, s:s + 1], X[:, dch, s0:s0 + Lc], ALU.mult, ALU.add,
                                  [pn, MODn, xres(c)], [xres(c)])
                  stage(f"wout_{half}")
                  if half == 1:
                      S.barrier()
              advance_mod(64)
              stage("wout")
              tap(f"x1_{l}", X[:], [128, 8, NTOK], F32, [])

              stt(A2[:], MOD[:, 32:40, :], 1.0, VEC[:, 8:16].unsqueeze(2).to_broadcast([128, 8, 2]), ALU.add, ALU.mult,
                  [MODn, "VEC"], ["A2"])
              def h2dst(k, c):
                  s0, Lc = GROUPS[c]
                  if c < 2:
                      return HB[:, k, s0:s0 + Lc], hbres(c)
                  return H2B[:, k, s0 - 1024:s0 - 1024 + Lc], ("H2B", c)

              do_norms(groups_all, A2, SH2, h2dst, "A2")
              stage("norm2")
              gcount = 0
              dcount = 0
              for bi, blk in enumerate(FFBLOCKS):
                  for fl, f in enumerate(blk):
                      slot, sr = load_piece(l, NP_GU + f)
                      sv = slot[:].rearrange("p (u k n) -> p u k n", u=2, k=8)
                      for c in groups_all:
                          s0, Lc = GROUPS[c]
                          i = gcount % 2
                          gcount += 1
                          pgn, pun = ("P0a", "P0b") if i == 0 else ("P1a", "P1b")
                          pg, pu = PSV[pgn][:, 0:Lc], PSV[pun][:, 0:Lc]
                          for k in range(8):
                              hsrc, hres = h2dst(k, c)
                              mm(pg, sv[:, 0, k, :], hsrc, k == 0, k == 7, [sr, hres], [pgn])
                          for k in range(8):
                              hsrc, hres = h2dst(k, c)
                              mm(pu, sv[:, 1, k, :], hsrc, k == 0, k == 7, [sr, hres], [pun])
                          sg, sgn = (Fs[0], "F0") if i == 0 else (Fs[1], "F1")
                          act(sg[:, 0:Lc], pg, AF.Silu, [pgn], [sgn])
                          tt(ACTB[:, fl, s0:s0 + Lc], pu, sg[:, 0:Lc], ALU.mult, [sgn, pun], [("ACTB", fl, c)])
                  nf = len(blk)
                  for dch in range(8):
                      slot, sr = load_piece(l, NP_WD + bi * 8 + dch, nf * 128)
                      sv = slot[:, 0:nf * 128].rearrange("p (f n) -> p f n", f=nf)
                      for c in groups_all:
                          s0, Lc = GROUPS[c]
                          s = sidx(c)
                          pn = ["P2a", "P2b", "P3"][dcount % 3]
                          dcount += 1
                          pd = PSV[pn][:, 0:Lc]
                          for fl in range(nf):
                              mm(pd, sv[:, fl, :], ACTB[:, fl, s0:s0 + Lc], fl == 0, fl == nf - 1, [sr, ("ACTB", fl, c)], [pn])
                          stt(X[:, dch, s0:s0 + Lc], pd, G2[:, dch, s:s + 1], X[:, dch, s0:s0 + Lc], ALU.mult, ALU.add,
                              [pn, MODn, xres(c)], [xres(c)])
              S.barrier()
              stage("ffn")
              tap(f"x2_{l}", X[:], [128, 8, NTOK], F32, [])


        try:
            run_layers()
        except _Stop:
            pass
        for t in range(16):
            Pt = P0 if t % 2 == 0 else P1
            pr = ["P0a", "P0b"] if t % 2 == 0 else ["P1a", "P1b"]
            for k in range(8):
                tr(Pt[:, k * 128:(k + 1) * 128], X[:, k, t * 128:(t + 1) * 128], IDF[:], [xres(t // 4), "IDF"], pr)
            stg = SCR[:, (t % 2) * 1024:(t % 2 + 1) * 1024]
            stg_r = ["F0", "F1"] if t % 2 == 0 else ["F2", "F3"]
            if t % 2 == 0:
                act(stg, Pt[:], AF.Copy, pr, stg_r)
            else:
                cpv(stg, Pt[:], pr, stg_r)
            dma_sp(out_d[t * 128:(t + 1) * 128, :], stg, stg_r, [("out", t)])
        S.emit(st)
    return nc, tap_d


_CACHE = {}


def prep_inputs(inp, n_layers=2):
    inp = {k: np.asarray(v, dtype=np.float32) for k, v in inp.items()}
    cbf, rope, pm = _consts()
    shared = {"cbf": np.ascontiguousarray(cbf.reshape(128, -1)), "idf": np.eye(128, dtype=np.float32),
              "rope": np.ascontiguousarray(rope.reshape(128, -1)), "pm": np.ascontiguousarray(pm.reshape(128, -1))}
    for l in range(n_layers):
        la = _layer_arrays(inp, l)
        for k, v in la.items():
            shared[f"{k}{l}"] = v
    maps = []
    for b in range(8):
        m = dict(shared)
        m["x"] = np.ascontiguousarray(inp["x"][b])
        m["ctx"] = np.ascontiguousarray(inp["ctx"][b])
        cv = np.zeros((128, 8, 2), np.float32)
        cv[:, :, 0] = inp["c"][b].reshape(8, 128).T
        cv[:, :, 1] = inp["c_ctx"].reshape(8, 128).T
        m["cvec"] = np.ascontiguousarray(cv.reshape(128, 16))
        maps.append(m)
    return maps


def kernel(**inputs):
    if "nc" not in _CACHE:
        _CACHE["nc"] = build(2)[0]
    nc = _CACHE["nc"]
    maps = prep_inputs(inputs, 2)
    res = run_bass_kernel_spmd(nc, maps, core_ids=list(range(8)))
    return np.stack([np.asarray(r["out"], dtype=np.float32) for r in res.results], 0)
```

```python
import contextlib
import numpy as np
import concourse.bass as bass
import concourse.mybir as mybir
from concourse.bass_utils import run_bass_kernel_spmd

F32 = mybir.dt.float32
BF16 = mybir.dt.bfloat16
AF = mybir.ActivationFunctionType
ALU = mybir.AluOpType

D = 1024
SEQ = 2048
CTX = 256
NTOK = SEQ + CTX
FF = 2816
NFF = 22
NEG = -30000.0
EPS = 1e-6
GROUPS = [(0, 512), (512, 512), (1024, 512), (1536, 512), (2048, 256)]
HALVES = [[0, 1], [2, 3, 4]]
HSTART = [0, 1024]
FFBLOCKS = [list(range(0, 8)), list(range(8, 15)), list(range(15, 22))]
PIECE = 2048
import os as _os
_DBG_DELAY = int(_os.environ.get('DBG_DELAY', '0'))
_DBG_SKIP = set(int(v) for v in _os.environ.get('DBG_SKIP_A', '').split(',') if v)


class _Op:
    __slots__ = ("eng", "fn", "dma", "deps", "has_dep", "sig")

    def __init__(self, eng, fn, dma):
        self.eng = eng
        self.fn = fn
        self.dma = dma
        self.deps = []
        self.has_dep = False
        self.sig = None


class Sched:
    COMPUTE = ("pe", "act", "dve")

    def __init__(self, nc, n_dma_sems=8):
        self.nc = nc
        self.ops = []
        self.res = {}
        self.n_dma_sems = n_dma_sems
        self.last = {}
        self.pending = {}
        self.dma_since = []

    def op(self, eng, fn, reads=(), writes=(), dma=False):
        o = _Op(eng, fn, dma)
        deps = {}
        for r in reads:
            st = self.res.get(r)
            if st is not None and st[0] is not None:
                deps[id(st[0])] = (st[0], True)
        for w in writes:
            st = self.res.get(w)
            if st is not None:
                if st[0] is not None and id(st[0]) not in deps:
                    deps[id(st[0])] = (st[0], False)
                for rd in st[1]:
                    if id(rd) not in deps:
                        deps[id(rd)] = (rd, False)
        for d in self.pending.pop(eng, ()):
            if id(d) not in deps:
                deps[id(d)] = (d, True)
        for d, raw in deps.values():
            if d is o:
                continue
            if not d.dma and not o.dma and d.eng == eng:
                if eng == "pe":
                    continue
            o.deps.append(d)
            d.has_dep = True
        for r in reads:
            st = self.res.get(r)
            if st is None:
                self.res[r] = [None, [o]]
            else:
                st[1].append(o)
        for w in writes:
            self.res[w] = [o, []]
        self.ops.append(o)
        self.last[eng] = o
        if dma:
            self.dma_since.append(o)
        return o

    def barrier(self):
        lasts = [o for e, o in self.last.items() if e in self.COMPUTE]
        lasts += self.dma_since
        self.dma_since = []
        for e in self.COMPUTE:
            self.pending[e] = self.pending.get(e, []) + lasts

    def emit(self, stack):
        nc = self.nc
        eobj = {"pe": nc.tensor, "act": nc.scalar, "dve": nc.vector, "pool": nc.gpsimd, "sp": nc.sync}
        semh = {}
        cnt = {}
        waited = {e: {} for e in eobj}
        dma_rr = {}
        dcnt = {}

        def get_sem(key):
            if key not in semh:
                semh[key] = stack.enter_context(nc.semaphore("s_" + "_".join(str(k) for k in key)))
            return semh[key]

        for o in self.ops:
            E = eobj[o.eng]
            need = {}
            for d in o.deps:
                key, val = d.sig
                if need.get(key, 0) < val:
                    need[key] = val
            if o.dma:
                i = dma_rr.get(o.eng, 0)
                dma_rr[o.eng] = (i + 1) % self.n_dma_sems
                k = ("dma", o.eng, i)
                n = dcnt.get(k, 0) + 1
                dcnt[k] = n
                if n > 1 and need.get(k, 0) < 16 * (n - 1):
                    need[k] = 16 * (n - 1)
            for key, val in need.items():
                if waited[o.eng].get(key, 0) < val:
                    E.wait_ge(get_sem(key), val)
                    waited[o.eng][key] = val
            ins = o.fn()
            if o.dma:
                ins.then_inc(get_sem(k), 16)
                o.sig = (k, 16 * n)
            elif o.has_dep:
                key = ("e", o.eng)
                cnt[key] = cnt.get(key, 0) + 1
                ins.then_inc(get_sem(key), 1)
                o.sig = (key, cnt[key])
        for key, n in dcnt.items():
            nc.sync.wait_ge(get_sem(key), 16 * n)
        for key, n in cnt.items():
            nc.sync.wait_ge(get_sem(key), n)


def _kmajor(w):
    K, N = w.shape
    return np.ascontiguousarray(w.reshape(K // 128, 128, N).transpose(1, 0, 2))


def _pad_piece(a):
    a = np.ascontiguousarray(a, dtype=np.float32).reshape(128, -1)
    out = np.zeros((128, PIECE), np.float32)
    out[:, : a.shape[1]] = a
    return out


def _dslot(interior, s):
    if 3 <= s <= 9:
        return 8 + (9 - s)
    if s >= 10:
        return (0 if interior else 4) + (13 - s)
    return (18 if interior else 15) + (2 - s)


def _consts():
    c = np.zeros((128, 6, 128), np.float32)
    c[:, 0, :] = np.eye(128)
    c[:, 1, :] = 1.0
    c[0:64, 2, 0:64] = 1.0
    c[64:128, 2, 64:128] = 1.0
    for m in range(128):
        if (m % 32) < 16:
            c[m + 16, 3, m] = -1.0
        else:
            c[m - 16, 3, m] = 1.0
    j = np.arange(128)[:, None]
    i = np.arange(128)[None, :]
    c[:, 4, :] = np.where(j >= i, 0.0, NEG)
    c[:, 5, :] = np.where(j <= i, 0.0, NEG)
    p = np.arange(128)
    idx = p % 64
    f = (idx % 16).astype(np.float64)
    inv = np.power(10000.0, -f / 16.0)
    t = np.arange(SEQ)
    pos = np.where((idx < 32)[:, None], (t // 64)[None, :], (t % 64)[None, :]).astype(np.float64)
    ang = pos * inv[:, None]
    rope = np.stack([np.cos(ang), np.sin(ang)], axis=1).astype(np.float32)
    pm = np.zeros((128, 20, 128), np.float32)
    jj = np.arange(128)[:, None]
    tt = np.arange(128)[None, :]
    for g, w in enumerate((2, 4, 8, 16)):
        h = w // 2
        eye = (jj == tt).astype(np.float32)
        pm[:, g * 5 + 0, :] = ((jj >= tt - h) & (jj < tt + h)) / w - eye
        lo = np.maximum(tt - h, 0)
        cntf = (tt + h) - lo
        pm[:, g * 5 + 1, :] = ((jj >= lo) & (jj < tt + h)) / cntf - eye
        hi = np.minimum(tt + h, 128)
        cntl = hi - (tt - h)
        pm[:, g * 5 + 2, :] = ((jj >= tt - h) & (jj < hi)) / cntl - eye
        pm[:, g * 5 + 3, :] = (jj >= 128 + tt - h) / w
        pm[:, g * 5 + 4, :] = (jj < tt + h - 128) / w
    return c, rope, pm


def _dtab(rpb):
    kc = np.arange(64)[:, None]
    qc = np.arange(64)[None, :]
    cs = np.clip(qc - 8, 0, 48)
    col_ok = (kc >= cs) & (kc < cs + 16)
    coff = np.clip(kc - qc, -15, 15) + 15
    out = np.full((128, 4, 21, 64), NEG, np.float32)
    for h in range(4):
        for interior in (True, False):
            for s in range(14):
                sl = _dslot(interior, s)
                for kr in range(2):
                    rho = s + kr
                    ok = 0 <= rho <= 14 and ((3 <= rho <= 10) or not interior)
                    if not ok:
                        continue
                    T = np.where(col_ok, rpb[h, rho][coff], NEG)
                    out[kr * 64:(kr + 1) * 64, h, sl, :] = T
    return out


def _layer_arrays(inp, l):
    f32 = np.float32
    w_in = inp["w_in"][l]
    pieces = []
    wm = _kmajor(inp["w_mod"][l])
    for jj in range(24):
        pieces.append(_pad_piece(wm[:, :, jj * 256:(jj + 1) * 256]))
    aq = w_in[:, 0:256]
    fm_cols = [np.concatenate([aq[:, 0:64], aq[:, 128:192]], 1), np.concatenate([aq[:, 64:128], aq[:, 192:256]], 1),
               w_in[:, 256:384], w_in[:, 1280:1408], w_in[:, 1408:1536], w_in[:, 1536:1664], w_in[:, 1664:1792]]
    fm = _kmajor(np.concatenate(fm_cols, 1))
    for a in range(4):
        pieces.append(_pad_piece(fm[:, :, a * 256:min((a + 1) * 256, 896)]))
    tm = [w_in[:, 512:768], w_in[:, 768:1024], w_in[:, 1024:1280], w_in[:, 1792:2048], w_in[:, 384:512]]
    for a in tm:
        pieces.append(_pad_piece(_kmajor(a)))
    wo = _kmajor(inp["w_out"][l])
    for a in range(4):
        pieces.append(_pad_piece(wo[:, :, a * 256:(a + 1) * 256]))
    wg = _kmajor(inp["w_gate"][l])
    wu = _kmajor(inp["w_up"][l])
    for f in range(NFF):
        pieces.append(_pad_piece(np.concatenate([wg[:, :, f * 128:(f + 1) * 128].reshape(128, -1),
                                                 wu[:, :, f * 128:(f + 1) * 128].reshape(128, -1)], 1)))
    wd = _kmajor(inp["w_down"][l])
    for blk in FFBLOCKS:
        for dch in range(8):
            pieces.append(_pad_piece(wd[:, blk[0]:blk[-1] + 1, dch * 128:(dch + 1) * 128]))
    W = np.stack(pieces, 0)
    vec = np.zeros((128, 26), f32)
    vec[:, 0:8] = inp["g_mix"][l].reshape(8, 128).T
    vec[:, 8:16] = inp["g_ffn"][l].reshape(8, 128).T
    vec[:, 16] = np.tile(inp["a_q_gain"][l], 2)
    vec[:, 17] = np.tile(inp["a_k_gain"][l], 2)
    vec[:, 18] = np.tile(inp["d_q_gain"][l], 2)
    vec[:, 19] = np.tile(inp["d_k_gain"][l], 2)
    vec[:, 20:22] = inp["c_scale"][l].reshape(2, 128).T
    vec[:, 22:26] = inp["b_b_s"][l].T
    bmod = np.ascontiguousarray(inp["b_mod"][l].reshape(48, 128).T)
    bc = np.zeros((128, 260), f32)
    bc[:, 0:4] = inp["a_sink"][l][None, :]
    bc[:, 4:260] = inp["b_v_gain"][l][None, :]
    sm = np.zeros((128, 6, 128), f32)
    sm[:, 0:4, :] = inp["b_w_s"][l].transpose(2, 0, 1)
    wp = inp["c_w_pool"][l]
    for g in range(4):
        o = (g % 2) * 64
        sm[o:o + 64, 4 + g // 2, o:o + 64] = wp[g]
    dt = _dtab(inp["d_rpb"][l])
    return {"W": W, "vec": vec, "bmod": bmod, "bc": bc, "sm": np.ascontiguousarray(sm.reshape(128, -1)), "dtab": np.ascontiguousarray(dt.reshape(128, -1))}


NP_MOD, NP_FM, NP_TM, NP_WO, NP_GU, NP_WD = 0, 24, 28, 33, 37, 59
NPIECES = 83


class _Stop(Exception):
    pass


def build(n_layers=2, taps=(), stop=None):
    nc = bass.Bass("TRN2", target_bir_lowering=False)
    dr = {}

    def din(name, shape):
        dr[name] = nc.dram_tensor(name, list(shape), F32, kind="ExternalInput").ap()
        return dr[name]

    x_d = din("x", [SEQ, D])
    ctx_d = din("ctx", [CTX, D])
    cvec_d = din("cvec", [128, 16])
    cbf_d = din("cbf", [128, 6 * 128])
    idf_d = din("idf", [128, 128])
    rope_d = din("rope", [128, 2 * SEQ])
    pm_d = din("pm", [128, 20 * 128])
    L = []
    for l in range(n_layers):
        L.append({
            "W": din(f"W{l}", [NPIECES, 128, PIECE]), "vec": din(f"vec{l}", [128, 26]), "bmod": din(f"bmod{l}", [128, 48]),
            "bc": din(f"bc{l}", [128, 260]), "sm": din(f"sm{l}", [128, 6 * 128]), "dtab": din(f"dtab{l}", [128, 4 * 21 * 64]),
        })
    out_d = nc.dram_tensor("out", [SEQ, D], F32, kind="ExternalOutput").ap()
    tap_d = {}

    st = contextlib.ExitStack()
    with st:
        S = Sched(nc)
        sb = lambda n, s, d: st.enter_context(nc.sbuf_tensor(n, list(s), d))
        ps = lambda n, s, d: st.enter_context(nc.psum_tensor(n, list(s), d))
        X = sb("X", [128, 8, NTOK], F32)
        HB = sb("HB", [128, 8, 1280], BF16)
        RR = sb("RR", [128, 32472], BF16)
        QT = RR[:, 0:9216].rearrange("p (k t) -> p k t", k=4)
        KT = RR[:, 9216:16128].rearrange("p (k t) -> p k t", k=3)
        VA = RR[:, 16128:18504].rearrange("p (t h e) -> p t h e", t=18, h=2)
        VD = RR[:, 18504:23256].rearrange("p (t h e) -> p t h e", t=18, h=4)
        OB = RR[:, 23256:27864].rearrange("p (t c) -> p t c", t=18)
        CY = RR[:, 27864:32472].rearrange("p (t c) -> p t c", t=18)
        ACTB = RR[:, 0:18432].rearrange("p (f t) -> p f t", f=8)
        H2B = RR[:, 18432:28672].rearrange("p (k t) -> p k t", k=8)
        RING = [sb(f"ring{i}", [128, PIECE], BF16) for i in range(4)]
        TAB = sb("TAB", [128, 5376], BF16)
        SCR = sb("SCR", [128, 2048], F32)
        Fs = [SCR[:, i * 512:(i + 1) * 512] for i in range(4)]
        SQ = [sb(f"SQ{i}", [128, 512], BF16) for i in range(2)]
        QG = sb("QG", [128, 512], BF16)
        RQ = sb("RQ", [128, 512], BF16)
        PT = [sb(f"PT{i}", [128, 896], BF16) for i in range(2)]
        OT = [sb(f"OT{i}", [128, 256], BF16) for i in range(2)]
        PC = sb("PC", [128, 2, 512], BF16)
        QZ = PC[:].rearrange("p a (b q) -> p (a b) q", q=128)
        VN = [sb(f"VN{i}", [128, 256], BF16) for i in range(2)]
        CBF = sb("CBF", [128, 6, 128], BF16)
        IDF = sb("IDF", [128, 128], F32)
        CV = sb("CV", [128, 8, 2], F32)
        CA = sb("CA", [128, 8, 2], BF16)
        SMALL = sb("SMALL", [128, 64], F32)
        IDB = CBF[:, 0, :]
        ONES = CBF[:, 1, :]
        BDONES = CBF[:, 2, :]
        ROTT = CBF[:, 3, :]
        MASKP = CBF[:, 4, :]
        MASKN = CBF[:, 5, :]
        P0 = ps("P0", [128, 1024], F32)
        P1 = ps("P1", [128, 1024], F32)
        P2 = ps("P2", [128, 1024], F32)
        P3 = ps("P3", [128, 512], F32)
        PST = ps("PST", [128, 1024], BF16)
        PSV = {"P0a": P0[:, 0:512], "P0b": P0[:, 512:1024], "P1a": P1[:, 0:512], "P1b": P1[:, 512:1024],
               "P2a": P2[:, 0:512], "P2b": P2[:, 512:1024], "P3": P3[:, :]}

        def mm(out, lhsT, rhs, start, stop, reads, writes):
            S.op("pe", lambda: nc.tensor.matmul(out, lhsT=lhsT, rhs=rhs, start=start, stop=stop), reads, writes)

        def tr(out, in_, ident, reads, writes):
            S.op("pe", lambda: nc.tensor.transpose(out, in_, ident), reads, writes)

        def act(out, in_, func, reads, writes, scale=None, bias=None):
            kw = {}
            if scale is not None:
                kw["scale"] = scale
            if bias is not None:
                kw["bias"] = bias
            S.op("act", lambda: nc.scalar.activation(out=out, in_=in_, func=func, **kw), reads, writes)

        def tt(out, in0, in1, op, reads, writes):
            S.op("dve", lambda: nc.vector.tensor_tensor(out=out, in0=in0, in1=in1, op=op), reads, writes)

        def stt(out, in0, scalar, in1, op0, op1, reads, writes):
            S.op("dve", lambda: nc.vector.scalar_tensor_tensor(out=out, in0=in0, scalar=scalar, in1=in1, op0=op0, op1=op1),
                 reads, writes)

        def ts(out, in0, s1, s2, op0, op1, reads, writes):
            S.op("dve", lambda: nc.vector.tensor_scalar(out=out, in0=in0, scalar1=s1, scalar2=s2, op0=op0, op1=op1),
                 reads, writes)

        def cpv(out, in_, reads, writes):
            S.op("dve", lambda: nc.vector.tensor_copy(out=out, in_=in_), reads, writes)

        def dma_sp(out, in_, reads, writes):
            S.op("sp", lambda: nc.sync.dma_start(out=out, in_=in_), reads, writes, dma=True)

        def dma_pool(out, in_, reads, writes):
            S.op("pool", lambda: nc.gpsimd.dma_start(out=out, in_=in_, max_dma_last_dim=4096), reads, writes, dma=True)

        def stage(name):
            if stop == name:
                S.barrier()
                raise _Stop()

        def tap(name, ap, shape, dtype, reads):
            if name not in taps:
                return
            S.barrier()
            t = nc.dram_tensor("tap_" + name, list(shape), dtype, kind="ExternalOutput").ap()
            tap_d[name] = t
            dma_sp(t, ap, list(reads) + ["tapsrc"], ["tap_" + name])

        ring_i = [0]
        ring_serial = {}

        def load_piece(l, idx, nelem=PIECE):
            i = ring_i[0]
            ring_i[0] = (i + 1) % 4
            dma_pool(RING[i][:, 0:nelem], L[l]["W"][idx, :, 0:nelem], [], [("slot", i)])
            ring_serial[("slot", i)] = ring_serial.get(("slot", i), 0) + 1
            return RING[i], ("slot", i)

        dma_sp(IDF[:], idf_d, [], ["IDF"])
        dma_sp(CV[:].rearrange("p k s -> p (k s)"), cvec_d, [], ["CV"])
        dma_pool(CBF[:].rearrange("p a b -> p (a b)"), cbf_d, [], ["CBF"])
        act(CA[:], CV[:], AF.Silu, ["CV"], ["CA"])
        for t in range(18):
            stg = SCR[:, (t % 2) * 1024:(t % 2 + 1) * 1024]
            stg_r = ["F0", "F1"] if t % 2 == 0 else ["F2", "F3"]
            src = x_d[t * 128:(t + 1) * 128, :] if t < 16 else ctx_d[(t - 16) * 128:(t - 15) * 128, :]
            dma_sp(stg, src, [], stg_r)
            Pt = P0 if t % 2 == 0 else P1
            pr = ["P0a", "P0b"] if t % 2 == 0 else ["P1a", "P1b"]
            for k in range(8):
                tr(Pt[:, k * 128:(k + 1) * 128], stg[:, k * 128:(k + 1) * 128], IDF[:], stg_r + ["IDF"], pr)
            dst = X[:, :, t * 128:(t + 1) * 128]
            src_ps = Pt[:].rearrange("p (k t) -> p k t", k=8)
            xr = [("X", t // 4 if t < 16 else 4)]
            if t % 2 == 0:
                act(dst, src_ps, AF.Copy, pr, xr)
            else:
                cpv(dst, src_ps, pr, xr)
        S.barrier()

        def xres(c):
            return ("X", c)

        def hbres(c):
            return ("HB", 2) if c == 4 else ("HB", c % 2)

        def grp_of_tile(t):
            return t // 4 if t < 16 else 4

        def sidx(c):
            return 1 if c == 4 else 0

        VEC = sb("VEC", [128, 26], F32)
        BMODs = [sb(f"BMOD{i}", [128, 48], F32) for i in range(2)]
        BC = sb("BC", [128, 260], F32)
        SM = sb("SM", [128, 6, 128], BF16)
        MODs = [sb(f"MOD{i}", [128, 48, 2], F32) for i in range(2)]
        A1 = sb("A1", [128, 8, 2], F32)
        A2 = sb("A2", [128, 8, 2], F32)
        GS = sb("GS", [128, 8], F32)

        def mod_gen(l, jj0=0, jj1=24):
            b = l % 2
            if jj0 == 0:
                dma_sp(BMODs[b][:], L[l]["bmod"], [], [f"BMOD{b}"])
            for jj in range(jj0, jj1):
                slot, sr = load_piece(l, NP_MOD + jj)
                sv = slot[:].rearrange("p (k n) -> p k n", k=8)
                for cc in range(2):
                    j = 2 * jj + cc
                    for k in range(8):
                        mm(P3[:, 2 * j:2 * j + 2], sv[:, k, cc * 128:(cc + 1) * 128], CA[:, k, :], k == 0, k == 7,
                           [sr, "CA"], ["P3"])
                yield
            c0, c1 = 2 * jj0, 2 * jj1
            tt(MODs[b][:, c0:c1, :], P3[:, 2 * c0:2 * c1].rearrange("p (j s) -> p j s", s=2),
               BMODs[b][:, c0:c1].unsqueeze(2).to_broadcast([128, c1 - c0, 2]), ALU.add, ["P3", f"BMOD{b}"], [f"MOD{b}"])
            yield

        def run_layers():
          for l in range(n_layers):
              last = l == n_layers - 1
              Ld = L[l]
              dma_sp(VEC[:], Ld["vec"], [], ["VEC"])
              dma_sp(BC[:], Ld["bc"], [], ["BC"])
              dma_pool(SM[:].rearrange("p a b -> p (a b)"), Ld["sm"], [], ["SM"])
              WST = SM[:, 0:4, :]
              WPBD = SM[:, 4:6, :]

              MOD = MODs[l % 2]
              MODn = f"MOD{l % 2}"
              late0 = [None]
              if l == 0:
                  for _ in mod_gen(0, 0, 8):
                      pass
                  late0[0] = mod_gen(0, 8, 24)
              pre = [mod_gen(l + 1) if not last else None]

              def _adv(box, n):
                  for _ in range(n):
                      if box[0] is not None:
                          try:
                              next(box[0])
                          except StopIteration:
                              box[0] = None

              def advance_mod(n=1):
                  _adv(pre, n)
              stt(A1[:], MOD[:, 8:16, :], 1.0, VEC[:, 0:8].unsqueeze(2).to_broadcast([128, 8, 2]), ALU.add, ALU.mult,
                  [MODn, "VEC"], ["A1"])
              ts(GS[:, 0:1], VEC[:, 16:17], 0.125, None, ALU.mult, ALU.bypass, ["VEC"], ["GS"])
              ts(GS[:, 1:2], VEC[:, 17:18], 1.0, None, ALU.mult, ALU.bypass, ["VEC"], ["GS"])
              ts(GS[:, 2:3], VEC[:, 18:19], 0.125, None, ALU.mult, ALU.bypass, ["VEC"], ["GS"])
              ts(GS[:, 3:4], VEC[:, 19:20], 1.0, None, ALU.mult, ALU.bypass, ["VEC"], ["GS"])
              act(GS[:, 4:8], BC[:, 0:4], AF.Exp, ["BC"], ["GS"])
              SH1, G1, SH2, G2 = MOD[:, 0:8, :], MOD[:, 16:24, :], MOD[:, 24:32, :], MOD[:, 40:48, :]
              stage("modload")
              tap(f"mod{l}", MOD[:], [128, 48, 2], F32, [MODn])

              stage("mod")
              def do_norms(groups, Amul, SHv, dst_fn, tag):
                  def bank(c):
                      return (PSV["P2a"], "P2a") if c % 2 == 0 else (PSV["P2b"], "P2b")

                  def stat_k(c, k):
                      s0, Lc = GROUPS[c]
                      psn, psr = bank(c)
                      sq = SQ[k % 2]
                      act(sq[:, 0:Lc], X[:, k, s0:s0 + Lc], AF.Square, [xres(c)], [f"SQ{k % 2}"])
                      mm(psn[:, 0:Lc], ONES, sq[:, 0:Lc], k == 0, k == 7, [f"SQ{k % 2}", "CBF"], [psr])

                  def stat_fin(c):
                      s0, Lc = GROUPS[c]
                      psn, psr = bank(c)
                      act(Fs[0][:, 0:Lc], psn[:, 0:Lc], AF.Ln, [psr], ["F0"], scale=1.0 / D, bias=EPS)
                      act(psn[:, 0:Lc], Fs[0][:, 0:Lc], AF.Exp, ["F0"], [psr], scale=-0.5)

                  def apply_k(c, k):
                      s0, Lc = GROUPS[c]
                      s = sidx(c)
                      psn, psr = bank(c)
                      tmp, tr_ = (Fs[2], "F2") if k % 2 == 0 else (Fs[3], "F3")
                      tt(tmp[:, 0:Lc], psn[:, 0:Lc], X[:, k, s0:s0 + Lc], ALU.mult, [xres(c), psr], [tr_])
                      dst, dres = dst_fn(k, c)
                      act(dst, tmp[:, 0:Lc], AF.Identity, [tr_, MODn, tag], [dres], scale=Amul[:, k, s:s + 1],
                          bias=SHv[:, k, s:s + 1])

                  n = len(groups)
                  for k in range(8):
                      stat_k(groups[0], k)
                  stat_fin(groups[0])
                  for i in range(n):
                      for k in range(8):
                          if i + 1 < n:
                              stat_k(groups[i + 1], k)
                          apply_k(groups[i], k)
                      if i + 1 < n:
                          stat_fin(groups[i + 1])

              groups_all = [0, 1, 2, 3] if last else [0, 1, 2, 3, 4]

              dma_pool(TAB[:, 0:4096], rope_d, [], ["TAB"])
              COS = TAB[:, 0:2048]
              SIN = TAB[:, 2048:4096]
              S.op("dve", lambda: nc.vector.memset(VA[:, :, :, 64:65], 1.0), [], ["VAones"])
              S.op("dve", lambda: nc.vector.memset(VD[:, :, :, 64:65], 1.0), [], ["VDones"])

              def hcols(c, half):
                  s0, Lc = GROUPS[c]
                  return s0 - HSTART[half], Lc

              for half in range(2):
                  hgroups = HALVES[half]
                  def hdst(k, c):
                      h0, Lc = hcols(c, half)
                      return HB[:, k, h0:h0 + Lc], hbres(c)

                  do_norms(hgroups, A1, SH1, hdst, "A1")
                  if half == 0:
                      tap(f"hT{l}", HB[:, :, 0:1024], [128, 8, 1024], BF16, [("HB", 0), ("HB", 1)])
                  stage(f"norm1_{half}")
                  combos = []
                  for a in range(4):
                      js = [2 * a, 2 * a + 1] if a < 3 else [6]
                      for ji, j in enumerate(js):
                          for c in hgroups:
                              if last and c == 4 and j in (0, 1, 3, 4):
                                  continue
                              combos.append((a, len(js), ji, j, c))
                  fm_slot = {}

                  def fm_piece(a, njs):
                      if a not in fm_slot:
                          slot, sr = load_piece(l, NP_FM + a, 8 * 128 * njs)
                          fm_slot[a] = (slot[:, 0:8 * 128 * njs].rearrange("p (k n) -> p k n", k=8), sr, ring_serial[sr])
                      assert ring_serial[fm_slot[a][1]] == fm_slot[a][2], "FM weight slot recycled while still in use"
                      return fm_slot[a][0], fm_slot[a][1]

                  def fm_info(k):
                      a, njs, ji, j, c = combos[k]
                      s0, Lc = GROUPS[c]
                      h0, _ = hcols(c, half)
                      pfn = ["P0a", "P0b", "P1a"][k % 3]
                      phn = ["P1b", "P2a"][k % 2]
                      prn = "P2b"
                      gcol = {0: 0, 1: 0, 2: 1, 3: 2, 4: 2, 5: 3, 6: 3}[j]
                      if j < 2:
                          dst, dres = QT[:, j, s0:s0 + Lc], ("QT", j, c)
                      elif j == 2:
                          dst, dres = KT[:, 0, s0:s0 + Lc], ("KT", 0, c)
                      elif j < 5:
                          dst, dres = QT[:, j - 1, s0:s0 + Lc], ("QT", j - 1, c)
                      else:
                          dst, dres = KT[:, j - 4, s0:s0 + Lc], ("KT", j - 4, c)
                      return dict(a=a, njs=njs, ji=ji, j=j, c=c, s0=s0, Lc=Lc, h0=h0, pfn=pfn, phn=phn, prn=prn,
                                  pf=PSV[pfn][:, 0:Lc], ph=PSV[phn][:, 0:Lc], pr=PSV[prn][:, 0:Lc],
                                  gain=GS[:, gcol:gcol + 1], dst=dst, dres=dres, rope=(j < 3 and c != 4),
                                  sq=SQ[k % 2], sqn=f"SQ{k % 2}")

                  def fm_head(k):
                      f = fm_info(k)
                      sv, sr = fm_piece(f["a"], f["njs"])
                      for kk in range(8):
                          mm(f["pf"], sv[:, kk, f["ji"] * 128:(f["ji"] + 1) * 128], HB[:, kk, f["h0"]:f["h0"] + f["Lc"]],
                             kk == 0, kk == 7, [sr, ("HB", f["c"])], [f["pfn"]])
                      act(f["sq"][:, 0:f["Lc"]], f["pf"], AF.Square, [f["pfn"]], [f["sqn"]])

                  def fm_qg(k):
                      f = fm_info(k)
                      if f["rope"]:
                          act(QG[:, 0:f["Lc"]], f["pf"], AF.Identity, [f["pfn"], "GS"], ["QG"], scale=f["gain"])

                  def fm_mid(k):
                      f = fm_info(k)
                      Lc, s0 = f["Lc"], f["s0"]
                      mm(f["ph"], BDONES, f["sq"][:, 0:Lc], True, True, [f["sqn"], "CBF"], [f["phn"]])
                      if f["rope"]:
                          mm(f["pr"], ROTT, QG[:, 0:Lc], True, True, ["QG", "CBF"], [f["prn"]])
                          tt(Fs[2][:, 0:Lc], QG[:, 0:Lc], COS[:, s0:s0 + Lc], ALU.mult, ["QG", "TAB"], ["F2"])

                  def fm_tail(k):
                      f = fm_info(k)
                      Lc, s0 = f["Lc"], f["s0"]
                      act(Fs[0][:, 0:Lc], f["ph"], AF.Ln, [f["phn"]], ["F0"], scale=1.0 / 64, bias=EPS)
                      act(Fs[1][:, 0:Lc], Fs[0][:, 0:Lc], AF.Exp, ["F0"], ["F1"], scale=-0.5)
                      if f["rope"]:
                          act(RQ[:, 0:Lc], f["pr"], AF.Copy, [f["prn"]], ["RQ"])
                          tt(Fs[3][:, 0:Lc], RQ[:, 0:Lc], SIN[:, s0:s0 + Lc], ALU.mult, ["RQ", "TAB"], ["F3"])
                          tt(Fs[2][:, 0:Lc], Fs[2][:, 0:Lc], Fs[3][:, 0:Lc], ALU.add, ["F2", "F3"], ["F2"])
                          tt(f["dst"], Fs[2][:, 0:Lc], Fs[1][:, 0:Lc], ALU.mult, ["F2", "F1"], [f["dres"]])
                      else:
                          stt(f["dst"], f["pf"], f["gain"], Fs[1][:, 0:Lc], ALU.mult, ALU.mult, [f["pfn"], "GS", "F1"], [f["dres"]])

                  nfm = len(combos)
                  fm_head(0)
                  fm_qg(0)
                  for k in range(nfm):
                      if k + 1 < nfm:
                          fm_head(k + 1)
                      fm_mid(k)
                      if k + 1 < nfm:
                          fm_qg(k + 1)
                      fm_tail(k)
                      if half == 0:
                          _adv(late0, 1)
                  _adv(late0, 64)
                  stage(f"fm_{half}")
                  tiles_h = []
                  for c in hgroups:
                      s0, Lc = GROUPS[c]
                      tiles_h += [(t, c) for t in range(s0 // 128, (s0 + Lc) // 128)]
                  tcount = [0]

                  def tm_proj(t, c, sv, srs, ncol):
                      sr, ser = srs
                      assert ring_serial[sr] == ser, "TM weight slot recycled while still in use"
                      h0 = t * 128 - HSTART[half]
                      pn = ["P0a", "P0b", "P1a", "P1b"][tcount[0] % 4]
                      tcount[0] += 1
                      pp = PSV[pn][:, 0:ncol]
                      for k in range(8):
                          mm(pp, HB[:, k, h0:h0 + 128], sv[:, k, :], k == 0, k == 7, [sr, hbres(c)], [pn])
                      return pp, pn

                  def tm_piece(pi):
                      ncol = 128 if pi == 4 else 256
                      slot, sr = load_piece(l, NP_TM + pi, 8 * ncol)
                      return slot[:, 0:8 * ncol].rearrange("p (k n) -> p k n", k=8), (sr, ring_serial[sr]), ncol

                  svU, srU, _ = tm_piece(0)
                  for (t, c) in tiles_h:
                      if last and c == 4:
                          continue
                      pp, pn = tm_proj(t, c, svU, srU, 256)
                      act(OB[:, t, :], pp, AF.Gelu_apprx_tanh, [pn], [("OB", t)])
                  svV, srV, _ = tm_piece(1)
                  svC, srC, _ = tm_piece(2)
                  svD, srD, _ = tm_piece(3)
                  svA, srA, _ = tm_piece(4)

                  def emit_z(t):
                      vn = VN[t % 2]
                      pzn = "P2a" if t % 2 == 0 else "P2b"
                      pz = PSV[pzn]
                      for g in range(4):
                          mm(pz[:, g * 64:(g + 1) * 64], WST[:, g, :], vn[:, g * 64:(g + 1) * 64], True, True,
                             [f"VN{t % 2}", "SM"], [pzn])
                      guf, gufn = (Fs[2], "F2") if t % 2 == 0 else (Fs[3], "F3")
                      act(guf[:, 0:256], OB[:, t, :], AF.Copy, [("OB", t)], [gufn])
                      for g in range(4):
                          stt(OB[:, t, g * 64:(g + 1) * 64], pz[:, g * 64:(g + 1) * 64], VEC[:, 22 + g:23 + g],
                              guf[:, g * 64:(g + 1) * 64], ALU.add, ALU.mult, [pzn, "VEC", gufn], [("OB", t)])

                  pending_z = []
                  for pi0 in range(0, len(tiles_h), 2):
                      pair = tiles_h[pi0:pi0 + 2]
                      fulls = [(t, c) for (t, c) in pair if not (last and c == 4)]
                      for (t, c) in fulls:
                          pp, pn = tm_proj(t, c, svV, srV, 256)
                          gv, gvn = (Fs[0], "F0") if t % 2 == 0 else (Fs[1], "F1")
                          gv = gv[:, 0:256]
                          act(gv, pp, AF.Gelu_apprx_tanh, [pn], [gvn])
                          o = (t % 2) * 16
                          stn = f"SMALL{t % 2}"
                          S.op("dve", lambda gv=gv, o=o: nc.vector.bn_stats(out=SMALL[:, o:o + 6], in_=gv), [gvn], [stn])
                          S.op("dve", lambda o=o: nc.vector.bn_aggr(out=SMALL[:, o + 8:o + 10], in_=SMALL[:, o:o + 6]),
                               [stn], [stn + "b"])
                      for tz in pending_z:
                          emit_z(tz)
                      pending_z = []
                      if len(fulls) == 2:
                          act(SMALL[:, 10:27:16], SMALL[:, 9:26:16], AF.Ln, ["SMALL0b", "SMALL1b"], ["SMALL0c", "SMALL1c"], bias=EPS)
                          act(SMALL[:, 11:28:16], SMALL[:, 10:27:16], AF.Exp, ["SMALL0c", "SMALL1c"], ["SMALL0d", "SMALL1d"], scale=-0.5)
                      for (t, c) in fulls:
                          o = (t % 2) * 16
                          stn = f"SMALL{t % 2}"
                          gv, gvn = (Fs[0], "F0") if t % 2 == 0 else (Fs[1], "F1")
                          gv = gv[:, 0:256]
                          if len(fulls) != 2:
                              act(SMALL[:, o + 10:o + 11], SMALL[:, o + 9:o + 10], AF.Ln, [stn + "b"], [stn + "c"], bias=EPS)
                              act(SMALL[:, o + 11:o + 12], SMALL[:, o + 10:o + 11], AF.Exp, [stn + "c"], [stn + "d"], scale=-0.5)
                          ts(gv, gv, SMALL[:, o + 8:o + 9], SMALL[:, o + 11:o + 12], ALU.subtract, ALU.mult,
                             [gvn, stn + "b", stn + "d"], [gvn])
                          tt(VN[t % 2][:], gv, BC[:, 4:260], ALU.mult, [gvn, "BC"], [f"VN{t % 2}"])
                      for (t, c) in fulls:
                          pp, pn = tm_proj(t, c, svC, srC, 256)
                          act(CY[:, t, :], pp, AF.Copy, [pn], [("CY", t)])
                      for (t, c) in pair:
                          pp, pn = tm_proj(t, c, svD, srD, 256)
                          cpv(VD[:, t, :, 0:64], pp.rearrange("p (h e) -> p h e", h=4), [pn, "VDones"], [("VD", t)])
                          pp, pn = tm_proj(t, c, svA, srA, 128)
                          cpv(VA[:, t, :, 0:64], pp.rearrange("p (h e) -> p h e", h=2), [pn, "VAones"], [("VA", t)])
                      pending_z = [t for (t, c) in fulls]
                  for tz in pending_z:
                      emit_z(tz)
              _adv(late0, 64)
              stage("p1")
              tap(f"qt{l}", QT[:, :, :], [128, 4, NTOK], BF16, [])
              tap(f"kt{l}", KT[:, :, :], [128, 3, NTOK], BF16, [])
              tap(f"ob{l}", OB[:, :, :], [128, 18, 256], BF16, [])
              tap(f"cy{l}", CY[:, :, :], [128, 18, 256], BF16, [])
              tap(f"vd{l}", VD[:, :, :, :], [128, 18, 4, 66], BF16, [])

              for half in range(2):
                  hgroups = [c for c in HALVES[half] if c in groups_all]
                  tiles_h = []
                  for c in hgroups:
                      s0, Lc = GROUPS[c]
                      tiles_h += [(t, c) for t in range(s0 // 128, (s0 + Lc) // 128)]
                  dma_pool(TAB[:, 0:2560], pm_d, [], ["TAB"])
                  PMv = TAB[:, 0:2560].rearrange("p (a t) -> p a t", a=20)
                  for (t, c) in tiles_h:
                      h0 = t * 128 - HSTART[half]
                      pv = PST[:, (t % 2) * 512:(t % 2) * 512 + 256]
                      pvn = "PST"
                      for cc in range(2):
                          tr(pv[:, cc * 128:(cc + 1) * 128], OB[:, t, cc * 128:(cc + 1) * 128], IDB, [("OB", t), "CBF"], [pvn])
                      cpv(HB[:, 2:4, h0:h0 + 128], pv.rearrange("p (c t) -> p c t", c=2), [pvn], [hbres(c)])
                  for c in hgroups:
                      s0, Lc = GROUPS[c]
                      h0, _ = hcols(c, half)
                      first_t = 16 if c == 4 else 0
                      last_t = 17 if c == 4 else 15
                      for t in range(s0 // 128, (s0 + Lc) // 128):
                          lo = (t * 128 - s0)
                          for g in range(4):
                              srcs = []
                              if t > first_t:
                                  srcs.append((t - 1, 3))
                              srcs.append((t, 1 if t == first_t else (2 if t == last_t else 0)))
                              if t < last_t:
                                  srcs.append((t + 1, 4))
                              o = (g % 2) * 64
                              pcn = "P0a" if g < 2 else "P0b"
                              outp = P0[o:o + 64, (g // 2) * 512 + lo:(g // 2) * 512 + lo + 128]
                              for i, (tj, v) in enumerate(srcs):
                                  mm(outp, CY[:, tj, g * 64:(g + 1) * 64], PMv[:, g * 5 + v, :], i == 0, i == len(srcs) - 1,
                                     [("CY", tj), "TAB"], [pcn])
                      for cc in range(2):
                          pcn = "P0a" if cc == 0 else "P0b"
                          pln = "P1a" if cc == 0 else "P1b"
                          act(PC[:, cc, 0:Lc], P0[:, cc * 512:cc * 512 + Lc], AF.Copy, [pcn], [("PC", cc)])
                          mm(P1[:, cc * 512:cc * 512 + Lc], WPBD[:, cc, :], PC[:, cc, 0:Lc], True, True, [("PC", cc), "SM"], [pln])
                          act(HB[:, 4 + cc, h0:h0 + Lc], P1[:, cc * 512:cc * 512 + Lc], AF.Identity, [pln, "VEC"], [hbres(c)],
                              scale=VEC[:, 20 + cc:21 + cc])
                  stage(f"p2a_{half}")
                  acount = [0]

                  def att_scores(job):
                      i = acount[0] % 2
                      acount[0] += 1
                      job["i"] = i
                      Sp = P0 if i == 0 else P1
                      Sn = ["P0a", "P0b"] if i == 0 else ["P1a", "P1b"]
                      chunks = job["chunks"]
                      n = len(chunks)
                      for ci, (kt, bias) in enumerate(chunks):
                          o = Sp[:, ci * 128:(ci + 1) * 128]
                          bn = [Sn[ci // 4]]
                          mm(o, job["K"](kt), QZ[:, job["slot"], :], True, bias is None,
                             [("QZ", job["slot"]), ("KT", job["kidx"], grp_of_tile(kt))], bn)
                          if bias is not None:
                              mm(o, IDB, bias, False, True, ["CBF", "TAB"], bn)
                          if ci == 3 and n > 4:
                              act(PT[i][:, 0:512], Sp[:, 0:512], AF.Exp, [Sn[0]], [f"PT{i}a"])
                      if n > 4:
                          act(PT[i][:, 512:n * 128], Sp[:, 512:n * 128], AF.Exp, [Sn[1]], [f"PT{i}b"])
                      else:
                          act(PT[i][:, 0:n * 128], Sp[:, 0:n * 128], AF.Exp, [Sn[0]], [f"PT{i}a"])

                  def att_pv(job):
                      i = job["i"]
                      chunks = job["chunks"]
                      n = len(chunks)
                      h = job["h"]
                      for ci, (kt, bias) in enumerate(chunks):
                          mm(job["O"][:, h * 66:h * 66 + 65], PT[i][:, ci * 128:(ci + 1) * 128], job["V"](kt), ci == 0, ci == n - 1,
                             [f"PT{i}a" if ci < 4 else f"PT{i}b", (job["vname"], kt), job["vname"] + "ones"], [job["On"]])

                  def finish_tile(t, c, Otile, Oname, sink, cat0):
                      h0 = t * 128 - HSTART[half]
                      Ov = Otile[:, 0:264].rearrange("p (h e) -> p h e", h=4)
                      o = 32 + (t % 2) * 8
                      dn = f"DEN{t % 2}"
                      if sink:
                          tt(SMALL[:, o:o + 4], Ov[:, :, 64], GS[:, 4:8], ALU.add, [Oname, "GS"], [dn])
                      else:
                          cpv(SMALL[:, o:o + 4], Ov[:, :, 64], [Oname], [dn])
                      S.op("dve", lambda o=o: nc.vector.reciprocal(out=SMALL[:, o + 4:o + 8], in_=SMALL[:, o:o + 4]), [dn], [dn + "r"])
                      ot = OT[t % 2]
                      otn = f"OT{t % 2}"
                      tt(ot[:].rearrange("p (h e) -> p h e", h=4), Ov[:, :, 0:64],
                         SMALL[:, o + 4:o + 8].unsqueeze(2).to_broadcast([128, 4, 64]), ALU.mult, [Oname, dn + "r"], [otn])
                      pv = PST[:, (t % 2) * 512:(t % 2) * 512 + 256]
                      pvn = "PST"
                      for cc in range(2):
                          tr(pv[:, cc * 128:(cc + 1) * 128], ot[:, cc * 128:(cc + 1) * 128], IDB, [otn, "CBF"], [pvn])
                      cpv(HB[:, cat0:cat0 + 2, h0:h0 + 128], pv.rearrange("p (c t) -> p c t", c=2), [pvn], [hbres(c)])

                  def run_attn(jobs, sink, cat0):
                      S.op("dve", lambda: nc.vector.memset(QZ, 0.0), [], [("PC", 0), ("PC", 1)] + [("QZ", i) for i in range(8)])
                      prev = None
                      for job in jobs + [None]:
                          if job is not None:
                              if job["h"] == 0:
                                  base = (job["t"] % 2) * 4
                                  names = [("QZ", base + i) for i in range(4)]
                                  for (dst_ap, src_ap, qr) in job["qcopies"]:
                                      cpv(dst_ap, src_ap, names + qr, names)
                              att_scores(job)
                          if prev is not None:
                              att_pv(prev)
                              if prev["last"]:
                                  finish_tile(prev["t"], prev["c"], prev["O"], prev["On"], sink, cat0)
                                  advance_mod()
                          prev = job

                  def tcols(t):
                      return slice(t * 128, (t + 1) * 128)

                  jobs = []
                  for (t, c) in tiles_h:
                      Ot, On = (PSV["P2a"], "P2a") if t % 2 == 0 else (PSV["P2b"], "P2b")
                      for h in range(4):
                          kv, g = h // 2, h % 2
                          po = kv * 64
                          if c == 4:
                              chunks = [(16, None), (17, None)]
                          else:
                              chunks = []
                              if t > 0:
                                  chunks.append((t - 1, MASKP))
                              chunks.append((t, None))
                              if t < 15:
                                  chunks.append((t + 1, MASKN))
                              chunks += [(16, None), (17, None)]
                          jobs.append({"t": t, "c": c, "h": h, "K": (lambda kt: KT[:, 0, tcols(kt)]), "po": po,
                                       "kidx": 0, "qres": ("QT", g, c), "vname": "VA",
                                       "qcopies": [(QZ[0:64, (t % 2) * 4:(t % 2) * 4 + 2, :], QT[0:64, 0:2, tcols(t)],
                                                    [("QT", 0, c), ("QT", 1, c)]),
                                                   (QZ[64:128, (t % 2) * 4 + 2:(t % 2) * 4 + 4, :], QT[64:128, 0:2, tcols(t)],
                                                    [("QT", 0, c), ("QT", 1, c)])],
                                       "slot": (t % 2) * 4 + h,
                                       "Q": QT[po:po + 64, g, tcols(t)], "V": (lambda kt, kv=kv: VA[:, kt, kv, 0:65]),
                                       "chunks": chunks, "O": Ot, "On": On, "last": h == 3})
                  run_attn(jobs, True, 0)
                  stage(f"attA_{half}")
                  dma_pool(TAB[:, 0:5376], Ld["dtab"], [], ["TAB"])
                  DT = TAB[:, 0:5376].rearrange("p (h s q) -> p h s q", h=4, s=21)
                  jobs = []
                  for (t, c) in tiles_h:
                      Ot, On = (PSV["P2a"], "P2a") if t % 2 == 0 else (PSV["P2b"], "P2b")
                      if c == 4:
                          deltas, interior = [], False
                      elif 2 <= t <= 13:
                          deltas, interior = [-2, -1, 0, 1, 2], True
                      elif t == 0:
                          deltas, interior = [0, 1, 2, 3], False
                      elif t == 1:
                          deltas, interior = [-1, 0, 1, 2], False
                      elif t == 14:
                          deltas, interior = [-2, -1, 0, 1], False
                      else:
                          deltas, interior = [-3, -2, -1, 0], False
                      for h in range(4):
                          po = (h % 2) * 64
                          ch = h // 2
                          chunks = []
                          for dl in deltas:
                              i0 = _dslot(interior, 2 * dl + 7)
                              i1 = _dslot(interior, 2 * dl + 6)
                              chunks.append((t + dl, DT[:, h, i0:i1 + 1:(i1 - i0), :]))
                          chunks += [(16, None), (17, None)]
                          jobs.append({"t": t, "c": c, "h": h, "K": (lambda kt, ch=ch: KT[:, 1 + ch, tcols(kt)]), "po": po,
                                       "kidx": 1 + ch, "qres": ("QT", 2 + ch, c), "vname": "VD",
                                       "qcopies": [(QZ[0:64, (t % 2) * 4:(t % 2) * 4 + 4:2, :], QT[0:64, 2:4, tcols(t)],
                                                    [("QT", 2, c), ("QT", 3, c)]),
                                                   (QZ[64:128, (t % 2) * 4 + 1:(t % 2) * 4 + 4:2, :], QT[64:128, 2:4, tcols(t)],
                                                    [("QT", 2, c), ("QT", 3, c)])],
                                       "slot": (t % 2) * 4 + h,
                                       "Q": QT[po:po + 64, 2 + ch, tcols(t)], "V": (lambda kt, h=h: VD[:, kt, h, 0:65]),
                                       "chunks": chunks, "O": Ot, "On": On, "last": h == 3})
                  run_attn(jobs, False, 6)
                  if half == 0:
                      tap(f"cat{l}", HB[:, :, 0:1024], [128, 8, 1024], BF16, [])
                  stage(f"attD_{half}")
                  wcount = 0
                  for a in range(4):
                      slot, sr = load_piece(l, NP_WO + a)
                      sv = slot[:].rearrange("p (k n) -> p k n", k=8)
                      for di in range(2):
                          dch = 2 * a + di
                          for c in hgroups:
                              s0, Lc = GROUPS[c]
                              h0, _ = hcols(c, half)
                              s = sidx(c)
                              pn = ["P0a", "P0b", "P1a", "P1b"][wcount % 4]
                              wcount += 1
                              pw = PSV[pn][:, 0:Lc]
                              for k in range(8):
                                  mm(pw, sv[:, k, di * 128:(di + 1) * 128], HB[:, k, h0:h0 + Lc], k == 0, k == 7,
                                     [sr, hbres(c)], [pn])
                              stt(X[:, dch, s0:s0 + Lc], pw, G1[:, dch, s:s + 1], X[:, dch, s0:s0 + Lc], ALU.mult, ALU.add,
                                  [pn, MODn, xres(c)], [xres(c)])
                  stage(f"wout_{half}")
                  if half == 1:
                      S.barrier()
              advance_mod(64)
              stage("wout")
              tap(f"x1_{l}", X[:], [128, 8, NTOK], F32, [])

              stt(A2[:], MOD[:, 32:40, :], 1.0, VEC[:, 8:16].unsqueeze(2).to_broadcast([128, 8, 2]), ALU.add, ALU.mult,
                  [MODn, "VEC"], ["A2"])
              def h2dst(k, c):
                  s0, Lc = GROUPS[c]
                  if c < 2:
                      return HB[:, k, s0:s0 + Lc], hbres(c)
                  return H2B[:, k, s0 - 1024:s0 - 1024 + Lc], ("H2B", c)

              do_norms(groups_all, A2, SH2, h2dst, "A2")
              stage("norm2")
              gcount = 0
              dcount = 0
              for bi, blk in enumerate(FFBLOCKS):
                  for fl, f in enumerate(blk):
                      slot, sr = load_piece(l, NP_GU + f)
                      sv = slot[:].rearrange("p (u k n) -> p u k n", u=2, k=8)
                      for c in groups_all:
                          s0, Lc = GROUPS[c]
                          i = gcount % 2
                          gcount += 1
                          pgn, pun = ("P0a", "P0b") if i == 0 else ("P1a", "P1b")
                          pg, pu = PSV[pgn][:, 0:Lc], PSV[pun][:, 0:Lc]
                          for k in range(8):
                              hsrc, hres = h2dst(k, c)
                              mm(pg, sv[:, 0, k, :], hsrc, k == 0, k == 7, [sr, hres], [pgn])
                          for k in range(8):
                              hsrc, hres = h2dst(k, c)
                              mm(pu, sv[:, 1, k, :], hsrc, k == 0, k == 7, [sr, hres], [pun])
                          sg, sgn = (Fs[0], "F0") if i == 0 else (Fs[1], "F1")
                          act(sg[:, 0:Lc], pg, AF.Silu, [pgn], [sgn])
                          tt(ACTB[:, fl, s0:s0 + Lc], pu, sg[:, 0:Lc], ALU.mult, [sgn, pun], [("ACTB", fl, c)])
                  nf = len(blk)
                  for dch in range(8):
                      slot, sr = load_piece(l, NP_WD + bi * 8 + dch, nf * 128)
                      sv = slot[:, 0:nf * 128].rearrange("p (f n) -> p f n", f=nf)
                      for c in groups_all:
                          s0, Lc = GROUPS[c]
                          s = sidx(c)
                          pn = ["P2a", "P2b", "P3"][dcount % 3]
                          dcount += 1
                          pd = PSV[pn][:, 0:Lc]
                          for fl in range(nf):
                              mm(pd, sv[:, fl, :], ACTB[:, fl, s0:s0 + Lc], fl == 0, fl == nf - 1, [sr, ("ACTB", fl, c)], [pn])
                          stt(X[:, dch, s0:s0 + Lc], pd, G2[:, dch, s:s + 1], X[:, dch, s0:s0 + Lc], ALU.mult, ALU.add,
                              [pn, MODn, xres(c)], [xres(c)])
              S.barrier()
              stage("ffn")
              tap(f"x2_{l}", X[:], [128, 8, NTOK], F32, [])


        try:
            run_layers()
        except _Stop:
            pass
        for t in range(16):
            Pt = P0 if t % 2 == 0 else P1
            pr = ["P0a", "P0b"] if t % 2 == 0 else ["P1a", "P1b"]
            for k in range(8):
                tr(Pt[:, k * 128:(k + 1) * 128], X[:, k, t * 128:(t + 1) * 128], IDF[:], [xres(t // 4), "IDF"], pr)
            stg = SCR[:, (t % 2) * 1024:(t % 2 + 1) * 1024]
            stg_r = ["F0", "F1"] if t % 2 == 0 else ["F2", "F3"]
            if t % 2 == 0:
                act(stg, Pt[:], AF.Copy, pr, stg_r)
            else:
                cpv(stg, Pt[:], pr, stg_r)
            dma_sp(out_d[t * 128:(t + 1) * 128, :], stg, stg_r, [("out", t)])
        S.emit(st)
    return nc, tap_d


_CACHE = {}


def prep_inputs(inp, n_layers=2):
    inp = {k: np.asarray(v, dtype=np.float32) for k, v in inp.items()}
    cbf, rope, pm = _consts()
    shared = {"cbf": np.ascontiguousarray(cbf.reshape(128, -1)), "idf": np.eye(128, dtype=np.float32),
              "rope": np.ascontiguousarray(rope.reshape(128, -1)), "pm": np.ascontiguousarray(pm.reshape(128, -1))}
    for l in range(n_layers):
        la = _layer_arrays(inp, l)
        for k, v in la.items():
            shared[f"{k}{l}"] = v
    maps = []
    for b in range(8):
        m = dict(shared)
        m["x"] = np.ascontiguousarray(inp["x"][b])
        m["ctx"] = np.ascontiguousarray(inp["ctx"][b])
        cv = np.zeros((128, 8, 2), np.float32)
        cv[:, :, 0] = inp["c"][b].reshape(8, 128).T
        cv[:, :, 1] = inp["c_ctx"].reshape(8, 128).T
        m["cvec"] = np.ascontiguousarray(cv.reshape(128, 16))
        maps.append(m)
    return maps


def kernel(**inputs):
    if "nc" not in _CACHE:
        _CACHE["nc"] = build(2)[0]
    nc = _CACHE["nc"]
    maps = prep_inputs(inputs, 2)
    res = run_bass_kernel_spmd(nc, maps, core_ids=list(range(8)))
    return np.stack([np.asarray(r["out"], dtype=np.float32) for r in res.results], 0)
```

```python
import contextlib
import numpy as np
import concourse.bass as bass
import concourse.mybir as mybir
from concourse.bass_utils import run_bass_kernel_spmd

F32 = mybir.dt.float32
BF16 = mybir.dt.bfloat16
AF = mybir.ActivationFunctionType
ALU = mybir.AluOpType

D = 1024
SEQ = 2048
CTX = 256
NTOK = SEQ + CTX
FF = 2816
NFF = 22
NEG = -30000.0
EPS = 1e-6
GROUPS = [(0, 512), (512, 512), (1024, 512), (1536, 512), (2048, 256)]
HALVES = [[0, 1], [2, 3, 4]]
HSTART = [0, 1024]
FFBLOCKS = [list(range(0, 8)), list(range(8, 15)), list(range(15, 22))]
PIECE = 2048
import os as _os
_DBG_DELAY = int(_os.environ.get('DBG_DELAY', '0'))
_DBG_SKIP = set(int(v) for v in _os.environ.get('DBG_SKIP_A', '').split(',') if v)


class _Op:
    __slots__ = ("eng", "fn", "dma", "deps", "has_dep", "sig")

    def __init__(self, eng, fn, dma):
        self.eng = eng
        self.fn = fn
        self.dma = dma
        self.deps = []
        self.has_dep = False
        self.sig = None


class Sched:
    COMPUTE = ("pe", "act", "dve")

    def __init__(self, nc, n_dma_sems=8):
        self.nc = nc
        self.ops = []
        self.res = {}
        self.n_dma_sems = n_dma_sems
        self.last = {}
        self.pending = {}
        self.dma_since = []

    def op(self, eng, fn, reads=(), writes=(), dma=False):
        o = _Op(eng, fn, dma)
        deps = {}
        for r in reads:
            st = self.res.get(r)
            if st is not None and st[0] is not None:
                deps[id(st[0])] = (st[0], True)
        for w in writes:
            st = self.res.get(w)
            if st is not None:
                if st[0] is not None and id(st[0]) not in deps:
                    deps[id(st[0])] = (st[0], False)
                for rd in st[1]:
                    if id(rd) not in deps:
                        deps[id(rd)] = (rd, False)
        for d in self.pending.pop(eng, ()):
            if id(d) not in deps:
                deps[id(d)] = (d, True)
        for d, raw in deps.values():
            if d is o:
                continue
            if not d.dma and not o.dma and d.eng == eng:
                if eng == "pe":
                    continue
            o.deps.append(d)
            d.has_dep = True
        for r in reads:
            st = self.res.get(r)
            if st is None:
                self.res[r] = [None, [o]]
            else:
                st[1].append(o)
        for w in writes:
            self.res[w] = [o, []]
        self.ops.append(o)
        self.last[eng] = o
        if dma:
            self.dma_since.append(o)
        return o

    def barrier(self):
        lasts = [o for e, o in self.last.items() if e in self.COMPUTE]
        lasts += self.dma_since
        self.dma_since = []
        for e in self.COMPUTE:
            self.pending[e] = self.pending.get(e, []) + lasts

    def emit(self, stack):
        nc = self.nc
        eobj = {"pe": nc.tensor, "act": nc.scalar, "dve": nc.vector, "pool": nc.gpsimd, "sp": nc.sync}
        semh = {}
        cnt = {}
        waited = {e: {} for e in eobj}
        dma_rr = {}
        dcnt = {}

        def get_sem(key):
            if key not in semh:
                semh[key] = stack.enter_context(nc.semaphore("s_" + "_".join(str(k) for k in key)))
            return semh[key]

        for o in self.ops:
            E = eobj[o.eng]
            need = {}
            for d in o.deps:
                key, val = d.sig
                if need.get(key, 0) < val:
                    need[key] = val
            if o.dma:
                i = dma_rr.get(o.eng, 0)
                dma_rr[o.eng] = (i + 1) % self.n_dma_sems
                k = ("dma", o.eng, i)
                n = dcnt.get(k, 0) + 1
                dcnt[k] = n
                if n > 1 and need.get(k, 0) < 16 * (n - 1):
                    need[k] = 16 * (n - 1)
            for key, val in need.items():
                if waited[o.eng].get(key, 0) < val:
                    E.wait_ge(get_sem(key), val)
                    waited[o.eng][key] = val
            ins = o.fn()
            if o.dma:
                ins.then_inc(get_sem(k), 16)
                o.sig = (k, 16 * n)
            elif o.has_dep:
                key = ("e", o.eng)
                cnt[key] = cnt.get(key, 0) + 1
                ins.then_inc(get_sem(key), 1)
                o.sig = (key, cnt[key])
        for key, n in dcnt.items():
            nc.sync.wait_ge(get_sem(key), 16 * n)
        for key, n in cnt.items():
            nc.sync.wait_ge(get_sem(key), n)


def _kmajor(w):
    K, N = w.shape
    return np.ascontiguousarray(w.reshape(K // 128, 128, N).transpose(1, 0, 2))


def _pad_piece(a):
    a = np.ascontiguousarray(a, dtype=np.float32).reshape(128, -1)
    out = np.zeros((128, PIECE), np.float32)
    out[:, : a.shape[1]] = a
    return out


def _dslot(interior, s):
    if 3 <= s <= 9:
        return 8 + (9 - s)
    if s >= 10:
        return (0 if interior else 4) + (13 - s)
    return (18 if interior else 15) + (2 - s)


def _consts():
    c = np.zeros((128, 6, 128), np.float32)
    c[:, 0, :] = np.eye(128)
    c[:, 1, :] = 1.0
    c[0:64, 2, 0:64] = 1.0
    c[64:128, 2, 64:128] = 1.0
    for m in range(128):
        if (m % 32) < 16:
            c[m + 16, 3, m] = -1.0
        else:
            c[m - 16, 3, m] = 1.0
    j = np.arange(128)[:, None]
    i = np.arange(128)[None, :]
    c[:, 4, :] = np.where(j >= i, 0.0, NEG)
    c[:, 5, :] = np.where(j <= i, 0.0, NEG)
    p = np.arange(128)
    idx = p % 64
    f = (idx % 16).astype(np.float64)
    inv = np.power(10000.0, -f / 16.0)
    t = np.arange(SEQ)
    pos = np.where((idx < 32)[:, None], (t // 64)[None, :], (t % 64)[None, :]).astype(np.float64)
    ang = pos * inv[:, None]
    rope = np.stack([np.cos(ang), np.sin(ang)], axis=1).astype(np.float32)
    pm = np.zeros((128, 20, 128), np.float32)
    jj = np.arange(128)[:, None]
    tt = np.arange(128)[None, :]
    for g, w in enumerate((2, 4, 8, 16)):
        h = w // 2
        eye = (jj == tt).astype(np.float32)
        pm[:, g * 5 + 0, :] = ((jj >= tt - h) & (jj < tt + h)) / w - eye
        lo = np.maximum(tt - h, 0)
        cntf = (tt + h) - lo
        pm[:, g * 5 + 1, :] = ((jj >= lo) & (jj < tt + h)) / cntf - eye
        hi = np.minimum(tt + h, 128)
        cntl = hi - (tt - h)
        pm[:, g * 5 + 2, :] = ((jj >= tt - h) & (jj < hi)) / cntl - eye
        pm[:, g * 5 + 3, :] = (jj >= 128 + tt - h) / w
        pm[:, g * 5 + 4, :] = (jj < tt + h - 128) / w
    return c, rope, pm


def _dtab(rpb):
    kc = np.arange(64)[:, None]
    qc = np.arange(64)[None, :]
    cs = np.clip(qc - 8, 0, 48)
    col_ok = (kc >= cs) & (kc < cs + 16)
    coff = np.clip(kc - qc, -15, 15) + 15
    out = np.full((128, 4, 21, 64), NEG, np.float32)
    for h in range(4):
        for interior in (True, False):
            for s in range(14):
                sl = _dslot(interior, s)
                for kr in range(2):
                    rho = s + kr
                    ok = 0 <= rho <= 14 and ((3 <= rho <= 10) or not interior)
                    if not ok:
                        continue
                    T = np.where(col_ok, rpb[h, rho][coff], NEG)
                    out[kr * 64:(kr + 1) * 64, h, sl, :] = T
    return out


def _layer_arrays(inp, l):
    f32 = np.float32
    w_in = inp["w_in"][l]
    pieces = []
    wm = _kmajor(inp["w_mod"][l])
    for jj in range(24):
        pieces.append(_pad_piece(wm[:, :, jj * 256:(jj + 1) * 256]))
    aq = w_in[:, 0:256]
    fm_cols = [np.concatenate([aq[:, 0:64], aq[:, 128:192]], 1), np.concatenate([aq[:, 64:128], aq[:, 192:256]], 1),
               w_in[:, 256:384], w_in[:, 1280:1408], w_in[:, 1408:1536], w_in[:, 1536:1664], w_in[:, 1664:1792]]
    fm = _kmajor(np.concatenate(fm_cols, 1))
    for a in range(4):
        pieces.append(_pad_piece(fm[:, :, a * 256:min((a + 1) * 256, 896)]))
    tm = [w_in[:, 512:768], w_in[:, 768:1024], w_in[:, 1024:1280], w_in[:, 1792:2048], w_in[:, 384:512]]
    for a in tm:
        pieces.append(_pad_piece(_kmajor(a)))
    wo = _kmajor(inp["w_out"][l])
    for a in range(4):
        pieces.append(_pad_piece(wo[:, :, a * 256:(a + 1) * 256]))
    wg = _kmajor(inp["w_gate"][l])
    wu = _kmajor(inp["w_up"][l])
    for f in range(NFF):
        pieces.append(_pad_piece(np.concatenate([wg[:, :, f * 128:(f + 1) * 128].reshape(128, -1),
                                                 wu[:, :, f * 128:(f + 1) * 128].reshape(128, -1)], 1)))
    wd = _kmajor(inp["w_down"][l])
    for blk in FFBLOCKS:
        for dch in range(8):
            pieces.append(_pad_piece(wd[:, blk[0]:blk[-1] + 1, dch * 128:(dch + 1) * 128]))
    W = np.stack(pieces, 0)
    vec = np.zeros((128, 26), f32)
    vec[:, 0:8] = inp["g_mix"][l].reshape(8, 128).T
    vec[:, 8:16] = inp["g_ffn"][l].reshape(8, 128).T
    vec[:, 16] = np.tile(inp["a_q_gain"][l], 2)
    vec[:, 17] = np.tile(inp["a_k_gain"][l], 2)
    vec[:, 18] = np.tile(inp["d_q_gain"][l], 2)
    vec[:, 19] = np.tile(inp["d_k_gain"][l], 2)
    vec[:, 20:22] = inp["c_scale"][l].reshape(2, 128).T
    vec[:, 22:26] = inp["b_b_s"][l].T
    bmod = np.ascontiguousarray(inp["b_mod"][l].reshape(48, 128).T)
    bc = np.zeros((128, 260), f32)
    bc[:, 0:4] = inp["a_sink"][l][None, :]
    bc[:, 4:260] = inp["b_v_gain"][l][None, :]
    sm = np.zeros((128, 6, 128), f32)
    sm[:, 0:4, :] = inp["b_w_s"][l].transpose(2, 0, 1)
    wp = inp["c_w_pool"][l]
    for g in range(4):
        o = (g % 2) * 64
        sm[o:o + 64, 4 + g // 2, o:o + 64] = wp[g]
    dt = _dtab(inp["d_rpb"][l])
    return {"W": W, "vec": vec, "bmod": bmod, "bc": bc, "sm": np.ascontiguousarray(sm.reshape(128, -1)), "dtab": np.ascontiguousarray(dt.reshape(128, -1))}


NP_MOD, NP_FM, NP_TM, NP_WO, NP_GU, NP_WD = 0, 24, 28, 33, 37, 59
NPIECES = 83


class _Stop(Exception):
    pass


def build(n_layers=2, taps=(), stop=None):
    nc = bass.Bass("TRN2", target_bir_lowering=False)
    dr = {}

    def din(name, shape):
        dr[name] = nc.dram_tensor(name, list(shape), F32, kind="ExternalInput").ap()
        return dr[name]

    x_d = din("x", [SEQ, D])
    ctx_d = din("ctx", [CTX, D])
    cvec_d = din("cvec", [128, 16])
    cbf_d = din("cbf", [128, 6 * 128])
    idf_d = din("idf", [128, 128])
    rope_d = din("rope", [128, 2 * SEQ])
    pm_d = din("pm", [128, 20 * 128])
    L = []
    for l in range(n_layers):
        L.append({
            "W": din(f"W{l}", [NPIECES, 128, PIECE]), "vec": din(f"vec{l}", [128, 26]), "bmod": din(f"bmod{l}", [128, 48]),
            "bc": din(f"bc{l}", [128, 260]), "sm": din(f"sm{l}", [128, 6 * 128]), "dtab": din(f"dtab{l}", [128, 4 * 21 * 64]),
        })
    out_d = nc.dram_tensor("out", [SEQ, D], F32, kind="ExternalOutput").ap()
    tap_d = {}

    st = contextlib.ExitStack()
    with st:
        S = Sched(nc)
        sb = lambda n, s, d: st.enter_context(nc.sbuf_tensor(n, list(s), d))
        ps = lambda n, s, d: st.enter_context(nc.psum_tensor(n, list(s), d))
        X = sb("X", [128, 8, NTOK], F32)
        HB = sb("HB", [128, 8, 1280], BF16)
        RR = sb("RR", [128, 32472], BF16)
        QT = RR[:, 0:9216].rearrange("p (k t) -> p k t", k=4)
        KT = RR[:, 9216:16128].rearrange("p (k t) -> p k t", k=3)
        VA = RR[:, 16128:18504].rearrange("p (t h e) -> p t h e", t=18, h=2)
        VD = RR[:, 18504:23256].rearrange("p (t h e) -> p t h e", t=18, h=4)
        OB = RR[:, 23256:27864].rearrange("p (t c) -> p t c", t=18)
        CY = RR[:, 27864:32472].rearrange("p (t c) -> p t c", t=18)
        ACTB = RR[:, 0:18432].rearrange("p (f t) -> p f t", f=8)
        H2B = RR[:, 18432:28672].rearrange("p (k t) -> p k t", k=8)
        RING = [sb(f"ring{i}", [128, PIECE], BF16) for i in range(4)]
        TAB = sb("TAB", [128, 5376], BF16)
        SCR = sb("SCR", [128, 2048], F32)
        Fs = [SCR[:, i * 512:(i + 1) * 512] for i in range(4)]
        SQ = [sb(f"SQ{i}", [128, 512], BF16) for i in range(2)]
        QG = sb("QG", [128, 512], BF16)
        RQ = sb("RQ", [128, 512], BF16)
        PT = [sb(f"PT{i}", [128, 896], BF16) for i in range(2)]
        OT = [sb(f"OT{i}", [128, 256], BF16) for i in range(2)]
        PC = sb("PC", [128, 2, 512], BF16)
        QZ = PC[:].rearrange("p a (b q) -> p (a b) q", q=128)
        VN = [sb(f"VN{i}", [128, 256], BF16) for i in range(2)]
        CBF = sb("CBF", [128, 6, 128], BF16)
        IDF = sb("IDF", [128, 128], F32)
        CV = sb("CV", [128, 8, 2], F32)
        CA = sb("CA", [128, 8, 2], BF16)
        SMALL = sb("SMALL", [128, 64], F32)
        IDB = CBF[:, 0, :]
        ONES = CBF[:, 1, :]
        BDONES = CBF[:, 2, :]
        ROTT = CBF[:, 3, :]
        MASKP = CBF[:, 4, :]
        MASKN = CBF[:, 5, :]
        P0 = ps("P0", [128, 1024], F32)
        P1 = ps("P1", [128, 1024], F32)
        P2 = ps("P2", [128, 1024], F32)
        P3 = ps("P3", [128, 512], F32)
        PST = ps("PST", [128, 1024], BF16)
        PSV = {"P0a": P0[:, 0:512], "P0b": P0[:, 512:1024], "P1a": P1[:, 0:512], "P1b": P1[:, 512:1024],
               "P2a": P2[:, 0:512], "P2b": P2[:, 512:1024], "P3": P3[:, :]}

        def mm(out, lhsT, rhs, start, stop, reads, writes):
            S.op("pe", lambda: nc.tensor.matmul(out, lhsT=lhsT, rhs=rhs, start=start, stop=stop), reads, writes)

        def tr(out, in_, ident, reads, writes):
            S.op("pe", lambda: nc.tensor.transpose(out, in_, ident), reads, writes)

        def act(out, in_, func, reads, writes, scale=None, bias=None):
            kw = {}
            if scale is not None:
                kw["scale"] = scale
            if bias is not None:
                kw["bias"] = bias
            S.op("act", lambda: nc.scalar.activation(out=out, in_=in_, func=func, **kw), reads, writes)

        def tt(out, in0, in1, op, reads, writes):
            S.op("dve", lambda: nc.vector.tensor_tensor(out=out, in0=in0, in1=in1, op=op), reads, writes)

        def stt(out, in0, scalar, in1, op0, op1, reads, writes):
            S.op("dve", lambda: nc.vector.scalar_tensor_tensor(out=out, in0=in0, scalar=scalar, in1=in1, op0=op0, op1=op1),
                 reads, writes)

        def ts(out, in0, s1, s2, op0, op1, reads, writes):
            S.op("dve", lambda: nc.vector.tensor_scalar(out=out, in0=in0, scalar1=s1, scalar2=s2, op0=op0, op1=op1),
                 reads, writes)

        def cpv(out, in_, reads, writes):
            S.op("dve", lambda: nc.vector.tensor_copy(out=out, in_=in_), reads, writes)

        def dma_sp(out, in_, reads, writes):
            S.op("sp", lambda: nc.sync.dma_start(out=out, in_=in_), reads, writes, dma=True)

        def dma_pool(out, in_, reads, writes):
            S.op("pool", lambda: nc.gpsimd.dma_start(out=out, in_=in_, max_dma_last_dim=4096), reads, writes, dma=True)

        def stage(name):
            if stop == name:
                S.barrier()
                raise _Stop()

        def tap(name, ap, shape, dtype, reads):
            if name not in taps:
                return
            S.barrier()
            t = nc.dram_tensor("tap_" + name, list(shape), dtype, kind="ExternalOutput").ap()
            tap_d[name] = t
            dma_sp(t, ap, list(reads) + ["tapsrc"], ["tap_" + name])

        ring_i = [0]
        ring_serial = {}

        def load_piece(l, idx, nelem=PIECE):
            i = ring_i[0]
            ring_i[0] = (i + 1) % 4
            dma_pool(RING[i][:, 0:nelem], L[l]["W"][idx, :, 0:nelem], [], [("slot", i)])
            ring_serial[("slot", i)] = ring_serial.get(("slot", i), 0) + 1
            return RING[i], ("slot", i)

        dma_sp(IDF[:], idf_d, [], ["IDF"])
        dma_sp(CV[:].rearrange("p k s -> p (k s)"), cvec_d, [], ["CV"])
        dma_pool(CBF[:].rearrange("p a b -> p (a b)"), cbf_d, [], ["CBF"])
        act(CA[:], CV[:], AF.Silu, ["CV"], ["CA"])
        for t in range(18):
            stg = SCR[:, (t % 2) * 1024:(t % 2 + 1) * 1024]
            stg_r = ["F0", "F1"] if t % 2 == 0 else ["F2", "F3"]
            src = x_d[t * 128:(t + 1) * 128, :] if t < 16 else ctx_d[(t - 16) * 128:(t - 15) * 128, :]
            dma_sp(stg, src, [], stg_r)
            Pt = P0 if t % 2 == 0 else P1
            pr = ["P0a", "P0b"] if t % 2 == 0 else ["P1a", "P1b"]
            for k in range(8):
                tr(Pt[:, k * 128:(k + 1) * 128], stg[:, k * 128:(k + 1) * 128], IDF[:], stg_r + ["IDF"], pr)
            dst = X[:, :, t * 128:(t + 1) * 128]
            src_ps = Pt[:].rearrange("p (k t) -> p k t", k=8)
            xr = [("X", t // 4 if t < 16 else 4)]
            if t % 2 == 0:
                act(dst, src_ps, AF.Copy, pr, xr)
            else:
                cpv(dst, src_ps, pr, xr)
        S.barrier()

        def xres(c):
            return ("X", c)

        def hbres(c):
            return ("HB", 2) if c == 4 else ("HB", c % 2)

        def grp_of_tile(t):
            return t // 4 if t < 16 else 4

        def sidx(c):
            return 1 if c == 4 else 0

        VEC = sb("VEC", [128, 26], F32)
        BMODs = [sb(f"BMOD{i}", [128, 48], F32) for i in range(2)]
        BC = sb("BC", [128, 260], F32)
        SM = sb("SM", [128, 6, 128], BF16)
        MODs = [sb(f"MOD{i}", [128, 48, 2], F32) for i in range(2)]
        A1 = sb("A1", [128, 8, 2], F32)
        A2 = sb("A2", [128, 8, 2], F32)
        GS = sb("GS", [128, 8], F32)

        def mod_gen(l, jj0=0, jj1=24):
            b = l % 2
            if jj0 == 0:
                dma_sp(BMODs[b][:], L[l]["bmod"], [], [f"BMOD{b}"])
            for jj in range(jj0, jj1):
                slot, sr = load_piece(l, NP_MOD + jj)
                sv = slot[:].rearrange("p (k n) -> p k n", k=8)
                for cc in range(2):
                    j = 2 * jj + cc
                    for k in range(8):
                        mm(P3[:, 2 * j:2 * j + 2], sv[:, k, cc * 128:(cc + 1) * 128], CA[:, k, :], k == 0, k == 7,
                           [sr, "CA"], ["P3"])
                yield
            c0, c1 = 2 * jj0, 2 * jj1
            tt(MODs[b][:, c0:c1, :], P3[:, 2 * c0:2 * c1].rearrange("p (j s) -> p j s", s=2),
               BMODs[b][:, c0:c1].unsqueeze(2).to_broadcast([128, c1 - c0, 2]), ALU.add, ["P3", f"BMOD{b}"], [f"MOD{b}"])
            yield

        def run_layers():
          for l in range(n_layers):
              last = l == n_layers - 1
              Ld = L[l]
              dma_sp(VEC[:], Ld["vec"], [], ["VEC"])
              dma_sp(BC[:], Ld["bc"], [], ["BC"])
              dma_pool(SM[:].rearrange("p a b -> p (a b)"), Ld["sm"], [], ["SM"])
              WST = SM[:, 0:4, :]
              WPBD = SM[:, 4:6, :]

              MOD = MODs[l % 2]
              MODn = f"MOD{l % 2}"
              late0 = [None]
              if l == 0:
                  for _ in mod_gen(0, 0, 8):
                      pass
                  late0[0] = mod_gen(0, 8, 24)
              pre = [mod_gen(l + 1) if not last else None]

              def _adv(box, n):
                  for _ in range(n):
                      if box[0] is not None:
                          try:
                              next(box[0])
                          except StopIteration:
                              box[0] = None

              def advance_mod(n=1):
                  _adv(pre, n)
              modtick = [0]
              stt(A1[:], MOD[:, 8:16, :], 1.0, VEC[:, 0:8].unsqueeze(2).to_broadcast([128, 8, 2]), ALU.add, ALU.mult,
                  [MODn, "VEC"], ["A1"])
              ts(GS[:, 0:1], VEC[:, 16:17], 0.125, None, ALU.mult, ALU.bypass, ["VEC"], ["GS"])
              ts(GS[:, 1:2], VEC[:, 17:18], 1.0, None, ALU.mult, ALU.bypass, ["VEC"], ["GS"])
              ts(GS[:, 2:3], VEC[:, 18:19], 0.125, None, ALU.mult, ALU.bypass, ["VEC"], ["GS"])
              ts(GS[:, 3:4], VEC[:, 19:20], 1.0, None, ALU.mult, ALU.bypass, ["VEC"], ["GS"])
              act(GS[:, 4:8], BC[:, 0:4], AF.Exp, ["BC"], ["GS"])
              SH1, G1, SH2, G2 = MOD[:, 0:8, :], MOD[:, 16:24, :], MOD[:, 24:32, :], MOD[:, 40:48, :]
              stage("modload")
              tap(f"mod{l}", MOD[:], [128, 48, 2], F32, [MODn])

              stage("mod")
              def do_norms(groups, Amul, SHv, dst_fn, tag):
                  def bank(c):
                      return (PSV["P2a"], "P2a") if c % 2 == 0 else (PSV["P2b"], "P2b")

                  def stat_k(c, k):
                      s0, Lc = GROUPS[c]
                      psn, psr = bank(c)
                      sq = SQ[k % 2]
                      act(sq[:, 0:Lc], X[:, k, s0:s0 + Lc], AF.Square, [xres(c)], [f"SQ{k % 2}"])
                      mm(psn[:, 0:Lc], ONES, sq[:, 0:Lc], k == 0, k == 7, [f"SQ{k % 2}", "CBF"], [psr])

                  def stat_fin(c):
                      s0, Lc = GROUPS[c]
                      psn, psr = bank(c)
                      act(Fs[0][:, 0:Lc], psn[:, 0:Lc], AF.Ln, [psr], ["F0"], scale=1.0 / D, bias=EPS)
                      act(psn[:, 0:Lc], Fs[0][:, 0:Lc], AF.Exp, ["F0"], [psr], scale=-0.5)

                  def apply_k(c, k):
                      s0, Lc = GROUPS[c]
                      s = sidx(c)
                      psn, psr = bank(c)
                      tmp, tr_ = (Fs[2], "F2") if k % 2 == 0 else (Fs[3], "F3")
                      tt(tmp[:, 0:Lc], psn[:, 0:Lc], X[:, k, s0:s0 + Lc], ALU.mult, [xres(c), psr], [tr_])
                      dst, dres = dst_fn(k, c)
                      act(dst, tmp[:, 0:Lc], AF.Identity, [tr_, MODn, tag], [dres], scale=Amul[:, k, s:s + 1],
                          bias=SHv[:, k, s:s + 1])

                  n = len(groups)
                  for k in range(8):
                      stat_k(groups[0], k)
                  stat_fin(groups[0])
                  for i in range(n):
                      for k in range(8):
                          if i + 1 < n:
                              stat_k(groups[i + 1], k)
                          apply_k(groups[i], k)
                      if i + 1 < n:
                          stat_fin(groups[i + 1])

              groups_all = [0, 1, 2, 3] if last else [0, 1, 2, 3, 4]

              dma_pool(TAB[:, 0:4096], rope_d, [], ["TAB"])
              COS = TAB[:, 0:2048]
              SIN = TAB[:, 2048:4096]
              S.op("dve", lambda: nc.vector.memset(VA[:, :, :, 64:65], 1.0), [], ["VAones"])
              S.op("dve", lambda: nc.vector.memset(VD[:, :, :, 64:65], 1.0), [], ["VDones"])

              def hcols(c, half):
                  s0, Lc = GROUPS[c]
                  return s0 - HSTART[half], Lc

              for half in range(2):
                  hgroups = HALVES[half]
                  def hdst(k, c):
                      h0, Lc = hcols(c, half)
                      return HB[:, k, h0:h0 + Lc], hbres(c)

                  do_norms(hgroups, A1, SH1, hdst, "A1")
                  if half == 0:
                      tap(f"hT{l}", HB[:, :, 0:1024], [128, 8, 1024], BF16, [("HB", 0), ("HB", 1)])
                  stage(f"norm1_{half}")
                  combos = []
                  for a in range(4):
                      js = [2 * a, 2 * a + 1] if a < 3 else [6]
                      for ji, j in enumerate(js):
                          for c in hgroups:
                              if last and c == 4 and j in (0, 1, 3, 4):
                                  continue
                              combos.append((a, len(js), ji, j, c))
                  fm_slot = {}

                  def fm_piece(a, njs):
                      if a not in fm_slot:
                          slot, sr = load_piece(l, NP_FM + a, 8 * 128 * njs)
                          fm_slot[a] = (slot[:, 0:8 * 128 * njs].rearrange("p (k n) -> p k n", k=8), sr, ring_serial[sr])
                      assert ring_serial[fm_slot[a][1]] == fm_slot[a][2], "FM weight slot recycled while still in use"
                      return fm_slot[a][0], fm_slot[a][1]

                  def fm_info(k):
                      a, njs, ji, j, c = combos[k]
                      s0, Lc = GROUPS[c]
                      h0, _ = hcols(c, half)
                      pfn = ["P0a", "P0b", "P1a"][k % 3]
                      phn = ["P1b", "P2a"][k % 2]
                      prn = "P2b"
                      gcol = {0: 0, 1: 0, 2: 1, 3: 2, 4: 2, 5: 3, 6: 3}[j]
                      if j < 2:
                          dst, dres = QT[:, j, s0:s0 + Lc], ("QT", j, c)
                      elif j == 2:
                          dst, dres = KT[:, 0, s0:s0 + Lc], ("KT", 0, c)
                      elif j < 5:
                          dst, dres = QT[:, j - 1, s0:s0 + Lc], ("QT", j - 1, c)
                      else:
                          dst, dres = KT[:, j - 4, s0:s0 + Lc], ("KT", j - 4, c)
                      return dict(a=a, njs=njs, ji=ji, j=j, c=c, s0=s0, Lc=Lc, h0=h0, pfn=pfn, phn=phn, prn=prn,
                                  pf=PSV[pfn][:, 0:Lc], ph=PSV[phn][:, 0:Lc], pr=PSV[prn][:, 0:Lc],
                                  gain=GS[:, gcol:gcol + 1], dst=dst, dres=dres, rope=(j < 3 and c != 4),
                                  sq=SQ[k % 2], sqn=f"SQ{k % 2}")

                  def fm_head(k):
                      f = fm_info(k)
                      sv, sr = fm_piece(f["a"], f["njs"])
                      for kk in range(8):
                          mm(f["pf"], sv[:, kk, f["ji"] * 128:(f["ji"] + 1) * 128], HB[:, kk, f["h0"]:f["h0"] + f["Lc"]],
                             kk == 0, kk == 7, [sr, ("HB", f["c"])], [f["pfn"]])
                      act(f["sq"][:, 0:f["Lc"]], f["pf"], AF.Square, [f["pfn"]], [f["sqn"]])

                  def fm_qg(k):
                      f = fm_info(k)
                      if f["rope"]:
                          act(QG[:, 0:f["Lc"]], f["pf"], AF.Identity, [f["pfn"], "GS"], ["QG"], scale=f["gain"])

                  def fm_mid(k):
                      f = fm_info(k)
                      Lc, s0 = f["Lc"], f["s0"]
                      mm(f["ph"], BDONES, f["sq"][:, 0:Lc], True, True, [f["sqn"], "CBF"], [f["phn"]])
                      if f["rope"]:
                          mm(f["pr"], ROTT, QG[:, 0:Lc], True, True, ["QG", "CBF"], [f["prn"]])
                          tt(Fs[2][:, 0:Lc], QG[:, 0:Lc], COS[:, s0:s0 + Lc], ALU.mult, ["QG", "TAB"], ["F2"])

                  def fm_tail(k):
                      f = fm_info(k)
                      Lc, s0 = f["Lc"], f["s0"]
                      act(Fs[0][:, 0:Lc], f["ph"], AF.Ln, [f["phn"]], ["F0"], scale=1.0 / 64, bias=EPS)
                      act(Fs[1][:, 0:Lc], Fs[0][:, 0:Lc], AF.Exp, ["F0"], ["F1"], scale=-0.5)
                      if f["rope"]:
                          act(RQ[:, 0:Lc], f["pr"], AF.Copy, [f["prn"]], ["RQ"])
                          tt(Fs[3][:, 0:Lc], RQ[:, 0:Lc], SIN[:, s0:s0 + Lc], ALU.mult, ["RQ", "TAB"], ["F3"])
                          tt(Fs[2][:, 0:Lc], Fs[2][:, 0:Lc], Fs[3][:, 0:Lc], ALU.add, ["F2", "F3"], ["F2"])
                          tt(f["dst"], Fs[2][:, 0:Lc], Fs[1][:, 0:Lc], ALU.mult, ["F2", "F1"], [f["dres"]])
                      else:
                          stt(f["dst"], f["pf"], f["gain"], Fs[1][:, 0:Lc], ALU.mult, ALU.mult, [f["pfn"], "GS", "F1"], [f["dres"]])

                  nfm = len(combos)
                  fm_head(0)
                  fm_qg(0)
                  for k in range(nfm):
                      if k + 1 < nfm:
                          fm_head(k + 1)
                      fm_mid(k)
                      if k + 1 < nfm:
                          fm_qg(k + 1)
                      fm_tail(k)
                      if half == 0:
                          _adv(late0, 1)
                  _adv(late0, 64)
                  stage(f"fm_{half}")
                  tiles_h = []
                  for c in hgroups:
                      s0, Lc = GROUPS[c]
                      tiles_h += [(t, c) for t in range(s0 // 128, (s0 + Lc) // 128)]
                  tcount = [0]

                  def tm_proj(t, c, sv, srs, ncol):
                      sr, ser = srs
                      assert ring_serial[sr] == ser, "TM weight slot recycled while still in use"
                      h0 = t * 128 - HSTART[half]
                      pn = ["P0a", "P0b", "P1a", "P1b"][tcount[0] % 4]
                      tcount[0] += 1
                      pp = PSV[pn][:, 0:ncol]
                      for k in range(8):
                          mm(pp, HB[:, k, h0:h0 + 128], sv[:, k, :], k == 0, k == 7, [sr, hbres(c)], [pn])
                      return pp, pn

                  def tm_piece(pi):
                      ncol = 128 if pi == 4 else 256
                      slot, sr = load_piece(l, NP_TM + pi, 8 * ncol)
                      return slot[:, 0:8 * ncol].rearrange("p (k n) -> p k n", k=8), (sr, ring_serial[sr]), ncol

                  svU, srU, _ = tm_piece(0)
                  for (t, c) in tiles_h:
                      if last and c == 4:
                          continue
                      pp, pn = tm_proj(t, c, svU, srU, 256)
                      act(OB[:, t, :], pp, AF.Gelu_apprx_tanh, [pn], [("OB", t)])
                  svV, srV, _ = tm_piece(1)
                  svC, srC, _ = tm_piece(2)
                  svD, srD, _ = tm_piece(3)
                  svA, srA, _ = tm_piece(4)

                  def emit_z(t):
                      vn = VN[t % 2]
                      pzn = "P2a" if t % 2 == 0 else "P2b"
                      pz = PSV[pzn]
                      for g in range(4):
                          mm(pz[:, g * 64:(g + 1) * 64], WST[:, g, :], vn[:, g * 64:(g + 1) * 64], True, True,
                             [f"VN{t % 2}", "SM"], [pzn])
                      guf, gufn = (Fs[2], "F2") if t % 2 == 0 else (Fs[3], "F3")
                      act(guf[:, 0:256], OB[:, t, :], AF.Copy, [("OB", t)], [gufn])
                      for g in range(4):
                          stt(OB[:, t, g * 64:(g + 1) * 64], pz[:, g * 64:(g + 1) * 64], VEC[:, 22 + g:23 + g],
                              guf[:, g * 64:(g + 1) * 64], ALU.add, ALU.mult, [pzn, "VEC", gufn], [("OB", t)])

                  pending_z = []
                  for pi0 in range(0, len(tiles_h), 2):
                      pair = tiles_h[pi0:pi0 + 2]
                      fulls = [(t, c) for (t, c) in pair if not (last and c == 4)]
                      for (t, c) in fulls:
                          pp, pn = tm_proj(t, c, svV, srV, 256)
                          gv, gvn = (Fs[0], "F0") if t % 2 == 0 else (Fs[1], "F1")
                          gv = gv[:, 0:256]
                          act(gv, pp, AF.Gelu_apprx_tanh, [pn], [gvn])
                          o = (t % 2) * 16
                          stn = f"SMALL{t % 2}"
                          S.op("dve", lambda gv=gv, o=o: nc.vector.bn_stats(out=SMALL[:, o:o + 6], in_=gv), [gvn], [stn])
                          S.op("dve", lambda o=o: nc.vector.bn_aggr(out=SMALL[:, o + 8:o + 10], in_=SMALL[:, o:o + 6]),
                               [stn], [stn + "b"])
                      for tz in pending_z:
                          emit_z(tz)
                      pending_z = []
                      if len(fulls) == 2:
                          act(SMALL[:, 10:27:16], SMALL[:, 9:26:16], AF.Ln, ["SMALL0b", "SMALL1b"], ["SMALL0c", "SMALL1c"], bias=EPS)
                          act(SMALL[:, 11:28:16], SMALL[:, 10:27:16], AF.Exp, ["SMALL0c", "SMALL1c"], ["SMALL0d", "SMALL1d"], scale=-0.5)
                      for (t, c) in fulls:
                          o = (t % 2) * 16
                          stn = f"SMALL{t % 2}"
                          gv, gvn = (Fs[0], "F0") if t % 2 == 0 else (Fs[1], "F1")
                          gv = gv[:, 0:256]
                          if len(fulls) != 2:
                              act(SMALL[:, o + 10:o + 11], SMALL[:, o + 9:o + 10], AF.Ln, [stn + "b"], [stn + "c"], bias=EPS)
                              act(SMALL[:, o + 11:o + 12], SMALL[:, o + 10:o + 11], AF.Exp, [stn + "c"], [stn + "d"], scale=-0.5)
                          ts(gv, gv, SMALL[:, o + 8:o + 9], SMALL[:, o + 11:o + 12], ALU.subtract, ALU.mult,
                             [gvn, stn + "b", stn + "d"], [gvn])
                          tt(VN[t % 2][:], gv, BC[:, 4:260], ALU.mult, [gvn, "BC"], [f"VN{t % 2}"])
                      for (t, c) in fulls:
                          pp, pn = tm_proj(t, c, svC, srC, 256)
                          act(CY[:, t, :], pp, AF.Copy, [pn], [("CY", t)])
                      for (t, c) in pair:
                          pp, pn = tm_proj(t, c, svD, srD, 256)
                          cpv(VD[:, t, :, 0:64], pp.rearrange("p (h e) -> p h e", h=4), [pn, "VDones"], [("VD", t)])
                          pp, pn = tm_proj(t, c, svA, srA, 128)
                          cpv(VA[:, t, :, 0:64], pp.rearrange("p (h e) -> p h e", h=2), [pn, "VAones"], [("VA", t)])
                      pending_z = [t for (t, c) in fulls]
                  for tz in pending_z:
                      emit_z(tz)
              _adv(late0, 64)
              stage("p1")
              tap(f"qt{l}", QT[:, :, :], [128, 4, NTOK], BF16, [])
              tap(f"kt{l}", KT[:, :, :], [128, 3, NTOK], BF16, [])
              tap(f"ob{l}", OB[:, :, :], [128, 18, 256], BF16, [])
              tap(f"cy{l}", CY[:, :, :], [128, 18, 256], BF16, [])
              tap(f"vd{l}", VD[:, :, :, :], [128, 18, 4, 66], BF16, [])

              for half in range(2):
                  hgroups = [c for c in HALVES[half] if c in groups_all]
                  tiles_h = []
                  for c in hgroups:
                      s0, Lc = GROUPS[c]
                      tiles_h += [(t, c) for t in range(s0 // 128, (s0 + Lc) // 128)]
                  dma_pool(TAB[:, 0:2560], pm_d, [], ["TAB"])
                  PMv = TAB[:, 0:2560].rearrange("p (a t) -> p a t", a=20)
                  for (t, c) in tiles_h:
                      h0 = t * 128 - HSTART[half]
                      pv = PST[:, (t % 2) * 512:(t % 2) * 512 + 256]
                      pvn = "PST"
                      for cc in range(2):
                          tr(pv[:, cc * 128:(cc + 1) * 128], OB[:, t, cc * 128:(cc + 1) * 128], IDB, [("OB", t), "CBF"], [pvn])
                      cpv(HB[:, 2:4, h0:h0 + 128], pv.rearrange("p (c t) -> p c t", c=2), [pvn], [hbres(c)])
                  for c in hgroups:
                      s0, Lc = GROUPS[c]
                      h0, _ = hcols(c, half)
                      first_t = 16 if c == 4 else 0
                      last_t = 17 if c == 4 else 15
                      for t in range(s0 // 128, (s0 + Lc) // 128):
                          lo = (t * 128 - s0)
                          for g in range(4):
                              srcs = []
                              if t > first_t:
                                  srcs.append((t - 1, 3))
                              srcs.append((t, 1 if t == first_t else (2 if t == last_t else 0)))
                              if t < last_t:
                                  srcs.append((t + 1, 4))
                              o = (g % 2) * 64
                              pcn = "P0a" if g < 2 else "P0b"
                              outp = P0[o:o + 64, (g // 2) * 512 + lo:(g // 2) * 512 + lo + 128]
                              for i, (tj, v) in enumerate(srcs):
                                  mm(outp, CY[:, tj, g * 64:(g + 1) * 64], PMv[:, g * 5 + v, :], i == 0, i == len(srcs) - 1,
                                     [("CY", tj), "TAB"], [pcn])
                      for cc in range(2):
                          pcn = "P0a" if cc == 0 else "P0b"
                          pln = "P1a" if cc == 0 else "P1b"
                          act(PC[:, cc, 0:Lc], P0[:, cc * 512:cc * 512 + Lc], AF.Copy, [pcn], [("PC", cc)])
                          mm(P1[:, cc * 512:cc * 512 + Lc], WPBD[:, cc, :], PC[:, cc, 0:Lc], True, True, [("PC", cc), "SM"], [pln])
                          act(HB[:, 4 + cc, h0:h0 + Lc], P1[:, cc * 512:cc * 512 + Lc], AF.Identity, [pln, "VEC"], [hbres(c)],
                              scale=VEC[:, 20 + cc:21 + cc])
                  stage(f"p2a_{half}")
                  acount = [0]

                  def att_scores(job):
                      i = acount[0] % 2
                      acount[0] += 1
                      job["i"] = i
                      Sp = P0 if i == 0 else P1
                      Sn = ["P0a", "P0b"] if i == 0 else ["P1a", "P1b"]
                      chunks = job["chunks"]
                      n = len(chunks)
                      for ci, (kt, bias) in enumerate(chunks):
                          o = Sp[:, ci * 128:(ci + 1) * 128]
                          bn = [Sn[ci // 4]]
                          mm(o, job["K"](kt), QZ[:, job["slot"], :], True, bias is None,
                             [("QZ", job["slot"]), ("KT", job["kidx"], grp_of_tile(kt))], bn)
                          if bias is not None:
                              mm(o, IDB, bias, False, True, ["CBF", "TAB"], bn)
                          if ci == 3 and n > 4:
                              act(PT[i][:, 0:512], Sp[:, 0:512], AF.Exp, [Sn[0]], [f"PT{i}a"])
                      if n > 4:
                          act(PT[i][:, 512:n * 128], Sp[:, 512:n * 128], AF.Exp, [Sn[1]], [f"PT{i}b"])
                      else:
                          act(PT[i][:, 0:n * 128], Sp[:, 0:n * 128], AF.Exp, [Sn[0]], [f"PT{i}a"])

                  def att_pv(job):
                      i = job["i"]
                      chunks = job["chunks"]
                      n = len(chunks)
                      h = job["h"]
                      for ci, (kt, bias) in enumerate(chunks):
                          mm(job["O"][:, h * 66:h * 66 + 65], PT[i][:, ci * 128:(ci + 1) * 128], job["V"](kt), ci == 0, ci == n - 1,
                             [f"PT{i}a" if ci < 4 else f"PT{i}b", (job["vname"], kt), job["vname"] + "ones"], [job["On"]])

                  def finish_tile(t, c, Otile, Oname, sink, cat0):
                      h0 = t * 128 - HSTART[half]
                      Ov = Otile[:, 0:264].rearrange("p (h e) -> p h e", h=4)
                      o = 32 + (t % 2) * 8
                      dn = f"DEN{t % 2}"
                      if sink:
                          tt(SMALL[:, o:o + 4], Ov[:, :, 64], GS[:, 4:8], ALU.add, [Oname, "GS"], [dn])
                      else:
                          cpv(SMALL[:, o:o + 4], Ov[:, :, 64], [Oname], [dn])
                      S.op("dve", lambda o=o: nc.vector.reciprocal(out=SMALL[:, o + 4:o + 8], in_=SMALL[:, o:o + 4]), [dn], [dn + "r"])
                      ot = OT[t % 2]
                      otn = f"OT{t % 2}"
                      tt(ot[:].rearrange("p (h e) -> p h e", h=4), Ov[:, :, 0:64],
                         SMALL[:, o + 4:o + 8].unsqueeze(2).to_broadcast([128, 4, 64]), ALU.mult, [Oname, dn + "r"], [otn])
                      pv = PST[:, (t % 2) * 512:(t % 2) * 512 + 256]
                      pvn = "PST"
                      for cc in range(2):
                          tr(pv[:, cc * 128:(cc + 1) * 128], ot[:, cc * 128:(cc + 1) * 128], IDB, [otn, "CBF"], [pvn])
                      cpv(HB[:, cat0:cat0 + 2, h0:h0 + 128], pv.rearrange("p (c t) -> p c t", c=2), [pvn], [hbres(c)])

                  def run_attn(jobs, sink, cat0):
                      S.op("dve", lambda: nc.vector.memset(QZ, 0.0), [], [("PC", 0), ("PC", 1)] + [("QZ", i) for i in range(8)])
                      prev = None
                      for job in jobs + [None]:
                          if job is not None:
                              if job["h"] == 0:
                                  base = (job["t"] % 2) * 4
                                  names = [("QZ", base + i) for i in range(4)]
                                  for (dst_ap, src_ap, qr) in job["qcopies"]:
                                      cpv(dst_ap, src_ap, names + qr, names)
                              att_scores(job)
                          if prev is not None:
                              att_pv(prev)
                              if prev["last"]:
                                  finish_tile(prev["t"], prev["c"], prev["O"], prev["On"], sink, cat0)
                                  modtick[0] += 1
                                  if modtick[0] % 3 != 0:
                                      advance_mod()
                          prev = job

                  def tcols(t):
                      return slice(t * 128, (t + 1) * 128)

                  jobs = []
                  for (t, c) in tiles_h:
                      Ot, On = (PSV["P2a"], "P2a") if t % 2 == 0 else (PSV["P2b"], "P2b")
                      for h in range(4):
                          kv, g = h // 2, h % 2
                          po = kv * 64
                          if c == 4:
                              chunks = [(16, None), (17, None)]
                          else:
                              chunks = []
                              if t > 0:
                                  chunks.append((t - 1, MASKP))
                              chunks.append((t, None))
                              if t < 15:
                                  chunks.append((t + 1, MASKN))
                              chunks += [(16, None), (17, None)]
                          jobs.append({"t": t, "c": c, "h": h, "K": (lambda kt: KT[:, 0, tcols(kt)]), "po": po,
                                       "kidx": 0, "qres": ("QT", g, c), "vname": "VA",
                                       "qcopies": [(QZ[0:64, (t % 2) * 4:(t % 2) * 4 + 2, :], QT[0:64, 0:2, tcols(t)],
                                                    [("QT", 0, c), ("QT", 1, c)]),
                                                   (QZ[64:128, (t % 2) * 4 + 2:(t % 2) * 4 + 4, :], QT[64:128, 0:2, tcols(t)],
                                                    [("QT", 0, c), ("QT", 1, c)])],
                                       "slot": (t % 2) * 4 + h,
                                       "Q": QT[po:po + 64, g, tcols(t)], "V": (lambda kt, kv=kv: VA[:, kt, kv, 0:65]),
                                       "chunks": chunks, "O": Ot, "On": On, "last": h == 3})
                  run_attn(jobs, True, 0)
                  stage(f"attA_{half}")
                  dma_pool(TAB[:, 0:5376], Ld["dtab"], [], ["TAB"])
                  DT = TAB[:, 0:5376].rearrange("p (h s q) -> p h s q", h=4, s=21)
                  jobs = []
                  for (t, c) in tiles_h:
                      Ot, On = (PSV["P2a"], "P2a") if t % 2 == 0 else (PSV["P2b"], "P2b")
                      if c == 4:
                          deltas, interior = [], False
                      elif 2 <= t <= 13:
                          deltas, interior = [-2, -1, 0, 1, 2], True
                      elif t == 0:
                          deltas, interior = [0, 1, 2, 3], False
                      elif t == 1:
                          deltas, interior = [-1, 0, 1, 2], False
                      elif t == 14:
                          deltas, interior = [-2, -1, 0, 1], False
                      else:
                          deltas, interior = [-3, -2, -1, 0], False
                      for h in range(4):
                          po = (h % 2) * 64
                          ch = h // 2
                          chunks = []
                          for dl in deltas:
                              i0 = _dslot(interior, 2 * dl + 7)
                              i1 = _dslot(interior, 2 * dl + 6)
                              chunks.append((t + dl, DT[:, h, i0:i1 + 1:(i1 - i0), :]))
                          chunks += [(16, None), (17, None)]
                          jobs.append({"t": t, "c": c, "h": h, "K": (lambda kt, ch=ch: KT[:, 1 + ch, tcols(kt)]), "po": po,
                                       "kidx": 1 + ch, "qres": ("QT", 2 + ch, c), "vname": "VD",
                                       "qcopies": [(QZ[0:64, (t % 2) * 4:(t % 2) * 4 + 4:2, :], QT[0:64, 2:4, tcols(t)],
                                                    [("QT", 2, c), ("QT", 3, c)]),
                                                   (QZ[64:128, (t % 2) * 4 + 1:(t % 2) * 4 + 4:2, :], QT[64:128, 2:4, tcols(t)],
                                                    [("QT", 2, c), ("QT", 3, c)])],
                                       "slot": (t % 2) * 4 + h,
                                       "Q": QT[po:po + 64, 2 + ch, tcols(t)], "V": (lambda kt, h=h: VD[:, kt, h, 0:65]),
                                       "chunks": chunks, "O": Ot, "On": On, "last": h == 3})
                  run_attn(jobs, False, 6)
                  if half == 0:
                      tap(f"cat{l}", HB[:, :, 0:1024], [128, 8, 1024], BF16, [])
                  stage(f"attD_{half}")
                  wcount = 0
                  for a in range(4):
                      slot, sr = load_piece(l, NP_WO + a)
                      sv = slot[:].rearrange("p (k n) -> p k n", k=8)
                      for di in range(2):
                          dch = 2 * a + di
                          for c in hgroups:
                              s0, Lc = GROUPS[c]
                              h0, _ = hcols(c, half)
                              s = sidx(c)
                              pn = ["P0a", "P0b", "P1a", "P1b"][wcount % 4]
                              wcount += 1
                              pw = PSV[pn][:, 0:Lc]
                              for k in range(8):
                                  mm(pw, sv[:, k, di * 128:(di + 1) * 128], HB[:, k, h0:h0 + Lc], k == 0, k == 7,
                                     [sr, hbres(c)], [pn])
                              stt(X[:, dch, s0:s0 + Lc], pw, G1[:, dch, s:s + 1], X[:, dch, s0:s0 + Lc], ALU.mult, ALU.add,
                                  [pn, MODn, xres(c)], [xres(c)])
                  stage(f"wout_{half}")
                  if half == 1:
                      S.barrier()
              advance_mod(64)
              stage("wout")
              tap(f"x1_{l}", X[:], [128, 8, NTOK], F32, [])

              stt(A2[:], MOD[:, 32:40, :], 1.0, VEC[:, 8:16].unsqueeze(2).to_broadcast([128, 8, 2]), ALU.add, ALU.mult,
                  [MODn, "VEC"], ["A2"])
              def h2dst(k, c):
                  s0, Lc = GROUPS[c]
                  if c < 2:
                      return HB[:, k, s0:s0 + Lc], hbres(c)
                  return H2B[:, k, s0 - 1024:s0 - 1024 + Lc], ("H2B", c)

              do_norms(groups_all, A2, SH2, h2dst, "A2")
              stage("norm2")
              gcount = 0
              dcount = 0
              for bi, blk in enumerate(FFBLOCKS):
                  for fl, f in enumerate(blk):
                      slot, sr = load_piece(l, NP_GU + f)
                      sv = slot[:].rearrange("p (u k n) -> p u k n", u=2, k=8)
                      for c in groups_all:
                          s0, Lc = GROUPS[c]
                          i = gcount % 2
                          gcount += 1
                          pgn, pun = ("P0a", "P0b") if i == 0 else ("P1a", "P1b")
                          pg, pu = PSV[pgn][:, 0:Lc], PSV[pun][:, 0:Lc]
                          for k in range(8):
                              hsrc, hres = h2dst(k, c)
                              mm(pg, sv[:, 0, k, :], hsrc, k == 0, k == 7, [sr, hres], [pgn])
                          for k in range(8):
                              hsrc, hres = h2dst(k, c)
                              mm(pu, sv[:, 1, k, :], hsrc, k == 0, k == 7, [sr, hres], [pun])
                          sg, sgn = (Fs[0], "F0") if i == 0 else (Fs[1], "F1")
                          act(sg[:, 0:Lc], pg, AF.Silu, [pgn], [sgn])
                          tt(ACTB[:, fl, s0:s0 + Lc], pu, sg[:, 0:Lc], ALU.mult, [sgn, pun], [("ACTB", fl, c)])
                  nf = len(blk)
                  for dch in range(8):
                      slot, sr = load_piece(l, NP_WD + bi * 8 + dch, nf * 128)
                      sv = slot[:, 0:nf * 128].rearrange("p (f n) -> p f n", f=nf)
                      for c in groups_all:
                          s0, Lc = GROUPS[c]
                          s = sidx(c)
                          pn = ["P2a", "P2b", "P3"][dcount % 3]
                          dcount += 1
                          pd = PSV[pn][:, 0:Lc]
                          for fl in range(nf):
                              mm(pd, sv[:, fl, :], ACTB[:, fl, s0:s0 + Lc], fl == 0, fl == nf - 1, [sr, ("ACTB", fl, c)], [pn])
                          stt(X[:, dch, s0:s0 + Lc], pd, G2[:, dch, s:s + 1], X[:, dch, s0:s0 + Lc], ALU.mult, ALU.add,
                              [pn, MODn, xres(c)], [xres(c)])
              S.barrier()
              stage("ffn")
              tap(f"x2_{l}", X[:], [128, 8, NTOK], F32, [])


        try:
            run_layers()
        except _Stop:
            pass
        for t in range(16):
            Pt = P0 if t % 2 == 0 else P1
            pr = ["P0a", "P0b"] if t % 2 == 0 else ["P1a", "P1b"]
            for k in range(8):
                tr(Pt[:, k * 128:(k + 1) * 128], X[:, k, t * 128:(t + 1) * 128], IDF[:], [xres(t // 4), "IDF"], pr)
            stg = SCR[:, (t % 2) * 1024:(t % 2 + 1) * 1024]
            stg_r = ["F0", "F1"] if t % 2 == 0 else ["F2", "F3"]
            if t % 2 == 0:
                act(stg, Pt[:], AF.Copy, pr, stg_r)
            else:
                cpv(stg, Pt[:], pr, stg_r)
            dma_sp(out_d[t * 128:(t + 1) * 128, :], stg, stg_r, [("out", t)])
        S.emit(st)
    return nc, tap_d


_CACHE = {}


def prep_inputs(inp, n_layers=2):
    inp = {k: np.asarray(v, dtype=np.float32) for k, v in inp.items()}
    cbf, rope, pm = _consts()
    shared = {"cbf": np.ascontiguousarray(cbf.reshape(128, -1)), "idf": np.eye(128, dtype=np.float32),
              "rope": np.ascontiguousarray(rope.reshape(128, -1)), "pm": np.ascontiguousarray(pm.reshape(128, -1))}
    for l in range(n_layers):
        la = _layer_arrays(inp, l)
        for k, v in la.items():
            shared[f"{k}{l}"] = v
    maps = []
    for b in range(8):
        m = dict(shared)
        m["x"] = np.ascontiguousarray(inp["x"][b])
        m["ctx"] = np.ascontiguousarray(inp["ctx"][b])
        cv = np.zeros((128, 8, 2), np.float32)
        cv[:, :, 0] = inp["c"][b].reshape(8, 128).T
        cv[:, :, 1] = inp["c_ctx"].reshape(8, 128).T
        m["cvec"] = np.ascontiguousarray(cv.reshape(128, 16))
        maps.append(m)
    return maps


def kernel(**inputs):
    if "nc" not in _CACHE:
        _CACHE["nc"] = build(2)[0]
    nc = _CACHE["nc"]
    maps = prep_inputs(inputs, 2)
    res = run_bass_kernel_spmd(nc, maps, core_ids=list(range(8)))
    return np.stack([np.asarray(r["out"], dtype=np.float32) for r in res.results], 0)
```
